# Optimizing a Trainium2 kernel written in Bass

```python
import jax, jax.numpy as jnp
from jax import lax
import numpy as np

D_MODEL = 1024
BATCH = 2
SEQ = 8192
DEPTH = 1
DEC_BATCH = 128
DEC_SEQ = 8
PAST_LEN = 16384
PAGE_SIZE = 128

HEAD_DIM = 64
ATT_HEADS = 8
KV_HEADS = 2
Q_PER_KV = ATT_HEADS // KV_HEADS
ATT_WIDTH = ATT_HEADS * HEAD_DIM
KV_WIDTH = KV_HEADS * HEAD_DIM
RWKV_HEADS = 8
RWKV_WIDTH = RWKV_HEADS * HEAD_DIM
MIX_WIDTH = ATT_WIDTH + RWKV_WIDTH
WINDOW = 128
BLOCK = 128
DECAY_LORA = 64
ICL_LORA = 64
GATE_LORA = 128
RWKV_BLOCK = 3 * RWKV_WIDTH + DECAY_LORA + ICL_LORA + GATE_LORA
ATT_PROJ = ATT_WIDTH + 2 * KV_WIDTH
PROJ_WIDTH = ATT_PROJ + RWKV_BLOCK
D_FF = -(-8 * D_MODEL // (3 * 256)) * 256
RMS_EPS = 1e-6
GN_EPS = 64e-5
L2_EPS = 1e-12
NEG_INF = -1e30

PROJ_SPLITS = [ATT_WIDTH, ATT_WIDTH + KV_WIDTH, ATT_PROJ]
RWKV_SPLITS = [RWKV_WIDTH, 2 * RWKV_WIDTH, 3 * RWKV_WIDTH,
               3 * RWKV_WIDTH + DECAY_LORA, 3 * RWKV_WIDTH + DECAY_LORA + ICL_LORA]

kernel_name = "hymba_swa_sink_rwkv7_decode_step"


def rmsnorm(x, g):
    xf = x.astype(jnp.float32)
    y = xf * lax.rsqrt(jnp.mean(xf * xf, axis=-1, keepdims=True) + RMS_EPS)
    return (y * g.astype(jnp.float32)).astype(x.dtype)


def attend_with_sinks(q, k, v, mask, sinks):
    s = jnp.einsum('...qkgd,...skd->...kgqs', q, k).astype(jnp.float32) * (HEAD_DIM ** -0.5)
    s = jnp.where(mask, s, NEG_INF)
    sink = jnp.broadcast_to(sinks.astype(jnp.float32).reshape(KV_HEADS, Q_PER_KV, 1, 1),
                            s.shape[:-1] + (1,))
    p = jax.nn.softmax(jnp.concatenate([s, sink], axis=-1), axis=-1)[..., :-1]
    return jnp.einsum('...kgqs,...skd->...qkgd', p.astype(v.dtype), v)


def swa_prompt(q, k, v, sinks):
    B, T = q.shape[0], q.shape[1]
    nb = T // BLOCK
    qb = q.reshape(B, nb, BLOCK, KV_HEADS, Q_PER_KV, HEAD_DIM)
    kb = k.reshape(B, nb, BLOCK, KV_HEADS, HEAD_DIM)
    vb = v.reshape(B, nb, BLOCK, KV_HEADS, HEAD_DIM)
    pad = ((0, 0), (1, 0), (0, 0), (0, 0), (0, 0))
    kk = jnp.concatenate([jnp.pad(kb, pad)[:, :-1], kb], axis=2)
    vv = jnp.concatenate([jnp.pad(vb, pad)[:, :-1], vb], axis=2)
    qi = jnp.arange(BLOCK)[:, None] + BLOCK
    kj = jnp.arange(2 * BLOCK)[None, :]
    band = (kj <= qi) & (qi - kj < WINDOW)
    first = (jnp.arange(nb) == 0)[:, None, None]
    mask = band[None] & ~(first & (kj < BLOCK)[None])
    o = attend_with_sinks(qb, kk, vv, mask[:, None, None], sinks)
    return o.reshape(B, T, ATT_WIDTH)


def swa_sample(q, k, v, k_buf, v_buf, sinks):
    B, T = q.shape[0], q.shape[1]
    wb = k_buf.shape[1]
    kk = jnp.concatenate([k_buf.astype(k.dtype), k], axis=1)
    vv = jnp.concatenate([v_buf.astype(v.dtype), v], axis=1)
    q_rel = wb + jnp.arange(T)[:, None]
    k_rel = jnp.arange(wb + T)[None, :]
    mask = (k_rel <= q_rel) & (q_rel - k_rel < WINDOW)
    o = attend_with_sinks(q.reshape(B, T, KV_HEADS, Q_PER_KV, HEAD_DIM), kk, vv, mask, sinks)
    return o.reshape(B, T, ATT_WIDTH), kk[:, -wb:], vv[:, -wb:]


def wkv_scan(S0, r, decay, k, v, a, b):
    def step(S, inp):
        r_t, w_t, k_t, v_t, a_t, b_t = inp
        sa = jnp.einsum('bhvk,bhk->bhv', S, a_t)
        S = S * w_t[:, :, None, :] + sa[..., None] * b_t[:, :, None, :] + v_t[..., None] * k_t[:, :, None, :]
        return S, jnp.einsum('bhvk,bhk->bhv', S, r_t)
    xs = tuple(jnp.moveaxis(t, 1, 0) for t in (r, decay, k, v, a, b))
    S, o = lax.scan(step, S0, xs)
    return jnp.moveaxis(o, 0, 1), S


def rwkv_mix(feat, feat_prev, S0, mu, w0, w2, a0, a2, g2, k_k, k_a, r_k, gn_g, gn_b):
    B, T = feat.shape[0], feat.shape[1]
    f = feat.astype(jnp.float32)
    prev = jnp.concatenate([feat_prev.astype(jnp.float32)[:, None], f[:, :-1]], axis=1)
    xs = f + (prev - f) * mu
    r, k, v, wl, al, gl = jnp.split(xs, RWKV_SPLITS, axis=-1)
    w = -jax.nn.softplus(-(w0 + jnp.tanh(wl) @ w2)) - 0.5
    decay = jnp.exp(-jnp.exp(w))
    a = jax.nn.sigmoid(a0 + al @ a2)
    g = jax.nn.sigmoid(gl) @ g2
    heads = lambda t: t.reshape(B, T, RWKV_HEADS, HEAD_DIM)
    kk = heads(k * k_k)
    kk = kk * lax.rsqrt(jnp.sum(kk * kk, axis=-1, keepdims=True) + L2_EPS)
    k = k * (1.0 + (a - 1.0) * k_a)
    rh, kh, vh, ah = heads(r), heads(k), heads(v), heads(a)
    o, S = wkv_scan(S0.astype(jnp.float32), rh, heads(decay), kh, vh, -kk, kk * ah)
    mean = jnp.mean(o, axis=-1, keepdims=True)
    var = jnp.mean(jnp.square(o - mean), axis=-1, keepdims=True)
    on = ((o - mean) * lax.rsqrt(var + GN_EPS)).reshape(B, T, RWKV_WIDTH) * gn_g + gn_b
    bonus = (jnp.sum(rh * kh * r_k, axis=-1, keepdims=True) * vh).reshape(B, T, RWKV_WIDTH)
    return ((on + bonus) * g).astype(feat.dtype), S


def hybrid_layer(x, h_prev, k_buf, v_buf, S0, g_mix, w_in, attn_sinks, rwkv_mu, w0, w2, a0, a2, g2,
                 k_k, k_a, r_k, gn_g, gn_b, w_out, g_ffn, w_gate, w_up, w_down):
    B, T = x.shape[0], x.shape[1]
    h = rmsnorm(x, g_mix)
    proj = h @ w_in
    q, k, v, feat = jnp.split(proj, PROJ_SPLITS, axis=-1)
    k = k.reshape(B, T, KV_HEADS, HEAD_DIM)
    v = v.reshape(B, T, KV_HEADS, HEAD_DIM)
    if k_buf is None:
        att = swa_prompt(q, k, v, attn_sinks)
        wp = min(WINDOW, T)
        k_win, v_win = k[:, T - wp:], v[:, T - wp:]
        feat_prev = jnp.zeros((B, RWKV_BLOCK), feat.dtype)
        S0 = jnp.zeros((B, RWKV_HEADS, HEAD_DIM, HEAD_DIM), jnp.float32)
    else:
        att, k_win, v_win = swa_sample(q, k, v, k_buf, v_buf, attn_sinks)
        feat_prev = h_prev.astype(h.dtype) @ w_in[:, ATT_PROJ:]
    rw, S = rwkv_mix(feat, feat_prev, S0, rwkv_mu, w0, w2, a0, a2, g2, k_k, k_a, r_k, gn_g, gn_b)
    x = x + jnp.concatenate([att, rw], axis=-1) @ w_out
    u = rmsnorm(x, g_ffn)
    x = x + (jax.nn.silu(u @ w_gate) * (u @ w_up)) @ w_down
    return x, k_win, v_win, S, h[:, -1]


def setup_inputs(seed: int = 0) -> dict:
    key = jax.random.key(seed)
    ks = jax.random.split(key, 32)
    nrm = lambda i, shape, scale: jax.random.normal(ks[i], shape, jnp.float32) * scale
    wb = min(WINDOW, PAST_LEN)
    return {
        "x_prompt": nrm(0, (BATCH, SEQ, D_MODEL), 1.0),
        "x_sample": nrm(1, (DEC_BATCH, DEC_SEQ, D_MODEL), 1.0),
        "cache_k": nrm(2, (DEPTH, DEC_BATCH, wb, KV_HEADS, HEAD_DIM), 1.0),
        "cache_v": nrm(3, (DEPTH, DEC_BATCH, wb, KV_HEADS, HEAD_DIM), 1.0),
        "state_wkv": nrm(4, (DEPTH, DEC_BATCH, RWKV_HEADS, HEAD_DIM, HEAD_DIM), 0.3),
        "state_shift": nrm(5, (DEPTH, DEC_BATCH, D_MODEL), 1.0),
        "g_mix": 1.0 + nrm(6, (DEPTH, D_MODEL), 0.05),
        "w_in": nrm(7, (DEPTH, D_MODEL, PROJ_WIDTH), D_MODEL ** -0.5),
        "attn_sinks": nrm(8, (DEPTH, ATT_HEADS), 1.0),
        "rwkv_mu": jax.random.uniform(ks[9], (DEPTH, RWKV_BLOCK), jnp.float32),
        "w0": jax.random.uniform(ks[10], (DEPTH, RWKV_WIDTH), jnp.float32, -6.0, -1.0),
        "w2": nrm(11, (DEPTH, DECAY_LORA, RWKV_WIDTH), 0.5 * DECAY_LORA ** -0.5),
        "a0": nrm(12, (DEPTH, RWKV_WIDTH), 0.1),
        "a2": nrm(13, (DEPTH, ICL_LORA, RWKV_WIDTH), 0.5 * ICL_LORA ** -0.5),
        "g2": nrm(14, (DEPTH, GATE_LORA, RWKV_WIDTH), GATE_LORA ** -0.5),
        "k_k": 0.85 + nrm(15, (DEPTH, RWKV_WIDTH), 0.05),
        "k_a": 1.0 + nrm(16, (DEPTH, RWKV_WIDTH), 0.05),
        "r_k": nrm(17, (DEPTH, RWKV_HEADS, HEAD_DIM), 0.1),
        "gn_g": 1.0 + nrm(18, (DEPTH, RWKV_WIDTH), 0.05),
        "gn_b": nrm(19, (DEPTH, RWKV_WIDTH), 0.02),
        "w_out": nrm(20, (DEPTH, MIX_WIDTH, D_MODEL), MIX_WIDTH ** -0.5),
        "g_ffn": 1.0 + nrm(21, (DEPTH, D_MODEL), 0.05),
        "w_gate": nrm(22, (DEPTH, D_MODEL, D_FF), D_MODEL ** -0.5),
        "w_up": nrm(23, (DEPTH, D_MODEL, D_FF), D_MODEL ** -0.5),
        "w_down": nrm(24, (DEPTH, D_FF, D_MODEL), D_FF ** -0.5),
        "g_final": 1.0 + nrm(25, (D_MODEL,), 0.05),
    }


def reference(x_prompt, x_sample, cache_k, cache_v, state_wkv, state_shift, g_mix, w_in, attn_sinks,
              rwkv_mu, w0, w2, a0, a2, g2, k_k, k_a, r_k, gn_g, gn_b, w_out, g_ffn, w_gate, w_up,
              w_down, g_final):
    xp, xs = x_prompt, x_sample
    kp_l, vp_l, sp_l, hp_l = [], [], [], []
    ks_l, vs_l, ss_l, hs_l = [], [], [], []
    for l in range(DEPTH):
        ws = (g_mix[l], w_in[l], attn_sinks[l], rwkv_mu[l], w0[l], w2[l], a0[l], a2[l], g2[l],
              k_k[l], k_a[l], r_k[l], gn_g[l], gn_b[l], w_out[l], g_ffn[l], w_gate[l], w_up[l], w_down[l])
        xp, kp, vp, sp, hp = hybrid_layer(xp, None, None, None, None, *ws)
        xs, kn, vn, sn, hn = hybrid_layer(xs, state_shift[l], cache_k[l], cache_v[l], state_wkv[l], *ws)
        kp_l.append(kp); vp_l.append(vp); sp_l.append(sp.astype(x_prompt.dtype)); hp_l.append(hp)
        ks_l.append(kn); vs_l.append(vn); ss_l.append(sn.astype(state_wkv.dtype)); hs_l.append(hn)
    y_prompt = rmsnorm(xp, g_final)
    y_sample = rmsnorm(xs, g_final)
    return (y_prompt, y_sample,
            jnp.stack(kp_l), jnp.stack(vp_l), jnp.stack(sp_l), jnp.stack(hp_l),
            jnp.stack(ks_l), jnp.stack(vs_l), jnp.stack(ss_l), jnp.stack(hs_l))
```

```python
import numpy as np
from contextlib import ExitStack
import concourse.bass as bass
import concourse.mybir as mybir
from concourse.bass_utils import run_bass_kernel_spmd
from concourse.alu_op_type import AluOpType as ALU

AF = mybir.ActivationFunctionType
AX = mybir.AxisListType
F32 = mybir.dt.float32
BF16 = mybir.dt.bfloat16

SAME_ENGINE_SYNC = True
SAME_ENGINE_RAW_ONLY = False
ENGS = ("pe", "act", "dve", "pool", "sp")
C0 = float(np.exp(-0.5))
NPRE = 48
NOWN = 16
NPT = NPRE + NOWN
NT = NPT + 1
D_FF = 2816
NFC = 22
ILV_CH = 2
ILV_MODE = 0
SKIPOPS = set()
NO_ILV = False
PSUM_PREFIXES = ("pj", "tpb", "l0", "sqp", "Zp", "pg", "pu", "pd")
OPLIMIT = 10 ** 9
NO_D2D = False
PHASES = {"A", "Sa", "Sb", "B"}


class Prog:
    def __init__(self, nc, tag=''):
        self.nc = nc
        self.tag = tag
        self.ins = []
        self.last_w = {}
        self.readers = {}

    def op(self, eng, fn, reads=(), writes=(), dma=None, n=1):
        if getattr(self, "cap", None) is not None:
            self.cap.append((eng, fn, list(reads), list(writes), dma, n))
            return -1
        idx = len(self.ins)
        if idx >= OPLIMIT:
            return idx
        writes = list(writes) + [k for k in reads if k.startswith(PSUM_PREFIXES) and k not in writes]
        deps = set()
        raw = set()
        for k in reads:
            if k in self.last_w:
                deps.add(self.last_w[k])
                raw.add(self.last_w[k])
        for k in writes:
            if k in self.last_w:
                deps.add(self.last_w[k])
            for r in self.readers.get(k, ()):
                deps.add(r)
        deps.discard(idx)
        if fn is None:
            writes = []
        self.ins.append(dict(eng=eng, fn=fn, deps=deps, raw=raw, dma=dma, used=False, n=n))
        for k in reads:
            self.readers.setdefault(k, []).append(idx)
        for k in writes:
            self.last_w[k] = idx
            self.readers[k] = []
        return idx

    def emit(self):
        nc = self.nc
        ins = self.ins
        for r in ins:
            if SAME_ENGINE_RAW_ONLY:
                r["deps"] = {d for d in r["deps"]
                             if not (ins[d]["eng"] == r["eng"] and ins[d]["dma"] is None and r["dma"] is None and d not in r["raw"])}
            for d in r["deps"]:
                ins[d]["used"] = True
        cnt = {e: 0 for e in ENGS}
        dmav = {}
        for r in ins:
            if r["dma"] is not None:
                r["sem"] = "dma_" + r["dma"]
                dmav[r["sem"]] = dmav.get(r["sem"], 0) + 16 * r["n"]
                r["val"] = dmav[r["sem"]]
            elif r["used"]:
                cnt[r["eng"]] += 1
                r["sem"] = "eng_" + r["eng"]
                r["val"] = cnt[r["eng"]]
            else:
                r["sem"] = None
                r["val"] = 0
        known = {e: {} for e in ENGS}
        for r in ins:
            e = r["eng"]
            kn = known[e]
            wd = {}
            for d in sorted(r["deps"]):
                rd = ins[d]
                s, v = rd["sem"], rd["val"]
                if rd["eng"] == e and rd["dma"] is None and not SAME_ENGINE_SYNC:
                    continue
                if kn.get(s, 0) >= v:
                    continue
                wd[s] = max(wd.get(s, 0), v)
                for s2, v2 in rd["clock"].items():
                    if kn.get(s2, 0) < v2:
                        kn[s2] = v2
            r["waits"] = sorted(wd.items())
            ck = dict(kn)
            if r["sem"] is not None:
                ck[r["sem"]] = r["val"]
            r["clock"] = ck
        semnames = sorted({r["sem"] for r in ins if r["sem"] is not None})
        self.stats = dict(n=len(ins), nsem=len(semnames),
                          nwaits=sum(len(r["waits"]) for r in ins),
                          per_eng={e: sum(1 for r in ins if r["eng"] == e) for e in ENGS})
        with ExitStack() as st:
            sems = {s: st.enter_context(nc.semaphore(s + self.tag)) for s in semnames}
            block = st.enter_context(nc.Block())
            reg = {"pe": block.tensor, "act": block.scalar, "dve": block.vector,
                   "pool": block.gpsimd, "sp": block.sync}

            def make(e):
                def body(eng):
                    for r in ins:
                        if r["eng"] != e:
                            continue
                        for s, v in r["waits"]:
                            eng.wait_ge(sems[s], v)
                        if r["fn"] is None:
                            continue
                        out = r["fn"](eng)
                        if r["dma"] is not None:
                            outs = out if isinstance(out, (list, tuple)) else [out]
                            assert len(outs) == r["n"], (len(outs), r["n"])
                            for o in outs:
                                o.then_inc(sems[r["sem"]], 16)
                        elif r["sem"] is not None:
                            o = out[-1] if isinstance(out, (list, tuple)) else out
                            o.then_inc(sems[r["sem"]], 1)
                return body

            for e in ENGS:
                reg[e](make(e))


def build():
    nc = bass.Bass("TRN2", target_bir_lowering=False)

    def din(name, shape):
        return nc.dram_tensor(name, shape, F32, kind="ExternalInput").ap()

    def dout(name, shape):
        return nc.dram_tensor(name, shape, F32, kind="ExternalOutput").ap()

    xw = din("xw", [NPT * 128, 1024])
    xs = din("xs", [128, 1024])
    hprev = din("hprev", [16, 1024])
    ck = din("ck", [16, 128, 128])
    cv = din("cv", [16, 128, 128])
    swkv = din("swkv", [16, 8, 64, 64])
    w_in = din("w_in", [1024, 2560])
    w_out = din("w_out", [1024, 1024])
    w_gate = din("w_gate", [1024, D_FF])
    w_up = din("w_up", [1024, D_FF])
    w_down = din("w_down", [D_FF, 1024])
    gvec = din("gvec", [3, 1024])
    mu_d = din("mu", [128, 14])
    vec4_d = din("vec4", [128, 7 * 4])
    w2a2_d = din("w2a2", [128, 512])
    g2_d = din("g2", [128, 512])
    sinks_d = din("sinks", [1, 8])
    masks_d = din("masks", [128, 7 * 128])
    ident_d = din("ident", [128, 128])
    bones_d = din("bones", [128, 128])
    rmask_d = din("rmask", [128, 256])
    e16_d = din("e16", [128, 16])

    y_p = dout("y_p", [NOWN * 128, 1024])
    y_s = dout("y_s", [128, 1024])
    kwin_p = dout("kwin_p", [128, 128])
    vwin_p = dout("vwin_p", [128, 128])
    wkv_p = dout("wkv_p", [8, 64, 64])
    shift_p = dout("shift_p", [1, 1024])
    kwin_s = dout("kwin_s", [16, 128, 128])
    vwin_s = dout("vwin_s", [16, 128, 128])
    wkv_s = dout("wkv_s", [16, 8, 64, 64])
    shift_s = dout("shift_s", [16, 1024])
    x1s = nc.dram_tensor("x1s", [17 * 128, 1024], F32, kind="Internal").ap()
    build.stats = {}

    W0, A0, KK, KA, RK, GNG, GNB = range(7)

    def phase(mode, per):
        pA, pSa, pSb = mode == "A", mode == "Sa", mode == "Sb"
        with ExitStack() as st:
            def sb(name, shape, dt=F32):
                return st.enter_context(nc.sbuf_tensor(name + "_" + mode, shape, dt))

            def psb(name, shape, dt=F32):
                return st.enter_context(nc.psum_tensor(name + "_" + mode, shape, dt))

            def rot(name, n, shape, dt=F32):
                return [sb(f"{name}{q}", shape, dt) for q in range(n)]
            P = Prog(nc, '_' + mode)
            outkeys = []
            if pA or pSa:
                Win = sb("Win", [128, 8, 2560], BF16)
            if pA or pSb:
                Wout = sb("Wout", [128, 8, 1024], BF16)
            gmix = sb("gmix", [128, 1024])
            mu = sb("mu", [128, 14])
            vec4 = sb("vec4", [128, 7, 4])
            w2a2 = sb("w2a2", [128, 512], BF16)
            g2b = sb("g2b", [128, 512], BF16)
            esink = sb("esink", [128, 8])
            MK = sb("MK", [128, 7, 128], BF16)
            MKL = sb("MKL", [128, 2, 4, 128], BF16)
            ident = sb("ident", [128, 128], BF16)
            ident32 = sb("ident32", [128, 128])
            bones = sb("bones", [128, 128], BF16)
            rmask = sb("rmask", [128, 2, 128])
            e16 = sb("e16", [128, 16], BF16)

            def cload(e):
                r = []
                if pA or pSa:
                    for kc in range(8):
                        r.append(e.dma_start(out=Win[:, kc, :], in_=w_in[kc * 128:(kc + 1) * 128, :]))
                if pA or pSb:
                    for kc in range(8):
                        r.append(e.dma_start(out=Wout[:, kc, :], in_=w_out[kc * 128:(kc + 1) * 128, :]))
                r.append(e.dma_start(out=w2a2[:], in_=w2a2_d[:, :]))
                r.append(e.dma_start(out=g2b[:], in_=g2_d[:, :]))
                r.append(e.dma_start(out=MK[:], in_=masks_d.rearrange("p (m t) -> p m t", m=7)))
                r.append(e.dma_start(out=ident[:], in_=ident_d[:, :]))
                r.append(e.dma_start(out=bones[:], in_=bones_d[:, :]))
                r.append(e.dma_start(out=e16[:], in_=e16_d[:, :]))
                return r
            ncl = 6 + (8 if (pA or pSa) else 0) + (8 if (pA or pSb) else 0)
            P.op("pool", cload, writes=["const"], dma="constb", n=ncl)

            def cload2(e):
                r = []
                r.append(e.dma_start(out=gmix[:], in_=gvec[0:1, :].broadcast_to([128, 1024])))
                r.append(e.dma_start(out=mu[:], in_=mu_d[:, :]))
                r.append(e.dma_start(out=vec4[:], in_=vec4_d.rearrange("p (a b) -> p a b", a=7)))
                r.append(e.dma_start(out=esink[:], in_=sinks_d[0:1, :].broadcast_to([128, 8])))
                r.append(e.dma_start(out=ident32[:], in_=ident_d[:, :]))
                r.append(e.dma_start(out=rmask[:], in_=rmask_d.rearrange("p (a t) -> p a t", a=2)))
                return r
            P.op("sp", cload2, writes=["const2"], dma="constf", n=6)
            P.op("act", lambda e: e.activation(out=esink[:], in_=esink[:], func=AF.Exp),
                 reads=["const2"], writes=["esink"])

            def mkl(e):
                r = None
                for kd in range(2):
                    for q in range(4):
                        r = e.tensor_copy(out=MKL[:, kd, q, :], in_=MK[:, 3 * kd + (q % 2), :])
                return r
            P.op("pool", mkl, reads=["const"], writes=["MKL"])

            def v4bc(ix, n=128):
                return vec4[:, ix, :].unsqueeze(2).broadcast_to([128, 4, n])

            R2 = 2 if pA else 1
            if pA or pSa:
                xt = rot("xt", 1 if pA else 1, [128, 1024])
                ss = rot("ss", 2, [128, 1])
                rs = rot("rs", 2, [128, 1])
                hb = rot("hb", R2, [128, 1024], BF16)
                hT = rot("hT", R2, [128, 8, 128], BF16)
            if pA:
                fT = rot("fT", 2, [128, 14, 128])
                qT = rot("qT", 4, [128, 4, 128], BF16)
                kvT = rot("kvT", 5, [128, 2, 128], BF16)
            else:
                fT, qT, kvT, fprev = per["fT"], per["qT"], per["kvT"], per["fprev"]
            NKV = len(kvT)
            if pSa:
                h32s = sb("h32s", [128, 1024])
                kvx = sb("kvx", [128, 4, 128])
                hp32 = sb("hp32", [16, 1024])
                hpb = sb("hpb", [16, 1024], BF16)
                hpT = sb("hpT", [128, 8, 16], BF16)
            if pA or pSb:
                Vaug = rot("Vaug", NKV, [128, 2, 65], BF16)
                xsT = sb("xsT", [128, 14, 128])
                fcar = sb("fcar", [128, 14])
                twal = sb("twal", [128, 128], BF16)
                sg = sb("sg", [128, 128], BF16)
                eT = sb("eT", [128, 4, 128])
                asT = sb("asT", [128, 4, 128])
                kkT = sb("kkT", [128, 4, 128])
                sqb = sb("sqb", [128, 4, 128], BF16)
                rn = sb("rn", [128, 4, 128])
                tmpA = sb("tmpA", [128, 4, 128])
                kmT = sb("kmT", [128, 4, 128])
                cumE = sb("cumE", [128, 4, 128])
                Eg = sb("Eg", [128, 4, 128])
                vb = sb("vb", [128, 4, 128], BF16)
                AR = rot("AR", R2, [128, 4, 2, 128], BF16)
                BTF = rot("BTF", R2, [128, 4, 128], BF16)
                KTF = rot("KTF", R2, [128, 4, 128], BF16)
                VTM = rot("VTM", R2, [128, 512], BF16)
                BTM = rot("BTM", R2, [128, 512], BF16)
                KTM = rot("KTM", R2, [128, 512], BF16)
                gC = rot("gC", R2, [128, 4, 16])
                gT = rot("gT", 3 if pA else 1, [128, 4, 128], BF16)
                bonT = rot("bonT", 3 if pA else 1, [128, 4, 128], BF16)
                if pSb:
                    SQ0 = sb("SQ0", [128, 8, 256], BF16)
                    NM = sb("NM", [128, 8, 384], BF16)
                    NJ = sb("NJ", [128, 2, 8, 128], BF16)
                    AJ = sb("AJ", [128, 2, 8, 128], BF16)
                else:
                    L0S = sb("L0S", [128, 8, 512], BF16)
                    A0S = sb("A0S", [128, 8, 128], BF16)
                    NA = sb("NA", [128, 2, 8, 256], BF16)
                Zb = rot("Zb", 2, [128, 512], BF16)
                Ub = sb("Ub", [128, 512], BF16)
                Hst = sb("Hst", [128, 4, 64])
                Hb = sb("Hb", [128, 4, 64], BF16)
                tS = sb("tS", [128, 4, 64])
                OTM = rot("OTM", R2, [128, 8, 64])
                osq = sb("osq", [128, 8, 64])
                st1 = sb("st1", [128, 8])
                st2 = sb("st2", [128, 8])
                st3 = sb("st3", [128, 8])
                onb = sb("onb", [128, 8, 64], BF16)
                PT = rot("PT", 4, [128, 4, 128], BF16)
                den = sb("den", [128, 8])
                attb = sb("attb", [128, 8, 64], BF16)
                mixT = rot("mixT", R2, [128, 8, 128], BF16)
                tmx = sb("tmx", [128, 4, 128])
                x1t = rot("x1t", 1, [128, 1024])
                if pA:
                    ATM = rot("ATM", R2, [128, 512], BF16)
                    BHF = sb("BHF", [128, 4, 128], BF16)
                    KHF = sb("KHF", [128, 4, 128], BF16)
                    Xb = rot("Xb", 2, [128, 2, 512], BF16)
                    WY = sb("WY", [128, 8, 128], BF16)
                    MTb = sb("MTb", [128, 4, 64], BF16)
                    WTF = sb("WTF", [128, 4, 128], BF16)
            if pA:
                kvx = tmx
                h32s = x1t[0]
            if pSb:
                S0q = sb("S0q", [64, 32, 64])
                H0s = sb("H0s", [128, 16, 4, 64])
                H0b = sb("H0b", [128, 16, 4, 64], BF16)
                Xex = sb("Xex", [128, 2, 16, 64], BF16)
                zh = sb("zh", [128, 4, 2, 128], BF16)
                KcT = sb("KcT", [128, 16, 128], BF16)
                Vca = sb("Vca", [128, 16, 2, 65], BF16)
                cstb = sb("cstb", [128, 16, 128], BF16)
                PTc = sb("PTc", [128, 2, 16, 32], BF16)
                PTx = rot("PTx", 2, [128, 16, 128], BF16)
                wso = sb("wso", [64, 4, 8, 64])
            h32k = "x1t0" if pA else "h32s"
            kvxk = "tmx" if pA else "kvx"

            pj = [psb(f"pj{q}", [128, 512]) for q in range(2)]
            tpb = psb("tpb", [128, 1024], BF16)
            l0 = [psb(f"l0{q}", [128, 512]) for q in range(2)]
            sqp = [psb(f"sqp{q}", [128, 512]) for q in range(2)]
            Zp = psb("Zp", [128, 512])
            cnts = {"pj": 0, "sqp": 0, "l0": 0}
            banks = {"pj": pj, "sqp": sqp, "l0": l0}

            def nextb(nm):
                cnts[nm] += 1
                q = cnts[nm] % 2
                return banks[nm][q], f"{nm}{q}"

            def nextpj():
                return nextb("pj")

            def nextsq():
                return nextb("sqp")

            def nextl0():
                return nextb("l0")

            if pA or pSb:
                P.op("pool", lambda e: e.memset(fcar[:], 0.0), writes=["fcar"])
                P.op("pool", lambda e: e.memset(Hst[:], 0.0), writes=["Hst"])
                P.op("pool", lambda e: e.memset(Hb[:], 0.0), writes=["Hb"])
                P.op("pool", lambda e: e.memset(tS[:], 0.0), writes=["tS"])
                for q in range(NKV):
                    P.op("pool", lambda e, q=q: e.memset(Vaug[q][:], 1.0), writes=[f"Vaug{q}"])
            if pSb:
                P.op("pool", lambda e: e.memset(Vca[:], 1.0), writes=["Vca"])
                for q in range(2):
                    P.op("pool", lambda e, q=q: e.memset(PTx[q][:], 0.0), writes=[f"PTx{q}"])

            def is_own(i):
                return i >= NPRE

            def is_samp(i):
                return i == NPT

            def S1(i):
                b = i % len(xt)
                kx = f"xt{b}"
                src = xs[:, :] if is_samp(i) else xw[i * 128:(i + 1) * 128, :]
                P.op("sp", lambda e: e.dma_start(out=xt[b][:], in_=src), writes=[kx], dma=kx)
                s2 = i % 2
                hq = i % len(hb)
                P.op("act", lambda e: e.activation(out=hb[hq][:], in_=xt[b][:], func=AF.Square, accum_out=ss[s2][:]),
                     reads=[kx], writes=[f"hb{hq}", f"ss{s2}"])
                P.op("act", lambda e: e.activation(out=rs[s2][:], in_=ss[s2][:], func=AF.Sqrt, scale=1.0 / 1024, bias=1e-6),
                     reads=[f"ss{s2}"], writes=[f"rs{s2}"])
                P.op("dve", lambda e: e.reciprocal(out=rs[s2][:], in_=rs[s2][:]), reads=[f"rs{s2}"], writes=[f"rs{s2}"])
                P.op("dve", lambda e: e.scalar_tensor_tensor(out=hb[hq][:], in0=xt[b][:], scalar=rs[s2][:, 0:1], in1=gmix[:],
                                                             op0=ALU.mult, op1=ALU.mult),
                     reads=[kx, f"rs{s2}", "const2"], writes=[f"hb{hq}"])
                if i == NPT - 1 or is_samp(i):
                    P.op("dve", lambda e: e.scalar_tensor_tensor(out=h32s[:], in0=xt[b][:], scalar=rs[s2][:, 0:1], in1=gmix[:],
                                                                 op0=ALU.mult, op1=ALU.mult),
                         reads=[kx, f"rs{s2}", "const2"], writes=[h32k])
                    if is_samp(i):
                        P.op("sp", lambda e: [e.dma_start(out=shift_s[q:q + 1, :], in_=h32s[8 * q + 7:8 * q + 8, :]) for q in range(16)],
                             reads=[h32k], writes=["o_shift_s"], dma="o_shift_s", n=16)
                        outkeys.append("o_shift_s")
                    else:
                        P.op("sp", lambda e: e.dma_start(out=shift_p[:, :], in_=h32s[127:128, :]), reads=[h32k],
                             writes=["o_shift_p"], dma="o_shift_p")
                        outkeys.append("o_shift_p")

                def tr(e):
                    r = None
                    for kc in range(8):
                        r = e.transpose(out=tpb[:, kc * 128:(kc + 1) * 128], in_=hb[hq][:, kc * 128:(kc + 1) * 128], identity=ident[:])
                    return r
                P.op("pe", tr, reads=[f"hb{hq}", "const"], writes=["tpb"])
                P.op("act", lambda e: e.copy(out=hT[hq][:].rearrange("p a b -> p (a b)"), in_=tpb[:]), reads=["tpb"], writes=[f"hT{hq}"])

            def proj_group(i, chunks, evac):
                hq = i % len(hT)
                bank, bk = nextpj()

                def mm(e):
                    r = None
                    for gi, c in enumerate(chunks):
                        for kc in range(8):
                            r = e.matmul(bank[:, gi * 128:(gi + 1) * 128], lhsT=Win[:, kc, c * 128:(c + 1) * 128],
                                         rhs=hT[hq][:, kc, :], start=(kc == 0), stop=(kc == 7))
                    return r
                P.op("pe", mm, reads=["const", f"hT{hq}"], writes=[bk])
                evac(bank, bk)

            def S2(i):
                own = is_own(i)
                qq = i % len(qT)
                fq = i % len(fT)
                if own:
                    def ev_q(bank, bk):
                        P.op("act", lambda e: e.activation(out=qT[qq][:].rearrange("p a b -> p (a b)"), in_=bank[:], func=AF.Copy, scale=0.125),
                             reads=[bk], writes=[f"qT{qq}"])
                    proj_group(i, [0, 1, 2, 3], ev_q)
                if own or i == NPRE - 1:
                    k3 = i % NKV

                    def ev_kv(bank, bk):
                        P.op("act", lambda e: e.copy(out=kvT[k3][:].rearrange("p a b -> p (a b)"), in_=bank[:, 0:256]),
                             reads=[bk], writes=[f"kvT{k3}"])
                        if i == NPT - 1 or is_samp(i):
                            P.op("dve", lambda e: e.tensor_copy(out=kvx[:, 0:2, :].rearrange("p a b -> p (a b)"), in_=bank[:, 0:256]),
                                 reads=[bk, f"kvT{k3}"], writes=[kvxk])
                    proj_group(i, [4, 5], ev_kv)
                    if i == NPT - 1 or is_samp(i):
                        bank, bk = nextpj()

                        def trkv(e):
                            e.transpose(out=bank[:, 0:128], in_=kvx[:, 0, :], identity=ident32[:])
                            return e.transpose(out=bank[:, 128:256], in_=kvx[:, 1, :], identity=ident32[:])
                        P.op("pe", trkv, reads=[kvxk, "const2"], writes=[bk])
                        P.op("dve", lambda e: e.tensor_copy(out=kvx[:, 2:4, :].rearrange("p a b -> p (a b)"), in_=bank[:, 0:256]),
                             reads=[bk, kvxk], writes=[kvxk])
                        if is_samp(i):
                            def okv(e):
                                r = []
                                for q in range(16):
                                    r.append(e.dma_start(out=kwin_s[q, 120:128, :], in_=kvx[8 * q:8 * q + 8, 2, :]))
                                    r.append(e.dma_start(out=vwin_s[q, 120:128, :], in_=kvx[8 * q:8 * q + 8, 3, :]))
                                if not NO_D2D:
                                    r.append(e.dma_start(out=kwin_s[:, 0:120, :], in_=ck[:, 8:128, :]))
                                    r.append(e.dma_start(out=vwin_s[:, 0:120, :], in_=cv[:, 8:128, :]))
                                return r
                            P.op("sp", okv, reads=[kvxk], writes=["o_kv_s"], dma="o_kv_s", n=(32 if NO_D2D else 34))
                            outkeys.append("o_kv_s")
                        else:
                            def okv(e):
                                r = []
                                r.append(e.dma_start(out=kwin_p[:, :], in_=kvx[:, 2, :]))
                                r.append(e.dma_start(out=vwin_p[:, :], in_=kvx[:, 3, :]))
                                return r
                            P.op("sp", okv, reads=[kvxk], writes=["o_kv_p"], dma="o_kv_p", n=2)
                            outkeys.append("o_kv_p")
                for gi, chunks in enumerate([[6, 7, 8, 9], [10, 11, 12, 13], [14, 15, 16, 17], [18, 19]]):
                    def ev_f(bank, bk, gi=gi, chunks=chunks):
                        n = len(chunks) * 128
                        dst = fT[fq][:, gi * 4:gi * 4 + len(chunks), :].rearrange("p a b -> p (a b)")
                        if gi % 2 == 0:
                            P.op("act", lambda e: e.copy(out=dst, in_=bank[:, 0:n]), reads=[bk], writes=[f"fT{fq}_{gi}"])
                        else:
                            P.op("dve", lambda e: e.tensor_copy(out=dst, in_=bank[:, 0:n]), reads=[bk], writes=[f"fT{fq}_{gi}"])
                    proj_group(i, chunks, ev_f)
                if is_samp(i):
                    P.op("sp", lambda e: e.dma_start(out=hp32[:], in_=hprev[:, :]), writes=["hp32"], dma="hp32")
                    P.op("dve", lambda e: e.tensor_copy(out=hpb[:], in_=hp32[:]), reads=["hp32"], writes=["hpb"])

                    def trh(e):
                        r = None
                        for kc in range(8):
                            r = e.transpose(out=tpb[:, kc * 16:(kc + 1) * 16], in_=hpb[:, kc * 128:(kc + 1) * 128], identity=ident[0:16, 0:16])
                        return r
                    P.op("pe", trh, reads=["hpb", "const"], writes=["tpb"])
                    P.op("act", lambda e: e.copy(out=hpT[:].rearrange("p a b -> p (a b)"), in_=tpb[:, 0:128]), reads=["tpb"], writes=["hpT"])
                    for half in range(2):
                        bank, bk = nextpj()

                        def mmp(e, half=half, bank=bank):
                            r = None
                            for ci in range(7):
                                c = 6 + half * 7 + ci
                                for kc in range(8):
                                    r = e.matmul(bank[:, ci * 16:(ci + 1) * 16], lhsT=Win[:, kc, c * 128:(c + 1) * 128],
                                                 rhs=hpT[:, kc, :], start=(kc == 0), stop=(kc == 7))
                            return r
                        P.op("pe", mmp, reads=["const", "hpT"], writes=[bk])
                        P.op("act", lambda e, half=half, bank=bank: e.copy(
                            out=fprev[:, half * 7:(half + 1) * 7, :].rearrange("p a b -> p (a b)"), in_=bank[:, 0:112]),
                            reads=[bk], writes=[f"fprev{half}"])

            def S3(i):
                s3 = i % R2
                own = is_own(i)
                samp = is_samp(i)
                fq = i % len(fT)
                fk = [f"fT{fq}_{g}" for g in range(4)]
                f = fT[fq]
                if samp:
                    f4 = f[:, :, :].rearrange("p c (s t) -> p c s t", t=8)
                    x4 = xsT[:, :, :].rearrange("p c (s t) -> p c s t", t=8)
                    P.op("pool", lambda e: e.tensor_tensor(out=x4[:, :, :, 1:8], in0=f4[:, :, :, 0:7], in1=f4[:, :, :, 1:8], op=ALU.subtract),
                         reads=fk, writes=["xsT"])
                    P.op("pool", lambda e: e.tensor_tensor(out=x4[:, :, :, 0], in0=fprev[:, :, :], in1=f4[:, :, :, 0], op=ALU.subtract),
                         reads=fk, writes=["xsT0"])
                else:
                    P.op("pool", lambda e: e.tensor_tensor(out=xsT[:, :, 1:128], in0=f[:, :, 0:127], in1=f[:, :, 1:128], op=ALU.subtract),
                         reads=fk, writes=["xsT"])
                    P.op("pool", lambda e: e.tensor_tensor(out=xsT[:, :, 0], in0=fcar[:, :], in1=f[:, :, 0], op=ALU.subtract),
                         reads=fk + ["fcar"], writes=["xsT0"])
                    P.op("pool", lambda e: e.tensor_copy(out=fcar[:, :], in_=f[:, :, 127]), reads=fk, writes=["fcar"])
                mu_bc = mu[:, :].unsqueeze(2).broadcast_to([128, 14, 128])
                P.op("pool", lambda e: e.tensor_tensor(out=xsT[:], in0=xsT[:], in1=mu_bc, op=ALU.mult),
                     reads=["xsT", "xsT0", "const2"], writes=["xsT", "xsT0"])
                P.op("pool", lambda e: e.tensor_tensor(out=xsT[:], in0=xsT[:], in1=f[:], op=ALU.add),
                     reads=["xsT", "xsT0"] + fk, writes=["xsT", "xsT0"])
                XK = ["xsT", "xsT0"]
                P.op("act", lambda e: e.activation(out=twal[0:64, :], in_=xsT[0:64, 12, :], func=AF.Tanh), reads=XK, writes=["twal_a"])
                P.op("act", lambda e: e.copy(out=twal[64:128, :], in_=xsT[64:128, 12, :]), reads=XK, writes=["twal_b"])
                P.op("act", lambda e: e.activation(out=sg[:], in_=xsT[:, 13, :], func=AF.Sigmoid), reads=XK, writes=["sg"])
                bw, bwk = nextpj()

                def mmw(e):
                    r = None
                    for cc in range(4):
                        r = e.matmul(bw[:, cc * 128:(cc + 1) * 128], lhsT=w2a2[0:64, cc * 128:(cc + 1) * 128], rhs=twal[0:64, :],
                                     start=True, stop=True)
                    return r
                P.op("pe", mmw, reads=["const", "twal_a"], writes=[bwk])
                for cc in range(4):
                    P.op("act", lambda e, cc=cc: e.activation(out=eT[:, cc, :], in_=bw[:, cc * 128:(cc + 1) * 128], func=AF.Sigmoid,
                                                              bias=vec4[:, W0, cc:cc + 1]),
                         reads=[bwk, "const2"], writes=[f"eT{cc}"])
                ba, bak = nextpj()

                def mma(e):
                    r = None
                    for cc in range(4):
                        r = e.matmul(ba[:, cc * 128:(cc + 1) * 128], lhsT=w2a2[64:128, cc * 128:(cc + 1) * 128], rhs=twal[64:128, :],
                                     start=True, stop=True)
                    return r
                P.op("pe", mma, reads=["const", "twal_b"], writes=[bak])
                for cc in range(4):
                    P.op("act", lambda e, cc=cc: e.activation(out=asT[:, cc, :], in_=ba[:, cc * 128:(cc + 1) * 128], func=AF.Sigmoid,
                                                              bias=vec4[:, A0, cc:cc + 1]),
                         reads=[bak, "const2"], writes=[f"asT{cc}"])
                EK = [f"eT{c}" for c in range(4)]
                AK = [f"asT{c}" for c in range(4)]
                if own:
                    bg, bgk = nextpj()

                    def mmg(e):
                        r = None
                        for cc in range(4):
                            r = e.matmul(bg[:, cc * 128:(cc + 1) * 128], lhsT=g2b[:, cc * 128:(cc + 1) * 128], rhs=sg[:], start=True, stop=True)
                        return r
                    P.op("pe", mmg, reads=["const", "sg"], writes=[bgk])
                    gq = i % len(gT)
                    P.op("act", lambda e: e.copy(out=gT[gq][:].rearrange("p a b -> p (a b)"), in_=bg[:]), reads=[bgk], writes=[f"gT{gq}"])
                P.op("dve", lambda e: e.tensor_tensor(out=kkT[:], in0=xsT[:, 4:8, :], in1=v4bc(KK), op=ALU.mult), reads=XK + ["const2"], writes=["kkT"])
                P.op("dve", lambda e: e.tensor_tensor(out=sqb[:], in0=kkT[:], in1=kkT[:], op=ALU.mult), reads=["kkT"], writes=["sqb"])
                bs, bsk = nextpj()
                P.op("pe", lambda e: e.matmul(bs[:], lhsT=bones[:], rhs=sqb[:].rearrange("p a b -> p (a b)"), start=True, stop=True),
                     reads=["const", "sqb"], writes=[bsk])
                P.op("act", lambda e: e.activation(out=rn[:].rearrange("p a b -> p (a b)"), in_=bs[:], func=AF.Sqrt, bias=1e-12),
                     reads=[bsk], writes=["rn"])
                P.op("dve", lambda e: e.reciprocal(out=rn[:], in_=rn[:]), reads=["rn"], writes=["rn"])
                P.op("dve", lambda e: e.tensor_tensor(out=kkT[:], in0=kkT[:], in1=rn[:], op=ALU.mult), reads=["kkT", "rn"], writes=["kkT"])
                P.op("dve", lambda e: e.scalar_tensor_tensor(out=tmpA[:], in0=asT[:], scalar=-1.0, in1=v4bc(KA), op0=ALU.add, op1=ALU.mult),
                     reads=AK + ["const2"], writes=["tmpA"])
                P.op("dve", lambda e: e.scalar_tensor_tensor(out=kmT[:], in0=tmpA[:], scalar=1.0, in1=xsT[:, 4:8, :], op0=ALU.add, op1=ALU.mult),
                     reads=["tmpA"] + XK, writes=["kmT"])
                rm = rmask[:, 1 if samp else 0, :]
                for cc in range(4):
                    P.op("dve", lambda e, cc=cc: e.tensor_tensor_scan(out=cumE[:, cc, :], data0=rm, data1=eT[:, cc, :], initial=0.0,
                                                                      op0=ALU.mult, op1=ALU.add),
                         reads=[f"eT{cc}", "const2"], writes=[f"cumE{cc}"])
                CK = [f"cumE{c}" for c in range(4)]
                P.op("pool", lambda e: e.tensor_tensor(out=rn[:], in0=cumE[:], in1=eT[:], op=ALU.subtract), reads=CK + EK + ["rn"], writes=["rn"])
                P.op("act", lambda e: e.activation(out=rn[:], in_=rn[:], func=AF.Exp, scale=-C0), reads=["rn"], writes=["rn"])
                P.op("act", lambda e: e.activation(out=Eg[:], in_=cumE[:], func=AF.Exp, scale=-C0), reads=CK, writes=["Eg"])
                P.op("act", lambda e: e.activation(out=cumE[:], in_=cumE[:], func=AF.Exp, scale=C0), reads=CK, writes=CK)
                Egi, Egx = cumE, rn
                if samp:
                    P.op("pool", lambda e: e.tensor_copy(out=gC[s3][:, :, :], in_=Eg[:, :, :].rearrange("p c (s t) -> p c s t", t=8)[:, :, :, 7]),
                         reads=["Eg"], writes=[f"gC{s3}"])
                else:
                    P.op("pool", lambda e: e.tensor_copy(out=gC[s3][:, :, 0], in_=Eg[:, :, 127]), reads=["Eg"], writes=[f"gC{s3}"])
                ARk = f"AR{s3}"
                P.op("dve", lambda e: e.tensor_tensor(out=AR[s3][:, :, 1, :], in0=xsT[:, 0:4, :], in1=Eg[:], op=ALU.mult),
                     reads=XK + ["Eg"], writes=[ARk + "r"])
                P.op("dve", lambda e: e.tensor_tensor(out=KTF[s3][:], in0=kmT[:], in1=Egi[:], op=ALU.mult), reads=["kmT"] + CK, writes=[f"KTF{s3}"])
                P.op("pool", lambda e: e.tensor_tensor(out=tmpA[:], in0=kkT[:], in1=asT[:], op=ALU.mult), reads=["kkT"] + AK, writes=["tmpA"])
                P.op("dve", lambda e: e.tensor_tensor(out=BTF[s3][:], in0=tmpA[:], in1=Egi[:], op=ALU.mult), reads=["tmpA"] + CK, writes=[f"BTF{s3}"])
                P.op("dve", lambda e: e.scalar_tensor_tensor(out=AR[s3][:, :, 0, :], in0=kkT[:], scalar=-1.0, in1=Egx[:], op0=ALU.mult, op1=ALU.mult),
                     reads=["kkT", "rn"], writes=[ARk + "a"])
                P.op("act", lambda e: e.copy(out=vb[:], in_=xsT[:, 8:12, :]), reads=XK, writes=["vb"])
                if own:
                    P.op("pool", lambda e: e.tensor_tensor(out=rn[:], in0=xsT[:, 0:4, :], in1=kmT[:], op=ALU.mult), reads=XK + ["kmT", "rn"], writes=["rn"])
                    P.op("dve", lambda e: e.tensor_tensor(out=sqb[:], in0=rn[:], in1=v4bc(RK), op=ALU.mult), reads=["rn", "const2"], writes=["sqb"])
                    bb, bbk = nextpj()
                    P.op("pe", lambda e: e.matmul(bb[:], lhsT=bones[:], rhs=sqb[:].rearrange("p a b -> p (a b)"), start=True, stop=True),
                         reads=["const", "sqb"], writes=[bbk])
                    P.op("dve", lambda e: e.tensor_tensor(out=bonT[i % len(bonT)][:].rearrange("p a b -> p (a b)"), in0=bb[:],
                                                          in1=xsT[:, 8:12, :].rearrange("p a b -> p (a b)"), op=ALU.mult),
                         reads=[bbk] + XK, writes=[f"bonT{i % len(bonT)}"])

                if pA:
                    gbc = gC[s3][:, :, 0:1].broadcast_to([128, 4, 128])
                    P.op("pool", lambda e: e.tensor_tensor(out=BHF[:], in0=BTF[s3][:], in1=gbc, op=ALU.mult), reads=[f"BTF{s3}", f"gC{s3}"], writes=["BHF"])
                    P.op("pool", lambda e: e.tensor_tensor(out=KHF[:], in0=KTF[s3][:], in1=gbc, op=ALU.mult), reads=[f"KTF{s3}", f"gC{s3}"], writes=["KHF"])
                    bsrc, ksrc, bsk, ksk = BHF, KHF, "BHF", "KHF"
                else:
                    bsrc, ksrc, bsk, ksk = BTF[s3], KTF[s3], f"BTF{s3}", f"KTF{s3}"

                def tr1(e):
                    r = None
                    for cc in range(4):
                        r = e.transpose(out=tpb[:, cc * 128:(cc + 1) * 128], in_=vb[:, cc, :], identity=ident[:])
                    for cc in range(4):
                        r = e.transpose(out=tpb[:, 512 + cc * 128:512 + (cc + 1) * 128], in_=bsrc[:, cc, :], identity=ident[:])
                    return r
                P.op("pe", tr1, reads=["vb", bsk, "const"], writes=["tpb"])
                P.op("act", lambda e: e.copy(out=VTM[s3][:], in_=tpb[:, 0:512]), reads=["tpb"], writes=[f"VTM{s3}"])
                P.op("dve", lambda e: e.tensor_copy(out=BTM[s3][:], in_=tpb[:, 512:1024]), reads=["tpb"], writes=[f"BTM{s3}"])

                def tr2(e):
                    r = None
                    for cc in range(4):
                        r = e.transpose(out=tpb[:, cc * 128:(cc + 1) * 128], in_=ksrc[:, cc, :], identity=ident[:])
                    if pA:
                        for cc in range(4):
                            r = e.transpose(out=tpb[:, 512 + cc * 128:512 + (cc + 1) * 128], in_=AR[s3][:, cc, 0, :], identity=ident[:])
                    return r
                P.op("pe", tr2, reads=[ksk, f"AR{s3}a", "const"], writes=["tpb"])
                P.op("act", lambda e: e.copy(out=KTM[s3][:], in_=tpb[:, 0:512]), reads=["tpb"], writes=[f"KTM{s3}"])
                if pA:
                    P.op("dve", lambda e: e.tensor_copy(out=ATM[s3][:], in_=tpb[:, 512:1024]), reads=["tpb"], writes=[f"ATM{s3}"])

            def nlev(i):
                return 3 if is_samp(i) else 7

            def S4(i):
                s3 = i % R2
                kd = 1 if is_samp(i) else 0
                ARk = [f"AR{s3}a", f"AR{s3}r"]
                for h in range(8):
                    cc, pb = h // 2, (h % 2) * 64
                    bank, bk = nextl0()

                    def mm0(e, cc=cc, pb=pb, bank=bank):
                        e.matmul(bank[:, 0:256], lhsT=BTF[s3][pb:pb + 64, cc, :], rhs=AR[s3][pb:pb + 64, cc, :, :], start=True, stop=True)
                        return e.matmul(bank[:, 256:512], lhsT=KTF[s3][pb:pb + 64, cc, :], rhs=AR[s3][pb:pb + 64, cc, :, :], start=True, stop=True)
                    P.op("pe", mm0, reads=ARk + [f"BTF{s3}", f"KTF{s3}"], writes=[bk])
                    P.op("dve", lambda e, h=h, bank=bank: e.tensor_tensor(out=SQ0[:, h, 0:128], in0=bank[:, 0:128], in1=MKL[:, kd, 0, :], op=ALU.mult),
                         reads=[bk, "MKL"], writes=[f"SQ0n{h}"])
                    P.op("dve", lambda e, h=h, bank=bank: e.tensor_tensor(out=NM[:, h, :], in0=bank[:, 128:512],
                                                                          in1=MKL[:, kd, 1:4, :].rearrange("p a b -> p (a b)"), op=ALU.mult),
                         reads=[bk, "MKL"], writes=[f"NM{h}"])
                for g4 in range(2):
                    bank, bk = nextsq()

                    def mma0(e, g4=g4, bank=bank):
                        r = None
                        for hh in range(4):
                            h = g4 * 4 + hh
                            cc, pb = h // 2, (h % 2) * 64
                            r = e.matmul(bank[:, hh * 128:(hh + 1) * 128], lhsT=AR[s3][pb:pb + 64, cc, 0, :], rhs=BTF[s3][pb:pb + 64, cc, :],
                                         start=True, stop=True)
                        return r
                    P.op("pe", mma0, reads=ARk + [f"BTF{s3}"], writes=[bk])
                    slbc = MK[:, 2 + 3 * kd, :].unsqueeze(1).broadcast_to([128, 4, 128])
                    P.op("dve", lambda e, g4=g4, bank=bank, slbc=slbc: e.tensor_tensor(
                        out=SQ0[:, g4 * 4:(g4 + 1) * 4, 128:256], in0=bank[:].rearrange("p (a b) -> p a b", a=4), in1=slbc, op=ALU.mult),
                        reads=[bk, "const"], writes=[f"SQ0a{g4 * 4 + q}" for q in range(4)])
                nl = nlev(i)
                for j in range(nl - 1):
                    for pr in range(4):
                        bank, bk = nextsq()
                        hs = (2 * pr, 2 * pr + 1)

                        def Nsrc(h, j=j):
                            return SQ0[:, h, 0:128] if j == 0 else NJ[:, (j - 1) % 2, h, :]

                        def Asrc(h, j=j):
                            return SQ0[:, h, 128:256] if j == 0 else AJ[:, (j - 1) % 2, h, :]
                        rk = []
                        for h in hs:
                            rk += ([f"SQ0n{h}", f"SQ0a{h}"] if j == 0 else [f"NJ{(j - 1) % 2}_{h}", f"AJ{(j - 1) % 2}_{h}"])
                        last = (j == nl - 2)

                        def mmsq(e, hs=hs, bank=bank, Nsrc=Nsrc, Asrc=Asrc, last=last):
                            r = None
                            for q, h in enumerate(hs):
                                r = e.matmul(bank[:, q * 256:q * 256 + 128], lhsT=Asrc(h), rhs=Nsrc(h), start=True, stop=True)
                                if not last:
                                    r = e.matmul(bank[:, q * 256 + 128:q * 256 + 256], lhsT=Nsrc(h), rhs=Asrc(h), start=True, stop=True)
                            return r
                        P.op("pe", mmsq, reads=rk, writes=[bk])
                        b3 = bank[:].rearrange("p (a b) -> p a b", a=2)
                        P.op("act", lambda e, j=j, pr=pr, b3=b3: e.copy(out=NJ[:, j % 2, 2 * pr:2 * pr + 2, :], in_=b3[:, :, 0:128]),
                             reads=[bk], writes=[f"NJ{j % 2}_{h}" for h in hs])
                        if not last:
                            P.op("dve", lambda e, j=j, pr=pr, b3=b3: e.tensor_copy(out=AJ[:, j % 2, 2 * pr:2 * pr + 2, :], in_=b3[:, :, 128:256]),
                                 reads=[bk], writes=[f"AJ{j % 2}_{h}" for h in hs])

            def S5(i):
                s3 = i % R2
                own = is_own(i)
                samp = is_samp(i)
                nl = nlev(i)
                ARk = [f"AR{s3}a", f"AR{s3}r"]
                if samp:
                    sample_h0(s3)

                def z0(e):
                    r = None
                    for h in range(8):
                        cc, pb = h // 2, (h % 2) * 64
                        if samp:
                            r = e.matmul(Zp[:, h * 64:(h + 1) * 64], lhsT=zh[pb:pb + 64, cc, 0, :], rhs=ident[pb:pb + 64, pb:pb + 64],
                                         start=(h == 0), stop=False, skip_group_check=True)
                        else:
                            r = e.matmul(Zp[:, h * 64:(h + 1) * 64], lhsT=AR[s3][pb:pb + 64, cc, 0, :], rhs=Hb[pb:pb + 64, cc, :],
                                         start=(h == 0), stop=False, skip_group_check=True)
                        r = e.matmul(Zp[:, h * 64:(h + 1) * 64], lhsT=NM[:, h, 128:256], rhs=VTM[s3][:, h * 64:(h + 1) * 64],
                                     start=False, stop=False, skip_group_check=True)
                    return r
                P.op("pe", z0, reads=ARk + ["Hb", "zh", "const", f"VTM{s3}"] + [f"NM{h}" for h in range(8)], writes=["Zp"])
                for j in range(nl):
                    zb = Zb[j % 2]
                    zk = f"Zb{j % 2}"
                    if j % 2 == 0:
                        P.op("act", lambda e, zb=zb: e.copy(out=zb[:], in_=Zp[:]), reads=["Zp"], writes=[zk])
                    else:
                        P.op("dve", lambda e, zb=zb: e.tensor_copy(out=zb[:], in_=Zp[:]), reads=["Zp"], writes=[zk])
                    rk = [zk] + ([f"SQ0n{h}" for h in range(8)] if j == 0 else [f"NJ{(j - 1) % 2}_{h}" for h in range(8)])

                    def ap(e, j=j, zb=zb):
                        r = None
                        for h in range(8):
                            lt = SQ0[:, h, 0:128] if j == 0 else NJ[:, (j - 1) % 2, h, :]
                            r = e.matmul(Zp[:, h * 64:(h + 1) * 64], lhsT=lt, rhs=zb[:, h * 64:(h + 1) * 64], start=False, stop=(j == nl - 1),
                                         skip_group_check=True)
                        return r
                    P.op("pe", ap, reads=rk, writes=["Zp"])
                P.op("act", lambda e: e.copy(out=Ub[:], in_=Zp[:]), reads=["Zp"], writes=["Ub"])
                if own:
                    ob, obk = nextpj()
                    oq = i % R2

                    def mo(e):
                        r = None
                        for h in range(8):
                            cc, pb = h // 2, (h % 2) * 64
                            o = ob[:, h * 64:(h + 1) * 64]
                            if samp:
                                e.matmul(o, lhsT=zh[pb:pb + 64, cc, 1, :], rhs=ident[pb:pb + 64, pb:pb + 64], start=True, stop=False)
                            else:
                                e.matmul(o, lhsT=AR[s3][pb:pb + 64, cc, 1, :], rhs=Hb[pb:pb + 64, cc, :], start=True, stop=False)
                            e.matmul(o, lhsT=NM[:, h, 0:128], rhs=Ub[:, h * 64:(h + 1) * 64], start=False, stop=False)
                            r = e.matmul(o, lhsT=NM[:, h, 256:384], rhs=VTM[s3][:, h * 64:(h + 1) * 64], start=False, stop=True)
                        return r
                    P.op("pe", mo, reads=ARk + ["Hb", "zh", "const", "Ub", f"VTM{s3}"] + [f"NM{h}" for h in range(8)], writes=[obk])
                    P.op("act", lambda e: e.copy(out=OTM[oq][:].rearrange("p a b -> p (a b)"), in_=ob[:]), reads=[obk], writes=[f"OTM{oq}"])
                if samp:
                    sample_state(s3)
                    return

                def su(e):
                    r = None
                    for h in range(8):
                        cc, pb = h // 2, (h % 2) * 64
                        o = Zp[pb:pb + 64, cc * 64:(cc + 1) * 64]
                        e.matmul(o, lhsT=BTM[s3][:, h * 64:(h + 1) * 64], rhs=Ub[:, h * 64:(h + 1) * 64], start=True, stop=False)
                        r = e.matmul(o, lhsT=KTM[s3][:, h * 64:(h + 1) * 64], rhs=VTM[s3][:, h * 64:(h + 1) * 64], start=False, stop=True)
                    return r
                P.op("pe", su, reads=["Ub", f"BTM{s3}", f"KTM{s3}", f"VTM{s3}"], writes=["Zp"])
                P.op("dve", lambda e: e.tensor_tensor(out=tS[:].rearrange("p a b -> p (a b)"), in0=Zp[:, 0:256],
                                                      in1=Hst[:].rearrange("p a b -> p (a b)"), op=ALU.add),
                     reads=["Zp", "Hst"], writes=["tS"])
                P.op("dve", lambda e: e.tensor_tensor(out=Hst[:], in0=tS[:], in1=gC[s3][:, :, 0:1].broadcast_to([128, 4, 64]), op=ALU.mult),
                     reads=["tS", f"gC{s3}"], writes=["Hst"])
                P.op("act", lambda e: e.copy(out=Hb[:], in_=Hst[:]), reads=["Hst"], writes=["Hb"])
                if i == NPT - 1:
                    bank, bk = nextpj()

                    def trs(e):
                        r = None
                        for cc in range(4):
                            r = e.transpose(out=bank[0:64, cc * 128:(cc + 1) * 128], in_=Hst[:, cc, :], identity=ident32[:])
                        return r
                    P.op("pe", trs, reads=["Hst", "const2"], writes=[bk])
                    P.op("dve", lambda e: e.tensor_copy(out=osq[0:64, :, :].rearrange("p a b -> p (a b)"), in_=bank[0:64, :]), reads=[bk, "osq"], writes=["osq"])
                    P.op("sp", lambda e: e.dma_start(out=wkv_p.rearrange("h v k -> v h k"), in_=osq[0:64, :, :]), reads=["osq"], writes=["o_wkv_p"], dma="o_wkv_p")
                    outkeys.append("o_wkv_p")

            def S4n(i):
                s3 = i % R2
                own = is_own(i)
                ARk = [f"AR{s3}a", f"AR{s3}r"]
                for h in range(8):
                    cc, pb = h // 2, (h % 2) * 64
                    bank, bk = nextl0()

                    def mm0(e, cc=cc, pb=pb, bank=bank):
                        e.matmul(bank[:, 0:256], lhsT=BTF[s3][pb:pb + 64, cc, :], rhs=AR[s3][pb:pb + 64, cc, :, :], start=True, stop=True)
                        return e.matmul(bank[:, 256:512], lhsT=KTF[s3][pb:pb + 64, cc, :], rhs=AR[s3][pb:pb + 64, cc, :, :], start=True, stop=True)
                    P.op("pe", mm0, reads=ARk + [f"BTF{s3}", f"KTF{s3}"], writes=[bk])
                    P.op("dve", lambda e, h=h, bank=bank: e.tensor_tensor(out=L0S[:, h, :], in0=bank[:],
                                                                          in1=MKL[:, 0, :, :].rearrange("p a b -> p (a b)"), op=ALU.mult),
                         reads=[bk, "MKL"], writes=[f"L0S{h}"])
                for g4 in range(2):
                    bank, bk = nextsq()

                    def mma0(e, g4=g4, bank=bank):
                        r = None
                        for hh in range(4):
                            h = g4 * 4 + hh
                            cc, pb = h // 2, (h % 2) * 64
                            r = e.matmul(bank[:, hh * 128:(hh + 1) * 128], lhsT=AR[s3][pb:pb + 64, cc, 0, :], rhs=BTF[s3][pb:pb + 64, cc, :],
                                         start=True, stop=True)
                        return r
                    P.op("pe", mma0, reads=ARk + [f"BTF{s3}"], writes=[bk])
                    slbc = MK[:, 2, :].unsqueeze(1).broadcast_to([128, 4, 128])
                    P.op("dve", lambda e, g4=g4, bank=bank, slbc=slbc: e.tensor_tensor(
                        out=A0S[:, g4 * 4:(g4 + 1) * 4, :], in0=bank[:].rearrange("p (a b) -> p a b", a=4), in1=slbc, op=ALU.mult),
                        reads=[bk, "const"], writes=[f"A0S{g4 * 4 + q}" for q in range(4)])
                for half in range(2):
                    bank, bk = l0[half], f"l0{half}"

                    def x0(e, half=half, bank=bank):
                        r = None
                        for hh in range(4):
                            h = half * 4 + hh
                            e.matmul(bank[:, hh * 128:hh * 128 + 64], lhsT=ident[:], rhs=ATM[s3][:, h * 64:(h + 1) * 64],
                                     start=(hh == 0), stop=False, skip_group_check=True)
                            r = e.matmul(bank[:, hh * 128 + 64:(hh + 1) * 128], lhsT=L0S[:, h, 256:384], rhs=VTM[s3][:, h * 64:(h + 1) * 64],
                                         start=False, stop=False, skip_group_check=True)
                        return r
                    P.op("pe", x0, reads=["const", f"ATM{s3}", f"VTM{s3}"] + [f"L0S{half * 4 + q}" for q in range(4)], writes=[bk])
                for j in range(7):
                    xb = Xb[j % 2]
                    P.op("act", lambda e, xb=xb: e.copy(out=xb[:, 0, :], in_=l0[0][:]), reads=["l00"], writes=[f"Xb{j % 2}_0"])
                    P.op("dve", lambda e, xb=xb: e.tensor_copy(out=xb[:, 1, :], in_=l0[1][:]), reads=["l01"], writes=[f"Xb{j % 2}_1"])
                    if j < 6:
                        last = (j == 5)
                        for pr in range(4):
                            bank, bk = nextsq()
                            hs = (2 * pr, 2 * pr + 1)

                            def Nsrc(h, j=j):
                                return L0S[:, h, 0:128] if j == 0 else NA[:, (j - 1) % 2, h, 0:128]

                            def Asrc(h, j=j):
                                return A0S[:, h, :] if j == 0 else NA[:, (j - 1) % 2, h, 128:256]
                            rk = []
                            for h in hs:
                                rk += ([f"L0S{h}", f"A0S{h}"] if j == 0 else [f"NA{(j - 1) % 2}_{h}"])

                            def mmsq(e, hs=hs, bank=bank, Nsrc=Nsrc, Asrc=Asrc, last=last):
                                r = None
                                for q, h in enumerate(hs):
                                    r = e.matmul(bank[:, q * 256:q * 256 + 128], lhsT=Asrc(h), rhs=Nsrc(h), start=True, stop=True)
                                    if not last:
                                        r = e.matmul(bank[:, q * 256 + 128:q * 256 + 256], lhsT=Nsrc(h), rhs=Asrc(h), start=True, stop=True)
                                return r
                            P.op("pe", mmsq, reads=rk, writes=[bk])
                            dstna = NA[:, j % 2, 2 * pr:2 * pr + 2, :].rearrange("p a b -> p (a b)")
                            if pr != 1:
                                P.op("act", lambda e, dstna=dstna, bank=bank: e.copy(out=dstna, in_=bank[:]),
                                     reads=[bk], writes=[f"NA{j % 2}_{h}" for h in hs])
                            else:
                                P.op("dve", lambda e, dstna=dstna, bank=bank: e.tensor_copy(out=dstna, in_=bank[:]),
                                     reads=[bk], writes=[f"NA{j % 2}_{h}" for h in hs])
                    for half in range(2):
                        bank, bk = l0[half], f"l0{half}"
                        rk = [f"Xb{j % 2}_{half}"] + ([f"L0S{half * 4 + q}" for q in range(4)] if j == 0 else [f"NA{(j - 1) % 2}_{half * 4 + q}" for q in range(4)])

                        def ap(e, j=j, xb=xb, half=half, bank=bank):
                            r = None
                            for hh in range(4):
                                h = half * 4 + hh
                                lt = L0S[:, h, 0:128] if j == 0 else NA[:, (j - 1) % 2, h, 0:128]
                                r = e.matmul(bank[:, hh * 128:(hh + 1) * 128], lhsT=lt, rhs=xb[:, half, hh * 128:(hh + 1) * 128],
                                             start=False, stop=(j == 6), skip_group_check=True)
                            return r
                        P.op("pe", ap, reads=rk + [bk], writes=[bk])
                P.op("act", lambda e: e.copy(out=WY[:, 0:4, :].rearrange("p a b -> p (a b)"), in_=l0[0][:]), reads=["l00"], writes=["WY0"])
                P.op("dve", lambda e: e.tensor_copy(out=WY[:, 4:8, :].rearrange("p a b -> p (a b)"), in_=l0[1][:]), reads=["l01"], writes=["WY1"])
                WK = ["WY0", "WY1"]
                mb, mbk = nextsq()

                def mmt(e):
                    r = None
                    for h in range(8):
                        cc, pb = h // 2, (h % 2) * 64
                        r = e.matmul(mb[pb:pb + 64, cc * 64:(cc + 1) * 64], lhsT=WY[:, h, 0:64], rhs=BTM[s3][:, h * 64:(h + 1) * 64], start=True, stop=True)
                    return r
                P.op("pe", mmt, reads=WK + [f"BTM{s3}"], writes=[mbk])
                P.op("act", lambda e: e.copy(out=MTb[:].rearrange("p a b -> p (a b)"), in_=mb[:, 0:256]), reads=[mbk], writes=["MTb"])

                def gp(e):
                    r = None
                    for h in range(8):
                        cc, pb = h // 2, (h % 2) * 64
                        o = Zp[pb:pb + 64, cc * 64:(cc + 1) * 64]
                        e.matmul(o, lhsT=BTM[s3][:, h * 64:(h + 1) * 64], rhs=WY[:, h, 64:128], start=(h < 2), stop=False, skip_group_check=True)
                        r = e.matmul(o, lhsT=KTM[s3][:, h * 64:(h + 1) * 64], rhs=VTM[s3][:, h * 64:(h + 1) * 64], start=False, stop=False, skip_group_check=True)
                    return r
                P.op("pe", gp, reads=WK + [f"BTM{s3}", f"KTM{s3}", f"VTM{s3}"], writes=["Zp"])
                if own:
                    wb, wbk = nextsq()
                    wb16 = wb[:].bitcast(BF16)

                    def trw(e):
                        r = None
                        for h in range(8):
                            cc, pb = h // 2, (h % 2) * 64
                            r = e.transpose(out=wb16[pb:pb + 64, cc * 128:(cc + 1) * 128], in_=WY[:, h, 0:64], identity=ident[:])
                        return r
                    P.op("pe", trw, reads=WK + ["const"], writes=[wbk])
                    P.op("act", lambda e: e.copy(out=WTF[:].rearrange("p a b -> p (a b)"), in_=wb16[:, 0:512]), reads=[wbk], writes=["WTFa", "WTFb"])

            def S5n(i):
                s3 = i % R2
                own = is_own(i)
                ARk = [f"AR{s3}a", f"AR{s3}r"]
                WK = ["WY0", "WY1"]
                if own:
                    ub, ubk = nextpj()

                    def mu_(e):
                        r = None
                        for h in range(8):
                            cc, pb = h // 2, (h % 2) * 64
                            o = ub[:, h * 64:(h + 1) * 64]
                            e.matmul(o, lhsT=WTF[pb:pb + 64, cc, :], rhs=Hb[pb:pb + 64, cc, :], start=True, stop=False)
                            r = e.matmul(o, lhsT=ident[:], rhs=WY[:, h, 64:128], start=False, stop=True)
                        return r
                    P.op("pe", mu_, reads=WK + ["WTFa", "WTFb", "Hb", "const"], writes=[ubk])
                    P.op("act", lambda e: e.copy(out=Ub[:], in_=ub[:]), reads=[ubk], writes=["Ub"])
                    ob, obk = nextpj()
                    oq = i % R2

                    def mo(e):
                        r = None
                        for h in range(8):
                            cc, pb = h // 2, (h % 2) * 64
                            o = ob[:, h * 64:(h + 1) * 64]
                            e.matmul(o, lhsT=AR[s3][pb:pb + 64, cc, 1, :], rhs=Hb[pb:pb + 64, cc, :], start=True, stop=False)
                            e.matmul(o, lhsT=L0S[:, h, 128:256], rhs=Ub[:, h * 64:(h + 1) * 64], start=False, stop=False)
                            r = e.matmul(o, lhsT=L0S[:, h, 384:512], rhs=VTM[s3][:, h * 64:(h + 1) * 64], start=False, stop=True)
                        return r
                    P.op("pe", mo, reads=ARk + ["Hb", "Ub", f"VTM{s3}"] + [f"L0S{h}" for h in range(8)], writes=[obk])
                    P.op("act", lambda e: e.copy(out=OTM[oq][:].rearrange("p a b -> p (a b)"), in_=ob[:]), reads=[obk], writes=[f"OTM{oq}"])

                def ch(e):
                    r = None
                    for h in range(8):
                        cc, pb = h // 2, (h % 2) * 64
                        r = e.matmul(Zp[pb:pb + 64, cc * 64:(cc + 1) * 64], lhsT=MTb[pb:pb + 64, cc, :], rhs=Hb[pb:pb + 64, cc, :],
                                     start=False, stop=True, skip_group_check=True)
                    return r
                P.op("pe", ch, reads=["MTb", "Hb", "Zp"], writes=["Zp"])
                P.op("dve", lambda e: e.tensor_tensor(out=Hst[:].rearrange("p a b -> p (a b)"), in0=Zp[:, 0:256],
                                                      in1=tS[:].rearrange("p a b -> p (a b)"), op=ALU.add),
                     reads=["Zp", "tS"], writes=["Hst"])
                P.op("act", lambda e: e.copy(out=Hb[:], in_=Hst[:]), reads=["Hst"], writes=["Hb"])
                if i + 1 < NPT:
                    n3 = (i + 1) % R2
                    P.op("pool", lambda e: e.tensor_tensor(out=tS[:], in0=Hst[:], in1=gC[n3][:, :, 0:1].broadcast_to([128, 4, 64]), op=ALU.mult),
                         reads=["Hst", f"gC{n3}"], writes=["tS"])
                if i == NPT - 1:
                    bank, bk = nextpj()

                    def trs(e):
                        r = None
                        for cc in range(4):
                            r = e.transpose(out=bank[0:64, cc * 128:(cc + 1) * 128], in_=Hst[:, cc, :], identity=ident32[:])
                        return r
                    P.op("pe", trs, reads=["Hst", "const2"], writes=[bk])
                    P.op("dve", lambda e: e.tensor_copy(out=osq[0:64, :, :].rearrange("p a b -> p (a b)"), in_=bank[0:64, :]), reads=[bk, "osq"], writes=["osq"])
                    P.op("sp", lambda e: e.dma_start(out=wkv_p.rearrange("h v k -> v h k"), in_=osq[0:64, :, :]), reads=["osq"], writes=["o_wkv_p"], dma="o_wkv_p")
                    outkeys.append("o_wkv_p")

            def sample_h0(s3):
                for q4 in range(4):
                    P.op("sp", lambda e, q4=q4: e.dma_start(out=S0q[:], in_=swkv[q4 * 4:(q4 + 1) * 4].rearrange("s h v k -> v (s h) k")),
                         writes=["S0q"], dma="S0q")
                    for sl_ in range(4):
                        s = q4 * 4 + sl_
                        bank, bk = nextpj()

                        def trs(e, sl_=sl_, bank=bank):
                            r = None
                            for cc in range(4):
                                r = e.transpose(out=bank[:, cc * 64:(cc + 1) * 64],
                                                in_=S0q[:, sl_ * 8 + 2 * cc:sl_ * 8 + 2 * cc + 2, :].rearrange("p a b -> p (a b)"),
                                                identity=ident32[0:64, 0:64])
                            return r
                        P.op("pe", trs, reads=["S0q", "const2"], writes=[bk])
                        P.op("dve", lambda e, s=s, bank=bank: e.tensor_copy(out=H0s[:, s, :, :].rearrange("p a b -> p (a b)"), in_=bank[:, 0:256]),
                             reads=[bk], writes=[f"H0s{s}"])
                        P.op("act", lambda e, s=s, bank=bank: e.copy(out=H0b[:, s, :, :].rearrange("p a b -> p (a b)"), in_=bank[:, 0:256]),
                             reads=[bk], writes=[f"H0b{s}"])
                for cc in range(4):
                    bank, bk = nextpj()

                    def mmz(e, cc=cc, bank=bank):
                        r = None
                        for hh in range(2):
                            pb = hh * 64
                            for s in range(16):
                                r = e.matmul(bank[pb:pb + 64, s * 16:s * 16 + 16],
                                             lhsT=H0b[pb:pb + 64, s, cc, :],
                                             rhs=AR[s3][pb:pb + 64, cc, :, s * 8:(s + 1) * 8], start=True, stop=True)
                        return r
                    P.op("pe", mmz, reads=[f"H0b{s}" for s in range(16)] + [f"AR{s3}a", f"AR{s3}r"], writes=[bk])
                    src = bank[:, 0:256].rearrange("p (s a t) -> p a s t", s=16, a=2)
                    for ar in range(2):
                        dst = zh[:, cc, ar, :].rearrange("p (s t) -> p s t", t=8)
                        P.op("dve", lambda e, src=src, dst=dst, ar=ar: e.tensor_copy(out=dst, in_=src[:, ar, :, :]), reads=[bk], writes=["zh"])

            def sample_state(s3):
                e16bc = e16[:, :].unsqueeze(2).broadcast_to([128, 16, 64])
                HK = [f"H0s{s}" for s in range(16)]
                for h in range(8):
                    cc, pb = h // 2, (h % 2) * 64
                    P.op("dve", lambda e, h=h: e.tensor_tensor(out=Xex[:, 0, :, :], in0=Ub[:, h * 64:(h + 1) * 64].unsqueeze(1).broadcast_to([128, 16, 64]),
                                                               in1=e16bc, op=ALU.mult),
                         reads=["Ub", "const"], writes=["Xex0"])
                    P.op("pool", lambda e, h=h: e.tensor_tensor(out=Xex[:, 1, :, :], in0=VTM[s3][:, h * 64:(h + 1) * 64].unsqueeze(1).broadcast_to([128, 16, 64]),
                                                                in1=e16bc, op=ALU.mult),
                         reads=[f"VTM{s3}", "const"], writes=["Xex1"])
                    for half in range(2):
                        bank, bk = nextl0()

                        def mms(e, h=h, half=half, bank=bank, pb=pb):
                            e.matmul(bank[pb:pb + 64, :], lhsT=BTM[s3][:, h * 64:(h + 1) * 64],
                                     rhs=Xex[:, 0, half * 8:(half + 1) * 8, :].rearrange("p a b -> p (a b)"), start=True, stop=False)
                            return e.matmul(bank[pb:pb + 64, :], lhsT=KTM[s3][:, h * 64:(h + 1) * 64],
                                            rhs=Xex[:, 1, half * 8:(half + 1) * 8, :].rearrange("p a b -> p (a b)"), start=False, stop=True)
                        P.op("pe", mms, reads=["Xex0", "Xex1", f"BTM{s3}", f"KTM{s3}"], writes=[bk])
                        P.op("dve", lambda e, half=half, cc=cc, bank=bank, pb=pb: e.tensor_tensor(
                            out=H0s[pb:pb + 64, half * 8:(half + 1) * 8, cc, :], in0=bank[pb:pb + 64, :].rearrange("p (s v) -> p s v", s=8),
                            in1=H0s[pb:pb + 64, half * 8:(half + 1) * 8, cc, :], op=ALU.add),
                            reads=[bk] + HK, writes=HK)
                for cc in range(4):
                    P.op("dve", lambda e, cc=cc: e.tensor_tensor(out=H0s[:, :, cc, :], in0=H0s[:, :, cc, :],
                                                                 in1=gC[s3][:, cc, :].unsqueeze(2).broadcast_to([128, 16, 64]), op=ALU.mult),
                         reads=HK + [f"gC{s3}"], writes=HK)
                for q4 in range(4):
                    for sl_ in range(4):
                        s = q4 * 4 + sl_
                        bank, bk = nextpj()

                        def trw(e, s=s, bank=bank):
                            r = None
                            for cc in range(4):
                                r = e.transpose(out=bank[0:64, cc * 128:(cc + 1) * 128], in_=H0s[:, s, cc, :], identity=ident32[:])
                            return r
                        P.op("pe", trw, reads=HK + ["const2"], writes=[bk])
                        if s % 2 == 0:
                            P.op("act", lambda e, sl_=sl_, bank=bank: e.copy(out=wso[:, sl_, :, :].rearrange("p a b -> p (a b)"), in_=bank[0:64, :]),
                                 reads=[bk], writes=[f"wso{sl_}"])
                        else:
                            P.op("dve", lambda e, sl_=sl_, bank=bank: e.tensor_copy(out=wso[:, sl_, :, :].rearrange("p a b -> p (a b)"), in_=bank[0:64, :]),
                                 reads=[bk], writes=[f"wso{sl_}"])
                    P.op("sp", lambda e, q4=q4: e.dma_start(out=wkv_s[q4 * 4:(q4 + 1) * 4].rearrange("s h v k -> v s h k"), in_=wso[:]),
                         reads=[f"wso{q}" for q in range(4)], writes=[f"o_wkv_s{q4}"] + [f"wso{q}" for q in range(4)], dma="o_wkv_s")
                    outkeys.append(f"o_wkv_s{q4}")

            def S6(i):
                qq = i % len(qT)
                s2 = i % R2
                samp = is_samp(i)
                kc3 = i % NKV
                kp3 = (i - 1) % NKV
                if i == NPRE:
                    P.op("pe", lambda e: e.transpose(out=tpb[:, 0:128], in_=kvT[kp3][:, 1, :], identity=ident[:]), reads=[f"kvT{kp3}", "const"], writes=["tpb"])
                    P.op("act", lambda e: e.copy(out=Vaug[kp3][:, :, 0:64], in_=tpb[:, 0:128].rearrange("p (g d) -> p g d", g=2)),
                         reads=["tpb"], writes=[f"Vaug{kp3}"])
                P.op("pe", lambda e: e.transpose(out=tpb[:, 0:128], in_=kvT[kc3][:, 1, :], identity=ident[:]), reads=[f"kvT{kc3}", "const"], writes=["tpb"])
                P.op("act", lambda e: e.copy(out=Vaug[kc3][:, :, 0:64], in_=tpb[:, 0:128].rearrange("p (g d) -> p g d", g=2)),
                     reads=["tpb"], writes=[f"Vaug{kc3}"])
                if samp:
                    sample_cache(qq)
                for g in range(2):
                    for kt in range(2):
                        if samp and kt == 0:
                            continue
                        bank, bk = nextl0()
                        kb = kvT[kp3] if kt == 0 else kvT[kc3]
                        kbk = f"kvT{kp3}" if kt == 0 else f"kvT{kc3}"
                        P.op("pe", lambda e, bank=bank, kb=kb, g=g: e.matmul(bank[:], lhsT=kb[g * 64:(g + 1) * 64, 0, :],
                                                                             rhs=qT[qq][g * 64:(g + 1) * 64, :, :], start=True, stop=True),
                             reads=[kbk, f"qT{qq}"], writes=[bk])
                        pt = PT[g * 2 + kt]
                        ptk = f"PT{g * 2 + kt}"
                        P.op("act", lambda e, bank=bank, pt=pt: e.activation(out=pt[:].rearrange("p a b -> p (a b)"), in_=bank[:], func=AF.Exp),
                             reads=[bk], writes=[ptk])
                        if kt == 0:
                            mi = 6 if i == NPRE else 2
                        else:
                            mi = 4 if samp else 1
                        mbc = MK[:, mi, :].unsqueeze(1).broadcast_to([128, 4, 128])
                        P.op("dve", lambda e, pt=pt, mbc=mbc: e.tensor_tensor(out=pt[:], in0=pt[:], in1=mbc, op=ALU.mult),
                             reads=[ptk, "const"], writes=[ptk])
                for g in range(2):
                    bank, bk = nextsq()
                    b3 = bank[:, 0:260].rearrange("p (a b) -> p a b", a=4)
                    for j in range(4):
                        h = g * 4 + j
                        if samp:
                            px = PTx[h % 2]
                            pxk = f"PTx{h % 2}"
                            P.op("dve", lambda e, g=g, j=j, px=px: [e.tensor_tensor(
                                out=px[:, s, s * 8:(s + 1) * 8], in0=PTc[:, g, s, j * 8:(j + 1) * 8], in1=MK[:, 2, 0:8], op=ALU.mult) for s in range(16)][-1],
                                reads=[f"PTc{g}", "const"], writes=[pxk])

                        def pv(e, g=g, j=j, b3=b3, h=h):
                            r = None
                            if samp:
                                for s in range(16):
                                    e.matmul(b3[:, j, :], lhsT=PTx[h % 2][:, s, :], rhs=Vca[:, s, g, :], start=(s == 0), stop=False)
                            else:
                                e.matmul(b3[:, j, :], lhsT=PT[g * 2][:, j, :], rhs=Vaug[kp3][:, g, :], start=True, stop=False)
                            r = e.matmul(b3[:, j, :], lhsT=PT[g * 2 + 1][:, j, :], rhs=Vaug[kc3][:, g, :], start=False, stop=True)
                            return r
                        rd = [f"PT{g * 2 + 1}", f"Vaug{kc3}"] + ([f"PTx{h % 2}", "Vca"] if samp else [f"PT{g * 2}", f"Vaug{kp3}"])
                        P.op("pe", pv, reads=rd, writes=[bk + f"_{j}"] + ([bk] if j == 0 else []))
                    bkj = [bk] + [bk + f"_{j}" for j in range(4)]
                    P.op("dve", lambda e, g=g, b3=b3: e.tensor_tensor(out=den[:, g * 4:(g + 1) * 4], in0=b3[:, :, 64], in1=esink[:, g * 4:(g + 1) * 4], op=ALU.add),
                         reads=bkj + ["esink"], writes=[f"den{g}"])
                    P.op("dve", lambda e, g=g: e.reciprocal(out=den[:, g * 4:(g + 1) * 4], in_=den[:, g * 4:(g + 1) * 4]), reads=[f"den{g}"], writes=[f"den{g}"])
                    P.op("dve", lambda e, g=g, b3=b3: e.tensor_tensor(out=attb[:, g * 4:(g + 1) * 4, :], in0=b3[:, :, 0:64],
                                                                      in1=den[:, g * 4:(g + 1) * 4].unsqueeze(2).broadcast_to([128, 4, 64]), op=ALU.mult),
                         reads=bkj + [f"den{g}"], writes=[f"attb{g}"])

                def tra(e):
                    r = None
                    for c in range(4):
                        r = e.transpose(out=tpb[:, c * 128:(c + 1) * 128], in_=attb[:, 2 * c:2 * c + 2, :].rearrange("p a b -> p (a b)"), identity=ident[:])
                    return r
                P.op("pe", tra, reads=["attb0", "attb1", "const"], writes=["tpb"])
                P.op("act", lambda e: e.copy(out=mixT[s2][:, 0:4, :].rearrange("p a b -> p (a b)"), in_=tpb[:, 0:512]), reads=["tpb"], writes=[f"mixT{s2}a"])

            def sample_cache(qq):
                P.op("pool", lambda e: e.dma_start(out=cstb[:], in_=ck.rearrange("s k d -> k s d")), writes=["cstb"], dma="cstb")
                for q in range(2):
                    def trc(e, q=q):
                        r = None
                        for ss_ in range(8):
                            r = e.transpose(out=tpb[:, ss_ * 128:(ss_ + 1) * 128], in_=cstb[:, q * 8 + ss_, :], identity=ident[:])
                        return r
                    P.op("pe", trc, reads=["cstb", "const"], writes=["tpb"])
                    P.op("act", lambda e, q=q: e.copy(out=KcT[:, q * 8:(q + 1) * 8, :].rearrange("p a b -> p (a b)"), in_=tpb[:]), reads=["tpb"], writes=[f"KcT{q}"])
                P.op("pool", lambda e: [e.dma_start(out=Vca[:, :, g, 0:64], in_=cv[:, :, g * 64:(g + 1) * 64].rearrange("s k d -> k s d")) for g in range(2)],
                     reads=["Vca"], writes=["Vca"], dma="Vca", n=2)
                for g in range(2):
                    bank, bk = nextl0()

                    def scc(e, g=g, bank=bank):
                        r = None
                        for s in range(16):
                            r = e.matmul(bank[:, s * 32:(s + 1) * 32], lhsT=KcT[g * 64:(g + 1) * 64, s, :],
                                         rhs=qT[qq][g * 64:(g + 1) * 64, :, s * 8:(s + 1) * 8], start=True, stop=True)
                        return r
                    P.op("pe", scc, reads=["KcT0", "KcT1", f"qT{qq}"], writes=[bk])
                    P.op("act", lambda e, g=g, bank=bank: e.activation(out=PTc[:, g, :, :].rearrange("p a b -> p (a b)"), in_=bank[:], func=AF.Exp),
                         reads=[bk], writes=[f"PTc{g}"])

            def S7(i):
                s2 = i % R2
                s3 = i % len(gT)
                j = i - NPRE
                o = OTM[i % R2]
                ok = f"OTM{i % R2}"
                P.op("dve", lambda e: e.tensor_reduce(out=st1[:], in_=o[:], axis=AX.X, op=ALU.add), reads=[ok], writes=["st1"])
                P.op("pool", lambda e: e.tensor_tensor(out=osq[:], in0=o[:], in1=o[:], op=ALU.mult), reads=[ok], writes=["osq"])
                P.op("dve", lambda e: e.tensor_reduce(out=st2[:], in_=osq[:], axis=AX.X, op=ALU.add), reads=["osq"], writes=["st2"])
                P.op("dve", lambda e: e.tensor_scalar(out=st1[:], in0=st1[:], scalar1=1.0 / 64, scalar2=None, op0=ALU.mult), reads=["st1"], writes=["st1"])
                P.op("dve", lambda e: e.tensor_tensor(out=st3[:], in0=st1[:], in1=st1[:], op=ALU.mult), reads=["st1"], writes=["st3"])
                P.op("dve", lambda e: e.scalar_tensor_tensor(out=st2[:], in0=st2[:], scalar=1.0 / 64, in1=st3[:], op0=ALU.mult, op1=ALU.subtract),
                     reads=["st2", "st3"], writes=["st2"])
                P.op("act", lambda e: e.activation(out=st2[:], in_=st2[:], func=AF.Sqrt, bias=64e-5), reads=["st2"], writes=["st2"])
                P.op("dve", lambda e: e.reciprocal(out=st2[:], in_=st2[:]), reads=["st2"], writes=["st2"])
                P.op("dve", lambda e: e.tensor_tensor(out=osq[:], in0=o[:], in1=st1[:, :].unsqueeze(2).broadcast_to([128, 8, 64]), op=ALU.subtract),
                     reads=[ok, "st1", "osq"], writes=["osq"])
                P.op("dve", lambda e: e.tensor_tensor(out=onb[:], in0=osq[:], in1=st2[:, :].unsqueeze(2).broadcast_to([128, 8, 64]), op=ALU.mult),
                     reads=["osq", "st2"], writes=["onb"])

                def tro(e):
                    r = None
                    for c in range(4):
                        r = e.transpose(out=tpb[:, c * 128:(c + 1) * 128], in_=onb[:, 2 * c:2 * c + 2, :].rearrange("p a b -> p (a b)"), identity=ident[:])
                    return r
                P.op("pe", tro, reads=["onb", "const"], writes=["tpb"])
                P.op("dve", lambda e: e.tensor_tensor(out=tmx[:], in0=tpb[:, 0:512].rearrange("p (a b) -> p a b", a=4), in1=v4bc(GNG), op=ALU.mult),
                     reads=["tpb", "const2", "tmx"], writes=["tmx"])
                P.op("pool", lambda e: e.tensor_tensor(out=tmx[:], in0=tmx[:], in1=v4bc(GNB), op=ALU.add), reads=["tmx", "const2"], writes=["tmx"])
                P.op("pool", lambda e: e.tensor_tensor(out=tmx[:], in0=tmx[:], in1=bonT[s3][:], op=ALU.add), reads=["tmx", f"bonT{s3}"], writes=["tmx"])
                P.op("dve", lambda e: e.tensor_tensor(out=mixT[s2][:, 4:8, :], in0=tmx[:], in1=gT[s3][:], op=ALU.mult),
                     reads=["tmx", f"gT{s3}"], writes=[f"mixT{s2}b"])
                xb = x1t[0]
                xk = "x1t0"
                src = xs[:, :] if is_samp(i) else xw[i * 128:(i + 1) * 128, :]
                P.op("sp", lambda e: e.dma_start(out=xb[:], in_=src), writes=[xk], dma=xk)
                for half in range(2):
                    bank, bk = nextpj()

                    def mo(e, half=half, bank=bank):
                        r = None
                        for kc in range(8):
                            r = e.matmul(bank[:], lhsT=mixT[s2][:, kc, :], rhs=Wout[:, kc, half * 512:(half + 1) * 512], start=(kc == 0), stop=(kc == 7))
                        return r
                    P.op("pe", mo, reads=[f"mixT{s2}a", f"mixT{s2}b", "const"], writes=[bk])
                    P.op("dve", lambda e, half=half, bank=bank: e.tensor_tensor(out=xb[:, half * 512:(half + 1) * 512], in0=bank[:],
                                                                                in1=xb[:, half * 512:(half + 1) * 512], op=ALU.add),
                         reads=[bk, xk], writes=[xk])
                P.op("sp", lambda e: e.dma_start(out=x1s[j * 128:(j + 1) * 128, :], in_=xb[:]), reads=[xk], writes=[f"x1s{j}", xk], dma=xk)
                outkeys.append(f"x1s{j}")

            if pA:
                def cap(fns):
                    P.cap = []
                    for fn, i in fns:
                        if 0 <= i < NPT:
                            fn(i)
                    out = P.cap
                    P.cap = None
                    return out

                for step in range(NPT + 6):
                    for fn, i in ((S7, step - 5), (S6, step - 4)):
                        if 0 <= i < NPT and is_own(i):
                            fn(i)
                    if 0 <= step - 4 < NPT:
                        S5n(step - 4)
                    la = cap([(S4n, step - 3)])
                    if ILV_MODE == 1:
                        if 0 <= step - 2 < NPT:
                            S3(step - 2)
                        lb = cap([(S2, step - 1), (S1, step)])
                    elif ILV_MODE == 2:
                        lb = cap([(S3, step - 2)])
                    else:
                        lb = cap([(S3, step - 2), (S2, step - 1), (S1, step)])
                    na, nb_ = len(la), len(lb)
                    ia = ib = 0
                    while ia < na or ib < nb_:
                        if ib >= nb_ or (ia < na and (NO_ILV or (ia // ILV_CH) * nb_ <= (ib // max(1, (ILV_CH * nb_) // max(1, na))) * na)):
                            P.op(*la[ia][:2], reads=la[ia][2], writes=la[ia][3], dma=la[ia][4], n=la[ia][5])
                            ia += 1
                        else:
                            if len(P.ins) not in SKIPOPS:
                                P.op(*lb[ib][:2], reads=lb[ib][2], writes=lb[ib][3], dma=lb[ib][4], n=lb[ib][5])
                            else:
                                P.op(lb[ib][0], None)
                            ib += 1
                    if ILV_MODE == 2:
                        for fn, i in ((S2, step - 1), (S1, step)):
                            if 0 <= i < NPT:
                                fn(i)
            elif pSa:
                S1(NPT)
                S2(NPT)
                outkeys.extend(["qT0", "kvT0", "fprev0", "fprev1"] + [f"fT0_{g}" for g in range(4)])
            else:
                for fn in (S3, S4, S5, S6, S7):
                    fn(NPT)
            P.op("sp", None, reads=list(outkeys))
            P.emit()
            build.stats[mode] = P.stats

    if "A" in PHASES:
        phase("A", None)
    with ExitStack() as stp:
        per = dict(
            fT=[stp.enter_context(nc.sbuf_tensor("fTs", [128, 14, 128], F32))],
            qT=[stp.enter_context(nc.sbuf_tensor("qTs", [128, 4, 128], BF16))],
            kvT=[stp.enter_context(nc.sbuf_tensor("kvTs", [128, 2, 128], BF16))],
            fprev=stp.enter_context(nc.sbuf_tensor("fprevs", [128, 14, 16], F32)),
        )
        if "Sa" in PHASES:
            phase("Sa", per)
        if "Sb" in PHASES:
            phase("Sb", per)

    if "B" not in PHASES:
        return nc
    with ExitStack() as st:
        def sb(name, shape, dt=F32):
            return st.enter_context(nc.sbuf_tensor(name, shape, dt))

        def psb(name, shape, dt=F32):
            return st.enter_context(nc.psum_tensor(name, shape, dt))
        P = Prog(nc, '_B')
        outk = []
        Wg = sb("Wg", [128, 8, D_FF], BF16)
        Wu = sb("Wu", [128, 8, D_FF], BF16)
        Wd = sb("Wd", [128, NFC, 1024], BF16)
        gffn = sb("gffn", [128, 1024])
        gfin = sb("gfin", [128, 1024])
        identb = sb("identb", [128, 128], BF16)

        P.op("pool", lambda e: e.dma_start(out=identb[:], in_=ident_d[:, :]), writes=["W"], dma="Wi")
        P.op("pool", lambda e: [e.dma_start(out=Wg[:, kc, :], in_=w_gate[kc * 128:(kc + 1) * 128, :]) for kc in range(8)],
             writes=["Wg"], dma="Wg", n=8)
        P.op("pool", lambda e: [e.dma_start(out=Wu[:, kc, :], in_=w_up[kc * 128:(kc + 1) * 128, :]) for kc in range(8)],
             writes=["Wu"], dma="Wu", n=8)
        P.op("pool", lambda e: [e.dma_start(out=Wd[:, fc, :], in_=w_down[fc * 128:(fc + 1) * 128, :]) for fc in range(NFC)],
             writes=["Wd"], dma="Wd", n=NFC)

        def cl(e):
            return [e.dma_start(out=gffn[:], in_=gvec[1:2, :].broadcast_to([128, 1024])),
                    e.dma_start(out=gfin[:], in_=gvec[2:3, :].broadcast_to([128, 1024]))]
        P.op("sp", cl, writes=["G"], dma="G", n=2)
        xg = sb("xg", [128, 4, 1024])
        junk2 = sb("junk2", [128, 1024], BF16)
        ub = sb("ub", [128, 1024], BF16)
        uT = sb("uT", [128, 8, 512], BF16)
        actT = sb("actT", [128, 11, 512], BF16)
        sgt = sb("sgt", [128, 512])
        ssb = sb("ssb", [128, 1])
        rsb = sb("rsb", [128, 1])
        yb = [sb(f"yb{q}", [128, 1024]) for q in range(2)]
        pg = [psb(f"pg{q}", [128, 512]) for q in range(2)]
        pu = [psb(f"pu{q}", [128, 512]) for q in range(2)]
        pd = [psb(f"pd{q}", [128, 512]) for q in range(2)]
        tpb2 = psb("tpb2", [128, 1024], BF16)
        cnt = [0]
        groups = [(0, 4), (4, 4), (8, 4), (12, 4), (16, 1)]
        for (t0, nt) in groups:
            N = nt * 128
            P.op("sp", lambda e, t0=t0, nt=nt: e.dma_start(out=xg[:, 0:nt, :], in_=x1s[t0 * 128:(t0 + nt) * 128, :].rearrange("(a p) d -> p a d", p=128)),
                 writes=["xg"], dma="xg")
            for a in range(nt):
                P.op("act", lambda e, a=a: e.activation(out=junk2[:], in_=xg[:, a, :], func=AF.Square, accum_out=ssb[:]), reads=["xg"], writes=["junk2", "ssb"])
                P.op("act", lambda e: e.activation(out=rsb[:], in_=ssb[:], func=AF.Sqrt, scale=1.0 / 1024, bias=1e-6), reads=["ssb"], writes=["rsb"])
                P.op("dve", lambda e: e.reciprocal(out=rsb[:], in_=rsb[:]), reads=["rsb"], writes=["rsb"])
                P.op("dve", lambda e, a=a: e.scalar_tensor_tensor(out=ub[:], in0=xg[:, a, :], scalar=rsb[:, 0:1], in1=gffn[:], op0=ALU.mult, op1=ALU.mult),
                     reads=["xg", "rsb", "G"], writes=["ub"])

                def tr(e):
                    r = None
                    for kc in range(8):
                        r = e.transpose(out=tpb2[:, kc * 128:(kc + 1) * 128], in_=ub[:, kc * 128:(kc + 1) * 128], identity=identb[:])
                    return r
                P.op("pe", tr, reads=["ub", "W"], writes=["tpb2"])
                P.op("act", lambda e, a=a: e.copy(out=uT[:, :, a * 128:(a + 1) * 128], in_=tpb2[:].rearrange("p (a b) -> p a b", a=8)),
                     reads=["tpb2"], writes=[f"uT{a}"])
            uk = [f"uT{a}" for a in range(nt)]
            for hf in range(2):
                for fi in range(11):
                    fc = hf * 11 + fi
                    cnt[0] += 1
                    b = cnt[0] % 2

                    def mg(e, fc=fc, b=b, N=N):
                        r = None
                        for kc in range(8):
                            r = e.matmul(pg[b][:, 0:N], lhsT=Wg[:, kc, fc * 128:(fc + 1) * 128], rhs=uT[:, kc, 0:N], start=(kc == 0), stop=(kc == 7))
                        return r
                    P.op("pe", mg, reads=["Wg"] + uk, writes=[f"pg{b}"])

                    def mu_(e, fc=fc, b=b, N=N):
                        r = None
                        for kc in range(8):
                            r = e.matmul(pu[b][:, 0:N], lhsT=Wu[:, kc, fc * 128:(fc + 1) * 128], rhs=uT[:, kc, 0:N], start=(kc == 0), stop=(kc == 7))
                        return r
                    P.op("pe", mu_, reads=["Wu"] + uk, writes=[f"pu{b}"])
                    P.op("act", lambda e, b=b, N=N: e.activation(out=sgt[:, 0:N], in_=pg[b][:, 0:N], func=AF.Silu), reads=[f"pg{b}"], writes=["sgt"])
                    P.op("dve", lambda e, b=b, N=N, fi=fi: e.tensor_tensor(out=actT[:, fi, 0:N], in0=pu[b][:, 0:N], in1=sgt[:, 0:N], op=ALU.mult),
                         reads=[f"pu{b}", "sgt"], writes=[f"actT{fi}"])
                ak = [f"actT{fi}" for fi in range(11)]
                for a in range(nt):
                    for half in range(2):
                        cnt[0] += 1
                        b = cnt[0] % 2

                        def md(e, a=a, half=half, b=b, hf=hf):
                            r = None
                            for fi in range(11):
                                r = e.matmul(pd[b][:], lhsT=actT[:, fi, a * 128:(a + 1) * 128], rhs=Wd[:, hf * 11 + fi, half * 512:(half + 1) * 512],
                                             start=(fi == 0), stop=(fi == 10))
                            return r
                        P.op("pe", md, reads=["Wd"] + ak, writes=[f"pd{b}"])
                        P.op("dve", lambda e, a=a, half=half, b=b: e.tensor_tensor(out=xg[:, a, half * 512:(half + 1) * 512], in0=pd[b][:],
                                                                                   in1=xg[:, a, half * 512:(half + 1) * 512], op=ALU.add),
                             reads=[f"pd{b}", "xg"], writes=["xg"])
            for a in range(nt):
                t = t0 + a
                y = yb[t % 2]
                yk = f"yb{t % 2}"
                P.op("act", lambda e, a=a: e.activation(out=junk2[:], in_=xg[:, a, :], func=AF.Square, accum_out=ssb[:]), reads=["xg"], writes=["junk2", "ssb"])
                P.op("act", lambda e: e.activation(out=rsb[:], in_=ssb[:], func=AF.Sqrt, scale=1.0 / 1024, bias=1e-6), reads=["ssb"], writes=["rsb"])
                P.op("dve", lambda e: e.reciprocal(out=rsb[:], in_=rsb[:]), reads=["rsb"], writes=["rsb"])
                P.op("dve", lambda e, a=a, y=y: e.scalar_tensor_tensor(out=y[:], in0=xg[:, a, :], scalar=rsb[:, 0:1], in1=gfin[:], op0=ALU.mult, op1=ALU.mult),
                     reads=["xg", "rsb", "G"], writes=[yk])
                dst = y_s[:, :] if t == 16 else y_p[t * 128:(t + 1) * 128, :]
                P.op("sp", lambda e, y=y, dst=dst: e.dma_start(out=dst, in_=y[:]), reads=[yk], writes=[f"oy{t}", yk], dma=yk)
                outk.append(f"oy{t}")
        P.op("sp", None, reads=outk)
        P.emit()
        build.stats["B"] = P.stats
    return nc


def _consts(p):
    s = np.arange(128)[:, None]
    t = np.arange(128)[None, :]
    su = (s < t).astype(np.float32)
    ui = (s <= t).astype(np.float32)
    sl = (s > t).astype(np.float32)
    same = ((s // 8) == (t // 8)).astype(np.float32)
    mfirst = sl if p > 0 else np.zeros_like(sl)
    masks = np.stack([su, ui, sl, su * same, ui * same, sl * same, mfirst], axis=1).reshape(128, 7 * 128)
    ident = np.eye(128, dtype=np.float32)
    bones = ((s // 64) == (t // 64)).astype(np.float32)
    rm = np.ones((128, 2, 128), np.float32)
    rm[:, 1, :] = (np.arange(128) % 8 != 0).astype(np.float32)[None, :]
    e16 = ((np.arange(128)[:, None] // 8) == np.arange(16)[None, :]).astype(np.float32)
    return dict(masks=np.ascontiguousarray(masks), ident=ident, bones=bones, rmask=rm.reshape(128, 256), e16=e16)


_NC = [None]


def kernel(x_prompt, x_sample, cache_k, cache_v, state_wkv, state_shift, g_mix, w_in, attn_sinks,
           rwkv_mu, w0, w2, a0, a2, g2, k_k, k_a, r_k, gn_g, gn_b, w_out, g_ffn, w_gate, w_up,
           w_down, g_final):
    f = lambda a: np.ascontiguousarray(np.asarray(a, dtype=np.float32))
    x_prompt, x_sample = f(x_prompt), f(x_sample)
    w_in0 = f(w_in)[0]
    qperm = np.concatenate([np.r_[j * 64:(j + 1) * 64, (4 + j) * 64:(5 + j) * 64] for j in range(4)])
    w_in_p = np.ascontiguousarray(np.concatenate([w_in0[:, qperm], w_in0[:, 512:]], axis=1))
    fm4 = lambda v: f(v).reshape(4, 128).T
    vec4 = np.ascontiguousarray(np.stack([fm4(w0[0]), fm4(a0[0]), fm4(k_k[0]), fm4(k_a[0]), fm4(f(r_k)[0].reshape(-1)),
                                          fm4(gn_g[0]), fm4(gn_b[0])], axis=1).reshape(128, 28))
    shared = dict(
        w_in=w_in_p, w_out=f(w_out)[0], w_gate=f(w_gate)[0], w_up=f(w_up)[0], w_down=f(w_down)[0],
        gvec=np.ascontiguousarray(np.stack([f(g_mix)[0], f(g_ffn)[0], f(g_final)], axis=0)),
        mu=np.ascontiguousarray(f(rwkv_mu)[0].reshape(14, 128).T),
        vec4=vec4,
        w2a2=np.ascontiguousarray(np.concatenate([f(w2)[0], f(a2)[0]], axis=0)),
        g2=f(g2)[0],
        sinks=f(attn_sinks)[0].reshape(1, 8),
    )
    in_maps = []
    for c in range(8):
        b, p = c // 4, c % 4
        xwin = np.zeros((NPT * 128, 1024), np.float32)
        nreal = (p + 1) * 2048
        xwin[NPT * 128 - nreal:] = x_prompt[b, 0:nreal]
        m = dict(shared)
        m.update(_consts(p))
        m.update(
            xw=xwin,
            xs=np.ascontiguousarray(x_sample[16 * c:16 * c + 16].reshape(128, 1024)),
            hprev=f(state_shift)[0, 16 * c:16 * c + 16],
            ck=np.ascontiguousarray(f(cache_k)[0, 16 * c:16 * c + 16].reshape(16, 128, 128)),
            cv=np.ascontiguousarray(f(cache_v)[0, 16 * c:16 * c + 16].reshape(16, 128, 128)),
            swkv=np.ascontiguousarray(f(state_wkv)[0, 16 * c:16 * c + 16]),
        )
        in_maps.append(m)
    if _NC[0] is None:
        _NC[0] = build()
    res = run_bass_kernel_spmd(_NC[0], in_maps, core_ids=list(range(8)))
    R = res.results
    y_prompt = np.stack([np.concatenate([R[b * 4 + p]["y_p"] for p in range(4)], axis=0) for b in range(2)], axis=0)
    y_sample = np.concatenate([R[c]["y_s"].reshape(16, 8, 1024) for c in range(8)], axis=0)
    kp = np.stack([R[b * 4 + 3]["kwin_p"].reshape(128, 2, 64) for b in range(2)], axis=0)[None]
    vp = np.stack([R[b * 4 + 3]["vwin_p"].reshape(128, 2, 64) for b in range(2)], axis=0)[None]
    sp = np.stack([R[b * 4 + 3]["wkv_p"] for b in range(2)], axis=0)[None]
    hp = np.stack([R[b * 4 + 3]["shift_p"].reshape(1024) for b in range(2)], axis=0)[None]
    ks = np.concatenate([R[c]["kwin_s"].reshape(16, 128, 2, 64) for c in range(8)], axis=0)[None]
    vs = np.concatenate([R[c]["vwin_s"].reshape(16, 128, 2, 64) for c in range(8)], axis=0)[None]
    ss_ = np.concatenate([R[c]["wkv_s"] for c in range(8)], axis=0)[None]
    hs = np.concatenate([R[c]["shift_s"] for c in range(8)], axis=0)[None]
    return tuple(np.ascontiguousarray(a.astype(np.float32)) for a in (y_prompt, y_sample, kp, vp, sp, hp, ks, vs, ss_, hs))
```

```python
import numpy as np
from contextlib import ExitStack
import concourse.bass as bass
import concourse.mybir as mybir
from concourse.bass_utils import run_bass_kernel_spmd
from concourse.alu_op_type import AluOpType as ALU

AF = mybir.ActivationFunctionType
AX = mybir.AxisListType
F32 = mybir.dt.float32
BF16 = mybir.dt.bfloat16

SAME_ENGINE_SYNC = True
SAME_ENGINE_RAW_ONLY = False
ENGS = ("pe", "act", "dve", "pool", "sp")
C0 = float(np.exp(-0.5))
NPRE = 48
NOWN = 16
NPT = NPRE + NOWN
NT = NPT + 1
D_FF = 2816
NFC = 22
ILV_CH = 2
ILV_MODE = 0
SKIPOPS = set()
NO_ILV = False
PSUM_PREFIXES = ("pj", "tpb", "l0", "sqp", "Zp", "pg", "pu", "pd")
OPLIMIT = 10 ** 9
NO_D2D = False
PHASES = {"A", "Sa", "Sb", "B"}


class Prog:
    def __init__(self, nc, tag=''):
        self.nc = nc
        self.tag = tag
        self.ins = []
        self.last_w = {}
        self.readers = {}

    def op(self, eng, fn, reads=(), writes=(), dma=None, n=1):
        if getattr(self, "cap", None) is not None:
            self.cap.append((eng, fn, list(reads), list(writes), dma, n))
            return -1
        idx = len(self.ins)
        if idx >= OPLIMIT:
            return idx
        writes = list(writes) + [k for k in reads if k.startswith(PSUM_PREFIXES) and k not in writes]
        deps = set()
        raw = set()
        for k in reads:
            if k in self.last_w:
                deps.add(self.last_w[k])
                raw.add(self.last_w[k])
        for k in writes:
            if k in self.last_w:
                deps.add(self.last_w[k])
            for r in self.readers.get(k, ()):
                deps.add(r)
        deps.discard(idx)
        if fn is None:
            writes = []
        self.ins.append(dict(eng=eng, fn=fn, deps=deps, raw=raw, dma=dma, used=False, n=n))
        for k in reads:
            self.readers.setdefault(k, []).append(idx)
        for k in writes:
            self.last_w[k] = idx
            self.readers[k] = []
        return idx

    def emit(self):
        nc = self.nc
        ins = self.ins
        for r in ins:
            if SAME_ENGINE_RAW_ONLY:
                r["deps"] = {d for d in r["deps"]
                             if not (ins[d]["eng"] == r["eng"] and ins[d]["dma"] is None and r["dma"] is None and d not in r["raw"])}
            for d in r["deps"]:
                ins[d]["used"] = True
        cnt = {e: 0 for e in ENGS}
        dmav = {}
        for r in ins:
            if r["dma"] is not None:
                r["sem"] = "dma_" + r["dma"]
                dmav[r["sem"]] = dmav.get(r["sem"], 0) + 16 * r["n"]
                r["val"] = dmav[r["sem"]]
            elif r["used"]:
                cnt[r["eng"]] += 1
                r["sem"] = "eng_" + r["eng"]
                r["val"] = cnt[r["eng"]]
            else:
                r["sem"] = None
                r["val"] = 0
        known = {e: {} for e in ENGS}
        for r in ins:
            e = r["eng"]
            kn = known[e]
            wd = {}
            for d in sorted(r["deps"]):
                rd = ins[d]
                s, v = rd["sem"], rd["val"]
                if rd["eng"] == e and rd["dma"] is None and not SAME_ENGINE_SYNC:
                    continue
                if kn.get(s, 0) >= v:
                    continue
                wd[s] = max(wd.get(s, 0), v)
                for s2, v2 in rd["clock"].items():
                    if kn.get(s2, 0) < v2:
                        kn[s2] = v2
            r["waits"] = sorted(wd.items())
            ck = dict(kn)
            if r["sem"] is not None:
                ck[r["sem"]] = r["val"]
            r["clock"] = ck
        semnames = sorted({r["sem"] for r in ins if r["sem"] is not None})
        self.stats = dict(n=len(ins), nsem=len(semnames),
                          nwaits=sum(len(r["waits"]) for r in ins),
                          per_eng={e: sum(1 for r in ins if r["eng"] == e) for e in ENGS})
        with ExitStack() as st:
            sems = {s: st.enter_context(nc.semaphore(s + self.tag)) for s in semnames}
            block = st.enter_context(nc.Block())
            reg = {"pe": block.tensor, "act": block.scalar, "dve": block.vector,
                   "pool": block.gpsimd, "sp": block.sync}

            def make(e):
                def body(eng):
                    for r in ins:
                        if r["eng"] != e:
                            continue
                        for s, v in r["waits"]:
                            eng.wait_ge(sems[s], v)
                        if r["fn"] is None:
                            continue
                        out = r["fn"](eng)
                        if r["dma"] is not None:
                            outs = out if isinstance(out, (list, tuple)) else [out]
                            assert len(outs) == r["n"], (len(outs), r["n"])
                            for o in outs:
                                o.then_inc(sems[r["sem"]], 16)
                        elif r["sem"] is not None:
                            o = out[-1] if isinstance(out, (list, tuple)) else out
                            o.then_inc(sems[r["sem"]], 1)
                return body

            for e in ENGS:
                reg[e](make(e))


def build():
    nc = bass.Bass("TRN2", target_bir_lowering=False)

    def din(name, shape):
        return nc.dram_tensor(name, shape, F32, kind="ExternalInput").ap()

    def dout(name, shape):
        return nc.dram_tensor(name, shape, F32, kind="ExternalOutput").ap()

    xw = din("xw", [NPT * 128, 1024])
    xs = din("xs", [128, 1024])
    hprev = din("hprev", [16, 1024])
    ck = din("ck", [16, 128, 128])
    cv = din("cv", [16, 128, 128])
    swkv = din("swkv", [16, 8, 64, 64])
    w_in = din("w_in", [1024, 2560])
    w_out = din("w_out", [1024, 1024])
    w_gate = din("w_gate", [1024, D_FF])
    w_up = din("w_up", [1024, D_FF])
    w_down = din("w_down", [D_FF, 1024])
    gvec = din("gvec", [3, 1024])
    mu_d = din("mu", [128, 14])
    vec4_d = din("vec4", [128, 7 * 4])
    w2a2_d = din("w2a2", [128, 512])
    g2_d = din("g2", [128, 512])
    sinks_d = din("sinks", [1, 8])
    masks_d = din("masks", [128, 7 * 128])
    ident_d = din("ident", [128, 128])
    bones_d = din("bones", [128, 128])
    rmask_d = din("rmask", [128, 256])
    e16_d = din("e16", [128, 16])

    y_p = dout("y_p", [NOWN * 128, 1024])
    y_s = dout("y_s", [128, 1024])
    kwin_p = dout("kwin_p", [128, 128])
    vwin_p = dout("vwin_p", [128, 128])
    wkv_p = dout("wkv_p", [8, 64, 64])
    shift_p = dout("shift_p", [1, 1024])
    kwin_s = dout("kwin_s", [16, 128, 128])
    vwin_s = dout("vwin_s", [16, 128, 128])
    wkv_s = dout("wkv_s", [16, 8, 64, 64])
    shift_s = dout("shift_s", [16, 1024])
    x1s = nc.dram_tensor("x1s", [17 * 128, 1024], F32, kind="Internal").ap()
    build.stats = {}

    W0, A0, KK, KA, RK, GNG, GNB = range(7)

    def phase(mode, per):
        pA, pSa, pSb = mode == "A", mode == "Sa", mode == "Sb"
        with ExitStack() as st:
            def sb(name, shape, dt=F32):
                return st.enter_context(nc.sbuf_tensor(name + "_" + mode, shape, dt))

            def psb(name, shape, dt=F32):
                return st.enter_context(nc.psum_tensor(name + "_" + mode, shape, dt))

            def rot(name, n, shape, dt=F32):
                return [sb(f"{name}{q}", shape, dt) for q in range(n)]
            P = Prog(nc, '_' + mode)
            outkeys = []
            if pA or pSa:
                Win = sb("Win", [128, 8, 2560], BF16)
            if pA or pSb:
                Wout = sb("Wout", [128, 8, 1024], BF16)
            gmix = sb("gmix", [128, 1024])
            mu = sb("mu", [128, 14])
            vec4 = sb("vec4", [128, 7, 4])
            w2a2 = sb("w2a2", [128, 512], BF16)
            g2b = sb("g2b", [128, 512], BF16)
            esink = sb("esink", [128, 8])
            MK = sb("MK", [128, 7, 128], BF16)
            MKL = sb("MKL", [128, 2, 4, 128], BF16)
            ident = sb("ident", [128, 128], BF16)
            ident32 = sb("ident32", [128, 128])
            bones = sb("bones", [128, 128], BF16)
            rmask = sb("rmask", [128, 2, 128])
            e16 = sb("e16", [128, 16], BF16)

            def cload(e):
                r = []
                if pA or pSa:
                    for kc in range(8):
                        r.append(e.dma_start(out=Win[:, kc, :], in_=w_in[kc * 128:(kc + 1) * 128, :]))
                if pA or pSb:
                    for kc in range(8):
                        r.append(e.dma_start(out=Wout[:, kc, :], in_=w_out[kc * 128:(kc + 1) * 128, :]))
                r.append(e.dma_start(out=w2a2[:], in_=w2a2_d[:, :]))
                r.append(e.dma_start(out=g2b[:], in_=g2_d[:, :]))
                r.append(e.dma_start(out=MK[:], in_=masks_d.rearrange("p (m t) -> p m t", m=7)))
                r.append(e.dma_start(out=ident[:], in_=ident_d[:, :]))
                r.append(e.dma_start(out=bones[:], in_=bones_d[:, :]))
                r.append(e.dma_start(out=e16[:], in_=e16_d[:, :]))
                return r
            ncl = 6 + (8 if (pA or pSa) else 0) + (8 if (pA or pSb) else 0)
            P.op("pool", cload, writes=["const"], dma="constb", n=ncl)

            def cload2(e):
                r = []
                r.append(e.dma_start(out=gmix[:], in_=gvec[0:1, :].broadcast_to([128, 1024])))
                r.append(e.dma_start(out=mu[:], in_=mu_d[:, :]))
                r.append(e.dma_start(out=vec4[:], in_=vec4_d.rearrange("p (a b) -> p a b", a=7)))
                r.append(e.dma_start(out=esink[:], in_=sinks_d[0:1, :].broadcast_to([128, 8])))
                r.append(e.dma_start(out=ident32[:], in_=ident_d[:, :]))
                r.append(e.dma_start(out=rmask[:], in_=rmask_d.rearrange("p (a t) -> p a t", a=2)))
                return r
            P.op("sp", cload2, writes=["const2"], dma="constf", n=6)
            P.op("act", lambda e: e.activation(out=esink[:], in_=esink[:], func=AF.Exp),
                 reads=["const2"], writes=["esink"])

            def mkl(e):
                r = None
                for kd in range(2):
                    for q in range(4):
                        r = e.tensor_copy(out=MKL[:, kd, q, :], in_=MK[:, 3 * kd + (q % 2), :])
                return r
            P.op("pool", mkl, reads=["const"], writes=["MKL"])

            def v4bc(ix, n=128):
                return vec4[:, ix, :].unsqueeze(2).broadcast_to([128, 4, n])

            R2 = 2 if pA else 1
            if pA or pSa:
                xt = rot("xt", 1 if pA else 1, [128, 1024])
                ss = rot("ss", 2, [128, 1])
                rs = rot("rs", 2, [128, 1])
                hb = rot("hb", R2, [128, 1024], BF16)
                hT = rot("hT", R2, [128, 8, 128], BF16)
            if pA:
                fT = rot("fT", 2, [128, 14, 128])
                qT = rot("qT", 4, [128, 4, 128], BF16)
                kvT = rot("kvT", 5, [128, 2, 128], BF16)
            else:
                fT, qT, kvT, fprev = per["fT"], per["qT"], per["kvT"], per["fprev"]
            NKV = len(kvT)
            if pSa:
                h32s = sb("h32s", [128, 1024])
                kvx = sb("kvx", [128, 4, 128])
                hp32 = sb("hp32", [16, 1024])
                hpb = sb("hpb", [16, 1024], BF16)
                hpT = sb("hpT", [128, 8, 16], BF16)
            if pA or pSb:
                Vaug = rot("Vaug", NKV, [128, 2, 65], BF16)
                xsT = sb("xsT", [128, 14, 128])
                fcar = sb("fcar", [128, 14])
                twal = sb("twal", [128, 128], BF16)
                sg = sb("sg", [128, 128], BF16)
                eT = sb("eT", [128, 4, 128])
                asT = sb("asT", [128, 4, 128])
                kkT = sb("kkT", [128, 4, 128])
                sqb = sb("sqb", [128, 4, 128], BF16)
                rn = sb("rn", [128, 4, 128])
                tmpA = sb("tmpA", [128, 4, 128])
                kmT = sb("kmT", [128, 4, 128])
                cumE = sb("cumE", [128, 4, 128])
                Eg = sb("Eg", [128, 4, 128])
                vb = sb("vb", [128, 4, 128], BF16)
                AR = rot("AR", R2, [128, 4, 2, 128], BF16)
                BTF = rot("BTF", R2, [128, 4, 128], BF16)
                KTF = rot("KTF", R2, [128, 4, 128], BF16)
                VTM = rot("VTM", R2, [128, 512], BF16)
                BTM = rot("BTM", R2, [128, 512], BF16)
                KTM = rot("KTM", R2, [128, 512], BF16)
                gC = rot("gC", R2, [128, 4, 16])
                gT = rot("gT", 3 if pA else 1, [128, 4, 128], BF16)
                bonT = rot("bonT", 3 if pA else 1, [128, 4, 128], BF16)
                if pSb:
                    SQ0 = sb("SQ0", [128, 8, 256], BF16)
                    NM = sb("NM", [128, 8, 384], BF16)
                    NJ = sb("NJ", [128, 2, 8, 128], BF16)
                    AJ = sb("AJ", [128, 2, 8, 128], BF16)
                else:
                    L0S = sb("L0S", [128, 8, 512], BF16)
                    A0S = sb("A0S", [128, 8, 128], BF16)
                    NA = sb("NA", [128, 2, 8, 256], BF16)
                Zb = rot("Zb", 2, [128, 512], BF16)
                Ub = sb("Ub", [128, 512], BF16)
                Hst = sb("Hst", [128, 4, 64])
                Hb = sb("Hb", [128, 4, 64], BF16)
                tS = sb("tS", [128, 4, 64])
                OTM = rot("OTM", R2, [128, 8, 64])
                osq = sb("osq", [128, 8, 64])
                st1 = sb("st1", [128, 8])
                st2 = sb("st2", [128, 8])
                st3 = sb("st3", [128, 8])
                onb = sb("onb", [128, 8, 64], BF16)
                PT = rot("PT", 4, [128, 4, 128], BF16)
                den = sb("den", [128, 8])
                attb = sb("attb", [128, 8, 64], BF16)
                mixT = rot("mixT", R2, [128, 8, 128], BF16)
                tmx = sb("tmx", [128, 4, 128])
                x1t = rot("x1t", 1, [128, 1024])
                if pA:
                    ATM = rot("ATM", R2, [128, 512], BF16)
                    BHF = sb("BHF", [128, 4, 128], BF16)
                    KHF = sb("KHF", [128, 4, 128], BF16)
                    Xb = rot("Xb", 2, [128, 2, 512], BF16)
                    WY = sb("WY", [128, 8, 128], BF16)
                    MTb = sb("MTb", [128, 4, 64], BF16)
                    WTF = sb("WTF", [128, 4, 128], BF16)
            if pA:
                kvx = tmx
                h32s = x1t[0]
            if pSb:
                S0q = sb("S0q", [64, 32, 64])
                H0s = sb("H0s", [128, 16, 4, 64])
                H0b = sb("H0b", [128, 16, 4, 64], BF16)
                Xex = sb("Xex", [128, 2, 16, 64], BF16)
                zh = sb("zh", [128, 4, 2, 128], BF16)
                KcT = sb("KcT", [128, 16, 128], BF16)
                Vca = sb("Vca", [128, 16, 2, 65], BF16)
                cstb = sb("cstb", [128, 16, 128], BF16)
                PTc = sb("PTc", [128, 2, 16, 32], BF16)
                PTx = rot("PTx", 2, [128, 16, 128], BF16)
                wso = sb("wso", [64, 4, 8, 64])
            h32k = "x1t0" if pA else "h32s"
            kvxk = "tmx" if pA else "kvx"

            pj = [psb(f"pj{q}", [128, 512]) for q in range(2)]
            tpb = psb("tpb", [128, 1024], BF16)
            l0 = [psb(f"l0{q}", [128, 512]) for q in range(2)]
            sqp = [psb(f"sqp{q}", [128, 512]) for q in range(2)]
            Zp = psb("Zp", [128, 512])
            cnts = {"pj": 0, "sqp": 0, "l0": 0}
            banks = {"pj": pj, "sqp": sqp, "l0": l0}

            def nextb(nm):
                cnts[nm] += 1
                q = cnts[nm] % 2
                return banks[nm][q], f"{nm}{q}"

            def nextpj():
                return nextb("pj")

            def nextsq():
                return nextb("sqp")

            def nextl0():
                return nextb("l0")

            if pA or pSb:
                P.op("pool", lambda e: e.memset(fcar[:], 0.0), writes=["fcar"])
                P.op("pool", lambda e: e.memset(Hst[:], 0.0), writes=["Hst"])
                P.op("pool", lambda e: e.memset(Hb[:], 0.0), writes=["Hb"])
                P.op("pool", lambda e: e.memset(tS[:], 0.0), writes=["tS"])
                for q in range(NKV):
                    P.op("pool", lambda e, q=q: e.memset(Vaug[q][:], 1.0), writes=[f"Vaug{q}"])
            if pSb:
                P.op("pool", lambda e: e.memset(Vca[:], 1.0), writes=["Vca"])
                for q in range(2):
                    P.op("pool", lambda e, q=q: e.memset(PTx[q][:], 0.0), writes=[f"PTx{q}"])

            def is_own(i):
                return i >= NPRE

            def is_samp(i):
                return i == NPT

            def S1(i):
                b = i % len(xt)
                kx = f"xt{b}"
                src = xs[:, :] if is_samp(i) else xw[i * 128:(i + 1) * 128, :]
                P.op("sp", lambda e: e.dma_start(out=xt[b][:], in_=src), writes=[kx], dma=kx)
                s2 = i % 2
                hq = i % len(hb)
                P.op("act", lambda e: e.activation(out=hb[hq][:], in_=xt[b][:], func=AF.Square, accum_out=ss[s2][:]),
                     reads=[kx], writes=[f"hb{hq}", f"ss{s2}"])
                P.op("act", lambda e: e.activation(out=rs[s2][:], in_=ss[s2][:], func=AF.Sqrt, scale=1.0 / 1024, bias=1e-6),
                     reads=[f"ss{s2}"], writes=[f"rs{s2}"])
                P.op("dve", lambda e: e.reciprocal(out=rs[s2][:], in_=rs[s2][:]), reads=[f"rs{s2}"], writes=[f"rs{s2}"])
                P.op("dve", lambda e: e.scalar_tensor_tensor(out=hb[hq][:], in0=xt[b][:], scalar=rs[s2][:, 0:1], in1=gmix[:],
                                                             op0=ALU.mult, op1=ALU.mult),
                     reads=[kx, f"rs{s2}", "const2"], writes=[f"hb{hq}"])
                if i == NPT - 1 or is_samp(i):
                    P.op("dve", lambda e: e.scalar_tensor_tensor(out=h32s[:], in0=xt[b][:], scalar=rs[s2][:, 0:1], in1=gmix[:],
                                                                 op0=ALU.mult, op1=ALU.mult),
                         reads=[kx, f"rs{s2}", "const2"], writes=[h32k])
                    if is_samp(i):
                        P.op("sp", lambda e: [e.dma_start(out=shift_s[q:q + 1, :], in_=h32s[8 * q + 7:8 * q + 8, :]) for q in range(16)],
                             reads=[h32k], writes=["o_shift_s"], dma="o_shift_s", n=16)
                        outkeys.append("o_shift_s")
                    else:
                        P.op("sp", lambda e: e.dma_start(out=shift_p[:, :], in_=h32s[127:128, :]), reads=[h32k],
                             writes=["o_shift_p"], dma="o_shift_p")
                        outkeys.append("o_shift_p")

                def tr(e):
                    r = None
                    for kc in range(8):
                        r = e.transpose(out=tpb[:, kc * 128:(kc + 1) * 128], in_=hb[hq][:, kc * 128:(kc + 1) * 128], identity=ident[:])
                    return r
                P.op("pe", tr, reads=[f"hb{hq}", "const"], writes=["tpb"])
                P.op("act", lambda e: e.copy(out=hT[hq][:].rearrange("p a b -> p (a b)"), in_=tpb[:]), reads=["tpb"], writes=[f"hT{hq}"])

            def proj_group(i, chunks, evac):
                hq = i % len(hT)
                bank, bk = nextpj()

                def mm(e):
                    r = None
                    for gi, c in enumerate(chunks):
                        for kc in range(8):
                            r = e.matmul(bank[:, gi * 128:(gi + 1) * 128], lhsT=Win[:, kc, c * 128:(c + 1) * 128],
                                         rhs=hT[hq][:, kc, :], start=(kc == 0), stop=(kc == 7))
                    return r
                P.op("pe", mm, reads=["const", f"hT{hq}"], writes=[bk])
                evac(bank, bk)

            def S2(i):
                own = is_own(i)
                qq = i % len(qT)
                fq = i % len(fT)
                if own:
                    def ev_q(bank, bk):
                        P.op("act", lambda e: e.activation(out=qT[qq][:].rearrange("p a b -> p (a b)"), in_=bank[:], func=AF.Copy, scale=0.125),
                             reads=[bk], writes=[f"qT{qq}"])
                    proj_group(i, [0, 1, 2, 3], ev_q)
                if own or i == NPRE - 1:
                    k3 = i % NKV

                    def ev_kv(bank, bk):
                        P.op("act", lambda e: e.copy(out=kvT[k3][:].rearrange("p a b -> p (a b)"), in_=bank[:, 0:256]),
                             reads=[bk], writes=[f"kvT{k3}"])
                        if i == NPT - 1 or is_samp(i):
                            P.op("dve", lambda e: e.tensor_copy(out=kvx[:, 0:2, :].rearrange("p a b -> p (a b)"), in_=bank[:, 0:256]),
                                 reads=[bk, f"kvT{k3}"], writes=[kvxk])
                    proj_group(i, [4, 5], ev_kv)
                    if i == NPT - 1 or is_samp(i):
                        bank, bk = nextpj()

                        def trkv(e):
                            e.transpose(out=bank[:, 0:128], in_=kvx[:, 0, :], identity=ident32[:])
                            return e.transpose(out=bank[:, 128:256], in_=kvx[:, 1, :], identity=ident32[:])
                        P.op("pe", trkv, reads=[kvxk, "const2"], writes=[bk])
                        P.op("dve", lambda e: e.tensor_copy(out=kvx[:, 2:4, :].rearrange("p a b -> p (a b)"), in_=bank[:, 0:256]),
                             reads=[bk, kvxk], writes=[kvxk])
                        if is_samp(i):
                            def okv(e):
                                r = []
                                for q in range(16):
                                    r.append(e.dma_start(out=kwin_s[q, 120:128, :], in_=kvx[8 * q:8 * q + 8, 2, :]))
                                    r.append(e.dma_start(out=vwin_s[q, 120:128, :], in_=kvx[8 * q:8 * q + 8, 3, :]))
                                if not NO_D2D:
                                    r.append(e.dma_start(out=kwin_s[:, 0:120, :], in_=ck[:, 8:128, :]))
                                    r.append(e.dma_start(out=vwin_s[:, 0:120, :], in_=cv[:, 8:128, :]))
                                return r
                            P.op("sp", okv, reads=[kvxk], writes=["o_kv_s"], dma="o_kv_s", n=(32 if NO_D2D else 34))
                            outkeys.append("o_kv_s")
                        else:
                            def okv(e):
                                r = []
                                r.append(e.dma_start(out=kwin_p[:, :], in_=kvx[:, 2, :]))
                                r.append(e.dma_start(out=vwin_p[:, :], in_=kvx[:, 3, :]))
                                return r
                            P.op("sp", okv, reads=[kvxk], writes=["o_kv_p"], dma="o_kv_p", n=2)
                            outkeys.append("o_kv_p")
                for gi, chunks in enumerate([[6, 7, 8, 9], [10, 11, 12, 13], [14, 15, 16, 17], [18, 19]]):
                    def ev_f(bank, bk, gi=gi, chunks=chunks):
                        n = len(chunks) * 128
                        dst = fT[fq][:, gi * 4:gi * 4 + len(chunks), :].rearrange("p a b -> p (a b)")
                        if gi % 2 == 0:
                            P.op("act", lambda e: e.copy(out=dst, in_=bank[:, 0:n]), reads=[bk], writes=[f"fT{fq}_{gi}"])
                        else:
                            P.op("dve", lambda e: e.tensor_copy(out=dst, in_=bank[:, 0:n]), reads=[bk], writes=[f"fT{fq}_{gi}"])
                    proj_group(i, chunks, ev_f)
                if is_samp(i):
                    P.op("sp", lambda e: e.dma_start(out=hp32[:], in_=hprev[:, :]), writes=["hp32"], dma="hp32")
                    P.op("dve", lambda e: e.tensor_copy(out=hpb[:], in_=hp32[:]), reads=["hp32"], writes=["hpb"])

                    def trh(e):
                        r = None
                        for kc in range(8):
                            r = e.transpose(out=tpb[:, kc * 16:(kc + 1) * 16], in_=hpb[:, kc * 128:(kc + 1) * 128], identity=ident[0:16, 0:16])
                        return r
                    P.op("pe", trh, reads=["hpb", "const"], writes=["tpb"])
                    P.op("act", lambda e: e.copy(out=hpT[:].rearrange("p a b -> p (a b)"), in_=tpb[:, 0:128]), reads=["tpb"], writes=["hpT"])
                    for half in range(2):
                        bank, bk = nextpj()

                        def mmp(e, half=half, bank=bank):
                            r = None
                            for ci in range(7):
                                c = 6 + half * 7 + ci
                                for kc in range(8):
                                    r = e.matmul(bank[:, ci * 16:(ci + 1) * 16], lhsT=Win[:, kc, c * 128:(c + 1) * 128],
                                                 rhs=hpT[:, kc, :], start=(kc == 0), stop=(kc == 7))
                            return r
                        P.op("pe", mmp, reads=["const", "hpT"], writes=[bk])
                        P.op("act", lambda e, half=half, bank=bank: e.copy(
                            out=fprev[:, half * 7:(half + 1) * 7, :].rearrange("p a b -> p (a b)"), in_=bank[:, 0:112]),
                            reads=[bk], writes=[f"fprev{half}"])

            def S3(i):
                s3 = i % R2
                own = is_own(i)
                samp = is_samp(i)
                fq = i % len(fT)
                fk = [f"fT{fq}_{g}" for g in range(4)]
                f = fT[fq]
                if samp:
                    f4 = f[:, :, :].rearrange("p c (s t) -> p c s t", t=8)
                    x4 = xsT[:, :, :].rearrange("p c (s t) -> p c s t", t=8)
                    P.op("pool", lambda e: e.tensor_tensor(out=x4[:, :, :, 1:8], in0=f4[:, :, :, 0:7], in1=f4[:, :, :, 1:8], op=ALU.subtract),
                         reads=fk, writes=["xsT"])
                    P.op("pool", lambda e: e.tensor_tensor(out=x4[:, :, :, 0], in0=fprev[:, :, :], in1=f4[:, :, :, 0], op=ALU.subtract),
                         reads=fk, writes=["xsT0"])
                else:
                    P.op("pool", lambda e: e.tensor_tensor(out=xsT[:, :, 1:128], in0=f[:, :, 0:127], in1=f[:, :, 1:128], op=ALU.subtract),
                         reads=fk, writes=["xsT"])
                    P.op("pool", lambda e: e.tensor_tensor(out=xsT[:, :, 0], in0=fcar[:, :], in1=f[:, :, 0], op=ALU.subtract),
                         reads=fk + ["fcar"], writes=["xsT0"])
                    P.op("pool", lambda e: e.tensor_copy(out=fcar[:, :], in_=f[:, :, 127]), reads=fk, writes=["fcar"])
                mu_bc = mu[:, :].unsqueeze(2).broadcast_to([128, 14, 128])
                P.op("dve", lambda e: e.tensor_tensor(out=xsT[:], in0=xsT[:], in1=mu_bc, op=ALU.mult),
                     reads=["xsT", "xsT0", "const2"], writes=["xsT", "xsT0"])
                P.op("pool", lambda e: e.tensor_tensor(out=xsT[:], in0=xsT[:], in1=f[:], op=ALU.add),
                     reads=["xsT", "xsT0"] + fk, writes=["xsT", "xsT0"])
                XK = ["xsT", "xsT0"]
                P.op("act", lambda e: e.activation(out=twal[0:64, :], in_=xsT[0:64, 12, :], func=AF.Tanh), reads=XK, writes=["twal_a"])
                P.op("act", lambda e: e.copy(out=twal[64:128, :], in_=xsT[64:128, 12, :]), reads=XK, writes=["twal_b"])
                P.op("act", lambda e: e.activation(out=sg[:], in_=xsT[:, 13, :], func=AF.Sigmoid), reads=XK, writes=["sg"])
                bw, bwk = nextpj()

                def mmw(e):
                    r = None
                    for cc in range(4):
                        r = e.matmul(bw[:, cc * 128:(cc + 1) * 128], lhsT=w2a2[0:64, cc * 128:(cc + 1) * 128], rhs=twal[0:64, :],
                                     start=True, stop=True)
                    return r
                P.op("pe", mmw, reads=["const", "twal_a"], writes=[bwk])
                for cc in range(4):
                    P.op("act", lambda e, cc=cc: e.activation(out=eT[:, cc, :], in_=bw[:, cc * 128:(cc + 1) * 128], func=AF.Sigmoid,
                                                              bias=vec4[:, W0, cc:cc + 1]),
                         reads=[bwk, "const2"], writes=[f"eT{cc}"])
                ba, bak = nextpj()

                def mma(e):
                    r = None
                    for cc in range(4):
                        r = e.matmul(ba[:, cc * 128:(cc + 1) * 128], lhsT=w2a2[64:128, cc * 128:(cc + 1) * 128], rhs=twal[64:128, :],
                                     start=True, stop=True)
                    return r
                P.op("pe", mma, reads=["const", "twal_b"], writes=[bak])
                for cc in range(4):
                    P.op("act", lambda e, cc=cc: e.activation(out=asT[:, cc, :], in_=ba[:, cc * 128:(cc + 1) * 128], func=AF.Sigmoid,
                                                              bias=vec4[:, A0, cc:cc + 1]),
                         reads=[bak, "const2"], writes=[f"asT{cc}"])
                EK = [f"eT{c}" for c in range(4)]
                AK = [f"asT{c}" for c in range(4)]
                if own:
                    bg, bgk = nextpj()

                    def mmg(e):
                        r = None
                        for cc in range(4):
                            r = e.matmul(bg[:, cc * 128:(cc + 1) * 128], lhsT=g2b[:, cc * 128:(cc + 1) * 128], rhs=sg[:], start=True, stop=True)
                        return r
                    P.op("pe", mmg, reads=["const", "sg"], writes=[bgk])
                    gq = i % len(gT)
                    P.op("act", lambda e: e.copy(out=gT[gq][:].rearrange("p a b -> p (a b)"), in_=bg[:]), reads=[bgk], writes=[f"gT{gq}"])
                P.op("dve", lambda e: e.tensor_tensor(out=kkT[:], in0=xsT[:, 4:8, :], in1=v4bc(KK), op=ALU.mult), reads=XK + ["const2"], writes=["kkT"])
                P.op("dve", lambda e: e.tensor_tensor(out=sqb[:], in0=kkT[:], in1=kkT[:], op=ALU.mult), reads=["kkT"], writes=["sqb"])
                bs, bsk = nextpj()
                P.op("pe", lambda e: e.matmul(bs[:], lhsT=bones[:], rhs=sqb[:].rearrange("p a b -> p (a b)"), start=True, stop=True),
                     reads=["const", "sqb"], writes=[bsk])
                P.op("act", lambda e: e.activation(out=rn[:].rearrange("p a b -> p (a b)"), in_=bs[:], func=AF.Sqrt, bias=1e-12),
                     reads=[bsk], writes=["rn"])
                P.op("dve", lambda e: e.reciprocal(out=rn[:], in_=rn[:]), reads=["rn"], writes=["rn"])
                P.op("dve", lambda e: e.tensor_tensor(out=kkT[:], in0=kkT[:], in1=rn[:], op=ALU.mult), reads=["kkT", "rn"], writes=["kkT"])
                P.op("dve", lambda e: e.scalar_tensor_tensor(out=tmpA[:], in0=asT[:], scalar=-1.0, in1=v4bc(KA), op0=ALU.add, op1=ALU.mult),
                     reads=AK + ["const2"], writes=["tmpA"])
                P.op("dve", lambda e: e.scalar_tensor_tensor(out=kmT[:], in0=tmpA[:], scalar=1.0, in1=xsT[:, 4:8, :], op0=ALU.add, op1=ALU.mult),
                     reads=["tmpA"] + XK, writes=["kmT"])
                rm = rmask[:, 1 if samp else 0, :]
                for cc in range(4):
                    P.op("dve", lambda e, cc=cc: e.tensor_tensor_scan(out=cumE[:, cc, :], data0=rm, data1=eT[:, cc, :], initial=0.0,
                                                                      op0=ALU.mult, op1=ALU.add),
                         reads=[f"eT{cc}", "const2"], writes=[f"cumE{cc}"])
                CK = [f"cumE{c}" for c in range(4)]
                P.op("pool", lambda e: e.tensor_tensor(out=rn[:], in0=cumE[:], in1=eT[:], op=ALU.subtract), reads=CK + EK + ["rn"], writes=["rn"])
                P.op("act", lambda e: e.activation(out=rn[:], in_=rn[:], func=AF.Exp, scale=-C0), reads=["rn"], writes=["rn"])
                P.op("act", lambda e: e.activation(out=Eg[:], in_=cumE[:], func=AF.Exp, scale=-C0), reads=CK, writes=["Eg"])
                P.op("act", lambda e: e.activation(out=cumE[:], in_=cumE[:], func=AF.Exp, scale=C0), reads=CK, writes=CK)
                Egi, Egx = cumE, rn
                if samp:
                    P.op("pool", lambda e: e.tensor_copy(out=gC[s3][:, :, :], in_=Eg[:, :, :].rearrange("p c (s t) -> p c s t", t=8)[:, :, :, 7]),
                         reads=["Eg"], writes=[f"gC{s3}"])
                else:
                    P.op("pool", lambda e: e.tensor_copy(out=gC[s3][:, :, 0], in_=Eg[:, :, 127]), reads=["Eg"], writes=[f"gC{s3}"])
                ARk = f"AR{s3}"
                P.op("dve", lambda e: e.tensor_tensor(out=AR[s3][:, :, 1, :], in0=xsT[:, 0:4, :], in1=Eg[:], op=ALU.mult),
                     reads=XK + ["Eg"], writes=[ARk + "r"])
                P.op("dve", lambda e: e.tensor_tensor(out=KTF[s3][:], in0=kmT[:], in1=Egi[:], op=ALU.mult), reads=["kmT"] + CK, writes=[f"KTF{s3}"])
                P.op("pool", lambda e: e.tensor_tensor(out=tmpA[:], in0=kkT[:], in1=asT[:], op=ALU.mult), reads=["kkT"] + AK, writes=["tmpA"])
                P.op("dve", lambda e: e.tensor_tensor(out=BTF[s3][:], in0=tmpA[:], in1=Egi[:], op=ALU.mult), reads=["tmpA"] + CK, writes=[f"BTF{s3}"])
                P.op("dve", lambda e: e.scalar_tensor_tensor(out=AR[s3][:, :, 0, :], in0=kkT[:], scalar=-1.0, in1=Egx[:], op0=ALU.mult, op1=ALU.mult),
                     reads=["kkT", "rn"], writes=[ARk + "a"])
                P.op("act", lambda e: e.copy(out=vb[:], in_=xsT[:, 8:12, :]), reads=XK, writes=["vb"])
                if own:
                    P.op("pool", lambda e: e.tensor_tensor(out=rn[:], in0=xsT[:, 0:4, :], in1=kmT[:], op=ALU.mult), reads=XK + ["kmT", "rn"], writes=["rn"])
                    P.op("dve", lambda e: e.tensor_tensor(out=sqb[:], in0=rn[:], in1=v4bc(RK), op=ALU.mult), reads=["rn", "const2"], writes=["sqb"])
                    bb, bbk = nextpj()
                    P.op("pe", lambda e: e.matmul(bb[:], lhsT=bones[:], rhs=sqb[:].rearrange("p a b -> p (a b)"), start=True, stop=True),
                         reads=["const", "sqb"], writes=[bbk])
                    P.op("dve", lambda e: e.tensor_tensor(out=bonT[i % len(bonT)][:].rearrange("p a b -> p (a b)"), in0=bb[:],
                                                          in1=xsT[:, 8:12, :].rearrange("p a b -> p (a b)"), op=ALU.mult),
                         reads=[bbk] + XK, writes=[f"bonT{i % len(bonT)}"])

                if pA:
                    gbc = gC[s3][:, :, 0:1].broadcast_to([128, 4, 128])
                    P.op("dve", lambda e: e.tensor_tensor(out=BHF[:], in0=BTF[s3][:], in1=gbc, op=ALU.mult), reads=[f"BTF{s3}", f"gC{s3}"], writes=["BHF"])
                    P.op("dve", lambda e: e.tensor_tensor(out=KHF[:], in0=KTF[s3][:], in1=gbc, op=ALU.mult), reads=[f"KTF{s3}", f"gC{s3}"], writes=["KHF"])
                    bsrc, ksrc, bsk, ksk = BHF, KHF, "BHF", "KHF"
                else:
                    bsrc, ksrc, bsk, ksk = BTF[s3], KTF[s3], f"BTF{s3}", f"KTF{s3}"

                def tr1(e):
                    r = None
                    for cc in range(4):
                        r = e.transpose(out=tpb[:, cc * 128:(cc + 1) * 128], in_=vb[:, cc, :], identity=ident[:])
                    for cc in range(4):
                        r = e.transpose(out=tpb[:, 512 + cc * 128:512 + (cc + 1) * 128], in_=bsrc[:, cc, :], identity=ident[:])
                    return r
                P.op("pe", tr1, reads=["vb", bsk, "const"], writes=["tpb"])
                P.op("act", lambda e: e.copy(out=VTM[s3][:], in_=tpb[:, 0:512]), reads=["tpb"], writes=[f"VTM{s3}"])
                P.op("dve", lambda e: e.tensor_copy(out=BTM[s3][:], in_=tpb[:, 512:1024]), reads=["tpb"], writes=[f"BTM{s3}"])

                def tr2(e):
                    r = None
                    for cc in range(4):
                        r = e.transpose(out=tpb[:, cc * 128:(cc + 1) * 128], in_=ksrc[:, cc, :], identity=ident[:])
                    if pA:
                        for cc in range(4):
                            r = e.transpose(out=tpb[:, 512 + cc * 128:512 + (cc + 1) * 128], in_=AR[s3][:, cc, 0, :], identity=ident[:])
                    return r
                P.op("pe", tr2, reads=[ksk, f"AR{s3}a", "const"], writes=["tpb"])
                P.op("act", lambda e: e.copy(out=KTM[s3][:], in_=tpb[:, 0:512]), reads=["tpb"], writes=[f"KTM{s3}"])
                if pA:
                    P.op("dve", lambda e: e.tensor_copy(out=ATM[s3][:], in_=tpb[:, 512:1024]), reads=["tpb"], writes=[f"ATM{s3}"])

            def nlev(i):
                return 3 if is_samp(i) else 7

            def S4(i):
                s3 = i % R2
                kd = 1 if is_samp(i) else 0
                ARk = [f"AR{s3}a", f"AR{s3}r"]
                for h in range(8):
                    cc, pb = h // 2, (h % 2) * 64
                    bank, bk = nextl0()

                    def mm0(e, cc=cc, pb=pb, bank=bank):
                        e.matmul(bank[:, 0:256], lhsT=BTF[s3][pb:pb + 64, cc, :], rhs=AR[s3][pb:pb + 64, cc, :, :], start=True, stop=True)
                        return e.matmul(bank[:, 256:512], lhsT=KTF[s3][pb:pb + 64, cc, :], rhs=AR[s3][pb:pb + 64, cc, :, :], start=True, stop=True)
                    P.op("pe", mm0, reads=ARk + [f"BTF{s3}", f"KTF{s3}"], writes=[bk])
                    P.op("dve", lambda e, h=h, bank=bank: e.tensor_tensor(out=SQ0[:, h, 0:128], in0=bank[:, 0:128], in1=MKL[:, kd, 0, :], op=ALU.mult),
                         reads=[bk, "MKL"], writes=[f"SQ0n{h}"])
                    P.op("dve", lambda e, h=h, bank=bank: e.tensor_tensor(out=NM[:, h, :], in0=bank[:, 128:512],
                                                                          in1=MKL[:, kd, 1:4, :].rearrange("p a b -> p (a b)"), op=ALU.mult),
                         reads=[bk, "MKL"], writes=[f"NM{h}"])
                for g4 in range(2):
                    bank, bk = nextsq()

                    def mma0(e, g4=g4, bank=bank):
                        r = None
                        for hh in range(4):
                            h = g4 * 4 + hh
                            cc, pb = h // 2, (h % 2) * 64
                            r = e.matmul(bank[:, hh * 128:(hh + 1) * 128], lhsT=AR[s3][pb:pb + 64, cc, 0, :], rhs=BTF[s3][pb:pb + 64, cc, :],
                                         start=True, stop=True)
                        return r
                    P.op("pe", mma0, reads=ARk + [f"BTF{s3}"], writes=[bk])
                    slbc = MK[:, 2 + 3 * kd, :].unsqueeze(1).broadcast_to([128, 4, 128])
                    P.op("dve", lambda e, g4=g4, bank=bank, slbc=slbc: e.tensor_tensor(
                        out=SQ0[:, g4 * 4:(g4 + 1) * 4, 128:256], in0=bank[:].rearrange("p (a b) -> p a b", a=4), in1=slbc, op=ALU.mult),
                        reads=[bk, "const"], writes=[f"SQ0a{g4 * 4 + q}" for q in range(4)])
                nl = nlev(i)
                for j in range(nl - 1):
                    for pr in range(4):
                        bank, bk = nextsq()
                        hs = (2 * pr, 2 * pr + 1)

                        def Nsrc(h, j=j):
                            return SQ0[:, h, 0:128] if j == 0 else NJ[:, (j - 1) % 2, h, :]

                        def Asrc(h, j=j):
                            return SQ0[:, h, 128:256] if j == 0 else AJ[:, (j - 1) % 2, h, :]
                        rk = []
                        for h in hs:
                            rk += ([f"SQ0n{h}", f"SQ0a{h}"] if j == 0 else [f"NJ{(j - 1) % 2}_{h}", f"AJ{(j - 1) % 2}_{h}"])
                        last = (j == nl - 2)

                        def mmsq(e, hs=hs, bank=bank, Nsrc=Nsrc, Asrc=Asrc, last=last):
                            r = None
                            for q, h in enumerate(hs):
                                r = e.matmul(bank[:, q * 256:q * 256 + 128], lhsT=Asrc(h), rhs=Nsrc(h), start=True, stop=True)
                                if not last:
                                    r = e.matmul(bank[:, q * 256 + 128:q * 256 + 256], lhsT=Nsrc(h), rhs=Asrc(h), start=True, stop=True)
                            return r
                        P.op("pe", mmsq, reads=rk, writes=[bk])
                        b3 = bank[:].rearrange("p (a b) -> p a b", a=2)
                        P.op("act", lambda e, j=j, pr=pr, b3=b3: e.copy(out=NJ[:, j % 2, 2 * pr:2 * pr + 2, :], in_=b3[:, :, 0:128]),
                             reads=[bk], writes=[f"NJ{j % 2}_{h}" for h in hs])
                        if not last:
                            P.op("dve", lambda e, j=j, pr=pr, b3=b3: e.tensor_copy(out=AJ[:, j % 2, 2 * pr:2 * pr + 2, :], in_=b3[:, :, 128:256]),
                                 reads=[bk], writes=[f"AJ{j % 2}_{h}" for h in hs])

            def S5(i):
                s3 = i % R2
                own = is_own(i)
                samp = is_samp(i)
                nl = nlev(i)
                ARk = [f"AR{s3}a", f"AR{s3}r"]
                if samp:
                    sample_h0(s3)

                def z0(e):
                    r = None
                    for h in range(8):
                        cc, pb = h // 2, (h % 2) * 64
                        if samp:
                            r = e.matmul(Zp[:, h * 64:(h + 1) * 64], lhsT=zh[pb:pb + 64, cc, 0, :], rhs=ident[pb:pb + 64, pb:pb + 64],
                                         start=(h == 0), stop=False, skip_group_check=True)
                        else:
                            r = e.matmul(Zp[:, h * 64:(h + 1) * 64], lhsT=AR[s3][pb:pb + 64, cc, 0, :], rhs=Hb[pb:pb + 64, cc, :],
                                         start=(h == 0), stop=False, skip_group_check=True)
                        r = e.matmul(Zp[:, h * 64:(h + 1) * 64], lhsT=NM[:, h, 128:256], rhs=VTM[s3][:, h * 64:(h + 1) * 64],
                                     start=False, stop=False, skip_group_check=True)
                    return r
                P.op("pe", z0, reads=ARk + ["Hb", "zh", "const", f"VTM{s3}"] + [f"NM{h}" for h in range(8)], writes=["Zp"])
                for j in range(nl):
                    zb = Zb[j % 2]
                    zk = f"Zb{j % 2}"
                    if j % 2 == 0:
                        P.op("act", lambda e, zb=zb: e.copy(out=zb[:], in_=Zp[:]), reads=["Zp"], writes=[zk])
                    else:
                        P.op("dve", lambda e, zb=zb: e.tensor_copy(out=zb[:], in_=Zp[:]), reads=["Zp"], writes=[zk])
                    rk = [zk] + ([f"SQ0n{h}" for h in range(8)] if j == 0 else [f"NJ{(j - 1) % 2}_{h}" for h in range(8)])

                    def ap(e, j=j, zb=zb):
                        r = None
                        for h in range(8):
                            lt = SQ0[:, h, 0:128] if j == 0 else NJ[:, (j - 1) % 2, h, :]
                            r = e.matmul(Zp[:, h * 64:(h + 1) * 64], lhsT=lt, rhs=zb[:, h * 64:(h + 1) * 64], start=False, stop=(j == nl - 1),
                                         skip_group_check=True)
                        return r
                    P.op("pe", ap, reads=rk, writes=["Zp"])
                P.op("act", lambda e: e.copy(out=Ub[:], in_=Zp[:]), reads=["Zp"], writes=["Ub"])
                if own:
                    ob, obk = nextpj()
                    oq = i % R2

                    def mo(e):
                        r = None
                        for h in range(8):
                            cc, pb = h // 2, (h % 2) * 64
                            o = ob[:, h * 64:(h + 1) * 64]
                            if samp:
                                e.matmul(o, lhsT=zh[pb:pb + 64, cc, 1, :], rhs=ident[pb:pb + 64, pb:pb + 64], start=True, stop=False)
                            else:
                                e.matmul(o, lhsT=AR[s3][pb:pb + 64, cc, 1, :], rhs=Hb[pb:pb + 64, cc, :], start=True, stop=False)
                            e.matmul(o, lhsT=NM[:, h, 0:128], rhs=Ub[:, h * 64:(h + 1) * 64], start=False, stop=False)
                            r = e.matmul(o, lhsT=NM[:, h, 256:384], rhs=VTM[s3][:, h * 64:(h + 1) * 64], start=False, stop=True)
                        return r
                    P.op("pe", mo, reads=ARk + ["Hb", "zh", "const", "Ub", f"VTM{s3}"] + [f"NM{h}" for h in range(8)], writes=[obk])
                    P.op("act", lambda e: e.copy(out=OTM[oq][:].rearrange("p a b -> p (a b)"), in_=ob[:]), reads=[obk], writes=[f"OTM{oq}"])
                if samp:
                    sample_state(s3)
                    return

                def su(e):
                    r = None
                    for h in range(8):
                        cc, pb = h // 2, (h % 2) * 64
                        o = Zp[pb:pb + 64, cc * 64:(cc + 1) * 64]
                        e.matmul(o, lhsT=BTM[s3][:, h * 64:(h + 1) * 64], rhs=Ub[:, h * 64:(h + 1) * 64], start=True, stop=False)
                        r = e.matmul(o, lhsT=KTM[s3][:, h * 64:(h + 1) * 64], rhs=VTM[s3][:, h * 64:(h + 1) * 64], start=False, stop=True)
                    return r
                P.op("pe", su, reads=["Ub", f"BTM{s3}", f"KTM{s3}", f"VTM{s3}"], writes=["Zp"])
                P.op("dve", lambda e: e.tensor_tensor(out=tS[:].rearrange("p a b -> p (a b)"), in0=Zp[:, 0:256],
                                                      in1=Hst[:].rearrange("p a b -> p (a b)"), op=ALU.add),
                     reads=["Zp", "Hst"], writes=["tS"])
                P.op("dve", lambda e: e.tensor_tensor(out=Hst[:], in0=tS[:], in1=gC[s3][:, :, 0:1].broadcast_to([128, 4, 64]), op=ALU.mult),
                     reads=["tS", f"gC{s3}"], writes=["Hst"])
                P.op("act", lambda e: e.copy(out=Hb[:], in_=Hst[:]), reads=["Hst"], writes=["Hb"])
                if i == NPT - 1:
                    bank, bk = nextpj()

                    def trs(e):
                        r = None
                        for cc in range(4):
                            r = e.transpose(out=bank[0:64, cc * 128:(cc + 1) * 128], in_=Hst[:, cc, :], identity=ident32[:])
                        return r
                    P.op("pe", trs, reads=["Hst", "const2"], writes=[bk])
                    P.op("dve", lambda e: e.tensor_copy(out=osq[0:64, :, :].rearrange("p a b -> p (a b)"), in_=bank[0:64, :]), reads=[bk, "osq"], writes=["osq"])
                    P.op("sp", lambda e: e.dma_start(out=wkv_p.rearrange("h v k -> v h k"), in_=osq[0:64, :, :]), reads=["osq"], writes=["o_wkv_p"], dma="o_wkv_p")
                    outkeys.append("o_wkv_p")

            def S4n(i):
                s3 = i % R2
                own = is_own(i)
                ARk = [f"AR{s3}a", f"AR{s3}r"]
                for h in range(8):
                    cc, pb = h // 2, (h % 2) * 64
                    bank, bk = nextl0()

                    def mm0(e, cc=cc, pb=pb, bank=bank):
                        e.matmul(bank[:, 0:256], lhsT=BTF[s3][pb:pb + 64, cc, :], rhs=AR[s3][pb:pb + 64, cc, :, :], start=True, stop=True)
                        return e.matmul(bank[:, 256:512], lhsT=KTF[s3][pb:pb + 64, cc, :], rhs=AR[s3][pb:pb + 64, cc, :, :], start=True, stop=True)
                    P.op("pe", mm0, reads=ARk + [f"BTF{s3}", f"KTF{s3}"], writes=[bk])
                    P.op("dve", lambda e, h=h, bank=bank: e.tensor_tensor(out=L0S[:, h, :], in0=bank[:],
                                                                          in1=MKL[:, 0, :, :].rearrange("p a b -> p (a b)"), op=ALU.mult),
                         reads=[bk, "MKL"], writes=[f"L0S{h}"])
                for g4 in range(2):
                    bank, bk = nextsq()

                    def mma0(e, g4=g4, bank=bank):
                        r = None
                        for hh in range(4):
                            h = g4 * 4 + hh
                            cc, pb = h // 2, (h % 2) * 64
                            r = e.matmul(bank[:, hh * 128:(hh + 1) * 128], lhsT=AR[s3][pb:pb + 64, cc, 0, :], rhs=BTF[s3][pb:pb + 64, cc, :],
                                         start=True, stop=True)
                        return r
                    P.op("pe", mma0, reads=ARk + [f"BTF{s3}"], writes=[bk])
                    slbc = MK[:, 2, :].unsqueeze(1).broadcast_to([128, 4, 128])
                    P.op("dve", lambda e, g4=g4, bank=bank, slbc=slbc: e.tensor_tensor(
                        out=A0S[:, g4 * 4:(g4 + 1) * 4, :], in0=bank[:].rearrange("p (a b) -> p a b", a=4), in1=slbc, op=ALU.mult),
                        reads=[bk, "const"], writes=[f"A0S{g4 * 4 + q}" for q in range(4)])
                for half in range(2):
                    bank, bk = l0[half], f"l0{half}"

                    def x0(e, half=half, bank=bank):
                        r = None
                        for hh in range(4):
                            h = half * 4 + hh
                            e.matmul(bank[:, hh * 128:hh * 128 + 64], lhsT=ident[:], rhs=ATM[s3][:, h * 64:(h + 1) * 64],
                                     start=(hh == 0), stop=False, skip_group_check=True)
                            r = e.matmul(bank[:, hh * 128 + 64:(hh + 1) * 128], lhsT=L0S[:, h, 256:384], rhs=VTM[s3][:, h * 64:(h + 1) * 64],
                                         start=False, stop=False, skip_group_check=True)
                        return r
                    P.op("pe", x0, reads=["const", f"ATM{s3}", f"VTM{s3}"] + [f"L0S{half * 4 + q}" for q in range(4)], writes=[bk])
                for j in range(7):
                    xb = Xb[j % 2]
                    P.op("act", lambda e, xb=xb: e.copy(out=xb[:, 0, :], in_=l0[0][:]), reads=["l00"], writes=[f"Xb{j % 2}_0"])
                    P.op("dve", lambda e, xb=xb: e.tensor_copy(out=xb[:, 1, :], in_=l0[1][:]), reads=["l01"], writes=[f"Xb{j % 2}_1"])
                    if j < 6:
                        last = (j == 5)
                        for pr in range(4):
                            bank, bk = nextsq()
                            hs = (2 * pr, 2 * pr + 1)

                            def Nsrc(h, j=j):
                                return L0S[:, h, 0:128] if j == 0 else NA[:, (j - 1) % 2, h, 0:128]

                            def Asrc(h, j=j):
                                return A0S[:, h, :] if j == 0 else NA[:, (j - 1) % 2, h, 128:256]
                            rk = []
                            for h in hs:
                                rk += ([f"L0S{h}", f"A0S{h}"] if j == 0 else [f"NA{(j - 1) % 2}_{h}"])

                            def mmsq(e, hs=hs, bank=bank, Nsrc=Nsrc, Asrc=Asrc, last=last):
                                r = None
                                for q, h in enumerate(hs):
                                    r = e.matmul(bank[:, q * 256:q * 256 + 128], lhsT=Asrc(h), rhs=Nsrc(h), start=True, stop=True)
                                    if not last:
                                        r = e.matmul(bank[:, q * 256 + 128:q * 256 + 256], lhsT=Nsrc(h), rhs=Asrc(h), start=True, stop=True)
                                return r
                            P.op("pe", mmsq, reads=rk, writes=[bk])
                            dstna = NA[:, j % 2, 2 * pr:2 * pr + 2, :].rearrange("p a b -> p (a b)")
                            if pr % 2 == 0:
                                P.op("act", lambda e, dstna=dstna, bank=bank: e.copy(out=dstna, in_=bank[:]),
                                     reads=[bk], writes=[f"NA{j % 2}_{h}" for h in hs])
                            else:
                                P.op("dve", lambda e, dstna=dstna, bank=bank: e.tensor_copy(out=dstna, in_=bank[:]),
                                     reads=[bk], writes=[f"NA{j % 2}_{h}" for h in hs])
                    for half in range(2):
                        bank, bk = l0[half], f"l0{half}"
                        rk = [f"Xb{j % 2}_{half}"] + ([f"L0S{half * 4 + q}" for q in range(4)] if j == 0 else [f"NA{(j - 1) % 2}_{half * 4 + q}" for q in range(4)])

                        def ap(e, j=j, xb=xb, half=half, bank=bank):
                            r = None
                            for hh in range(4):
                                h = half * 4 + hh
                                lt = L0S[:, h, 0:128] if j == 0 else NA[:, (j - 1) % 2, h, 0:128]
                                r = e.matmul(bank[:, hh * 128:(hh + 1) * 128], lhsT=lt, rhs=xb[:, half, hh * 128:(hh + 1) * 128],
                                             start=False, stop=(j == 6), skip_group_check=True)
                            return r
                        P.op("pe", ap, reads=rk + [bk], writes=[bk])
                P.op("act", lambda e: e.copy(out=WY[:, 0:4, :].rearrange("p a b -> p (a b)"), in_=l0[0][:]), reads=["l00"], writes=["WY0"])
                P.op("dve", lambda e: e.tensor_copy(out=WY[:, 4:8, :].rearrange("p a b -> p (a b)"), in_=l0[1][:]), reads=["l01"], writes=["WY1"])
                WK = ["WY0", "WY1"]
                mb, mbk = nextsq()

                def mmt(e):
                    r = None
                    for h in range(8):
                        cc, pb = h // 2, (h % 2) * 64
                        r = e.matmul(mb[pb:pb + 64, cc * 64:(cc + 1) * 64], lhsT=WY[:, h, 0:64], rhs=BTM[s3][:, h * 64:(h + 1) * 64], start=True, stop=True)
                    return r
                P.op("pe", mmt, reads=WK + [f"BTM{s3}"], writes=[mbk])
                P.op("act", lambda e: e.copy(out=MTb[:].rearrange("p a b -> p (a b)"), in_=mb[:, 0:256]), reads=[mbk], writes=["MTb"])

                def gp(e):
                    r = None
                    for h in range(8):
                        cc, pb = h // 2, (h % 2) * 64
                        o = Zp[pb:pb + 64, cc * 64:(cc + 1) * 64]
                        e.matmul(o, lhsT=BTM[s3][:, h * 64:(h + 1) * 64], rhs=WY[:, h, 64:128], start=(h < 2), stop=False, skip_group_check=True)
                        r = e.matmul(o, lhsT=KTM[s3][:, h * 64:(h + 1) * 64], rhs=VTM[s3][:, h * 64:(h + 1) * 64], start=False, stop=False, skip_group_check=True)
                    return r
                P.op("pe", gp, reads=WK + [f"BTM{s3}", f"KTM{s3}", f"VTM{s3}"], writes=["Zp"])
                if own:
                    wb, wbk = nextsq()
                    wb16 = wb[:].bitcast(BF16)

                    def trw(e):
                        r = None
                        for h in range(8):
                            cc, pb = h // 2, (h % 2) * 64
                            r = e.transpose(out=wb16[pb:pb + 64, cc * 128:(cc + 1) * 128], in_=WY[:, h, 0:64], identity=ident[:])
                        return r
                    P.op("pe", trw, reads=WK + ["const"], writes=[wbk])
                    P.op("act", lambda e: e.copy(out=WTF[:].rearrange("p a b -> p (a b)"), in_=wb16[:, 0:512]), reads=[wbk], writes=["WTFa", "WTFb"])

            def S5n(i):
                s3 = i % R2
                own = is_own(i)
                ARk = [f"AR{s3}a", f"AR{s3}r"]
                WK = ["WY0", "WY1"]
                if own:
                    ub, ubk = nextpj()

                    def mu_(e):
                        r = None
                        for h in range(8):
                            cc, pb = h // 2, (h % 2) * 64
                            o = ub[:, h * 64:(h + 1) * 64]
                            e.matmul(o, lhsT=WTF[pb:pb + 64, cc, :], rhs=Hb[pb:pb + 64, cc, :], start=True, stop=False)
                            r = e.matmul(o, lhsT=ident[:], rhs=WY[:, h, 64:128], start=False, stop=True)
                        return r
                    P.op("pe", mu_, reads=WK + ["WTFa", "WTFb", "Hb", "const"], writes=[ubk])
                    P.op("act", lambda e: e.copy(out=Ub[:], in_=ub[:]), reads=[ubk], writes=["Ub"])
                    ob, obk = nextpj()
                    oq = i % R2

                    def mo(e):
                        r = None
                        for h in range(8):
                            cc, pb = h // 2, (h % 2) * 64
                            o = ob[:, h * 64:(h + 1) * 64]
                            e.matmul(o, lhsT=AR[s3][pb:pb + 64, cc, 1, :], rhs=Hb[pb:pb + 64, cc, :], start=True, stop=False)
                            e.matmul(o, lhsT=L0S[:, h, 128:256], rhs=Ub[:, h * 64:(h + 1) * 64], start=False, stop=False)
                            r = e.matmul(o, lhsT=L0S[:, h, 384:512], rhs=VTM[s3][:, h * 64:(h + 1) * 64], start=False, stop=True)
                        return r
                    P.op("pe", mo, reads=ARk + ["Hb", "Ub", f"VTM{s3}"] + [f"L0S{h}" for h in range(8)], writes=[obk])
                    P.op("act", lambda e: e.copy(out=OTM[oq][:].rearrange("p a b -> p (a b)"), in_=ob[:]), reads=[obk], writes=[f"OTM{oq}"])

                def ch(e):
                    r = None
                    for h in range(8):
                        cc, pb = h // 2, (h % 2) * 64
                        r = e.matmul(Zp[pb:pb + 64, cc * 64:(cc + 1) * 64], lhsT=MTb[pb:pb + 64, cc, :], rhs=Hb[pb:pb + 64, cc, :],
                                     start=False, stop=True, skip_group_check=True)
                    return r
                P.op("pe", ch, reads=["MTb", "Hb", "Zp"], writes=["Zp"])
                P.op("dve", lambda e: e.tensor_tensor(out=Hst[:].rearrange("p a b -> p (a b)"), in0=Zp[:, 0:256],
                                                      in1=tS[:].rearrange("p a b -> p (a b)"), op=ALU.add),
                     reads=["Zp", "tS"], writes=["Hst"])
                P.op("act", lambda e: e.copy(out=Hb[:], in_=Hst[:]), reads=["Hst"], writes=["Hb"])
                if i + 1 < NPT:
                    n3 = (i + 1) % R2
                    P.op("pool", lambda e: e.tensor_tensor(out=tS[:], in0=Hst[:], in1=gC[n3][:, :, 0:1].broadcast_to([128, 4, 64]), op=ALU.mult),
                         reads=["Hst", f"gC{n3}"], writes=["tS"])
                if i == NPT - 1:
                    bank, bk = nextpj()

                    def trs(e):
                        r = None
                        for cc in range(4):
                            r = e.transpose(out=bank[0:64, cc * 128:(cc + 1) * 128], in_=Hst[:, cc, :], identity=ident32[:])
                        return r
                    P.op("pe", trs, reads=["Hst", "const2"], writes=[bk])
                    P.op("dve", lambda e: e.tensor_copy(out=osq[0:64, :, :].rearrange("p a b -> p (a b)"), in_=bank[0:64, :]), reads=[bk, "osq"], writes=["osq"])
                    P.op("sp", lambda e: e.dma_start(out=wkv_p.rearrange("h v k -> v h k"), in_=osq[0:64, :, :]), reads=["osq"], writes=["o_wkv_p"], dma="o_wkv_p")
                    outkeys.append("o_wkv_p")

            def sample_h0(s3):
                for q4 in range(4):
                    P.op("sp", lambda e, q4=q4: e.dma_start(out=S0q[:], in_=swkv[q4 * 4:(q4 + 1) * 4].rearrange("s h v k -> v (s h) k")),
                         writes=["S0q"], dma="S0q")
                    for sl_ in range(4):
                        s = q4 * 4 + sl_
                        bank, bk = nextpj()

                        def trs(e, sl_=sl_, bank=bank):
                            r = None
                            for cc in range(4):
                                r = e.transpose(out=bank[:, cc * 64:(cc + 1) * 64],
                                                in_=S0q[:, sl_ * 8 + 2 * cc:sl_ * 8 + 2 * cc + 2, :].rearrange("p a b -> p (a b)"),
                                                identity=ident32[0:64, 0:64])
                            return r
                        P.op("pe", trs, reads=["S0q", "const2"], writes=[bk])
                        P.op("dve", lambda e, s=s, bank=bank: e.tensor_copy(out=H0s[:, s, :, :].rearrange("p a b -> p (a b)"), in_=bank[:, 0:256]),
                             reads=[bk], writes=[f"H0s{s}"])
                        P.op("act", lambda e, s=s, bank=bank: e.copy(out=H0b[:, s, :, :].rearrange("p a b -> p (a b)"), in_=bank[:, 0:256]),
                             reads=[bk], writes=[f"H0b{s}"])
                for cc in range(4):
                    bank, bk = nextpj()

                    def mmz(e, cc=cc, bank=bank):
                        r = None
                        for hh in range(2):
                            pb = hh * 64
                            for s in range(16):
                                r = e.matmul(bank[pb:pb + 64, s * 16:s * 16 + 16],
                                             lhsT=H0b[pb:pb + 64, s, cc, :],
                                             rhs=AR[s3][pb:pb + 64, cc, :, s * 8:(s + 1) * 8], start=True, stop=True)
                        return r
                    P.op("pe", mmz, reads=[f"H0b{s}" for s in range(16)] + [f"AR{s3}a", f"AR{s3}r"], writes=[bk])
                    src = bank[:, 0:256].rearrange("p (s a t) -> p a s t", s=16, a=2)
                    for ar in range(2):
                        dst = zh[:, cc, ar, :].rearrange("p (s t) -> p s t", t=8)
                        P.op("dve", lambda e, src=src, dst=dst, ar=ar: e.tensor_copy(out=dst, in_=src[:, ar, :, :]), reads=[bk], writes=["zh"])

            def sample_state(s3):
                e16bc = e16[:, :].unsqueeze(2).broadcast_to([128, 16, 64])
                HK = [f"H0s{s}" for s in range(16)]
                for h in range(8):
                    cc, pb = h // 2, (h % 2) * 64
                    P.op("dve", lambda e, h=h: e.tensor_tensor(out=Xex[:, 0, :, :], in0=Ub[:, h * 64:(h + 1) * 64].unsqueeze(1).broadcast_to([128, 16, 64]),
                                                               in1=e16bc, op=ALU.mult),
                         reads=["Ub", "const"], writes=["Xex0"])
                    P.op("pool", lambda e, h=h: e.tensor_tensor(out=Xex[:, 1, :, :], in0=VTM[s3][:, h * 64:(h + 1) * 64].unsqueeze(1).broadcast_to([128, 16, 64]),
                                                                in1=e16bc, op=ALU.mult),
                         reads=[f"VTM{s3}", "const"], writes=["Xex1"])
                    for half in range(2):
                        bank, bk = nextl0()

                        def mms(e, h=h, half=half, bank=bank, pb=pb):
                            e.matmul(bank[pb:pb + 64, :], lhsT=BTM[s3][:, h * 64:(h + 1) * 64],
                                     rhs=Xex[:, 0, half * 8:(half + 1) * 8, :].rearrange("p a b -> p (a b)"), start=True, stop=False)
                            return e.matmul(bank[pb:pb + 64, :], lhsT=KTM[s3][:, h * 64:(h + 1) * 64],
                                            rhs=Xex[:, 1, half * 8:(half + 1) * 8, :].rearrange("p a b -> p (a b)"), start=False, stop=True)
                        P.op("pe", mms, reads=["Xex0", "Xex1", f"BTM{s3}", f"KTM{s3}"], writes=[bk])
                        P.op("dve", lambda e, half=half, cc=cc, bank=bank, pb=pb: e.tensor_tensor(
                            out=H0s[pb:pb + 64, half * 8:(half + 1) * 8, cc, :], in0=bank[pb:pb + 64, :].rearrange("p (s v) -> p s v", s=8),
                            in1=H0s[pb:pb + 64, half * 8:(half + 1) * 8, cc, :], op=ALU.add),
                            reads=[bk] + HK, writes=HK)
                for cc in range(4):
                    P.op("dve", lambda e, cc=cc: e.tensor_tensor(out=H0s[:, :, cc, :], in0=H0s[:, :, cc, :],
                                                                 in1=gC[s3][:, cc, :].unsqueeze(2).broadcast_to([128, 16, 64]), op=ALU.mult),
                         reads=HK + [f"gC{s3}"], writes=HK)
                for q4 in range(4):
                    for sl_ in range(4):
                        s = q4 * 4 + sl_
                        bank, bk = nextpj()

                        def trw(e, s=s, bank=bank):
                            r = None
                            for cc in range(4):
                                r = e.transpose(out=bank[0:64, cc * 128:(cc + 1) * 128], in_=H0s[:, s, cc, :], identity=ident32[:])
                            return r
                        P.op("pe", trw, reads=HK + ["const2"], writes=[bk])
                        if s % 2 == 0:
                            P.op("act", lambda e, sl_=sl_, bank=bank: e.copy(out=wso[:, sl_, :, :].rearrange("p a b -> p (a b)"), in_=bank[0:64, :]),
                                 reads=[bk], writes=[f"wso{sl_}"])
                        else:
                            P.op("dve", lambda e, sl_=sl_, bank=bank: e.tensor_copy(out=wso[:, sl_, :, :].rearrange("p a b -> p (a b)"), in_=bank[0:64, :]),
                                 reads=[bk], writes=[f"wso{sl_}"])
                    P.op("sp", lambda e, q4=q4: e.dma_start(out=wkv_s[q4 * 4:(q4 + 1) * 4].rearrange("s h v k -> v s h k"), in_=wso[:]),
                         reads=[f"wso{q}" for q in range(4)], writes=[f"o_wkv_s{q4}"] + [f"wso{q}" for q in range(4)], dma="o_wkv_s")
                    outkeys.append(f"o_wkv_s{q4}")

            def S6(i):
                qq = i % len(qT)
                s2 = i % R2
                samp = is_samp(i)
                kc3 = i % NKV
                kp3 = (i - 1) % NKV
                if i == NPRE:
                    P.op("pe", lambda e: e.transpose(out=tpb[:, 0:128], in_=kvT[kp3][:, 1, :], identity=ident[:]), reads=[f"kvT{kp3}", "const"], writes=["tpb"])
                    P.op("act", lambda e: e.copy(out=Vaug[kp3][:, :, 0:64], in_=tpb[:, 0:128].rearrange("p (g d) -> p g d", g=2)),
                         reads=["tpb"], writes=[f"Vaug{kp3}"])
                P.op("pe", lambda e: e.transpose(out=tpb[:, 0:128], in_=kvT[kc3][:, 1, :], identity=ident[:]), reads=[f"kvT{kc3}", "const"], writes=["tpb"])
                P.op("act", lambda e: e.copy(out=Vaug[kc3][:, :, 0:64], in_=tpb[:, 0:128].rearrange("p (g d) -> p g d", g=2)),
                     reads=["tpb"], writes=[f"Vaug{kc3}"])
                if samp:
                    sample_cache(qq)
                for g in range(2):
                    for kt in range(2):
                        if samp and kt == 0:
                            continue
                        bank, bk = nextl0()
                        kb = kvT[kp3] if kt == 0 else kvT[kc3]
                        kbk = f"kvT{kp3}" if kt == 0 else f"kvT{kc3}"
                        P.op("pe", lambda e, bank=bank, kb=kb, g=g: e.matmul(bank[:], lhsT=kb[g * 64:(g + 1) * 64, 0, :],
                                                                             rhs=qT[qq][g * 64:(g + 1) * 64, :, :], start=True, stop=True),
                             reads=[kbk, f"qT{qq}"], writes=[bk])
                        pt = PT[g * 2 + kt]
                        ptk = f"PT{g * 2 + kt}"
                        P.op("act", lambda e, bank=bank, pt=pt: e.activation(out=pt[:].rearrange("p a b -> p (a b)"), in_=bank[:], func=AF.Exp),
                             reads=[bk], writes=[ptk])
                        if kt == 0:
                            mi = 6 if i == NPRE else 2
                        else:
                            mi = 4 if samp else 1
                        mbc = MK[:, mi, :].unsqueeze(1).broadcast_to([128, 4, 128])
                        P.op("dve", lambda e, pt=pt, mbc=mbc: e.tensor_tensor(out=pt[:], in0=pt[:], in1=mbc, op=ALU.mult),
                             reads=[ptk, "const"], writes=[ptk])
                for g in range(2):
                    bank, bk = nextsq()
                    b3 = bank[:, 0:260].rearrange("p (a b) -> p a b", a=4)
                    for j in range(4):
                        h = g * 4 + j
                        if samp:
                            px = PTx[h % 2]
                            pxk = f"PTx{h % 2}"
                            P.op("dve", lambda e, g=g, j=j, px=px: [e.tensor_tensor(
                                out=px[:, s, s * 8:(s + 1) * 8], in0=PTc[:, g, s, j * 8:(j + 1) * 8], in1=MK[:, 2, 0:8], op=ALU.mult) for s in range(16)][-1],
                                reads=[f"PTc{g}", "const"], writes=[pxk])

                        def pv(e, g=g, j=j, b3=b3, h=h):
                            r = None
                            if samp:
                                for s in range(16):
                                    e.matmul(b3[:, j, :], lhsT=PTx[h % 2][:, s, :], rhs=Vca[:, s, g, :], start=(s == 0), stop=False)
                            else:
                                e.matmul(b3[:, j, :], lhsT=PT[g * 2][:, j, :], rhs=Vaug[kp3][:, g, :], start=True, stop=False)
                            r = e.matmul(b3[:, j, :], lhsT=PT[g * 2 + 1][:, j, :], rhs=Vaug[kc3][:, g, :], start=False, stop=True)
                            return r
                        rd = [f"PT{g * 2 + 1}", f"Vaug{kc3}"] + ([f"PTx{h % 2}", "Vca"] if samp else [f"PT{g * 2}", f"Vaug{kp3}"])
                        P.op("pe", pv, reads=rd, writes=[bk + f"_{j}"] + ([bk] if j == 0 else []))
                    bkj = [bk] + [bk + f"_{j}" for j in range(4)]
                    P.op("dve", lambda e, g=g, b3=b3: e.tensor_tensor(out=den[:, g * 4:(g + 1) * 4], in0=b3[:, :, 64], in1=esink[:, g * 4:(g + 1) * 4], op=ALU.add),
                         reads=bkj + ["esink"], writes=[f"den{g}"])
                    P.op("dve", lambda e, g=g: e.reciprocal(out=den[:, g * 4:(g + 1) * 4], in_=den[:, g * 4:(g + 1) * 4]), reads=[f"den{g}"], writes=[f"den{g}"])
                    P.op("dve", lambda e, g=g, b3=b3: e.tensor_tensor(out=attb[:, g * 4:(g + 1) * 4, :], in0=b3[:, :, 0:64],
                                                                      in1=den[:, g * 4:(g + 1) * 4].unsqueeze(2).broadcast_to([128, 4, 64]), op=ALU.mult),
                         reads=bkj + [f"den{g}"], writes=[f"attb{g}"])

                def tra(e):
                    r = None
                    for c in range(4):
                        r = e.transpose(out=tpb[:, c * 128:(c + 1) * 128], in_=attb[:, 2 * c:2 * c + 2, :].rearrange("p a b -> p (a b)"), identity=ident[:])
                    return r
                P.op("pe", tra, reads=["attb0", "attb1", "const"], writes=["tpb"])
                P.op("act", lambda e: e.copy(out=mixT[s2][:, 0:4, :].rearrange("p a b -> p (a b)"), in_=tpb[:, 0:512]), reads=["tpb"], writes=[f"mixT{s2}a"])

            def sample_cache(qq):
                P.op("pool", lambda e: e.dma_start(out=cstb[:], in_=ck.rearrange("s k d -> k s d")), writes=["cstb"], dma="cstb")
                for q in range(2):
                    def trc(e, q=q):
                        r = None
                        for ss_ in range(8):
                            r = e.transpose(out=tpb[:, ss_ * 128:(ss_ + 1) * 128], in_=cstb[:, q * 8 + ss_, :], identity=ident[:])
                        return r
                    P.op("pe", trc, reads=["cstb", "const"], writes=["tpb"])
                    P.op("act", lambda e, q=q: e.copy(out=KcT[:, q * 8:(q + 1) * 8, :].rearrange("p a b -> p (a b)"), in_=tpb[:]), reads=["tpb"], writes=[f"KcT{q}"])
                P.op("pool", lambda e: [e.dma_start(out=Vca[:, :, g, 0:64], in_=cv[:, :, g * 64:(g + 1) * 64].rearrange("s k d -> k s d")) for g in range(2)],
                     reads=["Vca"], writes=["Vca"], dma="Vca", n=2)
                for g in range(2):
                    bank, bk = nextl0()

                    def scc(e, g=g, bank=bank):
                        r = None
                        for s in range(16):
                            r = e.matmul(bank[:, s * 32:(s + 1) * 32], lhsT=KcT[g * 64:(g + 1) * 64, s, :],
                                         rhs=qT[qq][g * 64:(g + 1) * 64, :, s * 8:(s + 1) * 8], start=True, stop=True)
                        return r
                    P.op("pe", scc, reads=["KcT0", "KcT1", f"qT{qq}"], writes=[bk])
                    P.op("act", lambda e, g=g, bank=bank: e.activation(out=PTc[:, g, :, :].rearrange("p a b -> p (a b)"), in_=bank[:], func=AF.Exp),
                         reads=[bk], writes=[f"PTc{g}"])

            def S7(i):
                s2 = i % R2
                s3 = i % len(gT)
                j = i - NPRE
                o = OTM[i % R2]
                ok = f"OTM{i % R2}"
                P.op("dve", lambda e: e.tensor_reduce(out=st1[:], in_=o[:], axis=AX.X, op=ALU.add), reads=[ok], writes=["st1"])
                P.op("pool", lambda e: e.tensor_tensor(out=osq[:], in0=o[:], in1=o[:], op=ALU.mult), reads=[ok], writes=["osq"])
                P.op("dve", lambda e: e.tensor_reduce(out=st2[:], in_=osq[:], axis=AX.X, op=ALU.add), reads=["osq"], writes=["st2"])
                P.op("dve", lambda e: e.tensor_scalar(out=st1[:], in0=st1[:], scalar1=1.0 / 64, scalar2=None, op0=ALU.mult), reads=["st1"], writes=["st1"])
                P.op("dve", lambda e: e.tensor_tensor(out=st3[:], in0=st1[:], in1=st1[:], op=ALU.mult), reads=["st1"], writes=["st3"])
                P.op("dve", lambda e: e.scalar_tensor_tensor(out=st2[:], in0=st2[:], scalar=1.0 / 64, in1=st3[:], op0=ALU.mult, op1=ALU.subtract),
                     reads=["st2", "st3"], writes=["st2"])
                P.op("act", lambda e: e.activation(out=st2[:], in_=st2[:], func=AF.Sqrt, bias=64e-5), reads=["st2"], writes=["st2"])
                P.op("dve", lambda e: e.reciprocal(out=st2[:], in_=st2[:]), reads=["st2"], writes=["st2"])
                P.op("dve", lambda e: e.tensor_tensor(out=osq[:], in0=o[:], in1=st1[:, :].unsqueeze(2).broadcast_to([128, 8, 64]), op=ALU.subtract),
                     reads=[ok, "st1", "osq"], writes=["osq"])
                P.op("dve", lambda e: e.tensor_tensor(out=onb[:], in0=osq[:], in1=st2[:, :].unsqueeze(2).broadcast_to([128, 8, 64]), op=ALU.mult),
                     reads=["osq", "st2"], writes=["onb"])

                def tro(e):
                    r = None
                    for c in range(4):
                        r = e.transpose(out=tpb[:, c * 128:(c + 1) * 128], in_=onb[:, 2 * c:2 * c + 2, :].rearrange("p a b -> p (a b)"), identity=ident[:])
                    return r
                P.op("pe", tro, reads=["onb", "const"], writes=["tpb"])
                P.op("dve", lambda e: e.tensor_tensor(out=tmx[:], in0=tpb[:, 0:512].rearrange("p (a b) -> p a b", a=4), in1=v4bc(GNG), op=ALU.mult),
                     reads=["tpb", "const2", "tmx"], writes=["tmx"])
                P.op("pool", lambda e: e.tensor_tensor(out=tmx[:], in0=tmx[:], in1=v4bc(GNB), op=ALU.add), reads=["tmx", "const2"], writes=["tmx"])
                P.op("pool", lambda e: e.tensor_tensor(out=tmx[:], in0=tmx[:], in1=bonT[s3][:], op=ALU.add), reads=["tmx", f"bonT{s3}"], writes=["tmx"])
                P.op("dve", lambda e: e.tensor_tensor(out=mixT[s2][:, 4:8, :], in0=tmx[:], in1=gT[s3][:], op=ALU.mult),
                     reads=["tmx", f"gT{s3}"], writes=[f"mixT{s2}b"])
                xb = x1t[0]
                xk = "x1t0"
                src = xs[:, :] if is_samp(i) else xw[i * 128:(i + 1) * 128, :]
                P.op("sp", lambda e: e.dma_start(out=xb[:], in_=src), writes=[xk], dma=xk)
                for half in range(2):
                    bank, bk = nextpj()

                    def mo(e, half=half, bank=bank):
                        r = None
                        for kc in range(8):
                            r = e.matmul(bank[:], lhsT=mixT[s2][:, kc, :], rhs=Wout[:, kc, half * 512:(half + 1) * 512], start=(kc == 0), stop=(kc == 7))
                        return r
                    P.op("pe", mo, reads=[f"mixT{s2}a", f"mixT{s2}b", "const"], writes=[bk])
                    P.op("dve", lambda e, half=half, bank=bank: e.tensor_tensor(out=xb[:, half * 512:(half + 1) * 512], in0=bank[:],
                                                                                in1=xb[:, half * 512:(half + 1) * 512], op=ALU.add),
                         reads=[bk, xk], writes=[xk])
                P.op("sp", lambda e: e.dma_start(out=x1s[j * 128:(j + 1) * 128, :], in_=xb[:]), reads=[xk], writes=[f"x1s{j}", xk], dma=xk)
                outkeys.append(f"x1s{j}")

            if pA:
                def cap(fns):
                    P.cap = []
                    for fn, i in fns:
                        if 0 <= i < NPT:
                            fn(i)
                    out = P.cap
                    P.cap = None
                    return out

                for step in range(NPT + 6):
                    if 0 <= step - 4 < NPT and is_own(step - 4):
                        S6(step - 4)
                    if 0 <= step - 4 < NPT:
                        S5n(step - 4)
                    la = cap([(S4n, step - 3)])
                    own_fns = [(fn, i) for fn, i in ((S7, step - 5),) if 0 <= i < NPT and is_own(i)]
                    if ILV_MODE == 1:
                        if 0 <= step - 2 < NPT:
                            S3(step - 2)
                        lb = cap([(S2, step - 1), (S1, step)])
                    elif ILV_MODE == 2:
                        lb = cap([(S3, step - 2)])
                    else:
                        lb = cap(own_fns + [(S3, step - 2), (S2, step - 1), (S1, step)])
                    na, nb_ = len(la), len(lb)
                    ia = ib = 0
                    while ia < na or ib < nb_:
                        if ib >= nb_ or (ia < na and (NO_ILV or (ia // ILV_CH) * nb_ <= (ib // max(1, (ILV_CH * nb_) // max(1, na))) * na)):
                            P.op(*la[ia][:2], reads=la[ia][2], writes=la[ia][3], dma=la[ia][4], n=la[ia][5])
                            ia += 1
                        else:
                            if len(P.ins) not in SKIPOPS:
                                P.op(*lb[ib][:2], reads=lb[ib][2], writes=lb[ib][3], dma=lb[ib][4], n=lb[ib][5])
                            else:
                                P.op(lb[ib][0], None)
                            ib += 1
                    if ILV_MODE == 2:
                        for fn, i in ((S2, step - 1), (S1, step)):
                            if 0 <= i < NPT:
                                fn(i)
            elif pSa:
                S1(NPT)
                S2(NPT)
                outkeys.extend(["qT0", "kvT0", "fprev0", "fprev1"] + [f"fT0_{g}" for g in range(4)])
            else:
                for fn in (S3, S4, S5, S6, S7):
                    fn(NPT)
            P.op("sp", None, reads=list(outkeys))
            P.emit()
            build.stats[mode] = P.stats

    if "A" in PHASES:
        phase("A", None)
    with ExitStack() as stp:
        per = dict(
            fT=[stp.enter_context(nc.sbuf_tensor("fTs", [128, 14, 128], F32))],
            qT=[stp.enter_context(nc.sbuf_tensor("qTs", [128, 4, 128], BF16))],
            kvT=[stp.enter_context(nc.sbuf_tensor("kvTs", [128, 2, 128], BF16))],
            fprev=stp.enter_context(nc.sbuf_tensor("fprevs", [128, 14, 16], F32)),
        )
        if "Sa" in PHASES:
            phase("Sa", per)
        if "Sb" in PHASES:
            phase("Sb", per)

    if "B" not in PHASES:
        return nc
    with ExitStack() as st:
        def sb(name, shape, dt=F32):
            return st.enter_context(nc.sbuf_tensor(name, shape, dt))

        def psb(name, shape, dt=F32):
            return st.enter_context(nc.psum_tensor(name, shape, dt))
        P = Prog(nc, '_B')
        outk = []
        Wg = sb("Wg", [128, 8, D_FF], BF16)
        Wu = sb("Wu", [128, 8, D_FF], BF16)
        Wd = sb("Wd", [128, NFC, 1024], BF16)
        gffn = sb("gffn", [128, 1024])
        gfin = sb("gfin", [128, 1024])
        identb = sb("identb", [128, 128], BF16)

        P.op("pool", lambda e: e.dma_start(out=identb[:], in_=ident_d[:, :]), writes=["W"], dma="Wi")
        P.op("pool", lambda e: [e.dma_start(out=Wg[:, kc, :], in_=w_gate[kc * 128:(kc + 1) * 128, :]) for kc in range(8)],
             writes=["Wg"], dma="Wg", n=8)
        P.op("pool", lambda e: [e.dma_start(out=Wu[:, kc, :], in_=w_up[kc * 128:(kc + 1) * 128, :]) for kc in range(8)],
             writes=["Wu"], dma="Wu", n=8)
        P.op("pool", lambda e: [e.dma_start(out=Wd[:, fc, :], in_=w_down[fc * 128:(fc + 1) * 128, :]) for fc in range(NFC)],
             writes=["Wd"], dma="Wd", n=NFC)

        def cl(e):
            return [e.dma_start(out=gffn[:], in_=gvec[1:2, :].broadcast_to([128, 1024])),
                    e.dma_start(out=gfin[:], in_=gvec[2:3, :].broadcast_to([128, 1024]))]
        P.op("sp", cl, writes=["G"], dma="G", n=2)
        xg = sb("xg", [128, 4, 1024])
        junk2 = sb("junk2", [128, 1024], BF16)
        ub = sb("ub", [128, 1024], BF16)
        uT = sb("uT", [128, 8, 512], BF16)
        actT = sb("actT", [128, 11, 512], BF16)
        sgt = sb("sgt", [128, 512])
        ssb = sb("ssb", [128, 1])
        rsb = sb("rsb", [128, 1])
        yb = [sb(f"yb{q}", [128, 1024]) for q in range(2)]
        pg = [psb(f"pg{q}", [128, 512]) for q in range(2)]
        pu = [psb(f"pu{q}", [128, 512]) for q in range(2)]
        pd = [psb(f"pd{q}", [128, 512]) for q in range(2)]
        tpb2 = psb("tpb2", [128, 1024], BF16)
        cnt = [0]
        groups = [(0, 4), (4, 4), (8, 4), (12, 4), (16, 1)]
        for (t0, nt) in groups:
            N = nt * 128
            P.op("sp", lambda e, t0=t0, nt=nt: e.dma_start(out=xg[:, 0:nt, :], in_=x1s[t0 * 128:(t0 + nt) * 128, :].rearrange("(a p) d -> p a d", p=128)),
                 writes=["xg"], dma="xg")
            for a in range(nt):
                P.op("act", lambda e, a=a: e.activation(out=junk2[:], in_=xg[:, a, :], func=AF.Square, accum_out=ssb[:]), reads=["xg"], writes=["junk2", "ssb"])
                P.op("act", lambda e: e.activation(out=rsb[:], in_=ssb[:], func=AF.Sqrt, scale=1.0 / 1024, bias=1e-6), reads=["ssb"], writes=["rsb"])
                P.op("dve", lambda e: e.reciprocal(out=rsb[:], in_=rsb[:]), reads=["rsb"], writes=["rsb"])
                P.op("dve", lambda e, a=a: e.scalar_tensor_tensor(out=ub[:], in0=xg[:, a, :], scalar=rsb[:, 0:1], in1=gffn[:], op0=ALU.mult, op1=ALU.mult),
                     reads=["xg", "rsb", "G"], writes=["ub"])

                def tr(e):
                    r = None
                    for kc in range(8):
                        r = e.transpose(out=tpb2[:, kc * 128:(kc + 1) * 128], in_=ub[:, kc * 128:(kc + 1) * 128], identity=identb[:])
                    return r
                P.op("pe", tr, reads=["ub", "W"], writes=["tpb2"])
                P.op("act", lambda e, a=a: e.copy(out=uT[:, :, a * 128:(a + 1) * 128], in_=tpb2[:].rearrange("p (a b) -> p a b", a=8)),
                     reads=["tpb2"], writes=[f"uT{a}"])
            uk = [f"uT{a}" for a in range(nt)]
            for hf in range(2):
                for fi in range(11):
                    fc = hf * 11 + fi
                    cnt[0] += 1
                    b = cnt[0] % 2

                    def mg(e, fc=fc, b=b, N=N):
                        r = None
                        for kc in range(8):
                            r = e.matmul(pg[b][:, 0:N], lhsT=Wg[:, kc, fc * 128:(fc + 1) * 128], rhs=uT[:, kc, 0:N], start=(kc == 0), stop=(kc == 7))
                        return r
                    P.op("pe", mg, reads=["Wg"] + uk, writes=[f"pg{b}"])

                    def mu_(e, fc=fc, b=b, N=N):
                        r = None
                        for kc in range(8):
                            r = e.matmul(pu[b][:, 0:N], lhsT=Wu[:, kc, fc * 128:(fc + 1) * 128], rhs=uT[:, kc, 0:N], start=(kc == 0), stop=(kc == 7))
                        return r
                    P.op("pe", mu_, reads=["Wu"] + uk, writes=[f"pu{b}"])
                    P.op("act", lambda e, b=b, N=N: e.activation(out=sgt[:, 0:N], in_=pg[b][:, 0:N], func=AF.Silu), reads=[f"pg{b}"], writes=["sgt"])
                    P.op("dve", lambda e, b=b, N=N, fi=fi: e.tensor_tensor(out=actT[:, fi, 0:N], in0=pu[b][:, 0:N], in1=sgt[:, 0:N], op=ALU.mult),
                         reads=[f"pu{b}", "sgt"], writes=[f"actT{fi}"])
                ak = [f"actT{fi}" for fi in range(11)]
                for a in range(nt):
                    for half in range(2):
                        cnt[0] += 1
                        b = cnt[0] % 2

                        def md(e, a=a, half=half, b=b, hf=hf):
                            r = None
                            for fi in range(11):
                                r = e.matmul(pd[b][:], lhsT=actT[:, fi, a * 128:(a + 1) * 128], rhs=Wd[:, hf * 11 + fi, half * 512:(half + 1) * 512],
                                             start=(fi == 0), stop=(fi == 10))
                            return r
                        P.op("pe", md, reads=["Wd"] + ak, writes=[f"pd{b}"])
                        P.op("dve", lambda e, a=a, half=half, b=b: e.tensor_tensor(out=xg[:, a, half * 512:(half + 1) * 512], in0=pd[b][:],
                                                                                   in1=xg[:, a, half * 512:(half + 1) * 512], op=ALU.add),
                             reads=[f"pd{b}", "xg"], writes=["xg"])
            for a in range(nt):
                t = t0 + a
                y = yb[t % 2]
                yk = f"yb{t % 2}"
                P.op("act", lambda e, a=a: e.activation(out=junk2[:], in_=xg[:, a, :], func=AF.Square, accum_out=ssb[:]), reads=["xg"], writes=["junk2", "ssb"])
                P.op("act", lambda e: e.activation(out=rsb[:], in_=ssb[:], func=AF.Sqrt, scale=1.0 / 1024, bias=1e-6), reads=["ssb"], writes=["rsb"])
                P.op("dve", lambda e: e.reciprocal(out=rsb[:], in_=rsb[:]), reads=["rsb"], writes=["rsb"])
                P.op("dve", lambda e, a=a, y=y: e.scalar_tensor_tensor(out=y[:], in0=xg[:, a, :], scalar=rsb[:, 0:1], in1=gfin[:], op0=ALU.mult, op1=ALU.mult),
                     reads=["xg", "rsb", "G"], writes=[yk])
                dst = y_s[:, :] if t == 16 else y_p[t * 128:(t + 1) * 128, :]
                P.op("sp", lambda e, y=y, dst=dst: e.dma_start(out=dst, in_=y[:]), reads=[yk], writes=[f"oy{t}", yk], dma=yk)
                outk.append(f"oy{t}")
        P.op("sp", None, reads=outk)
        P.emit()
        build.stats["B"] = P.stats
    return nc


def _consts(p):
    s = np.arange(128)[:, None]
    t = np.arange(128)[None, :]
    su = (s < t).astype(np.float32)
    ui = (s <= t).astype(np.float32)
    sl = (s > t).astype(np.float32)
    same = ((s // 8) == (t // 8)).astype(np.float32)
    mfirst = sl if p > 0 else np.zeros_like(sl)
    masks = np.stack([su, ui, sl, su * same, ui * same, sl * same, mfirst], axis=1).reshape(128, 7 * 128)
    ident = np.eye(128, dtype=np.float32)
    bones = ((s // 64) == (t // 64)).astype(np.float32)
    rm = np.ones((128, 2, 128), np.float32)
    rm[:, 1, :] = (np.arange(128) % 8 != 0).astype(np.float32)[None, :]
    e16 = ((np.arange(128)[:, None] // 8) == np.arange(16)[None, :]).astype(np.float32)
    return dict(masks=np.ascontiguousarray(masks), ident=ident, bones=bones, rmask=rm.reshape(128, 256), e16=e16)


_NC = [None]


def kernel(x_prompt, x_sample, cache_k, cache_v, state_wkv, state_shift, g_mix, w_in, attn_sinks,
           rwkv_mu, w0, w2, a0, a2, g2, k_k, k_a, r_k, gn_g, gn_b, w_out, g_ffn, w_gate, w_up,
           w_down, g_final):
    f = lambda a: np.ascontiguousarray(np.asarray(a, dtype=np.float32))
    x_prompt, x_sample = f(x_prompt), f(x_sample)
    w_in0 = f(w_in)[0]
    qperm = np.concatenate([np.r_[j * 64:(j + 1) * 64, (4 + j) * 64:(5 + j) * 64] for j in range(4)])
    w_in_p = np.ascontiguousarray(np.concatenate([w_in0[:, qperm], w_in0[:, 512:]], axis=1))
    fm4 = lambda v: f(v).reshape(4, 128).T
    vec4 = np.ascontiguousarray(np.stack([fm4(w0[0]), fm4(a0[0]), fm4(k_k[0]), fm4(k_a[0]), fm4(f(r_k)[0].reshape(-1)),
                                          fm4(gn_g[0]), fm4(gn_b[0])], axis=1).reshape(128, 28))
    shared = dict(
        w_in=w_in_p, w_out=f(w_out)[0], w_gate=f(w_gate)[0], w_up=f(w_up)[0], w_down=f(w_down)[0],
        gvec=np.ascontiguousarray(np.stack([f(g_mix)[0], f(g_ffn)[0], f(g_final)], axis=0)),
        mu=np.ascontiguousarray(f(rwkv_mu)[0].reshape(14, 128).T),
        vec4=vec4,
        w2a2=np.ascontiguousarray(np.concatenate([f(w2)[0], f(a2)[0]], axis=0)),
        g2=f(g2)[0],
        sinks=f(attn_sinks)[0].reshape(1, 8),
    )
    in_maps = []
    for c in range(8):
        b, p = c // 4, c % 4
        xwin = np.zeros((NPT * 128, 1024), np.float32)
        nreal = (p + 1) * 2048
        xwin[NPT * 128 - nreal:] = x_prompt[b, 0:nreal]
        m = dict(shared)
        m.update(_consts(p))
        m.update(
            xw=xwin,
            xs=np.ascontiguousarray(x_sample[16 * c:16 * c + 16].reshape(128, 1024)),
            hprev=f(state_shift)[0, 16 * c:16 * c + 16],
            ck=np.ascontiguousarray(f(cache_k)[0, 16 * c:16 * c + 16].reshape(16, 128, 128)),
            cv=np.ascontiguousarray(f(cache_v)[0, 16 * c:16 * c + 16].reshape(16, 128, 128)),
            swkv=np.ascontiguousarray(f(state_wkv)[0, 16 * c:16 * c + 16]),
        )
        in_maps.append(m)
    if _NC[0] is None:
        _NC[0] = build()
    res = run_bass_kernel_spmd(_NC[0], in_maps, core_ids=list(range(8)))
    R = res.results
    y_prompt = np.stack([np.concatenate([R[b * 4 + p]["y_p"] for p in range(4)], axis=0) for b in range(2)], axis=0)
    y_sample = np.concatenate([R[c]["y_s"].reshape(16, 8, 1024) for c in range(8)], axis=0)
    kp = np.stack([R[b * 4 + 3]["kwin_p"].reshape(128, 2, 64) for b in range(2)], axis=0)[None]
    vp = np.stack([R[b * 4 + 3]["vwin_p"].reshape(128, 2, 64) for b in range(2)], axis=0)[None]
    sp = np.stack([R[b * 4 + 3]["wkv_p"] for b in range(2)], axis=0)[None]
    hp = np.stack([R[b * 4 + 3]["shift_p"].reshape(1024) for b in range(2)], axis=0)[None]
    ks = np.concatenate([R[c]["kwin_s"].reshape(16, 128, 2, 64) for c in range(8)], axis=0)[None]
    vs = np.concatenate([R[c]["vwin_s"].reshape(16, 128, 2, 64) for c in range(8)], axis=0)[None]
    ss_ = np.concatenate([R[c]["wkv_s"] for c in range(8)], axis=0)[None]
    hs = np.concatenate([R[c]["shift_s"] for c in range(8)], axis=0)[None]
    return tuple(np.ascontiguousarray(a.astype(np.float32)) for a in (y_prompt, y_sample, kp, vp, sp, hp, ks, vs, ss_, hs))
```

```python
import numpy as np
from contextlib import ExitStack
import concourse.bass as bass
import concourse.mybir as mybir
from concourse.bass_utils import run_bass_kernel_spmd
from concourse.alu_op_type import AluOpType as ALU

AF = mybir.ActivationFunctionType
AX = mybir.AxisListType
F32 = mybir.dt.float32
BF16 = mybir.dt.bfloat16

SAME_ENGINE_SYNC = True
SAME_ENGINE_RAW_ONLY = False
ENGS = ("pe", "act", "dve", "pool", "sp")
C0 = float(np.exp(-0.5))
NPRE = 48
NOWN = 16
NPT = NPRE + NOWN
NT = NPT + 1
D_FF = 2816
NFC = 22
ILV_CH = 2
ILV_MODE = 0
SKIPOPS = set()
NO_ILV = False
PSUM_PREFIXES = ("pj", "tpb", "l0", "sqp", "Zp", "pg", "pu", "pd")
OPLIMIT = 10 ** 9
NO_D2D = False
PHASES = {"A", "Sa", "Sb", "B"}


class Prog:
    def __init__(self, nc, tag=''):
        self.nc = nc
        self.tag = tag
        self.ins = []
        self.last_w = {}
        self.readers = {}

    def op(self, eng, fn, reads=(), writes=(), dma=None, n=1):
        if getattr(self, "cap", None) is not None:
            self.cap.append((eng, fn, list(reads), list(writes), dma, n))
            return -1
        idx = len(self.ins)
        if idx >= OPLIMIT:
            return idx
        writes = list(writes) + [k for k in reads if k.startswith(PSUM_PREFIXES) and k not in writes]
        deps = set()
        raw = set()
        for k in reads:
            if k in self.last_w:
                deps.add(self.last_w[k])
                raw.add(self.last_w[k])
        for k in writes:
            if k in self.last_w:
                deps.add(self.last_w[k])
            for r in self.readers.get(k, ()):
                deps.add(r)
        deps.discard(idx)
        if fn is None:
            writes = []
        self.ins.append(dict(eng=eng, fn=fn, deps=deps, raw=raw, dma=dma, used=False, n=n))
        for k in reads:
            self.readers.setdefault(k, []).append(idx)
        for k in writes:
            self.last_w[k] = idx
            self.readers[k] = []
        return idx

    def emit(self):
        nc = self.nc
        ins = self.ins
        for r in ins:
            if SAME_ENGINE_RAW_ONLY:
                r["deps"] = {d for d in r["deps"]
                             if not (ins[d]["eng"] == r["eng"] and ins[d]["dma"] is None and r["dma"] is None and d not in r["raw"])}
            for d in r["deps"]:
                ins[d]["used"] = True
        cnt = {e: 0 for e in ENGS}
        dmav = {}
        for r in ins:
            if r["dma"] is not None:
                r["sem"] = "dma_" + r["dma"]
                dmav[r["sem"]] = dmav.get(r["sem"], 0) + 16 * r["n"]
                r["val"] = dmav[r["sem"]]
            elif r["used"]:
                cnt[r["eng"]] += 1
                r["sem"] = "eng_" + r["eng"]
                r["val"] = cnt[r["eng"]]
            else:
                r["sem"] = None
                r["val"] = 0
        known = {e: {} for e in ENGS}
        for r in ins:
            e = r["eng"]
            kn = known[e]
            wd = {}
            for d in sorted(r["deps"]):
                rd = ins[d]
                s, v = rd["sem"], rd["val"]
                if rd["eng"] == e and rd["dma"] is None and not SAME_ENGINE_SYNC:
                    continue
                if kn.get(s, 0) >= v:
                    continue
                wd[s] = max(wd.get(s, 0), v)
                for s2, v2 in rd["clock"].items():
                    if kn.get(s2, 0) < v2:
                        kn[s2] = v2
            r["waits"] = sorted(wd.items())
            ck = dict(kn)
            if r["sem"] is not None:
                ck[r["sem"]] = r["val"]
            r["clock"] = ck
        semnames = sorted({r["sem"] for r in ins if r["sem"] is not None})
        self.stats = dict(n=len(ins), nsem=len(semnames),
                          nwaits=sum(len(r["waits"]) for r in ins),
                          per_eng={e: sum(1 for r in ins if r["eng"] == e) for e in ENGS})
        with ExitStack() as st:
            sems = {s: st.enter_context(nc.semaphore(s + self.tag)) for s in semnames}
            block = st.enter_context(nc.Block())
            reg = {"pe": block.tensor, "act": block.scalar, "dve": block.vector,
                   "pool": block.gpsimd, "sp": block.sync}

            def make(e):
                def body(eng):
                    for r in ins:
                        if r["eng"] != e:
                            continue
                        for s, v in r["waits"]:
                            eng.wait_ge(sems[s], v)
                        if r["fn"] is None:
                            continue
                        out = r["fn"](eng)
                        if r["dma"] is not None:
                            outs = out if isinstance(out, (list, tuple)) else [out]
                            assert len(outs) == r["n"], (len(outs), r["n"])
                            for o in outs:
                                o.then_inc(sems[r["sem"]], 16)
                        elif r["sem"] is not None:
                            o = out[-1] if isinstance(out, (list, tuple)) else out
                            o.then_inc(sems[r["sem"]], 1)
                return body

            for e in ENGS:
                reg[e](make(e))


def build():
    nc = bass.Bass("TRN2", target_bir_lowering=False)

    def din(name, shape):
        return nc.dram_tensor(name, shape, F32, kind="ExternalInput").ap()

    def dout(name, shape):
        return nc.dram_tensor(name, shape, F32, kind="ExternalOutput").ap()

    xw = din("xw", [NPT * 128, 1024])
    xs = din("xs", [128, 1024])
    hprev = din("hprev", [16, 1024])
    ck = din("ck", [16, 128, 128])
    cv = din("cv", [16, 128, 128])
    swkv = din("swkv", [16, 8, 64, 64])
    w_in = din("w_in", [1024, 2560])
    w_out = din("w_out", [1024, 1024])
    w_gate = din("w_gate", [1024, D_FF])
    w_up = din("w_up", [1024, D_FF])
    w_down = din("w_down", [D_FF, 1024])
    gvec = din("gvec", [3, 1024])
    mu_d = din("mu", [128, 14])
    vec4_d = din("vec4", [128, 7 * 4])
    w2a2_d = din("w2a2", [128, 512])
    g2_d = din("g2", [128, 512])
    sinks_d = din("sinks", [1, 8])
    masks_d = din("masks", [128, 7 * 128])
    ident_d = din("ident", [128, 128])
    bones_d = din("bones", [128, 128])
    rmask_d = din("rmask", [128, 256])
    e16_d = din("e16", [128, 16])

    y_p = dout("y_p", [NOWN * 128, 1024])
    y_s = dout("y_s", [128, 1024])
    kwin_p = dout("kwin_p", [128, 128])
    vwin_p = dout("vwin_p", [128, 128])
    wkv_p = dout("wkv_p", [8, 64, 64])
    shift_p = dout("shift_p", [1, 1024])
    kwin_s = dout("kwin_s", [16, 128, 128])
    vwin_s = dout("vwin_s", [16, 128, 128])
    wkv_s = dout("wkv_s", [16, 8, 64, 64])
    shift_s = dout("shift_s", [16, 1024])
    x1s = nc.dram_tensor("x1s", [17 * 128, 1024], F32, kind="Internal").ap()
    build.stats = {}

    W0, A0, KK, KA, RK, GNG, GNB = range(7)

    def phase(mode, per):
        pA, pSa, pSb = mode == "A", mode == "Sa", mode == "Sb"
        with ExitStack() as st:
            def sb(name, shape, dt=F32):
                return st.enter_context(nc.sbuf_tensor(name + "_" + mode, shape, dt))

            def psb(name, shape, dt=F32):
                return st.enter_context(nc.psum_tensor(name + "_" + mode, shape, dt))

            def rot(name, n, shape, dt=F32):
                return [sb(f"{name}{q}", shape, dt) for q in range(n)]
            P = Prog(nc, '_' + mode)
            outkeys = []
            if pA or pSa:
                Win = sb("Win", [128, 8, 2560], BF16)
            if pA or pSb:
                Wout = sb("Wout", [128, 8, 1024], BF16)
            gmix = sb("gmix", [128, 1024])
            mu = sb("mu", [128, 14])
            vec4 = sb("vec4", [128, 7, 4])
            w2a2 = sb("w2a2", [128, 512], BF16)
            g2b = sb("g2b", [128, 512], BF16)
            esink = sb("esink", [128, 8])
            MK = sb("MK", [128, 7, 128], BF16)
            MKL = sb("MKL", [128, 2, 4, 128], BF16)
            ident = sb("ident", [128, 128], BF16)
            ident32 = sb("ident32", [128, 128])
            bones = sb("bones", [128, 128], BF16)
            rmask = sb("rmask", [128, 2, 128])
            e16 = sb("e16", [128, 16], BF16)

            def cload(e):
                r = []
                if pA or pSa:
                    for kc in range(8):
                        r.append(e.dma_start(out=Win[:, kc, :], in_=w_in[kc * 128:(kc + 1) * 128, :]))
                if pA or pSb:
                    for kc in range(8):
                        r.append(e.dma_start(out=Wout[:, kc, :], in_=w_out[kc * 128:(kc + 1) * 128, :]))
                r.append(e.dma_start(out=w2a2[:], in_=w2a2_d[:, :]))
                r.append(e.dma_start(out=g2b[:], in_=g2_d[:, :]))
                r.append(e.dma_start(out=MK[:], in_=masks_d.rearrange("p (m t) -> p m t", m=7)))
                r.append(e.dma_start(out=ident[:], in_=ident_d[:, :]))
                r.append(e.dma_start(out=bones[:], in_=bones_d[:, :]))
                r.append(e.dma_start(out=e16[:], in_=e16_d[:, :]))
                return r
            ncl = 6 + (8 if (pA or pSa) else 0) + (8 if (pA or pSb) else 0)
            P.op("pool", cload, writes=["const"], dma="constb", n=ncl)

            def cload2(e):
                r = []
                r.append(e.dma_start(out=gmix[:], in_=gvec[0:1, :].broadcast_to([128, 1024])))
                r.append(e.dma_start(out=mu[:], in_=mu_d[:, :]))
                r.append(e.dma_start(out=vec4[:], in_=vec4_d.rearrange("p (a b) -> p a b", a=7)))
                r.append(e.dma_start(out=esink[:], in_=sinks_d[0:1, :].broadcast_to([128, 8])))
                r.append(e.dma_start(out=ident32[:], in_=ident_d[:, :]))
                r.append(e.dma_start(out=rmask[:], in_=rmask_d.rearrange("p (a t) -> p a t", a=2)))
                return r
            P.op("sp", cload2, writes=["const2"], dma="constf", n=6)
            P.op("act", lambda e: e.activation(out=esink[:], in_=esink[:], func=AF.Exp),
                 reads=["const2"], writes=["esink"])

            def mkl(e):
                r = None
                for kd in range(2):
                    for q in range(4):
                        r = e.tensor_copy(out=MKL[:, kd, q, :], in_=MK[:, 3 * kd + (q % 2), :])
                return r
            P.op("pool", mkl, reads=["const"], writes=["MKL"])

            def v4bc(ix, n=128):
                return vec4[:, ix, :].unsqueeze(2).broadcast_to([128, 4, n])

            R2 = 2 if pA else 1
            if pA or pSa:
                xt = rot("xt", 1 if pA else 1, [128, 1024])
                ss = rot("ss", 2, [128, 1])
                rs = rot("rs", 2, [128, 1])
                hb = rot("hb", R2, [128, 1024], BF16)
                hT = rot("hT", R2, [128, 8, 128], BF16)
            if pA:
                fT = rot("fT", 2, [128, 14, 128])
                qT = rot("qT", 4, [128, 4, 128], BF16)
                kvT = rot("kvT", 5, [128, 2, 128], BF16)
            else:
                fT, qT, kvT, fprev = per["fT"], per["qT"], per["kvT"], per["fprev"]
            NKV = len(kvT)
            if pSa:
                h32s = sb("h32s", [128, 1024])
                kvx = sb("kvx", [128, 4, 128])
                hp32 = sb("hp32", [16, 1024])
                hpb = sb("hpb", [16, 1024], BF16)
                hpT = sb("hpT", [128, 8, 16], BF16)
            if pA or pSb:
                Vaug = rot("Vaug", NKV, [128, 2, 65], BF16)
                xsT = sb("xsT", [128, 14, 128])
                fcar = sb("fcar", [128, 14])
                twal = sb("twal", [128, 128], BF16)
                sg = sb("sg", [128, 128], BF16)
                eT = sb("eT", [128, 4, 128])
                asT = sb("asT", [128, 4, 128])
                kkT = sb("kkT", [128, 4, 128])
                sqb = sb("sqb", [128, 4, 128], BF16)
                rn = sb("rn", [128, 4, 128])
                tmpA = sb("tmpA", [128, 4, 128])
                kmT = sb("kmT", [128, 4, 128])
                cumE = sb("cumE", [128, 4, 128])
                Eg = sb("Eg", [128, 4, 128])
                vb = sb("vb", [128, 4, 128], BF16)
                AR = rot("AR", R2, [128, 4, 2, 128], BF16)
                BTF = rot("BTF", R2, [128, 4, 128], BF16)
                KTF = rot("KTF", R2, [128, 4, 128], BF16)
                VTM = rot("VTM", R2, [128, 512], BF16)
                BTM = rot("BTM", R2, [128, 512], BF16)
                KTM = rot("KTM", R2, [128, 512], BF16)
                gC = rot("gC", R2, [128, 4, 16])
                gT = rot("gT", 3 if pA else 1, [128, 4, 128], BF16)
                bonT = rot("bonT", 3 if pA else 1, [128, 4, 128], BF16)
                if pSb:
                    SQ0 = sb("SQ0", [128, 8, 256], BF16)
                    NM = sb("NM", [128, 8, 384], BF16)
                    NJ = sb("NJ", [128, 2, 8, 128], BF16)
                    AJ = sb("AJ", [128, 2, 8, 128], BF16)
                else:
                    L0S = sb("L0S", [128, 8, 512], BF16)
                    A0S = sb("A0S", [128, 8, 128], BF16)
                    NA = sb("NA", [128, 2, 8, 256], BF16)
                Zb = rot("Zb", 2, [128, 512], BF16)
                Ub = sb("Ub", [128, 512], BF16)
                Hst = sb("Hst", [128, 4, 64])
                Hb = sb("Hb", [128, 4, 64], BF16)
                tS = sb("tS", [128, 4, 64])
                OTM = rot("OTM", R2, [128, 8, 64])
                osq = sb("osq", [128, 8, 64])
                st1 = sb("st1", [128, 8])
                st2 = sb("st2", [128, 8])
                st3 = sb("st3", [128, 8])
                onb = sb("onb", [128, 8, 64], BF16)
                PT = rot("PT", 4, [128, 4, 128], BF16)
                den = sb("den", [128, 8])
                attb = sb("attb", [128, 8, 64], BF16)
                mixT = rot("mixT", R2, [128, 8, 128], BF16)
                tmx = sb("tmx", [128, 4, 128])
                x1t = rot("x1t", 1, [128, 1024])
                if pA:
                    ATM = rot("ATM", R2, [128, 512], BF16)
                    BHF = sb("BHF", [128, 4, 128], BF16)
                    KHF = sb("KHF", [128, 4, 128], BF16)
                    Xb = rot("Xb", 2, [128, 2, 512], BF16)
                    WY = sb("WY", [128, 8, 128], BF16)
                    MTb = sb("MTb", [128, 4, 64], BF16)
                    WTF = sb("WTF", [128, 4, 128], BF16)
            if pA:
                kvx = tmx
                h32s = x1t[0]
            if pSb:
                S0q = sb("S0q", [64, 32, 64])
                H0s = sb("H0s", [128, 16, 4, 64])
                H0b = sb("H0b", [128, 16, 4, 64], BF16)
                Xex = sb("Xex", [128, 2, 16, 64], BF16)
                zh = sb("zh", [128, 4, 2, 128], BF16)
                KcT = sb("KcT", [128, 16, 128], BF16)
                Vca = sb("Vca", [128, 16, 2, 65], BF16)
                cstb = sb("cstb", [128, 16, 128], BF16)
                PTc = sb("PTc", [128, 2, 16, 32], BF16)
                PTx = rot("PTx", 2, [128, 16, 128], BF16)
                wso = sb("wso", [64, 4, 8, 64])
            h32k = "x1t0" if pA else "h32s"
            kvxk = "tmx" if pA else "kvx"

            pj = [psb(f"pj{q}", [128, 512]) for q in range(2)]
            tpb = psb("tpb", [128, 1024], BF16)
            l0 = [psb(f"l0{q}", [128, 512]) for q in range(2)]
            sqp = [psb(f"sqp{q}", [128, 512]) for q in range(2)]
            Zp = psb("Zp", [128, 512])
            cnts = {"pj": 0, "sqp": 0, "l0": 0}
            banks = {"pj": pj, "sqp": sqp, "l0": l0}

            def nextb(nm):
                cnts[nm] += 1
                q = cnts[nm] % 2
                return banks[nm][q], f"{nm}{q}"

            def nextpj():
                return nextb("pj")

            def nextsq():
                return nextb("sqp")

            def nextl0():
                return nextb("l0")

            if pA or pSb:
                P.op("pool", lambda e: e.memset(fcar[:], 0.0), writes=["fcar"])
                P.op("pool", lambda e: e.memset(Hst[:], 0.0), writes=["Hst"])
                P.op("pool", lambda e: e.memset(Hb[:], 0.0), writes=["Hb"])
                P.op("pool", lambda e: e.memset(tS[:], 0.0), writes=["tS"])
                for q in range(NKV):
                    P.op("pool", lambda e, q=q: e.memset(Vaug[q][:], 1.0), writes=[f"Vaug{q}"])
            if pSb:
                P.op("pool", lambda e: e.memset(Vca[:], 1.0), writes=["Vca"])
                for q in range(2):
                    P.op("pool", lambda e, q=q: e.memset(PTx[q][:], 0.0), writes=[f"PTx{q}"])

            def is_own(i):
                return i >= NPRE

            def is_samp(i):
                return i == NPT

            def S1(i):
                b = i % len(xt)
                kx = f"xt{b}"
                src = xs[:, :] if is_samp(i) else xw[i * 128:(i + 1) * 128, :]
                P.op("sp", lambda e: e.dma_start(out=xt[b][:], in_=src), writes=[kx], dma=kx)
                s2 = i % 2
                hq = i % len(hb)
                P.op("act", lambda e: e.activation(out=hb[hq][:], in_=xt[b][:], func=AF.Square, accum_out=ss[s2][:]),
                     reads=[kx], writes=[f"hb{hq}", f"ss{s2}"])
                P.op("act", lambda e: e.activation(out=rs[s2][:], in_=ss[s2][:], func=AF.Sqrt, scale=1.0 / 1024, bias=1e-6),
                     reads=[f"ss{s2}"], writes=[f"rs{s2}"])
                P.op("dve", lambda e: e.reciprocal(out=rs[s2][:], in_=rs[s2][:]), reads=[f"rs{s2}"], writes=[f"rs{s2}"])
                P.op("dve", lambda e: e.scalar_tensor_tensor(out=hb[hq][:], in0=xt[b][:], scalar=rs[s2][:, 0:1], in1=gmix[:],
                                                             op0=ALU.mult, op1=ALU.mult),
                     reads=[kx, f"rs{s2}", "const2"], writes=[f"hb{hq}"])
                if i == NPT - 1 or is_samp(i):
                    P.op("dve", lambda e: e.scalar_tensor_tensor(out=h32s[:], in0=xt[b][:], scalar=rs[s2][:, 0:1], in1=gmix[:],
                                                                 op0=ALU.mult, op1=ALU.mult),
                         reads=[kx, f"rs{s2}", "const2"], writes=[h32k])
                    if is_samp(i):
                        P.op("sp", lambda e: [e.dma_start(out=shift_s[q:q + 1, :], in_=h32s[8 * q + 7:8 * q + 8, :]) for q in range(16)],
                             reads=[h32k], writes=["o_shift_s"], dma="o_shift_s", n=16)
                        outkeys.append("o_shift_s")
                    else:
                        P.op("sp", lambda e: e.dma_start(out=shift_p[:, :], in_=h32s[127:128, :]), reads=[h32k],
                             writes=["o_shift_p"], dma="o_shift_p")
                        outkeys.append("o_shift_p")

                def tr(e):
                    r = None
                    for kc in range(8):
                        r = e.transpose(out=tpb[:, kc * 128:(kc + 1) * 128], in_=hb[hq][:, kc * 128:(kc + 1) * 128], identity=ident[:])
                    return r
                P.op("pe", tr, reads=[f"hb{hq}", "const"], writes=["tpb"])
                P.op("act", lambda e: e.copy(out=hT[hq][:].rearrange("p a b -> p (a b)"), in_=tpb[:]), reads=["tpb"], writes=[f"hT{hq}"])

            def proj_group(i, chunks, evac):
                hq = i % len(hT)
                bank, bk = nextpj()

                def mm(e):
                    r = None
                    for gi, c in enumerate(chunks):
                        for kc in range(8):
                            r = e.matmul(bank[:, gi * 128:(gi + 1) * 128], lhsT=Win[:, kc, c * 128:(c + 1) * 128],
                                         rhs=hT[hq][:, kc, :], start=(kc == 0), stop=(kc == 7))
                    return r
                P.op("pe", mm, reads=["const", f"hT{hq}"], writes=[bk])
                evac(bank, bk)

            def S2(i):
                own = is_own(i)
                qq = i % len(qT)
                fq = i % len(fT)
                if own:
                    def ev_q(bank, bk):
                        P.op("act", lambda e: e.activation(out=qT[qq][:].rearrange("p a b -> p (a b)"), in_=bank[:], func=AF.Copy, scale=0.125),
                             reads=[bk], writes=[f"qT{qq}"])
                    proj_group(i, [0, 1, 2, 3], ev_q)
                if own or i == NPRE - 1:
                    k3 = i % NKV

                    def ev_kv(bank, bk):
                        P.op("act", lambda e: e.copy(out=kvT[k3][:].rearrange("p a b -> p (a b)"), in_=bank[:, 0:256]),
                             reads=[bk], writes=[f"kvT{k3}"])
                        if i == NPT - 1 or is_samp(i):
                            P.op("dve", lambda e: e.tensor_copy(out=kvx[:, 0:2, :].rearrange("p a b -> p (a b)"), in_=bank[:, 0:256]),
                                 reads=[bk, f"kvT{k3}"], writes=[kvxk])
                    proj_group(i, [4, 5], ev_kv)
                    if i == NPT - 1 or is_samp(i):
                        bank, bk = nextpj()

                        def trkv(e):
                            e.transpose(out=bank[:, 0:128], in_=kvx[:, 0, :], identity=ident32[:])
                            return e.transpose(out=bank[:, 128:256], in_=kvx[:, 1, :], identity=ident32[:])
                        P.op("pe", trkv, reads=[kvxk, "const2"], writes=[bk])
                        P.op("dve", lambda e: e.tensor_copy(out=kvx[:, 2:4, :].rearrange("p a b -> p (a b)"), in_=bank[:, 0:256]),
                             reads=[bk, kvxk], writes=[kvxk])
                        if is_samp(i):
                            def okv(e):
                                r = []
                                for q in range(16):
                                    r.append(e.dma_start(out=kwin_s[q, 120:128, :], in_=kvx[8 * q:8 * q + 8, 2, :]))
                                    r.append(e.dma_start(out=vwin_s[q, 120:128, :], in_=kvx[8 * q:8 * q + 8, 3, :]))
                                if not NO_D2D:
                                    r.append(e.dma_start(out=kwin_s[:, 0:120, :], in_=ck[:, 8:128, :]))
                                    r.append(e.dma_start(out=vwin_s[:, 0:120, :], in_=cv[:, 8:128, :]))
                                return r
                            P.op("sp", okv, reads=[kvxk], writes=["o_kv_s"], dma="o_kv_s", n=(32 if NO_D2D else 34))
                            outkeys.append("o_kv_s")
                        else:
                            def okv(e):
                                r = []
                                r.append(e.dma_start(out=kwin_p[:, :], in_=kvx[:, 2, :]))
                                r.append(e.dma_start(out=vwin_p[:, :], in_=kvx[:, 3, :]))
                                return r
                            P.op("sp", okv, reads=[kvxk], writes=["o_kv_p"], dma="o_kv_p", n=2)
                            outkeys.append("o_kv_p")
                for gi, chunks in enumerate([[6, 7, 8, 9], [10, 11, 12, 13], [14, 15, 16, 17], [18, 19]]):
                    def ev_f(bank, bk, gi=gi, chunks=chunks):
                        n = len(chunks) * 128
                        dst = fT[fq][:, gi * 4:gi * 4 + len(chunks), :].rearrange("p a b -> p (a b)")
                        if gi % 2 == 0:
                            P.op("act", lambda e: e.copy(out=dst, in_=bank[:, 0:n]), reads=[bk], writes=[f"fT{fq}_{gi}"])
                        else:
                            P.op("dve", lambda e: e.tensor_copy(out=dst, in_=bank[:, 0:n]), reads=[bk], writes=[f"fT{fq}_{gi}"])
                    proj_group(i, chunks, ev_f)
                if is_samp(i):
                    P.op("sp", lambda e: e.dma_start(out=hp32[:], in_=hprev[:, :]), writes=["hp32"], dma="hp32")
                    P.op("dve", lambda e: e.tensor_copy(out=hpb[:], in_=hp32[:]), reads=["hp32"], writes=["hpb"])

                    def trh(e):
                        r = None
                        for kc in range(8):
                            r = e.transpose(out=tpb[:, kc * 16:(kc + 1) * 16], in_=hpb[:, kc * 128:(kc + 1) * 128], identity=ident[0:16, 0:16])
                        return r
                    P.op("pe", trh, reads=["hpb", "const"], writes=["tpb"])
                    P.op("act", lambda e: e.copy(out=hpT[:].rearrange("p a b -> p (a b)"), in_=tpb[:, 0:128]), reads=["tpb"], writes=["hpT"])
                    for half in range(2):
                        bank, bk = nextpj()

                        def mmp(e, half=half, bank=bank):
                            r = None
                            for ci in range(7):
                                c = 6 + half * 7 + ci
                                for kc in range(8):
                                    r = e.matmul(bank[:, ci * 16:(ci + 1) * 16], lhsT=Win[:, kc, c * 128:(c + 1) * 128],
                                                 rhs=hpT[:, kc, :], start=(kc == 0), stop=(kc == 7))
                            return r
                        P.op("pe", mmp, reads=["const", "hpT"], writes=[bk])
                        P.op("act", lambda e, half=half, bank=bank: e.copy(
                            out=fprev[:, half * 7:(half + 1) * 7, :].rearrange("p a b -> p (a b)"), in_=bank[:, 0:112]),
                            reads=[bk], writes=[f"fprev{half}"])

            def S3(i):
                s3 = i % R2
                own = is_own(i)
                samp = is_samp(i)
                fq = i % len(fT)
                fk = [f"fT{fq}_{g}" for g in range(4)]
                f = fT[fq]
                if samp:
                    f4 = f[:, :, :].rearrange("p c (s t) -> p c s t", t=8)
                    x4 = xsT[:, :, :].rearrange("p c (s t) -> p c s t", t=8)
                    P.op("pool", lambda e: e.tensor_tensor(out=x4[:, :, :, 1:8], in0=f4[:, :, :, 0:7], in1=f4[:, :, :, 1:8], op=ALU.subtract),
                         reads=fk, writes=["xsT"])
                    P.op("pool", lambda e: e.tensor_tensor(out=x4[:, :, :, 0], in0=fprev[:, :, :], in1=f4[:, :, :, 0], op=ALU.subtract),
                         reads=fk, writes=["xsT0"])
                else:
                    P.op("pool", lambda e: e.tensor_tensor(out=xsT[:, :, 1:128], in0=f[:, :, 0:127], in1=f[:, :, 1:128], op=ALU.subtract),
                         reads=fk, writes=["xsT"])
                    P.op("pool", lambda e: e.tensor_tensor(out=xsT[:, :, 0], in0=fcar[:, :], in1=f[:, :, 0], op=ALU.subtract),
                         reads=fk + ["fcar"], writes=["xsT0"])
                    P.op("pool", lambda e: e.tensor_copy(out=fcar[:, :], in_=f[:, :, 127]), reads=fk, writes=["fcar"])
                mu_bc = mu[:, :].unsqueeze(2).broadcast_to([128, 14, 128])
                P.op("dve", lambda e: e.tensor_tensor(out=xsT[:], in0=xsT[:], in1=mu_bc, op=ALU.mult),
                     reads=["xsT", "xsT0", "const2"], writes=["xsT", "xsT0"])
                P.op("pool", lambda e: e.tensor_tensor(out=xsT[:], in0=xsT[:], in1=f[:], op=ALU.add),
                     reads=["xsT", "xsT0"] + fk, writes=["xsT", "xsT0"])
                XK = ["xsT", "xsT0"]
                P.op("act", lambda e: e.activation(out=twal[0:64, :], in_=xsT[0:64, 12, :], func=AF.Tanh), reads=XK, writes=["twal_a"])
                P.op("act", lambda e: e.copy(out=twal[64:128, :], in_=xsT[64:128, 12, :]), reads=XK, writes=["twal_b"])
                P.op("act", lambda e: e.activation(out=sg[:], in_=xsT[:, 13, :], func=AF.Sigmoid), reads=XK, writes=["sg"])
                bw, bwk = nextpj()

                def mmw(e):
                    r = None
                    for cc in range(4):
                        r = e.matmul(bw[:, cc * 128:(cc + 1) * 128], lhsT=w2a2[0:64, cc * 128:(cc + 1) * 128], rhs=twal[0:64, :],
                                     start=True, stop=True)
                    return r
                P.op("pe", mmw, reads=["const", "twal_a"], writes=[bwk])
                for cc in range(4):
                    P.op("act", lambda e, cc=cc: e.activation(out=eT[:, cc, :], in_=bw[:, cc * 128:(cc + 1) * 128], func=AF.Sigmoid,
                                                              bias=vec4[:, W0, cc:cc + 1]),
                         reads=[bwk, "const2"], writes=[f"eT{cc}"])
                ba, bak = nextpj()

                def mma(e):
                    r = None
                    for cc in range(4):
                        r = e.matmul(ba[:, cc * 128:(cc + 1) * 128], lhsT=w2a2[64:128, cc * 128:(cc + 1) * 128], rhs=twal[64:128, :],
                                     start=True, stop=True)
                    return r
                P.op("pe", mma, reads=["const", "twal_b"], writes=[bak])
                for cc in range(4):
                    P.op("act", lambda e, cc=cc: e.activation(out=asT[:, cc, :], in_=ba[:, cc * 128:(cc + 1) * 128], func=AF.Sigmoid,
                                                              bias=vec4[:, A0, cc:cc + 1]),
                         reads=[bak, "const2"], writes=[f"asT{cc}"])
                EK = [f"eT{c}" for c in range(4)]
                AK = [f"asT{c}" for c in range(4)]
                if own:
                    bg, bgk = nextpj()

                    def mmg(e):
                        r = None
                        for cc in range(4):
                            r = e.matmul(bg[:, cc * 128:(cc + 1) * 128], lhsT=g2b[:, cc * 128:(cc + 1) * 128], rhs=sg[:], start=True, stop=True)
                        return r
                    P.op("pe", mmg, reads=["const", "sg"], writes=[bgk])
                    gq = i % len(gT)
                    P.op("act", lambda e: e.copy(out=gT[gq][:].rearrange("p a b -> p (a b)"), in_=bg[:]), reads=[bgk], writes=[f"gT{gq}"])
                P.op("dve", lambda e: e.tensor_tensor(out=kkT[:], in0=xsT[:, 4:8, :], in1=v4bc(KK), op=ALU.mult), reads=XK + ["const2"], writes=["kkT"])
                P.op("dve", lambda e: e.tensor_tensor(out=sqb[:], in0=kkT[:], in1=kkT[:], op=ALU.mult), reads=["kkT"], writes=["sqb"])
                bs, bsk = nextpj()
                P.op("pe", lambda e: e.matmul(bs[:], lhsT=bones[:], rhs=sqb[:].rearrange("p a b -> p (a b)"), start=True, stop=True),
                     reads=["const", "sqb"], writes=[bsk])
                P.op("act", lambda e: e.activation(out=rn[:].rearrange("p a b -> p (a b)"), in_=bs[:], func=AF.Sqrt, bias=1e-12),
                     reads=[bsk], writes=["rn"])
                P.op("dve", lambda e: e.reciprocal(out=rn[:], in_=rn[:]), reads=["rn"], writes=["rn"])
                P.op("dve", lambda e: e.tensor_tensor(out=kkT[:], in0=kkT[:], in1=rn[:], op=ALU.mult), reads=["kkT", "rn"], writes=["kkT"])
                P.op("dve", lambda e: e.scalar_tensor_tensor(out=tmpA[:], in0=asT[:], scalar=-1.0, in1=v4bc(KA), op0=ALU.add, op1=ALU.mult),
                     reads=AK + ["const2"], writes=["tmpA"])
                P.op("dve", lambda e: e.scalar_tensor_tensor(out=kmT[:], in0=tmpA[:], scalar=1.0, in1=xsT[:, 4:8, :], op0=ALU.add, op1=ALU.mult),
                     reads=["tmpA"] + XK, writes=["kmT"])
                rm = rmask[:, 1 if samp else 0, :]
                for cc in range(4):
                    P.op("dve", lambda e, cc=cc: e.tensor_tensor_scan(out=cumE[:, cc, :], data0=rm, data1=eT[:, cc, :], initial=0.0,
                                                                      op0=ALU.mult, op1=ALU.add),
                         reads=[f"eT{cc}", "const2"], writes=[f"cumE{cc}"])
                CK = [f"cumE{c}" for c in range(4)]
                P.op("pool", lambda e: e.tensor_tensor(out=rn[:], in0=cumE[:], in1=eT[:], op=ALU.subtract), reads=CK + EK + ["rn"], writes=["rn"])
                P.op("act", lambda e: e.activation(out=rn[:], in_=rn[:], func=AF.Exp, scale=-C0), reads=["rn"], writes=["rn"])
                P.op("act", lambda e: e.activation(out=Eg[:], in_=cumE[:], func=AF.Exp, scale=-C0), reads=CK, writes=["Eg"])
                P.op("act", lambda e: e.activation(out=cumE[:], in_=cumE[:], func=AF.Exp, scale=C0), reads=CK, writes=CK)
                Egi, Egx = cumE, rn
                if samp:
                    P.op("pool", lambda e: e.tensor_copy(out=gC[s3][:, :, :], in_=Eg[:, :, :].rearrange("p c (s t) -> p c s t", t=8)[:, :, :, 7]),
                         reads=["Eg"], writes=[f"gC{s3}"])
                else:
                    P.op("pool", lambda e: e.tensor_copy(out=gC[s3][:, :, 0], in_=Eg[:, :, 127]), reads=["Eg"], writes=[f"gC{s3}"])
                ARk = f"AR{s3}"
                P.op("dve", lambda e: e.tensor_tensor(out=AR[s3][:, :, 1, :], in0=xsT[:, 0:4, :], in1=Eg[:], op=ALU.mult),
                     reads=XK + ["Eg"], writes=[ARk + "r"])
                P.op("dve", lambda e: e.tensor_tensor(out=KTF[s3][:], in0=kmT[:], in1=Egi[:], op=ALU.mult), reads=["kmT"] + CK, writes=[f"KTF{s3}"])
                P.op("pool", lambda e: e.tensor_tensor(out=tmpA[:], in0=kkT[:], in1=asT[:], op=ALU.mult), reads=["kkT"] + AK, writes=["tmpA"])
                P.op("dve", lambda e: e.tensor_tensor(out=BTF[s3][:], in0=tmpA[:], in1=Egi[:], op=ALU.mult), reads=["tmpA"] + CK, writes=[f"BTF{s3}"])
                P.op("dve", lambda e: e.scalar_tensor_tensor(out=AR[s3][:, :, 0, :], in0=kkT[:], scalar=-1.0, in1=Egx[:], op0=ALU.mult, op1=ALU.mult),
                     reads=["kkT", "rn"], writes=[ARk + "a"])
                P.op("act", lambda e: e.copy(out=vb[:], in_=xsT[:, 8:12, :]), reads=XK, writes=["vb"])
                if own:
                    P.op("pool", lambda e: e.tensor_tensor(out=rn[:], in0=xsT[:, 0:4, :], in1=kmT[:], op=ALU.mult), reads=XK + ["kmT", "rn"], writes=["rn"])
                    P.op("dve", lambda e: e.tensor_tensor(out=sqb[:], in0=rn[:], in1=v4bc(RK), op=ALU.mult), reads=["rn", "const2"], writes=["sqb"])
                    bb, bbk = nextpj()
                    P.op("pe", lambda e: e.matmul(bb[:], lhsT=bones[:], rhs=sqb[:].rearrange("p a b -> p (a b)"), start=True, stop=True),
                         reads=["const", "sqb"], writes=[bbk])
                    P.op("dve", lambda e: e.tensor_tensor(out=bonT[i % len(bonT)][:].rearrange("p a b -> p (a b)"), in0=bb[:],
                                                          in1=xsT[:, 8:12, :].rearrange("p a b -> p (a b)"), op=ALU.mult),
                         reads=[bbk] + XK, writes=[f"bonT{i % len(bonT)}"])

                if pA:
                    gbc = gC[s3][:, :, 0:1].broadcast_to([128, 4, 128])
                    P.op("dve", lambda e: e.tensor_tensor(out=BHF[:], in0=BTF[s3][:], in1=gbc, op=ALU.mult), reads=[f"BTF{s3}", f"gC{s3}"], writes=["BHF"])
                    P.op("dve", lambda e: e.tensor_tensor(out=KHF[:], in0=KTF[s3][:], in1=gbc, op=ALU.mult), reads=[f"KTF{s3}", f"gC{s3}"], writes=["KHF"])
                    bsrc, ksrc, bsk, ksk = BHF, KHF, "BHF", "KHF"
                else:
                    bsrc, ksrc, bsk, ksk = BTF[s3], KTF[s3], f"BTF{s3}", f"KTF{s3}"

                def tr1(e):
                    r = None
                    for cc in range(4):
                        r = e.transpose(out=tpb[:, cc * 128:(cc + 1) * 128], in_=vb[:, cc, :], identity=ident[:])
                    for cc in range(4):
                        r = e.transpose(out=tpb[:, 512 + cc * 128:512 + (cc + 1) * 128], in_=bsrc[:, cc, :], identity=ident[:])
                    return r
                P.op("pe", tr1, reads=["vb", bsk, "const"], writes=["tpb"])
                P.op("act", lambda e: e.copy(out=VTM[s3][:], in_=tpb[:, 0:512]), reads=["tpb"], writes=[f"VTM{s3}"])
                P.op("dve", lambda e: e.tensor_copy(out=BTM[s3][:], in_=tpb[:, 512:1024]), reads=["tpb"], writes=[f"BTM{s3}"])

                def tr2(e):
                    r = None
                    for cc in range(4):
                        r = e.transpose(out=tpb[:, cc * 128:(cc + 1) * 128], in_=ksrc[:, cc, :], identity=ident[:])
                    if pA:
                        for cc in range(4):
                            r = e.transpose(out=tpb[:, 512 + cc * 128:512 + (cc + 1) * 128], in_=AR[s3][:, cc, 0, :], identity=ident[:])
                    return r
                P.op("pe", tr2, reads=[ksk, f"AR{s3}a", "const"], writes=["tpb"])
                P.op("act", lambda e: e.copy(out=KTM[s3][:], in_=tpb[:, 0:512]), reads=["tpb"], writes=[f"KTM{s3}"])
                if pA:
                    P.op("dve", lambda e: e.tensor_copy(out=ATM[s3][:], in_=tpb[:, 512:1024]), reads=["tpb"], writes=[f"ATM{s3}"])

            def nlev(i):
                return 3 if is_samp(i) else 7

            def S4(i):
                s3 = i % R2
                kd = 1 if is_samp(i) else 0
                ARk = [f"AR{s3}a", f"AR{s3}r"]
                for h in range(8):
                    cc, pb = h // 2, (h % 2) * 64
                    bank, bk = nextl0()

                    def mm0(e, cc=cc, pb=pb, bank=bank):
                        e.matmul(bank[:, 0:256], lhsT=BTF[s3][pb:pb + 64, cc, :], rhs=AR[s3][pb:pb + 64, cc, :, :], start=True, stop=True)
                        return e.matmul(bank[:, 256:512], lhsT=KTF[s3][pb:pb + 64, cc, :], rhs=AR[s3][pb:pb + 64, cc, :, :], start=True, stop=True)
                    P.op("pe", mm0, reads=ARk + [f"BTF{s3}", f"KTF{s3}"], writes=[bk])
                    P.op("dve", lambda e, h=h, bank=bank: e.tensor_tensor(out=SQ0[:, h, 0:128], in0=bank[:, 0:128], in1=MKL[:, kd, 0, :], op=ALU.mult),
                         reads=[bk, "MKL"], writes=[f"SQ0n{h}"])
                    P.op("dve", lambda e, h=h, bank=bank: e.tensor_tensor(out=NM[:, h, :], in0=bank[:, 128:512],
                                                                          in1=MKL[:, kd, 1:4, :].rearrange("p a b -> p (a b)"), op=ALU.mult),
                         reads=[bk, "MKL"], writes=[f"NM{h}"])
                for g4 in range(2):
                    bank, bk = nextsq()

                    def mma0(e, g4=g4, bank=bank):
                        r = None
                        for hh in range(4):
                            h = g4 * 4 + hh
                            cc, pb = h // 2, (h % 2) * 64
                            r = e.matmul(bank[:, hh * 128:(hh + 1) * 128], lhsT=AR[s3][pb:pb + 64, cc, 0, :], rhs=BTF[s3][pb:pb + 64, cc, :],
                                         start=True, stop=True)
                        return r
                    P.op("pe", mma0, reads=ARk + [f"BTF{s3}"], writes=[bk])
                    slbc = MK[:, 2 + 3 * kd, :].unsqueeze(1).broadcast_to([128, 4, 128])
                    P.op("dve", lambda e, g4=g4, bank=bank, slbc=slbc: e.tensor_tensor(
                        out=SQ0[:, g4 * 4:(g4 + 1) * 4, 128:256], in0=bank[:].rearrange("p (a b) -> p a b", a=4), in1=slbc, op=ALU.mult),
                        reads=[bk, "const"], writes=[f"SQ0a{g4 * 4 + q}" for q in range(4)])
                nl = nlev(i)
                for j in range(nl - 1):
                    for pr in range(4):
                        bank, bk = nextsq()
                        hs = (2 * pr, 2 * pr + 1)

                        def Nsrc(h, j=j):
                            return SQ0[:, h, 0:128] if j == 0 else NJ[:, (j - 1) % 2, h, :]

                        def Asrc(h, j=j):
                            return SQ0[:, h, 128:256] if j == 0 else AJ[:, (j - 1) % 2, h, :]
                        rk = []
                        for h in hs:
                            rk += ([f"SQ0n{h}", f"SQ0a{h}"] if j == 0 else [f"NJ{(j - 1) % 2}_{h}", f"AJ{(j - 1) % 2}_{h}"])
                        last = (j == nl - 2)

                        def mmsq(e, hs=hs, bank=bank, Nsrc=Nsrc, Asrc=Asrc, last=last):
                            r = None
                            for q, h in enumerate(hs):
                                r = e.matmul(bank[:, q * 256:q * 256 + 128], lhsT=Asrc(h), rhs=Nsrc(h), start=True, stop=True)
                                if not last:
                                    r = e.matmul(bank[:, q * 256 + 128:q * 256 + 256], lhsT=Nsrc(h), rhs=Asrc(h), start=True, stop=True)
                            return r
                        P.op("pe", mmsq, reads=rk, writes=[bk])
                        b3 = bank[:].rearrange("p (a b) -> p a b", a=2)
                        P.op("act", lambda e, j=j, pr=pr, b3=b3: e.copy(out=NJ[:, j % 2, 2 * pr:2 * pr + 2, :], in_=b3[:, :, 0:128]),
                             reads=[bk], writes=[f"NJ{j % 2}_{h}" for h in hs])
                        if not last:
                            P.op("dve", lambda e, j=j, pr=pr, b3=b3: e.tensor_copy(out=AJ[:, j % 2, 2 * pr:2 * pr + 2, :], in_=b3[:, :, 128:256]),
                                 reads=[bk], writes=[f"AJ{j % 2}_{h}" for h in hs])

            def S5(i):
                s3 = i % R2
                own = is_own(i)
                samp = is_samp(i)
                nl = nlev(i)
                ARk = [f"AR{s3}a", f"AR{s3}r"]
                if samp:
                    sample_h0(s3)

                def z0(e):
                    r = None
                    for h in range(8):
                        cc, pb = h // 2, (h % 2) * 64
                        if samp:
                            r = e.matmul(Zp[:, h * 64:(h + 1) * 64], lhsT=zh[pb:pb + 64, cc, 0, :], rhs=ident[pb:pb + 64, pb:pb + 64],
                                         start=(h == 0), stop=False, skip_group_check=True)
                        else:
                            r = e.matmul(Zp[:, h * 64:(h + 1) * 64], lhsT=AR[s3][pb:pb + 64, cc, 0, :], rhs=Hb[pb:pb + 64, cc, :],
                                         start=(h == 0), stop=False, skip_group_check=True)
                        r = e.matmul(Zp[:, h * 64:(h + 1) * 64], lhsT=NM[:, h, 128:256], rhs=VTM[s3][:, h * 64:(h + 1) * 64],
                                     start=False, stop=False, skip_group_check=True)
                    return r
                P.op("pe", z0, reads=ARk + ["Hb", "zh", "const", f"VTM{s3}"] + [f"NM{h}" for h in range(8)], writes=["Zp"])
                for j in range(nl):
                    zb = Zb[j % 2]
                    zk = f"Zb{j % 2}"
                    if j % 2 == 0:
                        P.op("act", lambda e, zb=zb: e.copy(out=zb[:], in_=Zp[:]), reads=["Zp"], writes=[zk])
                    else:
                        P.op("dve", lambda e, zb=zb: e.tensor_copy(out=zb[:], in_=Zp[:]), reads=["Zp"], writes=[zk])
                    rk = [zk] + ([f"SQ0n{h}" for h in range(8)] if j == 0 else [f"NJ{(j - 1) % 2}_{h}" for h in range(8)])

                    def ap(e, j=j, zb=zb):
                        r = None
                        for h in range(8):
                            lt = SQ0[:, h, 0:128] if j == 0 else NJ[:, (j - 1) % 2, h, :]
                            r = e.matmul(Zp[:, h * 64:(h + 1) * 64], lhsT=lt, rhs=zb[:, h * 64:(h + 1) * 64], start=False, stop=(j == nl - 1),
                                         skip_group_check=True)
                        return r
                    P.op("pe", ap, reads=rk, writes=["Zp"])
                P.op("act", lambda e: e.copy(out=Ub[:], in_=Zp[:]), reads=["Zp"], writes=["Ub"])
                if own:
                    ob, obk = nextpj()
                    oq = i % R2

                    def mo(e):
                        r = None
                        for h in range(8):
                            cc, pb = h // 2, (h % 2) * 64
                            o = ob[:, h * 64:(h + 1) * 64]
                            if samp:
                                e.matmul(o, lhsT=zh[pb:pb + 64, cc, 1, :], rhs=ident[pb:pb + 64, pb:pb + 64], start=True, stop=False)
                            else:
                                e.matmul(o, lhsT=AR[s3][pb:pb + 64, cc, 1, :], rhs=Hb[pb:pb + 64, cc, :], start=True, stop=False)
                            e.matmul(o, lhsT=NM[:, h, 0:128], rhs=Ub[:, h * 64:(h + 1) * 64], start=False, stop=False)
                            r = e.matmul(o, lhsT=NM[:, h, 256:384], rhs=VTM[s3][:, h * 64:(h + 1) * 64], start=False, stop=True)
                        return r
                    P.op("pe", mo, reads=ARk + ["Hb", "zh", "const", "Ub", f"VTM{s3}"] + [f"NM{h}" for h in range(8)], writes=[obk])
                    P.op("act", lambda e: e.copy(out=OTM[oq][:].rearrange("p a b -> p (a b)"), in_=ob[:]), reads=[obk], writes=[f"OTM{oq}"])
                if samp:
                    sample_state(s3)
                    return

                def su(e):
                    r = None
                    for h in range(8):
                        cc, pb = h // 2, (h % 2) * 64
                        o = Zp[pb:pb + 64, cc * 64:(cc + 1) * 64]
                        e.matmul(o, lhsT=BTM[s3][:, h * 64:(h + 1) * 64], rhs=Ub[:, h * 64:(h + 1) * 64], start=True, stop=False)
                        r = e.matmul(o, lhsT=KTM[s3][:, h * 64:(h + 1) * 64], rhs=VTM[s3][:, h * 64:(h + 1) * 64], start=False, stop=True)
                    return r
                P.op("pe", su, reads=["Ub", f"BTM{s3}", f"KTM{s3}", f"VTM{s3}"], writes=["Zp"])
                P.op("dve", lambda e: e.tensor_tensor(out=tS[:].rearrange("p a b -> p (a b)"), in0=Zp[:, 0:256],
                                                      in1=Hst[:].rearrange("p a b -> p (a b)"), op=ALU.add),
                     reads=["Zp", "Hst"], writes=["tS"])
                P.op("dve", lambda e: e.tensor_tensor(out=Hst[:], in0=tS[:], in1=gC[s3][:, :, 0:1].broadcast_to([128, 4, 64]), op=ALU.mult),
                     reads=["tS", f"gC{s3}"], writes=["Hst"])
                P.op("act", lambda e: e.copy(out=Hb[:], in_=Hst[:]), reads=["Hst"], writes=["Hb"])
                if i == NPT - 1:
                    bank, bk = nextpj()

                    def trs(e):
                        r = None
                        for cc in range(4):
                            r = e.transpose(out=bank[0:64, cc * 128:(cc + 1) * 128], in_=Hst[:, cc, :], identity=ident32[:])
                        return r
                    P.op("pe", trs, reads=["Hst", "const2"], writes=[bk])
                    P.op("dve", lambda e: e.tensor_copy(out=osq[0:64, :, :].rearrange("p a b -> p (a b)"), in_=bank[0:64, :]), reads=[bk, "osq"], writes=["osq"])
                    P.op("sp", lambda e: e.dma_start(out=wkv_p.rearrange("h v k -> v h k"), in_=osq[0:64, :, :]), reads=["osq"], writes=["o_wkv_p"], dma="o_wkv_p")
                    outkeys.append("o_wkv_p")

            def S4h(i, half):
                s3 = i % R2
                own = is_own(i)
                ARk = [f"AR{s3}a", f"AR{s3}r"]
                hs4 = list(range(4 * half, 4 * half + 4))
                lb_, lbk = l0[half], f"l0{half}"
                sb_, sbk = sqp[half], f"sqp{half}"
                for h in hs4:
                    cc, pb = h // 2, (h % 2) * 64

                    def mm0(e, cc=cc, pb=pb):
                        e.matmul(lb_[:, 0:256], lhsT=BTF[s3][pb:pb + 64, cc, :], rhs=AR[s3][pb:pb + 64, cc, :, :], start=True, stop=True)
                        return e.matmul(lb_[:, 256:512], lhsT=KTF[s3][pb:pb + 64, cc, :], rhs=AR[s3][pb:pb + 64, cc, :, :], start=True, stop=True)
                    P.op("pe", mm0, reads=ARk + [f"BTF{s3}", f"KTF{s3}"], writes=[lbk])
                    P.op("dve", lambda e, h=h: e.tensor_tensor(out=L0S[:, h, :], in0=lb_[:],
                                                               in1=MKL[:, 0, :, :].rearrange("p a b -> p (a b)"), op=ALU.mult),
                         reads=[lbk, "MKL"], writes=[f"L0S{h}"])

                def mma0(e):
                    r = None
                    for hh, h in enumerate(hs4):
                        cc, pb = h // 2, (h % 2) * 64
                        r = e.matmul(sb_[:, hh * 128:(hh + 1) * 128], lhsT=AR[s3][pb:pb + 64, cc, 0, :], rhs=BTF[s3][pb:pb + 64, cc, :],
                                     start=True, stop=True)
                    return r
                P.op("pe", mma0, reads=ARk + [f"BTF{s3}"], writes=[sbk])
                slbc = MK[:, 2, :].unsqueeze(1).broadcast_to([128, 4, 128])
                P.op("dve", lambda e: e.tensor_tensor(out=A0S[:, 4 * half:4 * half + 4, :], in0=sb_[:].rearrange("p (a b) -> p a b", a=4), in1=slbc, op=ALU.mult),
                     reads=[sbk, "const"], writes=[f"A0S{h}" for h in hs4])

                def x0(e):
                    r = None
                    for hh, h in enumerate(hs4):
                        e.matmul(lb_[:, hh * 128:hh * 128 + 64], lhsT=ident[:], rhs=ATM[s3][:, h * 64:(h + 1) * 64],
                                 start=(hh == 0), stop=False, skip_group_check=True)
                        r = e.matmul(lb_[:, hh * 128 + 64:(hh + 1) * 128], lhsT=L0S[:, h, 256:384], rhs=VTM[s3][:, h * 64:(h + 1) * 64],
                                     start=False, stop=False, skip_group_check=True)
                    return r
                P.op("pe", x0, reads=["const", f"ATM{s3}", f"VTM{s3}"] + [f"L0S{h}" for h in hs4], writes=[lbk])
                for j in range(7):
                    xb = Xb[j % 2]
                    xk = f"Xb{j % 2}_{half}"
                    if half == 0:
                        P.op("act", lambda e, xb=xb: e.copy(out=xb[:, half, :], in_=lb_[:]), reads=[lbk], writes=[xk])
                    else:
                        P.op("dve", lambda e, xb=xb: e.tensor_copy(out=xb[:, half, :], in_=lb_[:]), reads=[lbk], writes=[xk])
                    if j < 6:
                        last = (j == 5)
                        for pr in (2 * half, 2 * half + 1):
                            hs = (2 * pr, 2 * pr + 1)

                            def Nsrc(h, j=j):
                                return L0S[:, h, 0:128] if j == 0 else NA[:, (j - 1) % 2, h, 0:128]

                            def Asrc(h, j=j):
                                return A0S[:, h, :] if j == 0 else NA[:, (j - 1) % 2, h, 128:256]
                            rk = []
                            for h in hs:
                                rk += ([f"L0S{h}", f"A0S{h}"] if j == 0 else [f"NA{(j - 1) % 2}_{h}"])

                            def mmsq(e, hs=hs, Nsrc=Nsrc, Asrc=Asrc, last=last):
                                r = None
                                for q, h in enumerate(hs):
                                    r = e.matmul(sb_[:, q * 256:q * 256 + 128], lhsT=Asrc(h), rhs=Nsrc(h), start=True, stop=True)
                                    if not last:
                                        r = e.matmul(sb_[:, q * 256 + 128:q * 256 + 256], lhsT=Nsrc(h), rhs=Asrc(h), start=True, stop=True)
                                return r
                            P.op("pe", mmsq, reads=rk, writes=[sbk])
                            dstna = NA[:, j % 2, 2 * pr:2 * pr + 2, :].rearrange("p a b -> p (a b)")
                            if pr % 2 == 0:
                                P.op("act", lambda e, dstna=dstna: e.copy(out=dstna, in_=sb_[:]), reads=[sbk], writes=[f"NA{j % 2}_{h}" for h in hs])
                            else:
                                P.op("dve", lambda e, dstna=dstna: e.tensor_copy(out=dstna, in_=sb_[:]), reads=[sbk], writes=[f"NA{j % 2}_{h}" for h in hs])
                    rk = [xk] + ([f"L0S{h}" for h in hs4] if j == 0 else [f"NA{(j - 1) % 2}_{h}" for h in hs4])

                    def ap(e, j=j, xb=xb):
                        r = None
                        for hh, h in enumerate(hs4):
                            lt = L0S[:, h, 0:128] if j == 0 else NA[:, (j - 1) % 2, h, 0:128]
                            r = e.matmul(lb_[:, hh * 128:(hh + 1) * 128], lhsT=lt, rhs=xb[:, half, hh * 128:(hh + 1) * 128],
                                         start=False, stop=(j == 6), skip_group_check=True)
                        return r
                    P.op("pe", ap, reads=rk + [lbk], writes=[lbk])
                wk = f"WY{half}"
                if half == 0:
                    P.op("act", lambda e: e.copy(out=WY[:, 0:4, :].rearrange("p a b -> p (a b)"), in_=lb_[:]), reads=[lbk], writes=[wk])
                else:
                    P.op("dve", lambda e: e.tensor_copy(out=WY[:, 4:8, :].rearrange("p a b -> p (a b)"), in_=lb_[:]), reads=[lbk], writes=[wk])

                def mmt(e):
                    r = None
                    for h in hs4:
                        cc, pb = h // 2, (h % 2) * 64
                        r = e.matmul(sb_[pb:pb + 64, cc * 64:(cc + 1) * 64], lhsT=WY[:, h, 0:64], rhs=BTM[s3][:, h * 64:(h + 1) * 64], start=True, stop=True)
                    return r
                P.op("pe", mmt, reads=[wk, f"BTM{s3}"], writes=[sbk])
                P.op("act", lambda e: e.copy(out=MTb[:, 2 * half:2 * half + 2, :].rearrange("p a b -> p (a b)"), in_=sb_[:, 128 * half:128 * half + 128]),
                     reads=[sbk], writes=[f"MTb{half}"])
                if own:
                    wb16 = sb_[:].bitcast(BF16)

                    def trw(e):
                        r = None
                        for h in hs4:
                            cc, pb = h // 2, (h % 2) * 64
                            r = e.transpose(out=wb16[pb:pb + 64, cc * 128:(cc + 1) * 128], in_=WY[:, h, 0:64], identity=ident[:])
                        return r
                    P.op("pe", trw, reads=[wk, "const"], writes=[sbk])
                    P.op("act", lambda e: e.copy(out=WTF[:, 2 * half:2 * half + 2, :].rearrange("p a b -> p (a b)"), in_=wb16[:, 256 * half:256 * half + 256]),
                         reads=[sbk], writes=[f"WTF{half}"])

            def S4tail(i):
                s3 = i % R2

                def gp(e):
                    r = None
                    for h in range(8):
                        cc, pb = h // 2, (h % 2) * 64
                        o = Zp[pb:pb + 64, cc * 64:(cc + 1) * 64]
                        e.matmul(o, lhsT=BTM[s3][:, h * 64:(h + 1) * 64], rhs=WY[:, h, 64:128], start=(h < 2), stop=False, skip_group_check=True)
                        r = e.matmul(o, lhsT=KTM[s3][:, h * 64:(h + 1) * 64], rhs=VTM[s3][:, h * 64:(h + 1) * 64], start=False, stop=False, skip_group_check=True)
                    return r
                P.op("pe", gp, reads=["WY0", "WY1", f"BTM{s3}", f"KTM{s3}", f"VTM{s3}"], writes=["Zp"])

            def S5n(i):
                s3 = i % R2
                own = is_own(i)
                ARk = [f"AR{s3}a", f"AR{s3}r"]
                WK = ["WY0", "WY1"]
                if own:
                    ub, ubk = nextpj()

                    def mu_(e):
                        r = None
                        for h in range(8):
                            cc, pb = h // 2, (h % 2) * 64
                            o = ub[:, h * 64:(h + 1) * 64]
                            e.matmul(o, lhsT=WTF[pb:pb + 64, cc, :], rhs=Hb[pb:pb + 64, cc, :], start=True, stop=False)
                            r = e.matmul(o, lhsT=ident[:], rhs=WY[:, h, 64:128], start=False, stop=True)
                        return r
                    P.op("pe", mu_, reads=WK + ["WTF0", "WTF1", "Hb", "const"], writes=[ubk])
                    P.op("act", lambda e: e.copy(out=Ub[:], in_=ub[:]), reads=[ubk], writes=["Ub"])
                    ob, obk = nextpj()
                    oq = i % R2

                    def mo(e):
                        r = None
                        for h in range(8):
                            cc, pb = h // 2, (h % 2) * 64
                            o = ob[:, h * 64:(h + 1) * 64]
                            e.matmul(o, lhsT=AR[s3][pb:pb + 64, cc, 1, :], rhs=Hb[pb:pb + 64, cc, :], start=True, stop=False)
                            e.matmul(o, lhsT=L0S[:, h, 128:256], rhs=Ub[:, h * 64:(h + 1) * 64], start=False, stop=False)
                            r = e.matmul(o, lhsT=L0S[:, h, 384:512], rhs=VTM[s3][:, h * 64:(h + 1) * 64], start=False, stop=True)
                        return r
                    P.op("pe", mo, reads=ARk + ["Hb", "Ub", f"VTM{s3}"] + [f"L0S{h}" for h in range(8)], writes=[obk])
                    P.op("act", lambda e: e.copy(out=OTM[oq][:].rearrange("p a b -> p (a b)"), in_=ob[:]), reads=[obk], writes=[f"OTM{oq}"])

                def ch(e):
                    r = None
                    for h in range(8):
                        cc, pb = h // 2, (h % 2) * 64
                        r = e.matmul(Zp[pb:pb + 64, cc * 64:(cc + 1) * 64], lhsT=MTb[pb:pb + 64, cc, :], rhs=Hb[pb:pb + 64, cc, :],
                                     start=False, stop=True, skip_group_check=True)
                    return r
                P.op("pe", ch, reads=["MTb0", "MTb1", "Hb", "Zp"], writes=["Zp"])
                P.op("dve", lambda e: e.tensor_tensor(out=Hst[:].rearrange("p a b -> p (a b)"), in0=Zp[:, 0:256],
                                                      in1=tS[:].rearrange("p a b -> p (a b)"), op=ALU.add),
                     reads=["Zp", "tS"], writes=["Hst"])
                P.op("act", lambda e: e.copy(out=Hb[:], in_=Hst[:]), reads=["Hst"], writes=["Hb"])
                if i + 1 < NPT:
                    n3 = (i + 1) % R2
                    P.op("pool", lambda e: e.tensor_tensor(out=tS[:], in0=Hst[:], in1=gC[n3][:, :, 0:1].broadcast_to([128, 4, 64]), op=ALU.mult),
                         reads=["Hst", f"gC{n3}"], writes=["tS"])
                if i == NPT - 1:
                    bank, bk = nextpj()

                    def trs(e):
                        r = None
                        for cc in range(4):
                            r = e.transpose(out=bank[0:64, cc * 128:(cc + 1) * 128], in_=Hst[:, cc, :], identity=ident32[:])
                        return r
                    P.op("pe", trs, reads=["Hst", "const2"], writes=[bk])
                    P.op("dve", lambda e: e.tensor_copy(out=osq[0:64, :, :].rearrange("p a b -> p (a b)"), in_=bank[0:64, :]), reads=[bk, "osq"], writes=["osq"])
                    P.op("sp", lambda e: e.dma_start(out=wkv_p.rearrange("h v k -> v h k"), in_=osq[0:64, :, :]), reads=["osq"], writes=["o_wkv_p"], dma="o_wkv_p")
                    outkeys.append("o_wkv_p")

            def sample_h0(s3):
                for q4 in range(4):
                    P.op("sp", lambda e, q4=q4: e.dma_start(out=S0q[:], in_=swkv[q4 * 4:(q4 + 1) * 4].rearrange("s h v k -> v (s h) k")),
                         writes=["S0q"], dma="S0q")
                    for sl_ in range(4):
                        s = q4 * 4 + sl_
                        bank, bk = nextpj()

                        def trs(e, sl_=sl_, bank=bank):
                            r = None
                            for cc in range(4):
                                r = e.transpose(out=bank[:, cc * 64:(cc + 1) * 64],
                                                in_=S0q[:, sl_ * 8 + 2 * cc:sl_ * 8 + 2 * cc + 2, :].rearrange("p a b -> p (a b)"),
                                                identity=ident32[0:64, 0:64])
                            return r
                        P.op("pe", trs, reads=["S0q", "const2"], writes=[bk])
                        P.op("dve", lambda e, s=s, bank=bank: e.tensor_copy(out=H0s[:, s, :, :].rearrange("p a b -> p (a b)"), in_=bank[:, 0:256]),
                             reads=[bk], writes=[f"H0s{s}"])
                        P.op("act", lambda e, s=s, bank=bank: e.copy(out=H0b[:, s, :, :].rearrange("p a b -> p (a b)"), in_=bank[:, 0:256]),
                             reads=[bk], writes=[f"H0b{s}"])
                for cc in range(4):
                    bank, bk = nextpj()

                    def mmz(e, cc=cc, bank=bank):
                        r = None
                        for hh in range(2):
                            pb = hh * 64
                            for s in range(16):
                                r = e.matmul(bank[pb:pb + 64, s * 16:s * 16 + 16],
                                             lhsT=H0b[pb:pb + 64, s, cc, :],
                                             rhs=AR[s3][pb:pb + 64, cc, :, s * 8:(s + 1) * 8], start=True, stop=True)
                        return r
                    P.op("pe", mmz, reads=[f"H0b{s}" for s in range(16)] + [f"AR{s3}a", f"AR{s3}r"], writes=[bk])
                    src = bank[:, 0:256].rearrange("p (s a t) -> p a s t", s=16, a=2)
                    for ar in range(2):
                        dst = zh[:, cc, ar, :].rearrange("p (s t) -> p s t", t=8)
                        P.op("dve", lambda e, src=src, dst=dst, ar=ar: e.tensor_copy(out=dst, in_=src[:, ar, :, :]), reads=[bk], writes=["zh"])

            def sample_state(s3):
                e16bc = e16[:, :].unsqueeze(2).broadcast_to([128, 16, 64])
                HK = [f"H0s{s}" for s in range(16)]
                for h in range(8):
                    cc, pb = h // 2, (h % 2) * 64
                    P.op("dve", lambda e, h=h: e.tensor_tensor(out=Xex[:, 0, :, :], in0=Ub[:, h * 64:(h + 1) * 64].unsqueeze(1).broadcast_to([128, 16, 64]),
                                                               in1=e16bc, op=ALU.mult),
                         reads=["Ub", "const"], writes=["Xex0"])
                    P.op("pool", lambda e, h=h: e.tensor_tensor(out=Xex[:, 1, :, :], in0=VTM[s3][:, h * 64:(h + 1) * 64].unsqueeze(1).broadcast_to([128, 16, 64]),
                                                                in1=e16bc, op=ALU.mult),
                         reads=[f"VTM{s3}", "const"], writes=["Xex1"])
                    for half in range(2):
                        bank, bk = nextl0()

                        def mms(e, h=h, half=half, bank=bank, pb=pb):
                            e.matmul(bank[pb:pb + 64, :], lhsT=BTM[s3][:, h * 64:(h + 1) * 64],
                                     rhs=Xex[:, 0, half * 8:(half + 1) * 8, :].rearrange("p a b -> p (a b)"), start=True, stop=False)
                            return e.matmul(bank[pb:pb + 64, :], lhsT=KTM[s3][:, h * 64:(h + 1) * 64],
                                            rhs=Xex[:, 1, half * 8:(half + 1) * 8, :].rearrange("p a b -> p (a b)"), start=False, stop=True)
                        P.op("pe", mms, reads=["Xex0", "Xex1", f"BTM{s3}", f"KTM{s3}"], writes=[bk])
                        P.op("dve", lambda e, half=half, cc=cc, bank=bank, pb=pb: e.tensor_tensor(
                            out=H0s[pb:pb + 64, half * 8:(half + 1) * 8, cc, :], in0=bank[pb:pb + 64, :].rearrange("p (s v) -> p s v", s=8),
                            in1=H0s[pb:pb + 64, half * 8:(half + 1) * 8, cc, :], op=ALU.add),
                            reads=[bk] + HK, writes=HK)
                for cc in range(4):
                    P.op("dve", lambda e, cc=cc: e.tensor_tensor(out=H0s[:, :, cc, :], in0=H0s[:, :, cc, :],
                                                                 in1=gC[s3][:, cc, :].unsqueeze(2).broadcast_to([128, 16, 64]), op=ALU.mult),
                         reads=HK + [f"gC{s3}"], writes=HK)
                for q4 in range(4):
                    for sl_ in range(4):
                        s = q4 * 4 + sl_
                        bank, bk = nextpj()

                        def trw(e, s=s, bank=bank):
                            r = None
                            for cc in range(4):
                                r = e.transpose(out=bank[0:64, cc * 128:(cc + 1) * 128], in_=H0s[:, s, cc, :], identity=ident32[:])
                            return r
                        P.op("pe", trw, reads=HK + ["const2"], writes=[bk])
                        if s % 2 == 0:
                            P.op("act", lambda e, sl_=sl_, bank=bank: e.copy(out=wso[:, sl_, :, :].rearrange("p a b -> p (a b)"), in_=bank[0:64, :]),
                                 reads=[bk], writes=[f"wso{sl_}"])
                        else:
                            P.op("dve", lambda e, sl_=sl_, bank=bank: e.tensor_copy(out=wso[:, sl_, :, :].rearrange("p a b -> p (a b)"), in_=bank[0:64, :]),
                                 reads=[bk], writes=[f"wso{sl_}"])
                    P.op("sp", lambda e, q4=q4: e.dma_start(out=wkv_s[q4 * 4:(q4 + 1) * 4].rearrange("s h v k -> v s h k"), in_=wso[:]),
                         reads=[f"wso{q}" for q in range(4)], writes=[f"o_wkv_s{q4}"] + [f"wso{q}" for q in range(4)], dma="o_wkv_s")
                    outkeys.append(f"o_wkv_s{q4}")

            def S6(i):
                qq = i % len(qT)
                s2 = i % R2
                samp = is_samp(i)
                kc3 = i % NKV
                kp3 = (i - 1) % NKV
                if i == NPRE:
                    P.op("pe", lambda e: e.transpose(out=tpb[:, 0:128], in_=kvT[kp3][:, 1, :], identity=ident[:]), reads=[f"kvT{kp3}", "const"], writes=["tpb"])
                    P.op("act", lambda e: e.copy(out=Vaug[kp3][:, :, 0:64], in_=tpb[:, 0:128].rearrange("p (g d) -> p g d", g=2)),
                         reads=["tpb"], writes=[f"Vaug{kp3}"])
                P.op("pe", lambda e: e.transpose(out=tpb[:, 0:128], in_=kvT[kc3][:, 1, :], identity=ident[:]), reads=[f"kvT{kc3}", "const"], writes=["tpb"])
                P.op("act", lambda e: e.copy(out=Vaug[kc3][:, :, 0:64], in_=tpb[:, 0:128].rearrange("p (g d) -> p g d", g=2)),
                     reads=["tpb"], writes=[f"Vaug{kc3}"])
                if samp:
                    sample_cache(qq)
                for g in range(2):
                    for kt in range(2):
                        if samp and kt == 0:
                            continue
                        bank, bk = nextl0()
                        kb = kvT[kp3] if kt == 0 else kvT[kc3]
                        kbk = f"kvT{kp3}" if kt == 0 else f"kvT{kc3}"
                        P.op("pe", lambda e, bank=bank, kb=kb, g=g: e.matmul(bank[:], lhsT=kb[g * 64:(g + 1) * 64, 0, :],
                                                                             rhs=qT[qq][g * 64:(g + 1) * 64, :, :], start=True, stop=True),
                             reads=[kbk, f"qT{qq}"], writes=[bk])
                        pt = PT[g * 2 + kt]
                        ptk = f"PT{g * 2 + kt}"
                        P.op("act", lambda e, bank=bank, pt=pt: e.activation(out=pt[:].rearrange("p a b -> p (a b)"), in_=bank[:], func=AF.Exp),
                             reads=[bk], writes=[ptk])
                        if kt == 0:
                            mi = 6 if i == NPRE else 2
                        else:
                            mi = 4 if samp else 1
                        mbc = MK[:, mi, :].unsqueeze(1).broadcast_to([128, 4, 128])
                        P.op("dve", lambda e, pt=pt, mbc=mbc: e.tensor_tensor(out=pt[:], in0=pt[:], in1=mbc, op=ALU.mult),
                             reads=[ptk, "const"], writes=[ptk])
                for g in range(2):
                    bank, bk = nextsq()
                    b3 = bank[:, 0:260].rearrange("p (a b) -> p a b", a=4)
                    for j in range(4):
                        h = g * 4 + j
                        if samp:
                            px = PTx[h % 2]
                            pxk = f"PTx{h % 2}"
                            P.op("dve", lambda e, g=g, j=j, px=px: [e.tensor_tensor(
                                out=px[:, s, s * 8:(s + 1) * 8], in0=PTc[:, g, s, j * 8:(j + 1) * 8], in1=MK[:, 2, 0:8], op=ALU.mult) for s in range(16)][-1],
                                reads=[f"PTc{g}", "const"], writes=[pxk])

                        def pv(e, g=g, j=j, b3=b3, h=h):
                            r = None
                            if samp:
                                for s in range(16):
                                    e.matmul(b3[:, j, :], lhsT=PTx[h % 2][:, s, :], rhs=Vca[:, s, g, :], start=(s == 0), stop=False)
                            else:
                                e.matmul(b3[:, j, :], lhsT=PT[g * 2][:, j, :], rhs=Vaug[kp3][:, g, :], start=True, stop=False)
                            r = e.matmul(b3[:, j, :], lhsT=PT[g * 2 + 1][:, j, :], rhs=Vaug[kc3][:, g, :], start=False, stop=True)
                            return r
                        rd = [f"PT{g * 2 + 1}", f"Vaug{kc3}"] + ([f"PTx{h % 2}", "Vca"] if samp else [f"PT{g * 2}", f"Vaug{kp3}"])
                        P.op("pe", pv, reads=rd, writes=[bk + f"_{j}"] + ([bk] if j == 0 else []))
                    bkj = [bk] + [bk + f"_{j}" for j in range(4)]
                    P.op("dve", lambda e, g=g, b3=b3: e.tensor_tensor(out=den[:, g * 4:(g + 1) * 4], in0=b3[:, :, 64], in1=esink[:, g * 4:(g + 1) * 4], op=ALU.add),
                         reads=bkj + ["esink"], writes=[f"den{g}"])
                    P.op("dve", lambda e, g=g: e.reciprocal(out=den[:, g * 4:(g + 1) * 4], in_=den[:, g * 4:(g + 1) * 4]), reads=[f"den{g}"], writes=[f"den{g}"])
                    P.op("dve", lambda e, g=g, b3=b3: e.tensor_tensor(out=attb[:, g * 4:(g + 1) * 4, :], in0=b3[:, :, 0:64],
                                                                      in1=den[:, g * 4:(g + 1) * 4].unsqueeze(2).broadcast_to([128, 4, 64]), op=ALU.mult),
                         reads=bkj + [f"den{g}"], writes=[f"attb{g}"])

                def tra(e):
                    r = None
                    for c in range(4):
                        r = e.transpose(out=tpb[:, c * 128:(c + 1) * 128], in_=attb[:, 2 * c:2 * c + 2, :].rearrange("p a b -> p (a b)"), identity=ident[:])
                    return r
                P.op("pe", tra, reads=["attb0", "attb1", "const"], writes=["tpb"])
                P.op("act", lambda e: e.copy(out=mixT[s2][:, 0:4, :].rearrange("p a b -> p (a b)"), in_=tpb[:, 0:512]), reads=["tpb"], writes=[f"mixT{s2}a"])

            def sample_cache(qq):
                P.op("pool", lambda e: e.dma_start(out=cstb[:], in_=ck.rearrange("s k d -> k s d")), writes=["cstb"], dma="cstb")
                for q in range(2):
                    def trc(e, q=q):
                        r = None
                        for ss_ in range(8):
                            r = e.transpose(out=tpb[:, ss_ * 128:(ss_ + 1) * 128], in_=cstb[:, q * 8 + ss_, :], identity=ident[:])
                        return r
                    P.op("pe", trc, reads=["cstb", "const"], writes=["tpb"])
                    P.op("act", lambda e, q=q: e.copy(out=KcT[:, q * 8:(q + 1) * 8, :].rearrange("p a b -> p (a b)"), in_=tpb[:]), reads=["tpb"], writes=[f"KcT{q}"])
                P.op("pool", lambda e: [e.dma_start(out=Vca[:, :, g, 0:64], in_=cv[:, :, g * 64:(g + 1) * 64].rearrange("s k d -> k s d")) for g in range(2)],
                     reads=["Vca"], writes=["Vca"], dma="Vca", n=2)
                for g in range(2):
                    bank, bk = nextl0()

                    def scc(e, g=g, bank=bank):
                        r = None
                        for s in range(16):
                            r = e.matmul(bank[:, s * 32:(s + 1) * 32], lhsT=KcT[g * 64:(g + 1) * 64, s, :],
                                         rhs=qT[qq][g * 64:(g + 1) * 64, :, s * 8:(s + 1) * 8], start=True, stop=True)
                        return r
                    P.op("pe", scc, reads=["KcT0", "KcT1", f"qT{qq}"], writes=[bk])
                    P.op("act", lambda e, g=g, bank=bank: e.activation(out=PTc[:, g, :, :].rearrange("p a b -> p (a b)"), in_=bank[:], func=AF.Exp),
                         reads=[bk], writes=[f"PTc{g}"])

            def S7(i):
                s2 = i % R2
                s3 = i % len(gT)
                j = i - NPRE
                o = OTM[i % R2]
                ok = f"OTM{i % R2}"
                P.op("dve", lambda e: e.tensor_reduce(out=st1[:], in_=o[:], axis=AX.X, op=ALU.add), reads=[ok], writes=["st1"])
                P.op("pool", lambda e: e.tensor_tensor(out=osq[:], in0=o[:], in1=o[:], op=ALU.mult), reads=[ok], writes=["osq"])
                P.op("dve", lambda e: e.tensor_reduce(out=st2[:], in_=osq[:], axis=AX.X, op=ALU.add), reads=["osq"], writes=["st2"])
                P.op("dve", lambda e: e.tensor_scalar(out=st1[:], in0=st1[:], scalar1=1.0 / 64, scalar2=None, op0=ALU.mult), reads=["st1"], writes=["st1"])
                P.op("dve", lambda e: e.tensor_tensor(out=st3[:], in0=st1[:], in1=st1[:], op=ALU.mult), reads=["st1"], writes=["st3"])
                P.op("dve", lambda e: e.scalar_tensor_tensor(out=st2[:], in0=st2[:], scalar=1.0 / 64, in1=st3[:], op0=ALU.mult, op1=ALU.subtract),
                     reads=["st2", "st3"], writes=["st2"])
                P.op("act", lambda e: e.activation(out=st2[:], in_=st2[:], func=AF.Sqrt, bias=64e-5), reads=["st2"], writes=["st2"])
                P.op("dve", lambda e: e.reciprocal(out=st2[:], in_=st2[:]), reads=["st2"], writes=["st2"])
                P.op("dve", lambda e: e.tensor_tensor(out=osq[:], in0=o[:], in1=st1[:, :].unsqueeze(2).broadcast_to([128, 8, 64]), op=ALU.subtract),
                     reads=[ok, "st1", "osq"], writes=["osq"])
                P.op("dve", lambda e: e.tensor_tensor(out=onb[:], in0=osq[:], in1=st2[:, :].unsqueeze(2).broadcast_to([128, 8, 64]), op=ALU.mult),
                     reads=["osq", "st2"], writes=["onb"])

                def tro(e):
                    r = None
                    for c in range(4):
                        r = e.transpose(out=tpb[:, c * 128:(c + 1) * 128], in_=onb[:, 2 * c:2 * c + 2, :].rearrange("p a b -> p (a b)"), identity=ident[:])
                    return r
                P.op("pe", tro, reads=["onb", "const"], writes=["tpb"])
                P.op("dve", lambda e: e.tensor_tensor(out=tmx[:], in0=tpb[:, 0:512].rearrange("p (a b) -> p a b", a=4), in1=v4bc(GNG), op=ALU.mult),
                     reads=["tpb", "const2", "tmx"], writes=["tmx"])
                P.op("pool", lambda e: e.tensor_tensor(out=tmx[:], in0=tmx[:], in1=v4bc(GNB), op=ALU.add), reads=["tmx", "const2"], writes=["tmx"])
                P.op("pool", lambda e: e.tensor_tensor(out=tmx[:], in0=tmx[:], in1=bonT[s3][:], op=ALU.add), reads=["tmx", f"bonT{s3}"], writes=["tmx"])
                P.op("dve", lambda e: e.tensor_tensor(out=mixT[s2][:, 4:8, :], in0=tmx[:], in1=gT[s3][:], op=ALU.mult),
                     reads=["tmx", f"gT{s3}"], writes=[f"mixT{s2}b"])
                xb = x1t[0]
                xk = "x1t0"
                src = xs[:, :] if is_samp(i) else xw[i * 128:(i + 1) * 128, :]
                P.op("sp", lambda e: e.dma_start(out=xb[:], in_=src), writes=[xk], dma=xk)
                for half in range(2):
                    bank, bk = nextpj()

                    def mo(e, half=half, bank=bank):
                        r = None
                        for kc in range(8):
                            r = e.matmul(bank[:], lhsT=mixT[s2][:, kc, :], rhs=Wout[:, kc, half * 512:(half + 1) * 512], start=(kc == 0), stop=(kc == 7))
                        return r
                    P.op("pe", mo, reads=[f"mixT{s2}a", f"mixT{s2}b", "const"], writes=[bk])
                    P.op("dve", lambda e, half=half, bank=bank: e.tensor_tensor(out=xb[:, half * 512:(half + 1) * 512], in0=bank[:],
                                                                                in1=xb[:, half * 512:(half + 1) * 512], op=ALU.add),
                         reads=[bk, xk], writes=[xk])
                P.op("sp", lambda e: e.dma_start(out=x1s[j * 128:(j + 1) * 128, :], in_=xb[:]), reads=[xk], writes=[f"x1s{j}", xk], dma=xk)
                outkeys.append(f"x1s{j}")

            if pA:
                def cap(fns):
                    P.cap = []
                    for fn, i in fns:
                        if 0 <= i < NPT:
                            fn(i)
                    out = P.cap
                    P.cap = None
                    return out

                for step in range(NPT + 6):
                    for fn, i in ((S7, step - 5), (S6, step - 4)):
                        if 0 <= i < NPT and is_own(i):
                            fn(i)
                    if 0 <= step - 4 < NPT:
                        S5n(step - 4)
                    lists = [cap([(lambda i: S4h(i, 0), step - 3)]), cap([(lambda i: S4h(i, 1), step - 3)]),
                             cap([(S3, step - 2), (S2, step - 1), (S1, step)])]
                    chs = [ILV_CH, ILV_CH, 1]
                    pos = [0, 0, 0]
                    while any(pos[q] < len(lists[q]) for q in range(3)):
                        cand = [q for q in range(3) if pos[q] < len(lists[q])]
                        q = min(cand, key=lambda q: pos[q] / len(lists[q]))
                        for _ in range(chs[q]):
                            if pos[q] < len(lists[q]):
                                o = lists[q][pos[q]]
                                P.op(o[0], o[1], reads=o[2], writes=o[3], dma=o[4], n=o[5])
                                pos[q] += 1
                    if 0 <= step - 3 < NPT:
                        S4tail(step - 3)
            elif pSa:
                S1(NPT)
                S2(NPT)
                outkeys.extend(["qT0", "kvT0", "fprev0", "fprev1"] + [f"fT0_{g}" for g in range(4)])
            else:
                for fn in (S3, S4, S5, S6, S7):
                    fn(NPT)
            P.op("sp", None, reads=list(outkeys))
            P.emit()
            build.stats[mode] = P.stats

    if "A" in PHASES:
        phase("A", None)
    with ExitStack() as stp:
        per = dict(
            fT=[stp.enter_context(nc.sbuf_tensor("fTs", [128, 14, 128], F32))],
            qT=[stp.enter_context(nc.sbuf_tensor("qTs", [128, 4, 128], BF16))],
            kvT=[stp.enter_context(nc.sbuf_tensor("kvTs", [128, 2, 128], BF16))],
            fprev=stp.enter_context(nc.sbuf_tensor("fprevs", [128, 14, 16], F32)),
        )
        if "Sa" in PHASES:
            phase("Sa", per)
        if "Sb" in PHASES:
            phase("Sb", per)

    if "B" not in PHASES:
        return nc
    with ExitStack() as st:
        def sb(name, shape, dt=F32):
            return st.enter_context(nc.sbuf_tensor(name, shape, dt))

        def psb(name, shape, dt=F32):
            return st.enter_context(nc.psum_tensor(name, shape, dt))
        P = Prog(nc, '_B')
        outk = []
        Wg = sb("Wg", [128, 8, D_FF], BF16)
        Wu = sb("Wu", [128, 8, D_FF], BF16)
        Wd = sb("Wd", [128, NFC, 1024], BF16)
        gffn = sb("gffn", [128, 1024])
        gfin = sb("gfin", [128, 1024])
        identb = sb("identb", [128, 128], BF16)

        P.op("pool", lambda e: e.dma_start(out=identb[:], in_=ident_d[:, :]), writes=["W"], dma="Wi")
        P.op("pool", lambda e: [e.dma_start(out=Wg[:, kc, :], in_=w_gate[kc * 128:(kc + 1) * 128, :]) for kc in range(8)],
             writes=["Wg"], dma="Wg", n=8)
        P.op("pool", lambda e: [e.dma_start(out=Wu[:, kc, :], in_=w_up[kc * 128:(kc + 1) * 128, :]) for kc in range(8)],
             writes=["Wu"], dma="Wu", n=8)
        P.op("pool", lambda e: [e.dma_start(out=Wd[:, fc, :], in_=w_down[fc * 128:(fc + 1) * 128, :]) for fc in range(NFC)],
             writes=["Wd"], dma="Wd", n=NFC)

        def cl(e):
            return [e.dma_start(out=gffn[:], in_=gvec[1:2, :].broadcast_to([128, 1024])),
                    e.dma_start(out=gfin[:], in_=gvec[2:3, :].broadcast_to([128, 1024]))]
        P.op("sp", cl, writes=["G"], dma="G", n=2)
        xg = sb("xg", [128, 4, 1024])
        junk2 = sb("junk2", [128, 1024], BF16)
        ub = sb("ub", [128, 1024], BF16)
        uT = sb("uT", [128, 8, 512], BF16)
        actT = sb("actT", [128, 11, 512], BF16)
        sgt = sb("sgt", [128, 512])
        ssb = sb("ssb", [128, 1])
        rsb = sb("rsb", [128, 1])
        yb = [sb(f"yb{q}", [128, 1024]) for q in range(2)]
        pg = [psb(f"pg{q}", [128, 512]) for q in range(2)]
        pu = [psb(f"pu{q}", [128, 512]) for q in range(2)]
        pd = [psb(f"pd{q}", [128, 512]) for q in range(2)]
        tpb2 = psb("tpb2", [128, 1024], BF16)
        cnt = [0]
        groups = [(0, 4), (4, 4), (8, 4), (12, 4), (16, 1)]
        for (t0, nt) in groups:
            N = nt * 128
            P.op("sp", lambda e, t0=t0, nt=nt: e.dma_start(out=xg[:, 0:nt, :], in_=x1s[t0 * 128:(t0 + nt) * 128, :].rearrange("(a p) d -> p a d", p=128)),
                 writes=["xg"], dma="xg")
            for a in range(nt):
                P.op("act", lambda e, a=a: e.activation(out=junk2[:], in_=xg[:, a, :], func=AF.Square, accum_out=ssb[:]), reads=["xg"], writes=["junk2", "ssb"])
                P.op("act", lambda e: e.activation(out=rsb[:], in_=ssb[:], func=AF.Sqrt, scale=1.0 / 1024, bias=1e-6), reads=["ssb"], writes=["rsb"])
                P.op("dve", lambda e: e.reciprocal(out=rsb[:], in_=rsb[:]), reads=["rsb"], writes=["rsb"])
                P.op("dve", lambda e, a=a: e.scalar_tensor_tensor(out=ub[:], in0=xg[:, a, :], scalar=rsb[:, 0:1], in1=gffn[:], op0=ALU.mult, op1=ALU.mult),
                     reads=["xg", "rsb", "G"], writes=["ub"])

                def tr(e):
                    r = None
                    for kc in range(8):
                        r = e.transpose(out=tpb2[:, kc * 128:(kc + 1) * 128], in_=ub[:, kc * 128:(kc + 1) * 128], identity=identb[:])
                    return r
                P.op("pe", tr, reads=["ub", "W"], writes=["tpb2"])
                P.op("act", lambda e, a=a: e.copy(out=uT[:, :, a * 128:(a + 1) * 128], in_=tpb2[:].rearrange("p (a b) -> p a b", a=8)),
                     reads=["tpb2"], writes=[f"uT{a}"])
            uk = [f"uT{a}" for a in range(nt)]
            for hf in range(2):
                for fi in range(11):
                    fc = hf * 11 + fi
                    cnt[0] += 1
                    b = cnt[0] % 2

                    def mg(e, fc=fc, b=b, N=N):
                        r = None
                        for kc in range(8):
                            r = e.matmul(pg[b][:, 0:N], lhsT=Wg[:, kc, fc * 128:(fc + 1) * 128], rhs=uT[:, kc, 0:N], start=(kc == 0), stop=(kc == 7))
                        return r
                    P.op("pe", mg, reads=["Wg"] + uk, writes=[f"pg{b}"])

                    def mu_(e, fc=fc, b=b, N=N):
                        r = None
                        for kc in range(8):
                            r = e.matmul(pu[b][:, 0:N], lhsT=Wu[:, kc, fc * 128:(fc + 1) * 128], rhs=uT[:, kc, 0:N], start=(kc == 0), stop=(kc == 7))
                        return r
                    P.op("pe", mu_, reads=["Wu"] + uk, writes=[f"pu{b}"])
                    P.op("act", lambda e, b=b, N=N: e.activation(out=sgt[:, 0:N], in_=pg[b][:, 0:N], func=AF.Silu), reads=[f"pg{b}"], writes=["sgt"])
                    P.op("dve", lambda e, b=b, N=N, fi=fi: e.tensor_tensor(out=actT[:, fi, 0:N], in0=pu[b][:, 0:N], in1=sgt[:, 0:N], op=ALU.mult),
                         reads=[f"pu{b}", "sgt"], writes=[f"actT{fi}"])
                ak = [f"actT{fi}" for fi in range(11)]
                for a in range(nt):
                    for half in range(2):
                        cnt[0] += 1
                        b = cnt[0] % 2

                        def md(e, a=a, half=half, b=b, hf=hf):
                            r = None
                            for fi in range(11):
                                r = e.matmul(pd[b][:], lhsT=actT[:, fi, a * 128:(a + 1) * 128], rhs=Wd[:, hf * 11 + fi, half * 512:(half + 1) * 512],
                                             start=(fi == 0), stop=(fi == 10))
                            return r
                        P.op("pe", md, reads=["Wd"] + ak, writes=[f"pd{b}"])
                        P.op("dve", lambda e, a=a, half=half, b=b: e.tensor_tensor(out=xg[:, a, half * 512:(half + 1) * 512], in0=pd[b][:],
                                                                                   in1=xg[:, a, half * 512:(half + 1) * 512], op=ALU.add),
                             reads=[f"pd{b}", "xg"], writes=["xg"])
            for a in range(nt):
                t = t0 + a
                y = yb[t % 2]
                yk = f"yb{t % 2}"
                P.op("act", lambda e, a=a: e.activation(out=junk2[:], in_=xg[:, a, :], func=AF.Square, accum_out=ssb[:]), reads=["xg"], writes=["junk2", "ssb"])
                P.op("act", lambda e: e.activation(out=rsb[:], in_=ssb[:], func=AF.Sqrt, scale=1.0 / 1024, bias=1e-6), reads=["ssb"], writes=["rsb"])
                P.op("dve", lambda e: e.reciprocal(out=rsb[:], in_=rsb[:]), reads=["rsb"], writes=["rsb"])
                P.op("dve", lambda e, a=a, y=y: e.scalar_tensor_tensor(out=y[:], in0=xg[:, a, :], scalar=rsb[:, 0:1], in1=gfin[:], op0=ALU.mult, op1=ALU.mult),
                     reads=["xg", "rsb", "G"], writes=[yk])
                dst = y_s[:, :] if t == 16 else y_p[t * 128:(t + 1) * 128, :]
                P.op("sp", lambda e, y=y, dst=dst: e.dma_start(out=dst, in_=y[:]), reads=[yk], writes=[f"oy{t}", yk], dma=yk)
                outk.append(f"oy{t}")
        P.op("sp", None, reads=outk)
        P.emit()
        build.stats["B"] = P.stats
    return nc


def _consts(p):
    s = np.arange(128)[:, None]
    t = np.arange(128)[None, :]
    su = (s < t).astype(np.float32)
    ui = (s <= t).astype(np.float32)
    sl = (s > t).astype(np.float32)
    same = ((s // 8) == (t // 8)).astype(np.float32)
    mfirst = sl if p > 0 else np.zeros_like(sl)
    masks = np.stack([su, ui, sl, su * same, ui * same, sl * same, mfirst], axis=1).reshape(128, 7 * 128)
    ident = np.eye(128, dtype=np.float32)
    bones = ((s // 64) == (t // 64)).astype(np.float32)
    rm = np.ones((128, 2, 128), np.float32)
    rm[:, 1, :] = (np.arange(128) % 8 != 0).astype(np.float32)[None, :]
    e16 = ((np.arange(128)[:, None] // 8) == np.arange(16)[None, :]).astype(np.float32)
    return dict(masks=np.ascontiguousarray(masks), ident=ident, bones=bones, rmask=rm.reshape(128, 256), e16=e16)


_NC = [None]


def kernel(x_prompt, x_sample, cache_k, cache_v, state_wkv, state_shift, g_mix, w_in, attn_sinks,
           rwkv_mu, w0, w2, a0, a2, g2, k_k, k_a, r_k, gn_g, gn_b, w_out, g_ffn, w_gate, w_up,
           w_down, g_final):
    f = lambda a: np.ascontiguousarray(np.asarray(a, dtype=np.float32))
    x_prompt, x_sample = f(x_prompt), f(x_sample)
    w_in0 = f(w_in)[0]
    qperm = np.concatenate([np.r_[j * 64:(j + 1) * 64, (4 + j) * 64:(5 + j) * 64] for j in range(4)])
    w_in_p = np.ascontiguousarray(np.concatenate([w_in0[:, qperm], w_in0[:, 512:]], axis=1))
    fm4 = lambda v: f(v).reshape(4, 128).T
    vec4 = np.ascontiguousarray(np.stack([fm4(w0[0]), fm4(a0[0]), fm4(k_k[0]), fm4(k_a[0]), fm4(f(r_k)[0].reshape(-1)),
                                          fm4(gn_g[0]), fm4(gn_b[0])], axis=1).reshape(128, 28))
    shared = dict(
        w_in=w_in_p, w_out=f(w_out)[0], w_gate=f(w_gate)[0], w_up=f(w_up)[0], w_down=f(w_down)[0],
        gvec=np.ascontiguousarray(np.stack([f(g_mix)[0], f(g_ffn)[0], f(g_final)], axis=0)),
        mu=np.ascontiguousarray(f(rwkv_mu)[0].reshape(14, 128).T),
        vec4=vec4,
        w2a2=np.ascontiguousarray(np.concatenate([f(w2)[0], f(a2)[0]], axis=0)),
        g2=f(g2)[0],
        sinks=f(attn_sinks)[0].reshape(1, 8),
    )
    in_maps = []
    for c in range(8):
        b, p = c // 4, c % 4
        xwin = np.zeros((NPT * 128, 1024), np.float32)
        nreal = (p + 1) * 2048
        xwin[NPT * 128 - nreal:] = x_prompt[b, 0:nreal]
        m = dict(shared)
        m.update(_consts(p))
        m.update(
            xw=xwin,
            xs=np.ascontiguousarray(x_sample[16 * c:16 * c + 16].reshape(128, 1024)),
            hprev=f(state_shift)[0, 16 * c:16 * c + 16],
            ck=np.ascontiguousarray(f(cache_k)[0, 16 * c:16 * c + 16].reshape(16, 128, 128)),
            cv=np.ascontiguousarray(f(cache_v)[0, 16 * c:16 * c + 16].reshape(16, 128, 128)),
            swkv=np.ascontiguousarray(f(state_wkv)[0, 16 * c:16 * c + 16]),
        )
        in_maps.append(m)
    if _NC[0] is None:
        _NC[0] = build()
    res = run_bass_kernel_spmd(_NC[0], in_maps, core_ids=list(range(8)))
    R = res.results
    y_prompt = np.stack([np.concatenate([R[b * 4 + p]["y_p"] for p in range(4)], axis=0) for b in range(2)], axis=0)
    y_sample = np.concatenate([R[c]["y_s"].reshape(16, 8, 1024) for c in range(8)], axis=0)
    kp = np.stack([R[b * 4 + 3]["kwin_p"].reshape(128, 2, 64) for b in range(2)], axis=0)[None]
    vp = np.stack([R[b * 4 + 3]["vwin_p"].reshape(128, 2, 64) for b in range(2)], axis=0)[None]
    sp = np.stack([R[b * 4 + 3]["wkv_p"] for b in range(2)], axis=0)[None]
    hp = np.stack([R[b * 4 + 3]["shift_p"].reshape(1024) for b in range(2)], axis=0)[None]
    ks = np.concatenate([R[c]["kwin_s"].reshape(16, 128, 2, 64) for c in range(8)], axis=0)[None]
    vs = np.concatenate([R[c]["vwin_s"].reshape(16, 128, 2, 64) for c in range(8)], axis=0)[None]
    ss_ = np.concatenate([R[c]["wkv_s"] for c in range(8)], axis=0)[None]
    hs = np.concatenate([R[c]["shift_s"] for c in range(8)], axis=0)[None]
    return tuple(np.ascontiguousarray(a.astype(np.float32)) for a in (y_prompt, y_sample, kp, vp, sp, hp, ks, vs, ss_, hs))
```

```python
import numpy as np
from contextlib import ExitStack
import concourse.bass as bass
import concourse.mybir as mybir
from concourse.bass_utils import run_bass_kernel_spmd
from concourse.alu_op_type import AluOpType as ALU

AF = mybir.ActivationFunctionType
AX = mybir.AxisListType
F32 = mybir.dt.float32
BF16 = mybir.dt.bfloat16

SAME_ENGINE_SYNC = True
SAME_ENGINE_RAW_ONLY = False
ENGS = ("pe", "act", "dve", "pool", "sp")
C0 = float(np.exp(-0.5))
NPRE = 48
NOWN = 16
NPT = NPRE + NOWN
NT = NPT + 1
D_FF = 2816
NFC = 22
ILV_CH = 2
ILV_MODE = 0
SKIPOPS = set()
NO_ILV = False
PSUM_PREFIXES = ("pj", "tpb", "l0", "sqp", "Zp", "pg", "pu", "pd")
OPLIMIT = 10 ** 9
NO_D2D = False
PHASES = {"A", "Sa", "Sb", "B"}


class Prog:
    def __init__(self, nc, tag=''):
        self.nc = nc
        self.tag = tag
        self.ins = []
        self.last_w = {}
        self.readers = {}

    def op(self, eng, fn, reads=(), writes=(), dma=None, n=1):
        if getattr(self, "cap", None) is not None:
            self.cap.append((eng, fn, list(reads), list(writes), dma, n))
            return -1
        idx = len(self.ins)
        if idx >= OPLIMIT:
            return idx
        writes = list(writes) + [k for k in reads if k.startswith(PSUM_PREFIXES) and k not in writes]
        deps = set()
        raw = set()
        for k in reads:
            if k in self.last_w:
                deps.add(self.last_w[k])
                raw.add(self.last_w[k])
        for k in writes:
            if k in self.last_w:
                deps.add(self.last_w[k])
            for r in self.readers.get(k, ()):
                deps.add(r)
        deps.discard(idx)
        if fn is None:
            writes = []
        self.ins.append(dict(eng=eng, fn=fn, deps=deps, raw=raw, dma=dma, used=False, n=n))
        for k in reads:
            self.readers.setdefault(k, []).append(idx)
        for k in writes:
            self.last_w[k] = idx
            self.readers[k] = []
        return idx

    def emit(self):
        nc = self.nc
        ins = self.ins
        for r in ins:
            if SAME_ENGINE_RAW_ONLY:
                r["deps"] = {d for d in r["deps"]
                             if not (ins[d]["eng"] == r["eng"] and ins[d]["dma"] is None and r["dma"] is None and d not in r["raw"])}
            for d in r["deps"]:
                ins[d]["used"] = True
        cnt = {e: 0 for e in ENGS}
        dmav = {}
        for r in ins:
            if r["dma"] is not None:
                r["sem"] = "dma_" + r["dma"]
                dmav[r["sem"]] = dmav.get(r["sem"], 0) + 16 * r["n"]
                r["val"] = dmav[r["sem"]]
            elif r["used"]:
                cnt[r["eng"]] += 1
                r["sem"] = "eng_" + r["eng"]
                r["val"] = cnt[r["eng"]]
            else:
                r["sem"] = None
                r["val"] = 0
        known = {e: {} for e in ENGS}
        for r in ins:
            e = r["eng"]
            kn = known[e]
            wd = {}
            for d in sorted(r["deps"]):
                rd = ins[d]
                s, v = rd["sem"], rd["val"]
                if rd["eng"] == e and rd["dma"] is None and not SAME_ENGINE_SYNC:
                    continue
                if kn.get(s, 0) >= v:
                    continue
                wd[s] = max(wd.get(s, 0), v)
                for s2, v2 in rd["clock"].items():
                    if kn.get(s2, 0) < v2:
                        kn[s2] = v2
            r["waits"] = sorted(wd.items())
            ck = dict(kn)
            if r["sem"] is not None:
                ck[r["sem"]] = r["val"]
            r["clock"] = ck
        semnames = sorted({r["sem"] for r in ins if r["sem"] is not None})
        self.stats = dict(n=len(ins), nsem=len(semnames),
                          nwaits=sum(len(r["waits"]) for r in ins),
                          per_eng={e: sum(1 for r in ins if r["eng"] == e) for e in ENGS})
        with ExitStack() as st:
            sems = {s: st.enter_context(nc.semaphore(s + self.tag)) for s in semnames}
            block = st.enter_context(nc.Block())
            reg = {"pe": block.tensor, "act": block.scalar, "dve": block.vector,
                   "pool": block.gpsimd, "sp": block.sync}

            def make(e):
                def body(eng):
                    for r in ins:
                        if r["eng"] != e:
                            continue
                        for s, v in r["waits"]:
                            eng.wait_ge(sems[s], v)
                        if r["fn"] is None:
                            continue
                        out = r["fn"](eng)
                        if r["dma"] is not None:
                            outs = out if isinstance(out, (list, tuple)) else [out]
                            assert len(outs) == r["n"], (len(outs), r["n"])
                            for o in outs:
                                o.then_inc(sems[r["sem"]], 16)
                        elif r["sem"] is not None:
                            o = out[-1] if isinstance(out, (list, tuple)) else out
                            o.then_inc(sems[r["sem"]], 1)
                return body

            for e in ENGS:
                reg[e](make(e))


def build():
    nc = bass.Bass("TRN2", target_bir_lowering=False)

    def din(name, shape):
        return nc.dram_tensor(name, shape, F32, kind="ExternalInput").ap()

    def dout(name, shape):
        return nc.dram_tensor(name, shape, F32, kind="ExternalOutput").ap()

    xw = din("xw", [NPT * 128, 1024])
    xs = din("xs", [128, 1024])
    hprev = din("hprev", [16, 1024])
    ck = din("ck", [16, 128, 128])
    cv = din("cv", [16, 128, 128])
    swkv = din("swkv", [16, 8, 64, 64])
    w_in = din("w_in", [1024, 2560])
    w_out = din("w_out", [1024, 1024])
    w_gate = din("w_gate", [1024, D_FF])
    w_up = din("w_up", [1024, D_FF])
    w_down = din("w_down", [D_FF, 1024])
    gvec = din("gvec", [3, 1024])
    mu_d = din("mu", [128, 14])
    vec4_d = din("vec4", [128, 7 * 4])
    w2a2_d = din("w2a2", [128, 512])
    g2_d = din("g2", [128, 512])
    sinks_d = din("sinks", [1, 8])
    masks_d = din("masks", [128, 7 * 128])
    ident_d = din("ident", [128, 128])
    bones_d = din("bones", [128, 128])
    rmask_d = din("rmask", [128, 256])
    e16_d = din("e16", [128, 16])

    y_p = dout("y_p", [NOWN * 128, 1024])
    y_s = dout("y_s", [128, 1024])
    kwin_p = dout("kwin_p", [128, 128])
    vwin_p = dout("vwin_p", [128, 128])
    wkv_p = dout("wkv_p", [8, 64, 64])
    shift_p = dout("shift_p", [1, 1024])
    kwin_s = dout("kwin_s", [16, 128, 128])
    vwin_s = dout("vwin_s", [16, 128, 128])
    wkv_s = dout("wkv_s", [16, 8, 64, 64])
    shift_s = dout("shift_s", [16, 1024])
    x1s = nc.dram_tensor("x1s", [17 * 128, 1024], F32, kind="Internal").ap()
    build.stats = {}

    W0, A0, KK, KA, RK, GNG, GNB = range(7)

    def phase(mode, per):
        pA, pSa, pSb = mode == "A", mode == "Sa", mode == "Sb"
        with ExitStack() as st:
            def sb(name, shape, dt=F32):
                return st.enter_context(nc.sbuf_tensor(name + "_" + mode, shape, dt))

            def psb(name, shape, dt=F32):
                return st.enter_context(nc.psum_tensor(name + "_" + mode, shape, dt))

            def rot(name, n, shape, dt=F32):
                return [sb(f"{name}{q}", shape, dt) for q in range(n)]
            P = Prog(nc, '_' + mode)
            outkeys = []
            if pA or pSa:
                Win = sb("Win", [128, 8, 2560], BF16)
            if pA or pSb:
                Wout = sb("Wout", [128, 8, 1024], BF16)
            gmix = sb("gmix", [128, 1024])
            mu = sb("mu", [128, 14])
            vec4 = sb("vec4", [128, 7, 4])
            w2a2 = sb("w2a2", [128, 512], BF16)
            g2b = sb("g2b", [128, 512], BF16)
            esink = sb("esink", [128, 8])
            MK = sb("MK", [128, 7, 128], BF16)
            MKL = sb("MKL", [128, 2, 4, 128], BF16)
            ident = sb("ident", [128, 128], BF16)
            ident32 = sb("ident32", [128, 128])
            bones = sb("bones", [128, 128], BF16)
            rmask = sb("rmask", [128, 2, 128])
            e16 = sb("e16", [128, 16], BF16)

            def cload(e):
                r = []
                r.append(e.dma_start(out=ident[:], in_=ident_d[:, :]))
                r.append(e.dma_start(out=w2a2[:], in_=w2a2_d[:, :]))
                r.append(e.dma_start(out=g2b[:], in_=g2_d[:, :]))
                r.append(e.dma_start(out=MK[:], in_=masks_d.rearrange("p (m t) -> p m t", m=7)))
                r.append(e.dma_start(out=bones[:], in_=bones_d[:, :]))
                r.append(e.dma_start(out=e16[:], in_=e16_d[:, :]))
                return r
            P.op("pool", cload, writes=["const"], dma="constb", n=6)
            if pA or pSa:
                P.op("pool", lambda e: [e.dma_start(out=Win[:, kc, :], in_=w_in[kc * 128:(kc + 1) * 128, :]) for kc in range(8)],
                     writes=["Win"], dma="Win", n=8)
            if pA or pSb:
                P.op("pool", lambda e: [e.dma_start(out=Wout[:, kc, :], in_=w_out[kc * 128:(kc + 1) * 128, :]) for kc in range(8)],
                     writes=["Wout"], dma="Wout", n=8)

            def cload2(e):
                r = []
                r.append(e.dma_start(out=gmix[:], in_=gvec[0:1, :].broadcast_to([128, 1024])))
                r.append(e.dma_start(out=mu[:], in_=mu_d[:, :]))
                r.append(e.dma_start(out=vec4[:], in_=vec4_d.rearrange("p (a b) -> p a b", a=7)))
                r.append(e.dma_start(out=esink[:], in_=sinks_d[0:1, :].broadcast_to([128, 8])))
                r.append(e.dma_start(out=ident32[:], in_=ident_d[:, :]))
                r.append(e.dma_start(out=rmask[:], in_=rmask_d.rearrange("p (a t) -> p a t", a=2)))
                return r
            P.op("sp", cload2, writes=["const2"], dma="constf", n=6)
            P.op("act", lambda e: e.activation(out=esink[:], in_=esink[:], func=AF.Exp),
                 reads=["const2"], writes=["esink"])

            def mkl(e):
                r = None
                for kd in range(2):
                    for q in range(4):
                        r = e.tensor_copy(out=MKL[:, kd, q, :], in_=MK[:, 3 * kd + (q % 2), :])
                return r
            P.op("pool", mkl, reads=["const"], writes=["MKL"])

            def v4bc(ix, n=128):
                return vec4[:, ix, :].unsqueeze(2).broadcast_to([128, 4, n])

            R2 = 2 if pA else 1
            if pA or pSa:
                xt = rot("xt", 1 if pA else 1, [128, 1024])
                ss = rot("ss", 2, [128, 1])
                rs = rot("rs", 2, [128, 1])
                hb = rot("hb", R2, [128, 1024], BF16)
                hT = rot("hT", R2, [128, 8, 128], BF16)
            if pA:
                fT = rot("fT", 2, [128, 14, 128])
                qT = rot("qT", 4, [128, 4, 128], BF16)
                kvT = rot("kvT", 5, [128, 2, 128], BF16)
            else:
                fT, qT, kvT, fprev = per["fT"], per["qT"], per["kvT"], per["fprev"]
            NKV = len(kvT)
            if pSa:
                h32s = sb("h32s", [128, 1024])
                kvx = sb("kvx", [128, 4, 128])
                hp32 = sb("hp32", [16, 1024])
                hpb = sb("hpb", [16, 1024], BF16)
                hpT = sb("hpT", [128, 8, 16], BF16)
            if pA or pSb:
                Vaug = rot("Vaug", NKV, [128, 2, 65], BF16)
                xsT = sb("xsT", [128, 14, 128])
                fcar = sb("fcar", [128, 14])
                twal = sb("twal", [128, 128], BF16)
                sg = sb("sg", [128, 128], BF16)
                eT = sb("eT", [128, 4, 128])
                asT = sb("asT", [128, 4, 128])
                kkT = sb("kkT", [128, 4, 128])
                sqb = sb("sqb", [128, 4, 128], BF16)
                rn = sb("rn", [128, 4, 128])
                tmpA = sb("tmpA", [128, 4, 128])
                kmT = sb("kmT", [128, 4, 128])
                cumE = sb("cumE", [128, 4, 128])
                Eg = sb("Eg", [128, 4, 128])
                vb = sb("vb", [128, 4, 128], BF16)
                AR = rot("AR", R2, [128, 4, 2, 128], BF16)
                BTF = rot("BTF", R2, [128, 4, 128], BF16)
                KTF = rot("KTF", R2, [128, 4, 128], BF16)
                VTM = rot("VTM", R2, [128, 512], BF16)
                BTM = rot("BTM", R2, [128, 512], BF16)
                KTM = rot("KTM", R2, [128, 512], BF16)
                gC = rot("gC", R2, [128, 4, 16])
                gT = rot("gT", 3 if pA else 1, [128, 4, 128], BF16)
                bonT = rot("bonT", 3 if pA else 1, [128, 4, 128], BF16)
                if pSb:
                    SQ0 = sb("SQ0", [128, 8, 256], BF16)
                    NM = sb("NM", [128, 8, 384], BF16)
                    NJ = sb("NJ", [128, 2, 8, 128], BF16)
                    AJ = sb("AJ", [128, 2, 8, 128], BF16)
                else:
                    L0S = sb("L0S", [128, 8, 512], BF16)
                    A0S = sb("A0S", [128, 8, 128], BF16)
                    NA = sb("NA", [128, 2, 8, 256], BF16)
                Zb = rot("Zb", 2, [128, 512], BF16)
                Ub = sb("Ub", [128, 512], BF16)
                Hst = sb("Hst", [128, 4, 64])
                Hb = sb("Hb", [128, 4, 64], BF16)
                tS = sb("tS", [128, 4, 64])
                OTM = rot("OTM", R2, [128, 8, 64])
                osq = sb("osq", [128, 8, 64])
                st1 = sb("st1", [128, 8])
                st2 = sb("st2", [128, 8])
                st3 = sb("st3", [128, 8])
                onb = sb("onb", [128, 8, 64], BF16)
                PT = rot("PT", 4, [128, 4, 128], BF16)
                den = sb("den", [128, 8])
                attb = sb("attb", [128, 8, 64], BF16)
                mixT = rot("mixT", R2, [128, 8, 128], BF16)
                tmx = sb("tmx", [128, 4, 128])
                x1t = rot("x1t", 1, [128, 1024])
                if pA:
                    ATM = rot("ATM", R2, [128, 512], BF16)
                    BHF = sb("BHF", [128, 4, 128], BF16)
                    KHF = sb("KHF", [128, 4, 128], BF16)
                    Xb = rot("Xb", 2, [128, 2, 512], BF16)
                    WY = sb("WY", [128, 8, 128], BF16)
                    MTb = sb("MTb", [128, 4, 64], BF16)
                    WTF = sb("WTF", [128, 4, 128], BF16)
            if pA:
                kvx = tmx
                h32s = x1t[0]
            if pSb:
                S0q = sb("S0q", [64, 32, 64])
                H0s = sb("H0s", [128, 16, 4, 64])
                H0b = sb("H0b", [128, 16, 4, 64], BF16)
                Xex = sb("Xex", [128, 2, 16, 64], BF16)
                zh = sb("zh", [128, 4, 2, 128], BF16)
                KcT = sb("KcT", [128, 16, 128], BF16)
                Vca = sb("Vca", [128, 16, 2, 65], BF16)
                cstb = sb("cstb", [128, 16, 128], BF16)
                PTc = sb("PTc", [128, 2, 16, 32], BF16)
                PTx = rot("PTx", 2, [128, 16, 128], BF16)
                wso = sb("wso", [64, 4, 8, 64])
            h32k = "x1t0" if pA else "h32s"
            kvxk = "tmx" if pA else "kvx"

            pj = [psb(f"pj{q}", [128, 512]) for q in range(2)]
            tpb = psb("tpb", [128, 1024], BF16)
            l0 = [psb(f"l0{q}", [128, 512]) for q in range(2)]
            sqp = [psb(f"sqp{q}", [128, 512]) for q in range(2)]
            Zp = psb("Zp", [128, 512])
            cnts = {"pj": 0, "sqp": 0, "l0": 0}
            banks = {"pj": pj, "sqp": sqp, "l0": l0}

            def nextb(nm):
                cnts[nm] += 1
                q = cnts[nm] % 2
                return banks[nm][q], f"{nm}{q}"

            def nextpj():
                return nextb("pj")

            def nextsq():
                return nextb("sqp")

            def nextl0():
                return nextb("l0")

            if pA or pSb:
                P.op("pool", lambda e: e.memset(fcar[:], 0.0), writes=["fcar"])
                P.op("pool", lambda e: e.memset(Hst[:], 0.0), writes=["Hst"])
                P.op("pool", lambda e: e.memset(Hb[:], 0.0), writes=["Hb"])
                P.op("pool", lambda e: e.memset(tS[:], 0.0), writes=["tS"])
                for q in range(NKV):
                    P.op("pool", lambda e, q=q: e.memset(Vaug[q][:], 1.0), writes=[f"Vaug{q}"])
            if pSb:
                P.op("pool", lambda e: e.memset(Vca[:], 1.0), writes=["Vca"])
                for q in range(2):
                    P.op("pool", lambda e, q=q: e.memset(PTx[q][:], 0.0), writes=[f"PTx{q}"])

            def is_own(i):
                return i >= NPRE

            def is_samp(i):
                return i == NPT

            def S1(i):
                b = i % len(xt)
                kx = f"xt{b}"
                src = xs[:, :] if is_samp(i) else xw[i * 128:(i + 1) * 128, :]
                P.op("sp", lambda e: e.dma_start(out=xt[b][:], in_=src), writes=[kx], dma=kx)
                s2 = i % 2
                hq = i % len(hb)
                P.op("act", lambda e: e.activation(out=hb[hq][:], in_=xt[b][:], func=AF.Square, accum_out=ss[s2][:]),
                     reads=[kx], writes=[f"hb{hq}", f"ss{s2}"])
                P.op("act", lambda e: e.activation(out=rs[s2][:], in_=ss[s2][:], func=AF.Sqrt, scale=1.0 / 1024, bias=1e-6),
                     reads=[f"ss{s2}"], writes=[f"rs{s2}"])
                P.op("dve", lambda e: e.reciprocal(out=rs[s2][:], in_=rs[s2][:]), reads=[f"rs{s2}"], writes=[f"rs{s2}"])
                P.op("dve", lambda e: e.scalar_tensor_tensor(out=hb[hq][:], in0=xt[b][:], scalar=rs[s2][:, 0:1], in1=gmix[:],
                                                             op0=ALU.mult, op1=ALU.mult),
                     reads=[kx, f"rs{s2}", "const2"], writes=[f"hb{hq}"])
                if i == NPT - 1 or is_samp(i):
                    P.op("dve", lambda e: e.scalar_tensor_tensor(out=h32s[:], in0=xt[b][:], scalar=rs[s2][:, 0:1], in1=gmix[:],
                                                                 op0=ALU.mult, op1=ALU.mult),
                         reads=[kx, f"rs{s2}", "const2"], writes=[h32k])
                    if is_samp(i):
                        P.op("sp", lambda e: [e.dma_start(out=shift_s[q:q + 1, :], in_=h32s[8 * q + 7:8 * q + 8, :]) for q in range(16)],
                             reads=[h32k], writes=["o_shift_s"], dma="o_shift_s", n=16)
                        outkeys.append("o_shift_s")
                    else:
                        P.op("sp", lambda e: e.dma_start(out=shift_p[:, :], in_=h32s[127:128, :]), reads=[h32k],
                             writes=["o_shift_p"], dma="o_shift_p")
                        outkeys.append("o_shift_p")

                def tr(e):
                    r = None
                    for kc in range(8):
                        r = e.transpose(out=tpb[:, kc * 128:(kc + 1) * 128], in_=hb[hq][:, kc * 128:(kc + 1) * 128], identity=ident[:])
                    return r
                P.op("pe", tr, reads=[f"hb{hq}", "const"], writes=["tpb"])
                P.op("act", lambda e: e.copy(out=hT[hq][:].rearrange("p a b -> p (a b)"), in_=tpb[:]), reads=["tpb"], writes=[f"hT{hq}"])

            def proj_group(i, chunks, evac):
                hq = i % len(hT)
                bank, bk = nextpj()

                def mm(e):
                    r = None
                    for gi, c in enumerate(chunks):
                        for kc in range(8):
                            r = e.matmul(bank[:, gi * 128:(gi + 1) * 128], lhsT=Win[:, kc, c * 128:(c + 1) * 128],
                                         rhs=hT[hq][:, kc, :], start=(kc == 0), stop=(kc == 7))
                    return r
                P.op("pe", mm, reads=["Win", f"hT{hq}"], writes=[bk])
                evac(bank, bk)

            def S2(i):
                own = is_own(i)
                qq = i % len(qT)
                fq = i % len(fT)
                if own:
                    def ev_q(bank, bk):
                        P.op("act", lambda e: e.activation(out=qT[qq][:].rearrange("p a b -> p (a b)"), in_=bank[:], func=AF.Copy, scale=0.125),
                             reads=[bk], writes=[f"qT{qq}"])
                    proj_group(i, [0, 1, 2, 3], ev_q)
                if own or i == NPRE - 1:
                    k3 = i % NKV

                    def ev_kv(bank, bk):
                        P.op("act", lambda e: e.copy(out=kvT[k3][:].rearrange("p a b -> p (a b)"), in_=bank[:, 0:256]),
                             reads=[bk], writes=[f"kvT{k3}"])
                        if i == NPT - 1 or is_samp(i):
                            P.op("dve", lambda e: e.tensor_copy(out=kvx[:, 0:2, :].rearrange("p a b -> p (a b)"), in_=bank[:, 0:256]),
                                 reads=[bk, f"kvT{k3}"], writes=[kvxk])
                    proj_group(i, [4, 5], ev_kv)
                    if i == NPT - 1 or is_samp(i):
                        bank, bk = nextpj()

                        def trkv(e):
                            e.transpose(out=bank[:, 0:128], in_=kvx[:, 0, :], identity=ident32[:])
                            return e.transpose(out=bank[:, 128:256], in_=kvx[:, 1, :], identity=ident32[:])
                        P.op("pe", trkv, reads=[kvxk, "const2"], writes=[bk])
                        P.op("dve", lambda e: e.tensor_copy(out=kvx[:, 2:4, :].rearrange("p a b -> p (a b)"), in_=bank[:, 0:256]),
                             reads=[bk, kvxk], writes=[kvxk])
                        if is_samp(i):
                            def okv(e):
                                r = []
                                for q in range(16):
                                    r.append(e.dma_start(out=kwin_s[q, 120:128, :], in_=kvx[8 * q:8 * q + 8, 2, :]))
                                    r.append(e.dma_start(out=vwin_s[q, 120:128, :], in_=kvx[8 * q:8 * q + 8, 3, :]))
                                if not NO_D2D:
                                    r.append(e.dma_start(out=kwin_s[:, 0:120, :], in_=ck[:, 8:128, :]))
                                    r.append(e.dma_start(out=vwin_s[:, 0:120, :], in_=cv[:, 8:128, :]))
                                return r
                            P.op("sp", okv, reads=[kvxk], writes=["o_kv_s"], dma="o_kv_s", n=(32 if NO_D2D else 34))
                            outkeys.append("o_kv_s")
                        else:
                            def okv(e):
                                r = []
                                r.append(e.dma_start(out=kwin_p[:, :], in_=kvx[:, 2, :]))
                                r.append(e.dma_start(out=vwin_p[:, :], in_=kvx[:, 3, :]))
                                return r
                            P.op("sp", okv, reads=[kvxk], writes=["o_kv_p"], dma="o_kv_p", n=2)
                            outkeys.append("o_kv_p")
                for gi, chunks in enumerate([[6, 7, 8, 9], [10, 11, 12, 13], [14, 15, 16, 17], [18, 19]]):
                    def ev_f(bank, bk, gi=gi, chunks=chunks):
                        n = len(chunks) * 128
                        dst = fT[fq][:, gi * 4:gi * 4 + len(chunks), :].rearrange("p a b -> p (a b)")
                        if gi % 2 == 0:
                            P.op("act", lambda e: e.copy(out=dst, in_=bank[:, 0:n]), reads=[bk], writes=[f"fT{fq}_{gi}"])
                        else:
                            P.op("dve", lambda e: e.tensor_copy(out=dst, in_=bank[:, 0:n]), reads=[bk], writes=[f"fT{fq}_{gi}"])
                    proj_group(i, chunks, ev_f)
                if is_samp(i):
                    P.op("sp", lambda e: e.dma_start(out=hp32[:], in_=hprev[:, :]), writes=["hp32"], dma="hp32")
                    P.op("dve", lambda e: e.tensor_copy(out=hpb[:], in_=hp32[:]), reads=["hp32"], writes=["hpb"])

                    def trh(e):
                        r = None
                        for kc in range(8):
                            r = e.transpose(out=tpb[:, kc * 16:(kc + 1) * 16], in_=hpb[:, kc * 128:(kc + 1) * 128], identity=ident[0:16, 0:16])
                        return r
                    P.op("pe", trh, reads=["hpb", "const"], writes=["tpb"])
                    P.op("act", lambda e: e.copy(out=hpT[:].rearrange("p a b -> p (a b)"), in_=tpb[:, 0:128]), reads=["tpb"], writes=["hpT"])
                    for half in range(2):
                        bank, bk = nextpj()

                        def mmp(e, half=half, bank=bank):
                            r = None
                            for ci in range(7):
                                c = 6 + half * 7 + ci
                                for kc in range(8):
                                    r = e.matmul(bank[:, ci * 16:(ci + 1) * 16], lhsT=Win[:, kc, c * 128:(c + 1) * 128],
                                                 rhs=hpT[:, kc, :], start=(kc == 0), stop=(kc == 7))
                            return r
                        P.op("pe", mmp, reads=["Win", "hpT"], writes=[bk])
                        P.op("act", lambda e, half=half, bank=bank: e.copy(
                            out=fprev[:, half * 7:(half + 1) * 7, :].rearrange("p a b -> p (a b)"), in_=bank[:, 0:112]),
                            reads=[bk], writes=[f"fprev{half}"])

            def S3(i):
                s3 = i % R2
                own = is_own(i)
                samp = is_samp(i)
                fq = i % len(fT)
                fk = [f"fT{fq}_{g}" for g in range(4)]
                f = fT[fq]
                if samp:
                    f4 = f[:, :, :].rearrange("p c (s t) -> p c s t", t=8)
                    x4 = xsT[:, :, :].rearrange("p c (s t) -> p c s t", t=8)
                    P.op("pool", lambda e: e.tensor_tensor(out=x4[:, :, :, 1:8], in0=f4[:, :, :, 0:7], in1=f4[:, :, :, 1:8], op=ALU.subtract),
                         reads=fk, writes=["xsT"])
                    P.op("pool", lambda e: e.tensor_tensor(out=x4[:, :, :, 0], in0=fprev[:, :, :], in1=f4[:, :, :, 0], op=ALU.subtract),
                         reads=fk, writes=["xsT0"])
                else:
                    P.op("pool", lambda e: e.tensor_tensor(out=xsT[:, :, 1:128], in0=f[:, :, 0:127], in1=f[:, :, 1:128], op=ALU.subtract),
                         reads=fk, writes=["xsT"])
                    P.op("pool", lambda e: e.tensor_tensor(out=xsT[:, :, 0], in0=fcar[:, :], in1=f[:, :, 0], op=ALU.subtract),
                         reads=fk + ["fcar"], writes=["xsT0"])
                    P.op("pool", lambda e: e.tensor_copy(out=fcar[:, :], in_=f[:, :, 127]), reads=fk, writes=["fcar"])
                mu_bc = mu[:, :].unsqueeze(2).broadcast_to([128, 14, 128])
                P.op("dve", lambda e: e.tensor_tensor(out=xsT[:], in0=xsT[:], in1=mu_bc, op=ALU.mult),
                     reads=["xsT", "xsT0", "const2"], writes=["xsT", "xsT0"])
                P.op("pool", lambda e: e.tensor_tensor(out=xsT[:], in0=xsT[:], in1=f[:], op=ALU.add),
                     reads=["xsT", "xsT0"] + fk, writes=["xsT", "xsT0"])
                XK = ["xsT", "xsT0"]
                P.op("act", lambda e: e.activation(out=twal[0:64, :], in_=xsT[0:64, 12, :], func=AF.Tanh), reads=XK, writes=["twal_a"])
                P.op("act", lambda e: e.copy(out=twal[64:128, :], in_=xsT[64:128, 12, :]), reads=XK, writes=["twal_b"])
                P.op("act", lambda e: e.activation(out=sg[:], in_=xsT[:, 13, :], func=AF.Sigmoid), reads=XK, writes=["sg"])
                bw, bwk = nextpj()

                def mmw(e):
                    r = None
                    for cc in range(4):
                        r = e.matmul(bw[:, cc * 128:(cc + 1) * 128], lhsT=w2a2[0:64, cc * 128:(cc + 1) * 128], rhs=twal[0:64, :],
                                     start=True, stop=True)
                    return r
                P.op("pe", mmw, reads=["const", "twal_a"], writes=[bwk])
                for cc in range(4):
                    P.op("act", lambda e, cc=cc: e.activation(out=eT[:, cc, :], in_=bw[:, cc * 128:(cc + 1) * 128], func=AF.Sigmoid,
                                                              bias=vec4[:, W0, cc:cc + 1]),
                         reads=[bwk, "const2"], writes=[f"eT{cc}"])
                ba, bak = nextpj()

                def mma(e):
                    r = None
                    for cc in range(4):
                        r = e.matmul(ba[:, cc * 128:(cc + 1) * 128], lhsT=w2a2[64:128, cc * 128:(cc + 1) * 128], rhs=twal[64:128, :],
                                     start=True, stop=True)
                    return r
                P.op("pe", mma, reads=["const", "twal_b"], writes=[bak])
                for cc in range(4):
                    P.op("act", lambda e, cc=cc: e.activation(out=asT[:, cc, :], in_=ba[:, cc * 128:(cc + 1) * 128], func=AF.Sigmoid,
                                                              bias=vec4[:, A0, cc:cc + 1]),
                         reads=[bak, "const2"], writes=[f"asT{cc}"])
                EK = [f"eT{c}" for c in range(4)]
                AK = [f"asT{c}" for c in range(4)]
                if own:
                    bg, bgk = nextpj()

                    def mmg(e):
                        r = None
                        for cc in range(4):
                            r = e.matmul(bg[:, cc * 128:(cc + 1) * 128], lhsT=g2b[:, cc * 128:(cc + 1) * 128], rhs=sg[:], start=True, stop=True)
                        return r
                    P.op("pe", mmg, reads=["const", "sg"], writes=[bgk])
                    gq = i % len(gT)
                    P.op("act", lambda e: e.copy(out=gT[gq][:].rearrange("p a b -> p (a b)"), in_=bg[:]), reads=[bgk], writes=[f"gT{gq}"])
                P.op("dve", lambda e: e.tensor_tensor(out=kkT[:], in0=xsT[:, 4:8, :], in1=v4bc(KK), op=ALU.mult), reads=XK + ["const2"], writes=["kkT"])
                P.op("dve", lambda e: e.tensor_tensor(out=sqb[:], in0=kkT[:], in1=kkT[:], op=ALU.mult), reads=["kkT"], writes=["sqb"])
                bs, bsk = nextpj()
                P.op("pe", lambda e: e.matmul(bs[:], lhsT=bones[:], rhs=sqb[:].rearrange("p a b -> p (a b)"), start=True, stop=True),
                     reads=["const", "sqb"], writes=[bsk])
                P.op("act", lambda e: e.activation(out=rn[:].rearrange("p a b -> p (a b)"), in_=bs[:], func=AF.Sqrt, bias=1e-12),
                     reads=[bsk], writes=["rn"])
                P.op("dve", lambda e: e.reciprocal(out=rn[:], in_=rn[:]), reads=["rn"], writes=["rn"])
                P.op("dve", lambda e: e.tensor_tensor(out=kkT[:], in0=kkT[:], in1=rn[:], op=ALU.mult), reads=["kkT", "rn"], writes=["kkT"])
                P.op("dve", lambda e: e.scalar_tensor_tensor(out=tmpA[:], in0=asT[:], scalar=-1.0, in1=v4bc(KA), op0=ALU.add, op1=ALU.mult),
                     reads=AK + ["const2"], writes=["tmpA"])
                P.op("dve", lambda e: e.scalar_tensor_tensor(out=kmT[:], in0=tmpA[:], scalar=1.0, in1=xsT[:, 4:8, :], op0=ALU.add, op1=ALU.mult),
                     reads=["tmpA"] + XK, writes=["kmT"])
                rm = rmask[:, 1 if samp else 0, :]
                for cc in range(4):
                    P.op("dve", lambda e, cc=cc: e.tensor_tensor_scan(out=cumE[:, cc, :], data0=rm, data1=eT[:, cc, :], initial=0.0,
                                                                      op0=ALU.mult, op1=ALU.add),
                         reads=[f"eT{cc}", "const2"], writes=[f"cumE{cc}"])
                CK = [f"cumE{c}" for c in range(4)]
                P.op("pool", lambda e: e.tensor_tensor(out=rn[:], in0=cumE[:], in1=eT[:], op=ALU.subtract), reads=CK + EK + ["rn"], writes=["rn"])
                P.op("act", lambda e: e.activation(out=rn[:], in_=rn[:], func=AF.Exp, scale=-C0), reads=["rn"], writes=["rn"])
                P.op("act", lambda e: e.activation(out=Eg[:], in_=cumE[:], func=AF.Exp, scale=-C0), reads=CK, writes=["Eg"])
                P.op("act", lambda e: e.activation(out=cumE[:], in_=cumE[:], func=AF.Exp, scale=C0), reads=CK, writes=CK)
                Egi, Egx = cumE, rn
                if samp:
                    P.op("pool", lambda e: e.tensor_copy(out=gC[s3][:, :, :], in_=Eg[:, :, :].rearrange("p c (s t) -> p c s t", t=8)[:, :, :, 7]),
                         reads=["Eg"], writes=[f"gC{s3}"])
                else:
                    P.op("pool", lambda e: e.tensor_copy(out=gC[s3][:, :, 0], in_=Eg[:, :, 127]), reads=["Eg"], writes=[f"gC{s3}"])
                ARk = f"AR{s3}"
                P.op("dve", lambda e: e.tensor_tensor(out=AR[s3][:, :, 1, :], in0=xsT[:, 0:4, :], in1=Eg[:], op=ALU.mult),
                     reads=XK + ["Eg"], writes=[ARk + "r"])
                P.op("dve", lambda e: e.tensor_tensor(out=KTF[s3][:], in0=kmT[:], in1=Egi[:], op=ALU.mult), reads=["kmT"] + CK, writes=[f"KTF{s3}"])
                P.op("pool", lambda e: e.tensor_tensor(out=tmpA[:], in0=kkT[:], in1=asT[:], op=ALU.mult), reads=["kkT"] + AK, writes=["tmpA"])
                P.op("dve", lambda e: e.tensor_tensor(out=BTF[s3][:], in0=tmpA[:], in1=Egi[:], op=ALU.mult), reads=["tmpA"] + CK, writes=[f"BTF{s3}"])
                P.op("dve", lambda e: e.scalar_tensor_tensor(out=AR[s3][:, :, 0, :], in0=kkT[:], scalar=-1.0, in1=Egx[:], op0=ALU.mult, op1=ALU.mult),
                     reads=["kkT", "rn"], writes=[ARk + "a"])
                P.op("act", lambda e: e.copy(out=vb[:], in_=xsT[:, 8:12, :]), reads=XK, writes=["vb"])
                if own:
                    P.op("pool", lambda e: e.tensor_tensor(out=rn[:], in0=xsT[:, 0:4, :], in1=kmT[:], op=ALU.mult), reads=XK + ["kmT", "rn"], writes=["rn"])
                    P.op("dve", lambda e: e.tensor_tensor(out=sqb[:], in0=rn[:], in1=v4bc(RK), op=ALU.mult), reads=["rn", "const2"], writes=["sqb"])
                    bb, bbk = nextpj()
                    P.op("pe", lambda e: e.matmul(bb[:], lhsT=bones[:], rhs=sqb[:].rearrange("p a b -> p (a b)"), start=True, stop=True),
                         reads=["const", "sqb"], writes=[bbk])
                    P.op("dve", lambda e: e.tensor_tensor(out=bonT[i % len(bonT)][:].rearrange("p a b -> p (a b)"), in0=bb[:],
                                                          in1=xsT[:, 8:12, :].rearrange("p a b -> p (a b)"), op=ALU.mult),
                         reads=[bbk] + XK, writes=[f"bonT{i % len(bonT)}"])

                if pA:
                    gbc = gC[s3][:, :, 0:1].broadcast_to([128, 4, 128])
                    P.op("dve", lambda e: e.tensor_tensor(out=BHF[:], in0=BTF[s3][:], in1=gbc, op=ALU.mult), reads=[f"BTF{s3}", f"gC{s3}"], writes=["BHF"])
                    P.op("dve", lambda e: e.tensor_tensor(out=KHF[:], in0=KTF[s3][:], in1=gbc, op=ALU.mult), reads=[f"KTF{s3}", f"gC{s3}"], writes=["KHF"])
                    bsrc, ksrc, bsk, ksk = BHF, KHF, "BHF", "KHF"
                else:
                    bsrc, ksrc, bsk, ksk = BTF[s3], KTF[s3], f"BTF{s3}", f"KTF{s3}"

                def tr1(e):
                    r = None
                    for cc in range(4):
                        r = e.transpose(out=tpb[:, cc * 128:(cc + 1) * 128], in_=vb[:, cc, :], identity=ident[:])
                    for cc in range(4):
                        r = e.transpose(out=tpb[:, 512 + cc * 128:512 + (cc + 1) * 128], in_=bsrc[:, cc, :], identity=ident[:])
                    return r
                P.op("pe", tr1, reads=["vb", bsk, "const"], writes=["tpb"])
                P.op("act", lambda e: e.copy(out=VTM[s3][:], in_=tpb[:, 0:512]), reads=["tpb"], writes=[f"VTM{s3}"])
                P.op("dve", lambda e: e.tensor_copy(out=BTM[s3][:], in_=tpb[:, 512:1024]), reads=["tpb"], writes=[f"BTM{s3}"])

                def tr2(e):
                    r = None
                    for cc in range(4):
                        r = e.transpose(out=tpb[:, cc * 128:(cc + 1) * 128], in_=ksrc[:, cc, :], identity=ident[:])
                    if pA:
                        for cc in range(4):
                            r = e.transpose(out=tpb[:, 512 + cc * 128:512 + (cc + 1) * 128], in_=AR[s3][:, cc, 0, :], identity=ident[:])
                    return r
                P.op("pe", tr2, reads=[ksk, f"AR{s3}a", "const"], writes=["tpb"])
                P.op("act", lambda e: e.copy(out=KTM[s3][:], in_=tpb[:, 0:512]), reads=["tpb"], writes=[f"KTM{s3}"])
                if pA:
                    P.op("dve", lambda e: e.tensor_copy(out=ATM[s3][:], in_=tpb[:, 512:1024]), reads=["tpb"], writes=[f"ATM{s3}"])

            def nlev(i):
                return 3 if is_samp(i) else 7

            def S4(i):
                s3 = i % R2
                kd = 1 if is_samp(i) else 0
                ARk = [f"AR{s3}a", f"AR{s3}r"]
                for h in range(8):
                    cc, pb = h // 2, (h % 2) * 64
                    bank, bk = nextl0()

                    def mm0(e, cc=cc, pb=pb, bank=bank):
                        e.matmul(bank[:, 0:256], lhsT=BTF[s3][pb:pb + 64, cc, :], rhs=AR[s3][pb:pb + 64, cc, :, :], start=True, stop=True)
                        return e.matmul(bank[:, 256:512], lhsT=KTF[s3][pb:pb + 64, cc, :], rhs=AR[s3][pb:pb + 64, cc, :, :], start=True, stop=True)
                    P.op("pe", mm0, reads=ARk + [f"BTF{s3}", f"KTF{s3}"], writes=[bk])
                    P.op("dve", lambda e, h=h, bank=bank: e.tensor_tensor(out=SQ0[:, h, 0:128], in0=bank[:, 0:128], in1=MKL[:, kd, 0, :], op=ALU.mult),
                         reads=[bk, "MKL"], writes=[f"SQ0n{h}"])
                    P.op("dve", lambda e, h=h, bank=bank: e.tensor_tensor(out=NM[:, h, :], in0=bank[:, 128:512],
                                                                          in1=MKL[:, kd, 1:4, :].rearrange("p a b -> p (a b)"), op=ALU.mult),
                         reads=[bk, "MKL"], writes=[f"NM{h}"])
                for g4 in range(2):
                    bank, bk = nextsq()

                    def mma0(e, g4=g4, bank=bank):
                        r = None
                        for hh in range(4):
                            h = g4 * 4 + hh
                            cc, pb = h // 2, (h % 2) * 64
                            r = e.matmul(bank[:, hh * 128:(hh + 1) * 128], lhsT=AR[s3][pb:pb + 64, cc, 0, :], rhs=BTF[s3][pb:pb + 64, cc, :],
                                         start=True, stop=True)
                        return r
                    P.op("pe", mma0, reads=ARk + [f"BTF{s3}"], writes=[bk])
                    slbc = MK[:, 2 + 3 * kd, :].unsqueeze(1).broadcast_to([128, 4, 128])
                    P.op("dve", lambda e, g4=g4, bank=bank, slbc=slbc: e.tensor_tensor(
                        out=SQ0[:, g4 * 4:(g4 + 1) * 4, 128:256], in0=bank[:].rearrange("p (a b) -> p a b", a=4), in1=slbc, op=ALU.mult),
                        reads=[bk, "const"], writes=[f"SQ0a{g4 * 4 + q}" for q in range(4)])
                nl = nlev(i)
                for j in range(nl - 1):
                    for pr in range(4):
                        bank, bk = nextsq()
                        hs = (2 * pr, 2 * pr + 1)

                        def Nsrc(h, j=j):
                            return SQ0[:, h, 0:128] if j == 0 else NJ[:, (j - 1) % 2, h, :]

                        def Asrc(h, j=j):
                            return SQ0[:, h, 128:256] if j == 0 else AJ[:, (j - 1) % 2, h, :]
                        rk = []
                        for h in hs:
                            rk += ([f"SQ0n{h}", f"SQ0a{h}"] if j == 0 else [f"NJ{(j - 1) % 2}_{h}", f"AJ{(j - 1) % 2}_{h}"])
                        last = (j == nl - 2)

                        def mmsq(e, hs=hs, bank=bank, Nsrc=Nsrc, Asrc=Asrc, last=last):
                            r = None
                            for q, h in enumerate(hs):
                                r = e.matmul(bank[:, q * 256:q * 256 + 128], lhsT=Asrc(h), rhs=Nsrc(h), start=True, stop=True)
                                if not last:
                                    r = e.matmul(bank[:, q * 256 + 128:q * 256 + 256], lhsT=Nsrc(h), rhs=Asrc(h), start=True, stop=True)
                            return r
                        P.op("pe", mmsq, reads=rk, writes=[bk])
                        b3 = bank[:].rearrange("p (a b) -> p a b", a=2)
                        P.op("act", lambda e, j=j, pr=pr, b3=b3: e.copy(out=NJ[:, j % 2, 2 * pr:2 * pr + 2, :], in_=b3[:, :, 0:128]),
                             reads=[bk], writes=[f"NJ{j % 2}_{h}" for h in hs])
                        if not last:
                            P.op("dve", lambda e, j=j, pr=pr, b3=b3: e.tensor_copy(out=AJ[:, j % 2, 2 * pr:2 * pr + 2, :], in_=b3[:, :, 128:256]),
                                 reads=[bk], writes=[f"AJ{j % 2}_{h}" for h in hs])

            def S5(i):
                s3 = i % R2
                own = is_own(i)
                samp = is_samp(i)
                nl = nlev(i)
                ARk = [f"AR{s3}a", f"AR{s3}r"]
                if samp:
                    sample_h0(s3)

                def z0(e):
                    r = None
                    for h in range(8):
                        cc, pb = h // 2, (h % 2) * 64
                        if samp:
                            r = e.matmul(Zp[:, h * 64:(h + 1) * 64], lhsT=zh[pb:pb + 64, cc, 0, :], rhs=ident[pb:pb + 64, pb:pb + 64],
                                         start=(h == 0), stop=False, skip_group_check=True)
                        else:
                            r = e.matmul(Zp[:, h * 64:(h + 1) * 64], lhsT=AR[s3][pb:pb + 64, cc, 0, :], rhs=Hb[pb:pb + 64, cc, :],
                                         start=(h == 0), stop=False, skip_group_check=True)
                        r = e.matmul(Zp[:, h * 64:(h + 1) * 64], lhsT=NM[:, h, 128:256], rhs=VTM[s3][:, h * 64:(h + 1) * 64],
                                     start=False, stop=False, skip_group_check=True)
                    return r
                P.op("pe", z0, reads=ARk + ["Hb", "zh", "const", f"VTM{s3}"] + [f"NM{h}" for h in range(8)], writes=["Zp"])
                for j in range(nl):
                    zb = Zb[j % 2]
                    zk = f"Zb{j % 2}"
                    if j % 2 == 0:
                        P.op("act", lambda e, zb=zb: e.copy(out=zb[:], in_=Zp[:]), reads=["Zp"], writes=[zk])
                    else:
                        P.op("dve", lambda e, zb=zb: e.tensor_copy(out=zb[:], in_=Zp[:]), reads=["Zp"], writes=[zk])
                    rk = [zk] + ([f"SQ0n{h}" for h in range(8)] if j == 0 else [f"NJ{(j - 1) % 2}_{h}" for h in range(8)])

                    def ap(e, j=j, zb=zb):
                        r = None
                        for h in range(8):
                            lt = SQ0[:, h, 0:128] if j == 0 else NJ[:, (j - 1) % 2, h, :]
                            r = e.matmul(Zp[:, h * 64:(h + 1) * 64], lhsT=lt, rhs=zb[:, h * 64:(h + 1) * 64], start=False, stop=(j == nl - 1),
                                         skip_group_check=True)
                        return r
                    P.op("pe", ap, reads=rk, writes=["Zp"])
                P.op("act", lambda e: e.copy(out=Ub[:], in_=Zp[:]), reads=["Zp"], writes=["Ub"])
                if own:
                    ob, obk = nextpj()
                    oq = i % R2

                    def mo(e):
                        r = None
                        for h in range(8):
                            cc, pb = h // 2, (h % 2) * 64
                            o = ob[:, h * 64:(h + 1) * 64]
                            if samp:
                                e.matmul(o, lhsT=zh[pb:pb + 64, cc, 1, :], rhs=ident[pb:pb + 64, pb:pb + 64], start=True, stop=False)
                            else:
                                e.matmul(o, lhsT=AR[s3][pb:pb + 64, cc, 1, :], rhs=Hb[pb:pb + 64, cc, :], start=True, stop=False)
                            e.matmul(o, lhsT=NM[:, h, 0:128], rhs=Ub[:, h * 64:(h + 1) * 64], start=False, stop=False)
                            r = e.matmul(o, lhsT=NM[:, h, 256:384], rhs=VTM[s3][:, h * 64:(h + 1) * 64], start=False, stop=True)
                        return r
                    P.op("pe", mo, reads=ARk + ["Hb", "zh", "const", "Ub", f"VTM{s3}"] + [f"NM{h}" for h in range(8)], writes=[obk])
                    P.op("act", lambda e: e.copy(out=OTM[oq][:].rearrange("p a b -> p (a b)"), in_=ob[:]), reads=[obk], writes=[f"OTM{oq}"])
                if samp:
                    sample_state(s3)
                    return

                def su(e):
                    r = None
                    for h in range(8):
                        cc, pb = h // 2, (h % 2) * 64
                        o = Zp[pb:pb + 64, cc * 64:(cc + 1) * 64]
                        e.matmul(o, lhsT=BTM[s3][:, h * 64:(h + 1) * 64], rhs=Ub[:, h * 64:(h + 1) * 64], start=True, stop=False)
                        r = e.matmul(o, lhsT=KTM[s3][:, h * 64:(h + 1) * 64], rhs=VTM[s3][:, h * 64:(h + 1) * 64], start=False, stop=True)
                    return r
                P.op("pe", su, reads=["Ub", f"BTM{s3}", f"KTM{s3}", f"VTM{s3}"], writes=["Zp"])
                P.op("dve", lambda e: e.tensor_tensor(out=tS[:].rearrange("p a b -> p (a b)"), in0=Zp[:, 0:256],
                                                      in1=Hst[:].rearrange("p a b -> p (a b)"), op=ALU.add),
                     reads=["Zp", "Hst"], writes=["tS"])
                P.op("dve", lambda e: e.tensor_tensor(out=Hst[:], in0=tS[:], in1=gC[s3][:, :, 0:1].broadcast_to([128, 4, 64]), op=ALU.mult),
                     reads=["tS", f"gC{s3}"], writes=["Hst"])
                P.op("act", lambda e: e.copy(out=Hb[:], in_=Hst[:]), reads=["Hst"], writes=["Hb"])
                if i == NPT - 1:
                    bank, bk = nextpj()

                    def trs(e):
                        r = None
                        for cc in range(4):
                            r = e.transpose(out=bank[0:64, cc * 128:(cc + 1) * 128], in_=Hst[:, cc, :], identity=ident32[:])
                        return r
                    P.op("pe", trs, reads=["Hst", "const2"], writes=[bk])
                    P.op("dve", lambda e: e.tensor_copy(out=osq[0:64, :, :].rearrange("p a b -> p (a b)"), in_=bank[0:64, :]), reads=[bk, "osq"], writes=["osq"])
                    P.op("sp", lambda e: e.dma_start(out=wkv_p.rearrange("h v k -> v h k"), in_=osq[0:64, :, :]), reads=["osq"], writes=["o_wkv_p"], dma="o_wkv_p")
                    outkeys.append("o_wkv_p")

            def S4h(i, half):
                s3 = i % R2
                own = is_own(i)
                ARk = [f"AR{s3}a", f"AR{s3}r"]
                hs4 = list(range(4 * half, 4 * half + 4))
                lb_, lbk = l0[half], f"l0{half}"
                sb_, sbk = sqp[half], f"sqp{half}"
                for h in hs4:
                    cc, pb = h // 2, (h % 2) * 64

                    def mm0(e, cc=cc, pb=pb):
                        e.matmul(lb_[:, 0:256], lhsT=BTF[s3][pb:pb + 64, cc, :], rhs=AR[s3][pb:pb + 64, cc, :, :], start=True, stop=True)
                        return e.matmul(lb_[:, 256:512], lhsT=KTF[s3][pb:pb + 64, cc, :], rhs=AR[s3][pb:pb + 64, cc, :, :], start=True, stop=True)
                    P.op("pe", mm0, reads=ARk + [f"BTF{s3}", f"KTF{s3}"], writes=[lbk])
                    P.op("dve", lambda e, h=h: e.tensor_tensor(out=L0S[:, h, :], in0=lb_[:],
                                                               in1=MKL[:, 0, :, :].rearrange("p a b -> p (a b)"), op=ALU.mult),
                         reads=[lbk, "MKL"], writes=[f"L0S{h}"])

                def mma0(e):
                    r = None
                    for hh, h in enumerate(hs4):
                        cc, pb = h // 2, (h % 2) * 64
                        r = e.matmul(sb_[:, hh * 128:(hh + 1) * 128], lhsT=AR[s3][pb:pb + 64, cc, 0, :], rhs=BTF[s3][pb:pb + 64, cc, :],
                                     start=True, stop=True)
                    return r
                P.op("pe", mma0, reads=ARk + [f"BTF{s3}"], writes=[sbk])
                slbc = MK[:, 2, :].unsqueeze(1).broadcast_to([128, 4, 128])
                P.op("dve", lambda e: e.tensor_tensor(out=A0S[:, 4 * half:4 * half + 4, :], in0=sb_[:].rearrange("p (a b) -> p a b", a=4), in1=slbc, op=ALU.mult),
                     reads=[sbk, "const"], writes=[f"A0S{h}" for h in hs4])

                def x0(e):
                    r = None
                    for hh, h in enumerate(hs4):
                        e.matmul(lb_[:, hh * 128:hh * 128 + 64], lhsT=ident[:], rhs=ATM[s3][:, h * 64:(h + 1) * 64],
                                 start=(hh == 0), stop=False, skip_group_check=True)
                        r = e.matmul(lb_[:, hh * 128 + 64:(hh + 1) * 128], lhsT=L0S[:, h, 256:384], rhs=VTM[s3][:, h * 64:(h + 1) * 64],
                                     start=False, stop=False, skip_group_check=True)
                    return r
                P.op("pe", x0, reads=["const", f"ATM{s3}", f"VTM{s3}"] + [f"L0S{h}" for h in hs4], writes=[lbk])
                for j in range(7):
                    xb = Xb[j % 2]
                    xk = f"Xb{j % 2}_{half}"
                    if half == 0:
                        P.op("act", lambda e, xb=xb: e.copy(out=xb[:, half, :], in_=lb_[:]), reads=[lbk], writes=[xk])
                    else:
                        P.op("dve", lambda e, xb=xb: e.tensor_copy(out=xb[:, half, :], in_=lb_[:]), reads=[lbk], writes=[xk])
                    if j < 6:
                        last = (j == 5)
                        for pr in (2 * half, 2 * half + 1):
                            hs = (2 * pr, 2 * pr + 1)

                            def Nsrc(h, j=j):
                                return L0S[:, h, 0:128] if j == 0 else NA[:, (j - 1) % 2, h, 0:128]

                            def Asrc(h, j=j):
                                return A0S[:, h, :] if j == 0 else NA[:, (j - 1) % 2, h, 128:256]
                            rk = []
                            for h in hs:
                                rk += ([f"L0S{h}", f"A0S{h}"] if j == 0 else [f"NA{(j - 1) % 2}_{h}"])

                            def mmsq(e, hs=hs, Nsrc=Nsrc, Asrc=Asrc, last=last):
                                r = None
                                for q, h in enumerate(hs):
                                    r = e.matmul(sb_[:, q * 256:q * 256 + 128], lhsT=Asrc(h), rhs=Nsrc(h), start=True, stop=True)
                                    if not last:
                                        r = e.matmul(sb_[:, q * 256 + 128:q * 256 + 256], lhsT=Nsrc(h), rhs=Asrc(h), start=True, stop=True)
                                return r
                            P.op("pe", mmsq, reads=rk, writes=[sbk])
                            dstna = NA[:, j % 2, 2 * pr:2 * pr + 2, :].rearrange("p a b -> p (a b)")
                            if pr % 2 == 0:
                                P.op("act", lambda e, dstna=dstna: e.copy(out=dstna, in_=sb_[:]), reads=[sbk], writes=[f"NA{j % 2}_{h}" for h in hs])
                            else:
                                P.op("dve", lambda e, dstna=dstna: e.tensor_copy(out=dstna, in_=sb_[:]), reads=[sbk], writes=[f"NA{j % 2}_{h}" for h in hs])
                    rk = [xk] + ([f"L0S{h}" for h in hs4] if j == 0 else [f"NA{(j - 1) % 2}_{h}" for h in hs4])

                    def ap(e, j=j, xb=xb):
                        r = None
                        for hh, h in enumerate(hs4):
                            lt = L0S[:, h, 0:128] if j == 0 else NA[:, (j - 1) % 2, h, 0:128]
                            r = e.matmul(lb_[:, hh * 128:(hh + 1) * 128], lhsT=lt, rhs=xb[:, half, hh * 128:(hh + 1) * 128],
                                         start=False, stop=(j == 6), skip_group_check=True)
                        return r
                    P.op("pe", ap, reads=rk + [lbk], writes=[lbk])
                wk = f"WY{half}"
                if half == 0:
                    P.op("act", lambda e: e.copy(out=WY[:, 0:4, :].rearrange("p a b -> p (a b)"), in_=lb_[:]), reads=[lbk], writes=[wk])
                else:
                    P.op("dve", lambda e: e.tensor_copy(out=WY[:, 4:8, :].rearrange("p a b -> p (a b)"), in_=lb_[:]), reads=[lbk], writes=[wk])

                def mmt(e):
                    r = None
                    for h in hs4:
                        cc, pb = h // 2, (h % 2) * 64
                        r = e.matmul(sb_[pb:pb + 64, cc * 64:(cc + 1) * 64], lhsT=WY[:, h, 0:64], rhs=BTM[s3][:, h * 64:(h + 1) * 64], start=True, stop=True)
                    return r
                P.op("pe", mmt, reads=[wk, f"BTM{s3}"], writes=[sbk])
                P.op("act", lambda e: e.copy(out=MTb[:, 2 * half:2 * half + 2, :].rearrange("p a b -> p (a b)"), in_=sb_[:, 128 * half:128 * half + 128]),
                     reads=[sbk], writes=[f"MTb{half}"])
                if own:
                    wb16 = sb_[:].bitcast(BF16)

                    def trw(e):
                        r = None
                        for h in hs4:
                            cc, pb = h // 2, (h % 2) * 64
                            r = e.transpose(out=wb16[pb:pb + 64, cc * 128:(cc + 1) * 128], in_=WY[:, h, 0:64], identity=ident[:])
                        return r
                    P.op("pe", trw, reads=[wk, "const"], writes=[sbk])
                    P.op("act", lambda e: e.copy(out=WTF[:, 2 * half:2 * half + 2, :].rearrange("p a b -> p (a b)"), in_=wb16[:, 256 * half:256 * half + 256]),
                         reads=[sbk], writes=[f"WTF{half}"])

            def S4tail(i):
                s3 = i % R2

                def gp(e):
                    r = None
                    for h in range(8):
                        cc, pb = h // 2, (h % 2) * 64
                        o = Zp[pb:pb + 64, cc * 64:(cc + 1) * 64]
                        e.matmul(o, lhsT=BTM[s3][:, h * 64:(h + 1) * 64], rhs=WY[:, h, 64:128], start=(h < 2), stop=False, skip_group_check=True)
                        r = e.matmul(o, lhsT=KTM[s3][:, h * 64:(h + 1) * 64], rhs=VTM[s3][:, h * 64:(h + 1) * 64], start=False, stop=False, skip_group_check=True)
                    return r
                P.op("pe", gp, reads=["WY0", "WY1", f"BTM{s3}", f"KTM{s3}", f"VTM{s3}"], writes=["Zp"])

            def S5n(i):
                s3 = i % R2
                own = is_own(i)
                ARk = [f"AR{s3}a", f"AR{s3}r"]
                WK = ["WY0", "WY1"]
                if own:
                    ub, ubk = nextpj()

                    def mu_(e):
                        r = None
                        for h in range(8):
                            cc, pb = h // 2, (h % 2) * 64
                            o = ub[:, h * 64:(h + 1) * 64]
                            e.matmul(o, lhsT=WTF[pb:pb + 64, cc, :], rhs=Hb[pb:pb + 64, cc, :], start=True, stop=False)
                            r = e.matmul(o, lhsT=ident[:], rhs=WY[:, h, 64:128], start=False, stop=True)
                        return r
                    P.op("pe", mu_, reads=WK + ["WTF0", "WTF1", "Hb", "const"], writes=[ubk])
                    P.op("act", lambda e: e.copy(out=Ub[:], in_=ub[:]), reads=[ubk], writes=["Ub"])
                    ob, obk = nextpj()
                    oq = i % R2

                    def mo(e):
                        r = None
                        for h in range(8):
                            cc, pb = h // 2, (h % 2) * 64
                            o = ob[:, h * 64:(h + 1) * 64]
                            e.matmul(o, lhsT=AR[s3][pb:pb + 64, cc, 1, :], rhs=Hb[pb:pb + 64, cc, :], start=True, stop=False)
                            e.matmul(o, lhsT=L0S[:, h, 128:256], rhs=Ub[:, h * 64:(h + 1) * 64], start=False, stop=False)
                            r = e.matmul(o, lhsT=L0S[:, h, 384:512], rhs=VTM[s3][:, h * 64:(h + 1) * 64], start=False, stop=True)
                        return r
                    P.op("pe", mo, reads=ARk + ["Hb", "Ub", f"VTM{s3}"] + [f"L0S{h}" for h in range(8)], writes=[obk])
                    P.op("act", lambda e: e.copy(out=OTM[oq][:].rearrange("p a b -> p (a b)"), in_=ob[:]), reads=[obk], writes=[f"OTM{oq}"])

                def ch(e):
                    r = None
                    for h in range(8):
                        cc, pb = h // 2, (h % 2) * 64
                        r = e.matmul(Zp[pb:pb + 64, cc * 64:(cc + 1) * 64], lhsT=MTb[pb:pb + 64, cc, :], rhs=Hb[pb:pb + 64, cc, :],
                                     start=False, stop=True, skip_group_check=True)
                    return r
                P.op("pe", ch, reads=["MTb0", "MTb1", "Hb", "Zp"], writes=["Zp"])
                P.op("dve", lambda e: e.tensor_tensor(out=Hst[:].rearrange("p a b -> p (a b)"), in0=Zp[:, 0:256],
                                                      in1=tS[:].rearrange("p a b -> p (a b)"), op=ALU.add),
                     reads=["Zp", "tS"], writes=["Hst"])
                P.op("act", lambda e: e.copy(out=Hb[:], in_=Hst[:]), reads=["Hst"], writes=["Hb"])
                if i + 1 < NPT:
                    n3 = (i + 1) % R2
                    P.op("pool", lambda e: e.tensor_tensor(out=tS[:], in0=Hst[:], in1=gC[n3][:, :, 0:1].broadcast_to([128, 4, 64]), op=ALU.mult),
                         reads=["Hst", f"gC{n3}"], writes=["tS"])
                if i == NPT - 1:
                    bank, bk = nextpj()

                    def trs(e):
                        r = None
                        for cc in range(4):
                            r = e.transpose(out=bank[0:64, cc * 128:(cc + 1) * 128], in_=Hst[:, cc, :], identity=ident32[:])
                        return r
                    P.op("pe", trs, reads=["Hst", "const2"], writes=[bk])
                    P.op("dve", lambda e: e.tensor_copy(out=osq[0:64, :, :].rearrange("p a b -> p (a b)"), in_=bank[0:64, :]), reads=[bk, "osq"], writes=["osq"])
                    P.op("sp", lambda e: e.dma_start(out=wkv_p.rearrange("h v k -> v h k"), in_=osq[0:64, :, :]), reads=["osq"], writes=["o_wkv_p"], dma="o_wkv_p")
                    outkeys.append("o_wkv_p")

            def sample_h0(s3):
                for q4 in range(4):
                    P.op("sp", lambda e, q4=q4: e.dma_start(out=S0q[:], in_=swkv[q4 * 4:(q4 + 1) * 4].rearrange("s h v k -> v (s h) k")),
                         writes=["S0q"], dma="S0q")
                    for sl_ in range(4):
                        s = q4 * 4 + sl_
                        bank, bk = nextpj()

                        def trs(e, sl_=sl_, bank=bank):
                            r = None
                            for cc in range(4):
                                r = e.transpose(out=bank[:, cc * 64:(cc + 1) * 64],
                                                in_=S0q[:, sl_ * 8 + 2 * cc:sl_ * 8 + 2 * cc + 2, :].rearrange("p a b -> p (a b)"),
                                                identity=ident32[0:64, 0:64])
                            return r
                        P.op("pe", trs, reads=["S0q", "const2"], writes=[bk])
                        P.op("dve", lambda e, s=s, bank=bank: e.tensor_copy(out=H0s[:, s, :, :].rearrange("p a b -> p (a b)"), in_=bank[:, 0:256]),
                             reads=[bk], writes=[f"H0s{s}"])
                        P.op("act", lambda e, s=s, bank=bank: e.copy(out=H0b[:, s, :, :].rearrange("p a b -> p (a b)"), in_=bank[:, 0:256]),
                             reads=[bk], writes=[f"H0b{s}"])
                for cc in range(4):
                    bank, bk = nextpj()

                    def mmz(e, cc=cc, bank=bank):
                        r = None
                        for hh in range(2):
                            pb = hh * 64
                            for s in range(16):
                                r = e.matmul(bank[pb:pb + 64, s * 16:s * 16 + 16],
                                             lhsT=H0b[pb:pb + 64, s, cc, :],
                                             rhs=AR[s3][pb:pb + 64, cc, :, s * 8:(s + 1) * 8], start=True, stop=True)
                        return r
                    P.op("pe", mmz, reads=[f"H0b{s}" for s in range(16)] + [f"AR{s3}a", f"AR{s3}r"], writes=[bk])
                    src = bank[:, 0:256].rearrange("p (s a t) -> p a s t", s=16, a=2)
                    for ar in range(2):
                        dst = zh[:, cc, ar, :].rearrange("p (s t) -> p s t", t=8)
                        P.op("dve", lambda e, src=src, dst=dst, ar=ar: e.tensor_copy(out=dst, in_=src[:, ar, :, :]), reads=[bk], writes=["zh"])

            def sample_state(s3):
                e16bc = e16[:, :].unsqueeze(2).broadcast_to([128, 16, 64])
                HK = [f"H0s{s}" for s in range(16)]
                for h in range(8):
                    cc, pb = h // 2, (h % 2) * 64
                    P.op("dve", lambda e, h=h: e.tensor_tensor(out=Xex[:, 0, :, :], in0=Ub[:, h * 64:(h + 1) * 64].unsqueeze(1).broadcast_to([128, 16, 64]),
                                                               in1=e16bc, op=ALU.mult),
                         reads=["Ub", "const"], writes=["Xex0"])
                    P.op("pool", lambda e, h=h: e.tensor_tensor(out=Xex[:, 1, :, :], in0=VTM[s3][:, h * 64:(h + 1) * 64].unsqueeze(1).broadcast_to([128, 16, 64]),
                                                                in1=e16bc, op=ALU.mult),
                         reads=[f"VTM{s3}", "const"], writes=["Xex1"])
                    for half in range(2):
                        bank, bk = nextl0()

                        def mms(e, h=h, half=half, bank=bank, pb=pb):
                            e.matmul(bank[pb:pb + 64, :], lhsT=BTM[s3][:, h * 64:(h + 1) * 64],
                                     rhs=Xex[:, 0, half * 8:(half + 1) * 8, :].rearrange("p a b -> p (a b)"), start=True, stop=False)
                            return e.matmul(bank[pb:pb + 64, :], lhsT=KTM[s3][:, h * 64:(h + 1) * 64],
                                            rhs=Xex[:, 1, half * 8:(half + 1) * 8, :].rearrange("p a b -> p (a b)"), start=False, stop=True)
                        P.op("pe", mms, reads=["Xex0", "Xex1", f"BTM{s3}", f"KTM{s3}"], writes=[bk])
                        P.op("dve", lambda e, half=half, cc=cc, bank=bank, pb=pb: e.tensor_tensor(
                            out=H0s[pb:pb + 64, half * 8:(half + 1) * 8, cc, :], in0=bank[pb:pb + 64, :].rearrange("p (s v) -> p s v", s=8),
                            in1=H0s[pb:pb + 64, half * 8:(half + 1) * 8, cc, :], op=ALU.add),
                            reads=[bk] + HK, writes=HK)
                for cc in range(4):
                    P.op("dve", lambda e, cc=cc: e.tensor_tensor(out=H0s[:, :, cc, :], in0=H0s[:, :, cc, :],
                                                                 in1=gC[s3][:, cc, :].unsqueeze(2).broadcast_to([128, 16, 64]), op=ALU.mult),
                         reads=HK + [f"gC{s3}"], writes=HK)
                for q4 in range(4):
                    for sl_ in range(4):
                        s = q4 * 4 + sl_
                        bank, bk = nextpj()

                        def trw(e, s=s, bank=bank):
                            r = None
                            for cc in range(4):
                                r = e.transpose(out=bank[0:64, cc * 128:(cc + 1) * 128], in_=H0s[:, s, cc, :], identity=ident32[:])
                            return r
                        P.op("pe", trw, reads=HK + ["const2"], writes=[bk])
                        if s % 2 == 0:
                            P.op("act", lambda e, sl_=sl_, bank=bank: e.copy(out=wso[:, sl_, :, :].rearrange("p a b -> p (a b)"), in_=bank[0:64, :]),
                                 reads=[bk], writes=[f"wso{sl_}"])
                        else:
                            P.op("dve", lambda e, sl_=sl_, bank=bank: e.tensor_copy(out=wso[:, sl_, :, :].rearrange("p a b -> p (a b)"), in_=bank[0:64, :]),
                                 reads=[bk], writes=[f"wso{sl_}"])
                    P.op("sp", lambda e, q4=q4: e.dma_start(out=wkv_s[q4 * 4:(q4 + 1) * 4].rearrange("s h v k -> v s h k"), in_=wso[:]),
                         reads=[f"wso{q}" for q in range(4)], writes=[f"o_wkv_s{q4}"] + [f"wso{q}" for q in range(4)], dma="o_wkv_s")
                    outkeys.append(f"o_wkv_s{q4}")

            def S6(i):
                qq = i % len(qT)
                s2 = i % R2
                samp = is_samp(i)
                kc3 = i % NKV
                kp3 = (i - 1) % NKV
                if i == NPRE:
                    P.op("pe", lambda e: e.transpose(out=tpb[:, 0:128], in_=kvT[kp3][:, 1, :], identity=ident[:]), reads=[f"kvT{kp3}", "const"], writes=["tpb"])
                    P.op("act", lambda e: e.copy(out=Vaug[kp3][:, :, 0:64], in_=tpb[:, 0:128].rearrange("p (g d) -> p g d", g=2)),
                         reads=["tpb"], writes=[f"Vaug{kp3}"])
                P.op("pe", lambda e: e.transpose(out=tpb[:, 0:128], in_=kvT[kc3][:, 1, :], identity=ident[:]), reads=[f"kvT{kc3}", "const"], writes=["tpb"])
                P.op("act", lambda e: e.copy(out=Vaug[kc3][:, :, 0:64], in_=tpb[:, 0:128].rearrange("p (g d) -> p g d", g=2)),
                     reads=["tpb"], writes=[f"Vaug{kc3}"])
                if samp:
                    sample_cache(qq)
                for g in range(2):
                    for kt in range(2):
                        if samp and kt == 0:
                            continue
                        bank, bk = nextl0()
                        kb = kvT[kp3] if kt == 0 else kvT[kc3]
                        kbk = f"kvT{kp3}" if kt == 0 else f"kvT{kc3}"
                        P.op("pe", lambda e, bank=bank, kb=kb, g=g: e.matmul(bank[:], lhsT=kb[g * 64:(g + 1) * 64, 0, :],
                                                                             rhs=qT[qq][g * 64:(g + 1) * 64, :, :], start=True, stop=True),
                             reads=[kbk, f"qT{qq}"], writes=[bk])
                        pt = PT[g * 2 + kt]
                        ptk = f"PT{g * 2 + kt}"
                        P.op("act", lambda e, bank=bank, pt=pt: e.activation(out=pt[:].rearrange("p a b -> p (a b)"), in_=bank[:], func=AF.Exp),
                             reads=[bk], writes=[ptk])
                        if kt == 0:
                            mi = 6 if i == NPRE else 2
                        else:
                            mi = 4 if samp else 1
                        mbc = MK[:, mi, :].unsqueeze(1).broadcast_to([128, 4, 128])
                        P.op("dve", lambda e, pt=pt, mbc=mbc: e.tensor_tensor(out=pt[:], in0=pt[:], in1=mbc, op=ALU.mult),
                             reads=[ptk, "const"], writes=[ptk])
                for g in range(2):
                    bank, bk = nextsq()
                    b3 = bank[:, 0:260].rearrange("p (a b) -> p a b", a=4)
                    for j in range(4):
                        h = g * 4 + j
                        if samp:
                            px = PTx[h % 2]
                            pxk = f"PTx{h % 2}"
                            P.op("dve", lambda e, g=g, j=j, px=px: [e.tensor_tensor(
                                out=px[:, s, s * 8:(s + 1) * 8], in0=PTc[:, g, s, j * 8:(j + 1) * 8], in1=MK[:, 2, 0:8], op=ALU.mult) for s in range(16)][-1],
                                reads=[f"PTc{g}", "const"], writes=[pxk])

                        def pv(e, g=g, j=j, b3=b3, h=h):
                            r = None
                            if samp:
                                for s in range(16):
                                    e.matmul(b3[:, j, :], lhsT=PTx[h % 2][:, s, :], rhs=Vca[:, s, g, :], start=(s == 0), stop=False)
                            else:
                                e.matmul(b3[:, j, :], lhsT=PT[g * 2][:, j, :], rhs=Vaug[kp3][:, g, :], start=True, stop=False)
                            r = e.matmul(b3[:, j, :], lhsT=PT[g * 2 + 1][:, j, :], rhs=Vaug[kc3][:, g, :], start=False, stop=True)
                            return r
                        rd = [f"PT{g * 2 + 1}", f"Vaug{kc3}"] + ([f"PTx{h % 2}", "Vca"] if samp else [f"PT{g * 2}", f"Vaug{kp3}"])
                        P.op("pe", pv, reads=rd, writes=[bk + f"_{j}"] + ([bk] if j == 0 else []))
                    bkj = [bk] + [bk + f"_{j}" for j in range(4)]
                    P.op("dve", lambda e, g=g, b3=b3: e.tensor_tensor(out=den[:, g * 4:(g + 1) * 4], in0=b3[:, :, 64], in1=esink[:, g * 4:(g + 1) * 4], op=ALU.add),
                         reads=bkj + ["esink"], writes=[f"den{g}"])
                    P.op("dve", lambda e, g=g: e.reciprocal(out=den[:, g * 4:(g + 1) * 4], in_=den[:, g * 4:(g + 1) * 4]), reads=[f"den{g}"], writes=[f"den{g}"])
                    P.op("dve", lambda e, g=g, b3=b3: e.tensor_tensor(out=attb[:, g * 4:(g + 1) * 4, :], in0=b3[:, :, 0:64],
                                                                      in1=den[:, g * 4:(g + 1) * 4].unsqueeze(2).broadcast_to([128, 4, 64]), op=ALU.mult),
                         reads=bkj + [f"den{g}"], writes=[f"attb{g}"])

                def tra(e):
                    r = None
                    for c in range(4):
                        r = e.transpose(out=tpb[:, c * 128:(c + 1) * 128], in_=attb[:, 2 * c:2 * c + 2, :].rearrange("p a b -> p (a b)"), identity=ident[:])
                    return r
                P.op("pe", tra, reads=["attb0", "attb1", "const"], writes=["tpb"])
                P.op("act", lambda e: e.copy(out=mixT[s2][:, 0:4, :].rearrange("p a b -> p (a b)"), in_=tpb[:, 0:512]), reads=["tpb"], writes=[f"mixT{s2}a"])

            def sample_cache(qq):
                P.op("pool", lambda e: e.dma_start(out=cstb[:], in_=ck.rearrange("s k d -> k s d")), writes=["cstb"], dma="cstb")
                for q in range(2):
                    def trc(e, q=q):
                        r = None
                        for ss_ in range(8):
                            r = e.transpose(out=tpb[:, ss_ * 128:(ss_ + 1) * 128], in_=cstb[:, q * 8 + ss_, :], identity=ident[:])
                        return r
                    P.op("pe", trc, reads=["cstb", "const"], writes=["tpb"])
                    P.op("act", lambda e, q=q: e.copy(out=KcT[:, q * 8:(q + 1) * 8, :].rearrange("p a b -> p (a b)"), in_=tpb[:]), reads=["tpb"], writes=[f"KcT{q}"])
                P.op("pool", lambda e: [e.dma_start(out=Vca[:, :, g, 0:64], in_=cv[:, :, g * 64:(g + 1) * 64].rearrange("s k d -> k s d")) for g in range(2)],
                     reads=["Vca"], writes=["Vca"], dma="Vca", n=2)
                for g in range(2):
                    bank, bk = nextl0()

                    def scc(e, g=g, bank=bank):
                        r = None
                        for s in range(16):
                            r = e.matmul(bank[:, s * 32:(s + 1) * 32], lhsT=KcT[g * 64:(g + 1) * 64, s, :],
                                         rhs=qT[qq][g * 64:(g + 1) * 64, :, s * 8:(s + 1) * 8], start=True, stop=True)
                        return r
                    P.op("pe", scc, reads=["KcT0", "KcT1", f"qT{qq}"], writes=[bk])
                    P.op("act", lambda e, g=g, bank=bank: e.activation(out=PTc[:, g, :, :].rearrange("p a b -> p (a b)"), in_=bank[:], func=AF.Exp),
                         reads=[bk], writes=[f"PTc{g}"])

            def S7(i):
                s2 = i % R2
                s3 = i % len(gT)
                j = i - NPRE
                o = OTM[i % R2]
                ok = f"OTM{i % R2}"
                P.op("dve", lambda e: e.tensor_reduce(out=st1[:], in_=o[:], axis=AX.X, op=ALU.add), reads=[ok], writes=["st1"])
                P.op("pool", lambda e: e.tensor_tensor(out=osq[:], in0=o[:], in1=o[:], op=ALU.mult), reads=[ok], writes=["osq"])
                P.op("dve", lambda e: e.tensor_reduce(out=st2[:], in_=osq[:], axis=AX.X, op=ALU.add), reads=["osq"], writes=["st2"])
                P.op("dve", lambda e: e.tensor_scalar(out=st1[:], in0=st1[:], scalar1=1.0 / 64, scalar2=None, op0=ALU.mult), reads=["st1"], writes=["st1"])
                P.op("dve", lambda e: e.tensor_tensor(out=st3[:], in0=st1[:], in1=st1[:], op=ALU.mult), reads=["st1"], writes=["st3"])
                P.op("dve", lambda e: e.scalar_tensor_tensor(out=st2[:], in0=st2[:], scalar=1.0 / 64, in1=st3[:], op0=ALU.mult, op1=ALU.subtract),
                     reads=["st2", "st3"], writes=["st2"])
                P.op("act", lambda e: e.activation(out=st2[:], in_=st2[:], func=AF.Sqrt, bias=64e-5), reads=["st2"], writes=["st2"])
                P.op("dve", lambda e: e.reciprocal(out=st2[:], in_=st2[:]), reads=["st2"], writes=["st2"])
                P.op("dve", lambda e: e.tensor_tensor(out=osq[:], in0=o[:], in1=st1[:, :].unsqueeze(2).broadcast_to([128, 8, 64]), op=ALU.subtract),
                     reads=[ok, "st1", "osq"], writes=["osq"])
                P.op("dve", lambda e: e.tensor_tensor(out=onb[:], in0=osq[:], in1=st2[:, :].unsqueeze(2).broadcast_to([128, 8, 64]), op=ALU.mult),
                     reads=["osq", "st2"], writes=["onb"])

                def tro(e):
                    r = None
                    for c in range(4):
                        r = e.transpose(out=tpb[:, c * 128:(c + 1) * 128], in_=onb[:, 2 * c:2 * c + 2, :].rearrange("p a b -> p (a b)"), identity=ident[:])
                    return r
                P.op("pe", tro, reads=["onb", "const"], writes=["tpb"])
                P.op("dve", lambda e: e.tensor_tensor(out=tmx[:], in0=tpb[:, 0:512].rearrange("p (a b) -> p a b", a=4), in1=v4bc(GNG), op=ALU.mult),
                     reads=["tpb", "const2", "tmx"], writes=["tmx"])
                P.op("pool", lambda e: e.tensor_tensor(out=tmx[:], in0=tmx[:], in1=v4bc(GNB), op=ALU.add), reads=["tmx", "const2"], writes=["tmx"])
                P.op("pool", lambda e: e.tensor_tensor(out=tmx[:], in0=tmx[:], in1=bonT[s3][:], op=ALU.add), reads=["tmx", f"bonT{s3}"], writes=["tmx"])
                P.op("dve", lambda e: e.tensor_tensor(out=mixT[s2][:, 4:8, :], in0=tmx[:], in1=gT[s3][:], op=ALU.mult),
                     reads=["tmx", f"gT{s3}"], writes=[f"mixT{s2}b"])
                xb = x1t[0]
                xk = "x1t0"
                src = xs[:, :] if is_samp(i) else xw[i * 128:(i + 1) * 128, :]
                P.op("sp", lambda e: e.dma_start(out=xb[:], in_=src), writes=[xk], dma=xk)
                for half in range(2):
                    bank, bk = nextpj()

                    def mo(e, half=half, bank=bank):
                        r = None
                        for kc in range(8):
                            r = e.matmul(bank[:], lhsT=mixT[s2][:, kc, :], rhs=Wout[:, kc, half * 512:(half + 1) * 512], start=(kc == 0), stop=(kc == 7))
                        return r
                    P.op("pe", mo, reads=[f"mixT{s2}a", f"mixT{s2}b", "Wout"], writes=[bk])
                    P.op("dve", lambda e, half=half, bank=bank: e.tensor_tensor(out=xb[:, half * 512:(half + 1) * 512], in0=bank[:],
                                                                                in1=xb[:, half * 512:(half + 1) * 512], op=ALU.add),
                         reads=[bk, xk], writes=[xk])
                P.op("sp", lambda e: e.dma_start(out=x1s[j * 128:(j + 1) * 128, :], in_=xb[:]), reads=[xk], writes=[f"x1s{j}", xk], dma=xk)
                outkeys.append(f"x1s{j}")

            if pA:
                def cap(fns):
                    P.cap = []
                    for fn, i in fns:
                        if 0 <= i < NPT:
                            fn(i)
                    out = P.cap
                    P.cap = None
                    return out

                for step in range(NPT + 6):
                    for fn, i in ((S7, step - 5), (S6, step - 4)):
                        if 0 <= i < NPT and is_own(i):
                            fn(i)
                    if 0 <= step - 4 < NPT:
                        S5n(step - 4)
                    lists = [cap([(lambda i: S4h(i, 0), step - 3)]), cap([(lambda i: S4h(i, 1), step - 3)]),
                             cap([(S3, step - 2), (S2, step - 1), (S1, step)])]
                    chs = [ILV_CH, ILV_CH, 1]
                    pos = [0, 0, 0]
                    while any(pos[q] < len(lists[q]) for q in range(3)):
                        cand = [q for q in range(3) if pos[q] < len(lists[q])]
                        q = min(cand, key=lambda q: pos[q] / len(lists[q]))
                        for _ in range(chs[q]):
                            if pos[q] < len(lists[q]):
                                o = lists[q][pos[q]]
                                P.op(o[0], o[1], reads=o[2], writes=o[3], dma=o[4], n=o[5])
                                pos[q] += 1
                    if 0 <= step - 3 < NPT:
                        S4tail(step - 3)
            elif pSa:
                S1(NPT)
                S2(NPT)
                outkeys.extend(["qT0", "kvT0", "fprev0", "fprev1"] + [f"fT0_{g}" for g in range(4)])
            else:
                for fn in (S3, S4, S5, S6, S7):
                    fn(NPT)
            P.op("sp", None, reads=list(outkeys))
            P.emit()
            build.stats[mode] = P.stats

    if "A" in PHASES:
        phase("A", None)
    with ExitStack() as stp:
        per = dict(
            fT=[stp.enter_context(nc.sbuf_tensor("fTs", [128, 14, 128], F32))],
            qT=[stp.enter_context(nc.sbuf_tensor("qTs", [128, 4, 128], BF16))],
            kvT=[stp.enter_context(nc.sbuf_tensor("kvTs", [128, 2, 128], BF16))],
            fprev=stp.enter_context(nc.sbuf_tensor("fprevs", [128, 14, 16], F32)),
        )
        if "Sa" in PHASES:
            phase("Sa", per)
        if "Sb" in PHASES:
            phase("Sb", per)

    if "B" not in PHASES:
        return nc
    with ExitStack() as st:
        def sb(name, shape, dt=F32):
            return st.enter_context(nc.sbuf_tensor(name, shape, dt))

        def psb(name, shape, dt=F32):
            return st.enter_context(nc.psum_tensor(name, shape, dt))
        P = Prog(nc, '_B')
        outk = []
        Wg = sb("Wg", [128, 8, D_FF], BF16)
        Wu = sb("Wu", [128, 8, D_FF], BF16)
        Wd = sb("Wd", [128, NFC, 1024], BF16)
        gffn = sb("gffn", [128, 1024])
        gfin = sb("gfin", [128, 1024])
        identb = sb("identb", [128, 128], BF16)

        P.op("pool", lambda e: e.dma_start(out=identb[:], in_=ident_d[:, :]), writes=["W"], dma="Wi")
        P.op("pool", lambda e: [e.dma_start(out=Wg[:, kc, :], in_=w_gate[kc * 128:(kc + 1) * 128, :]) for kc in range(8)],
             writes=["Wg"], dma="Wg", n=8)
        P.op("pool", lambda e: [e.dma_start(out=Wu[:, kc, :], in_=w_up[kc * 128:(kc + 1) * 128, :]) for kc in range(8)],
             writes=["Wu"], dma="Wu", n=8)
        P.op("pool", lambda e: [e.dma_start(out=Wd[:, fc, :], in_=w_down[fc * 128:(fc + 1) * 128, :]) for fc in range(NFC)],
             writes=["Wd"], dma="Wd", n=NFC)

        def cl(e):
            return [e.dma_start(out=gffn[:], in_=gvec[1:2, :].broadcast_to([128, 1024])),
                    e.dma_start(out=gfin[:], in_=gvec[2:3, :].broadcast_to([128, 1024]))]
        P.op("sp", cl, writes=["G"], dma="G", n=2)
        xgs = [sb(f"xg{q}", [128, 4, 1024]) for q in range(2)]
        junk2 = sb("junk2", [128, 1024], BF16)
        ub = sb("ub", [128, 1024], BF16)
        uT = sb("uT", [128, 8, 512], BF16)
        actT = sb("actT", [128, 11, 512], BF16)
        sgt = sb("sgt", [128, 512])
        ssb = sb("ssb", [128, 1])
        rsb = sb("rsb", [128, 1])
        yb = [sb(f"yb{q}", [128, 1024]) for q in range(2)]
        pg = [psb(f"pg{q}", [128, 512]) for q in range(2)]
        pu = [psb(f"pu{q}", [128, 512]) for q in range(2)]
        pd = [psb(f"pd{q}", [128, 512]) for q in range(2)]
        tpb2 = psb("tpb2", [128, 1024], BF16)
        cnt = [0]
        groups = [(0, 4), (4, 4), (8, 4), (12, 4), (16, 1)]

        def pro_load(g):
            t0, nt = groups[g]
            xg = xgs[g % 2]
            P.op("sp", lambda e: e.dma_start(out=xg[:, 0:nt, :], in_=x1s[t0 * 128:(t0 + nt) * 128, :].rearrange("(a p) d -> p a d", p=128)),
                 writes=[f"xg{g % 2}"], dma=f"xg{g % 2}")

        def pro_tile(g, a):
            xg = xgs[g % 2]
            xk = f"xg{g % 2}"
            P.op("act", lambda e: e.activation(out=junk2[:], in_=xg[:, a, :], func=AF.Square, accum_out=ssb[:]), reads=[xk], writes=["junk2", "ssb"])
            P.op("act", lambda e: e.activation(out=rsb[:], in_=ssb[:], func=AF.Sqrt, scale=1.0 / 1024, bias=1e-6), reads=["ssb"], writes=["rsb"])
            P.op("dve", lambda e: e.reciprocal(out=rsb[:], in_=rsb[:]), reads=["rsb"], writes=["rsb"])
            P.op("dve", lambda e: e.scalar_tensor_tensor(out=ub[:], in0=xg[:, a, :], scalar=rsb[:, 0:1], in1=gffn[:], op0=ALU.mult, op1=ALU.mult),
                 reads=[xk, "rsb", "G"], writes=["ub"])

            def tr(e):
                r = None
                for kc in range(8):
                    r = e.transpose(out=tpb2[:, kc * 128:(kc + 1) * 128], in_=ub[:, kc * 128:(kc + 1) * 128], identity=identb[:])
                return r
            P.op("pe", tr, reads=["ub", "W"], writes=["tpb2"])
            P.op("act", lambda e: e.copy(out=uT[:, :, a * 128:(a + 1) * 128], in_=tpb2[:].rearrange("p (a b) -> p a b", a=8)),
                 reads=["tpb2"], writes=[f"uT{a}"])

        pro_load(0)
        for a in range(groups[0][1]):
            pro_tile(0, a)
        for g, (t0, nt) in enumerate(groups):
            N = nt * 128
            xg = xgs[g % 2]
            xk = f"xg{g % 2}"
            if g + 1 < len(groups):
                pro_load(g + 1)
            uk = [f"uT{a}" for a in range(nt)]
            for hf in range(2):
                for fi in range(11):
                    fc = hf * 11 + fi
                    cnt[0] += 1
                    b = cnt[0] % 2

                    def mg(e, fc=fc, b=b, N=N):
                        r = None
                        for kc in range(8):
                            r = e.matmul(pg[b][:, 0:N], lhsT=Wg[:, kc, fc * 128:(fc + 1) * 128], rhs=uT[:, kc, 0:N], start=(kc == 0), stop=(kc == 7))
                        return r
                    P.op("pe", mg, reads=["Wg"] + uk, writes=[f"pg{b}"])

                    def mu_(e, fc=fc, b=b, N=N):
                        r = None
                        for kc in range(8):
                            r = e.matmul(pu[b][:, 0:N], lhsT=Wu[:, kc, fc * 128:(fc + 1) * 128], rhs=uT[:, kc, 0:N], start=(kc == 0), stop=(kc == 7))
                        return r
                    P.op("pe", mu_, reads=["Wu"] + uk, writes=[f"pu{b}"])
                    P.op("act", lambda e, b=b, N=N: e.activation(out=sgt[:, 0:N], in_=pg[b][:, 0:N], func=AF.Silu), reads=[f"pg{b}"], writes=["sgt"])
                    P.op("dve", lambda e, b=b, N=N, fi=fi: e.tensor_tensor(out=actT[:, fi, 0:N], in0=pu[b][:, 0:N], in1=sgt[:, 0:N], op=ALU.mult),
                         reads=[f"pu{b}", "sgt"], writes=[f"actT{fi}"])
                ak = [f"actT{fi}" for fi in range(11)]
                for a in range(nt):
                    for half in range(2):
                        cnt[0] += 1
                        b = cnt[0] % 2

                        def md(e, a=a, half=half, b=b, hf=hf):
                            r = None
                            for fi in range(11):
                                r = e.matmul(pd[b][:], lhsT=actT[:, fi, a * 128:(a + 1) * 128], rhs=Wd[:, hf * 11 + fi, half * 512:(half + 1) * 512],
                                             start=(fi == 0), stop=(fi == 10))
                            return r
                        P.op("pe", md, reads=["Wd"] + ak, writes=[f"pd{b}"])
                        P.op("dve", lambda e, a=a, half=half, b=b, xg=xg: e.tensor_tensor(out=xg[:, a, half * 512:(half + 1) * 512], in0=pd[b][:],
                                                                                          in1=xg[:, a, half * 512:(half + 1) * 512], op=ALU.add),
                             reads=[f"pd{b}", xk], writes=[xk])
                    if hf == 1 and g + 1 < len(groups) and a < groups[g + 1][1]:
                        pro_tile(g + 1, a)
            for a in range(nt):
                t = t0 + a
                y = yb[t % 2]
                yk = f"yb{t % 2}"
                P.op("act", lambda e, a=a, xg=xg: e.activation(out=junk2[:], in_=xg[:, a, :], func=AF.Square, accum_out=ssb[:]), reads=[xk], writes=["junk2", "ssb"])
                P.op("act", lambda e: e.activation(out=rsb[:], in_=ssb[:], func=AF.Sqrt, scale=1.0 / 1024, bias=1e-6), reads=["ssb"], writes=["rsb"])
                P.op("dve", lambda e: e.reciprocal(out=rsb[:], in_=rsb[:]), reads=["rsb"], writes=["rsb"])
                P.op("dve", lambda e, a=a, y=y, xg=xg: e.scalar_tensor_tensor(out=y[:], in0=xg[:, a, :], scalar=rsb[:, 0:1], in1=gfin[:], op0=ALU.mult, op1=ALU.mult),
                     reads=[xk, "rsb", "G"], writes=[yk])
                dst = y_s[:, :] if t == 16 else y_p[t * 128:(t + 1) * 128, :]
                P.op("sp", lambda e, y=y, dst=dst: e.dma_start(out=dst, in_=y[:]), reads=[yk], writes=[f"oy{t}", yk], dma=yk)
                outk.append(f"oy{t}")
        P.op("sp", None, reads=outk)
        P.emit()
        build.stats["B"] = P.stats
    return nc


def _consts(p):
    s = np.arange(128)[:, None]
    t = np.arange(128)[None, :]
    su = (s < t).astype(np.float32)
    ui = (s <= t).astype(np.float32)
    sl = (s > t).astype(np.float32)
    same = ((s // 8) == (t // 8)).astype(np.float32)
    mfirst = sl if p > 0 else np.zeros_like(sl)
    masks = np.stack([su, ui, sl, su * same, ui * same, sl * same, mfirst], axis=1).reshape(128, 7 * 128)
    ident = np.eye(128, dtype=np.float32)
    bones = ((s // 64) == (t // 64)).astype(np.float32)
    rm = np.ones((128, 2, 128), np.float32)
    rm[:, 1, :] = (np.arange(128) % 8 != 0).astype(np.float32)[None, :]
    e16 = ((np.arange(128)[:, None] // 8) == np.arange(16)[None, :]).astype(np.float32)
    return dict(masks=np.ascontiguousarray(masks), ident=ident, bones=bones, rmask=rm.reshape(128, 256), e16=e16)


_NC = [None]


def kernel(x_prompt, x_sample, cache_k, cache_v, state_wkv, state_shift, g_mix, w_in, attn_sinks,
           rwkv_mu, w0, w2, a0, a2, g2, k_k, k_a, r_k, gn_g, gn_b, w_out, g_ffn, w_gate, w_up,
           w_down, g_final):
    f = lambda a: np.ascontiguousarray(np.asarray(a, dtype=np.float32))
    x_prompt, x_sample = f(x_prompt), f(x_sample)
    w_in0 = f(w_in)[0]
    qperm = np.concatenate([np.r_[j * 64:(j + 1) * 64, (4 + j) * 64:(5 + j) * 64] for j in range(4)])
    w_in_p = np.ascontiguousarray(np.concatenate([w_in0[:, qperm], w_in0[:, 512:]], axis=1))
    fm4 = lambda v: f(v).reshape(4, 128).T
    vec4 = np.ascontiguousarray(np.stack([fm4(w0[0]), fm4(a0[0]), fm4(k_k[0]), fm4(k_a[0]), fm4(f(r_k)[0].reshape(-1)),
                                          fm4(gn_g[0]), fm4(gn_b[0])], axis=1).reshape(128, 28))
    shared = dict(
        w_in=w_in_p, w_out=f(w_out)[0], w_gate=f(w_gate)[0], w_up=f(w_up)[0], w_down=f(w_down)[0],
        gvec=np.ascontiguousarray(np.stack([f(g_mix)[0], f(g_ffn)[0], f(g_final)], axis=0)),
        mu=np.ascontiguousarray(f(rwkv_mu)[0].reshape(14, 128).T),
        vec4=vec4,
        w2a2=np.ascontiguousarray(np.concatenate([f(w2)[0], f(a2)[0]], axis=0)),
        g2=f(g2)[0],
        sinks=f(attn_sinks)[0].reshape(1, 8),
    )
    in_maps = []
    for c in range(8):
        b, p = c // 4, c % 4
        xwin = np.zeros((NPT * 128, 1024), np.float32)
        nreal = (p + 1) * 2048
        xwin[NPT * 128 - nreal:] = x_prompt[b, 0:nreal]
        m = dict(shared)
        m.update(_consts(p))
        m.update(
            xw=xwin,
            xs=np.ascontiguousarray(x_sample[16 * c:16 * c + 16].reshape(128, 1024)),
            hprev=f(state_shift)[0, 16 * c:16 * c + 16],
            ck=np.ascontiguousarray(f(cache_k)[0, 16 * c:16 * c + 16].reshape(16, 128, 128)),
            cv=np.ascontiguousarray(f(cache_v)[0, 16 * c:16 * c + 16].reshape(16, 128, 128)),
            swkv=np.ascontiguousarray(f(state_wkv)[0, 16 * c:16 * c + 16]),
        )
        in_maps.append(m)
    if _NC[0] is None:
        _NC[0] = build()
    res = run_bass_kernel_spmd(_NC[0], in_maps, core_ids=list(range(8)))
    R = res.results
    y_prompt = np.stack([np.concatenate([R[b * 4 + p]["y_p"] for p in range(4)], axis=0) for b in range(2)], axis=0)
    y_sample = np.concatenate([R[c]["y_s"].reshape(16, 8, 1024) for c in range(8)], axis=0)
    kp = np.stack([R[b * 4 + 3]["kwin_p"].reshape(128, 2, 64) for b in range(2)], axis=0)[None]
    vp = np.stack([R[b * 4 + 3]["vwin_p"].reshape(128, 2, 64) for b in range(2)], axis=0)[None]
    sp = np.stack([R[b * 4 + 3]["wkv_p"] for b in range(2)], axis=0)[None]
    hp = np.stack([R[b * 4 + 3]["shift_p"].reshape(1024) for b in range(2)], axis=0)[None]
    ks = np.concatenate([R[c]["kwin_s"].reshape(16, 128, 2, 64) for c in range(8)], axis=0)[None]
    vs = np.concatenate([R[c]["vwin_s"].reshape(16, 128, 2, 64) for c in range(8)], axis=0)[None]
    ss_ = np.concatenate([R[c]["wkv_s"] for c in range(8)], axis=0)[None]
    hs = np.concatenate([R[c]["shift_s"] for c in range(8)], axis=0)[None]
    return tuple(np.ascontiguousarray(a.astype(np.float32)) for a in (y_prompt, y_sample, kp, vp, sp, hp, ks, vs, ss_, hs))
```

```python
import numpy as np
from contextlib import ExitStack
import concourse.bass as bass
import concourse.mybir as mybir
from concourse.bass_utils import run_bass_kernel_spmd
from concourse.alu_op_type import AluOpType as ALU

AF = mybir.ActivationFunctionType
AX = mybir.AxisListType
F32 = mybir.dt.float32
BF16 = mybir.dt.bfloat16

SAME_ENGINE_SYNC = True
SAME_ENGINE_RAW_ONLY = False
ENGS = ("pe", "act", "dve", "pool", "sp")
C0 = float(np.exp(-0.5))
NPRE = 48
NOWN = 16
NPT = NPRE + NOWN
NT = NPT + 1
D_FF = 2816
NFC = 22
ILV_CH = 2
ILV_MODE = 0
SKIPOPS = set()
NO_ILV = False
PSUM_PREFIXES = ("pj", "tpb", "l0", "sqp", "Zp", "pg", "pu", "pd")
OPLIMIT = 10 ** 9
NO_D2D = False
PHASES = {"A", "Sa", "Sb", "B"}


class Prog:
    def __init__(self, nc, tag=''):
        self.nc = nc
        self.tag = tag
        self.ins = []
        self.last_w = {}
        self.readers = {}

    def op(self, eng, fn, reads=(), writes=(), dma=None, n=1):
        if getattr(self, "cap", None) is not None:
            self.cap.append((eng, fn, list(reads), list(writes), dma, n))
            return -1
        idx = len(self.ins)
        if idx >= OPLIMIT:
            return idx
        writes = list(writes) + [k for k in reads if k.startswith(PSUM_PREFIXES) and k not in writes]
        deps = set()
        raw = set()
        for k in reads:
            if k in self.last_w:
                deps.add(self.last_w[k])
                raw.add(self.last_w[k])
        for k in writes:
            if k in self.last_w:
                deps.add(self.last_w[k])
            for r in self.readers.get(k, ()):
                deps.add(r)
        deps.discard(idx)
        if fn is None:
            writes = []
        self.ins.append(dict(eng=eng, fn=fn, deps=deps, raw=raw, dma=dma, used=False, n=n))
        for k in reads:
            self.readers.setdefault(k, []).append(idx)
        for k in writes:
            self.last_w[k] = idx
            self.readers[k] = []
        return idx

    def emit(self):
        nc = self.nc
        ins = self.ins
        for r in ins:
            if SAME_ENGINE_RAW_ONLY:
                r["deps"] = {d for d in r["deps"]
                             if not (ins[d]["eng"] == r["eng"] and ins[d]["dma"] is None and r["dma"] is None and d not in r["raw"])}
            for d in r["deps"]:
                ins[d]["used"] = True
        cnt = {e: 0 for e in ENGS}
        dmav = {}
        for r in ins:
            if r["dma"] is not None:
                r["sem"] = "dma_" + r["dma"]
                dmav[r["sem"]] = dmav.get(r["sem"], 0) + 16 * r["n"]
                r["val"] = dmav[r["sem"]]
            elif r["used"]:
                cnt[r["eng"]] += 1
                r["sem"] = "eng_" + r["eng"]
                r["val"] = cnt[r["eng"]]
            else:
                r["sem"] = None
                r["val"] = 0
        known = {e: {} for e in ENGS}
        for r in ins:
            e = r["eng"]
            kn = known[e]
            wd = {}
            for d in sorted(r["deps"]):
                rd = ins[d]
                s, v = rd["sem"], rd["val"]
                if rd["eng"] == e and rd["dma"] is None and not SAME_ENGINE_SYNC:
                    continue
                if kn.get(s, 0) >= v:
                    continue
                wd[s] = max(wd.get(s, 0), v)
                for s2, v2 in rd["clock"].items():
                    if kn.get(s2, 0) < v2:
                        kn[s2] = v2
            r["waits"] = sorted(wd.items())
            ck = dict(kn)
            if r["sem"] is not None:
                ck[r["sem"]] = r["val"]
            r["clock"] = ck
        semnames = sorted({r["sem"] for r in ins if r["sem"] is not None})
        self.stats = dict(n=len(ins), nsem=len(semnames),
                          nwaits=sum(len(r["waits"]) for r in ins),
                          per_eng={e: sum(1 for r in ins if r["eng"] == e) for e in ENGS})
        with ExitStack() as st:
            sems = {s: st.enter_context(nc.semaphore(s + self.tag)) for s in semnames}
            block = st.enter_context(nc.Block())
            reg = {"pe": block.tensor, "act": block.scalar, "dve": block.vector,
                   "pool": block.gpsimd, "sp": block.sync}

            def make(e):
                def body(eng):
                    for r in ins:
                        if r["eng"] != e:
                            continue
                        for s, v in r["waits"]:
                            eng.wait_ge(sems[s], v)
                        if r["fn"] is None:
                            continue
                        out = r["fn"](eng)
                        if r["dma"] is not None:
                            outs = out if isinstance(out, (list, tuple)) else [out]
                            assert len(outs) == r["n"], (len(outs), r["n"])
                            for o in outs:
                                o.then_inc(sems[r["sem"]], 16)
                        elif r["sem"] is not None:
                            o = out[-1] if isinstance(out, (list, tuple)) else out
                            o.then_inc(sems[r["sem"]], 1)
                return body

            for e in ENGS:
                reg[e](make(e))


def build():
    nc = bass.Bass("TRN2", target_bir_lowering=False)

    def din(name, shape):
        return nc.dram_tensor(name, shape, F32, kind="ExternalInput").ap()

    def dout(name, shape):
        return nc.dram_tensor(name, shape, F32, kind="ExternalOutput").ap()

    xw = din("xw", [NPT * 128, 1024])
    xs = din("xs", [128, 1024])
    hprev = din("hprev", [16, 1024])
    ck = din("ck", [16, 128, 128])
    cv = din("cv", [16, 128, 128])
    swkv = din("swkv", [16, 8, 64, 64])
    w_in = din("w_in", [1024, 2560])
    w_out = din("w_out", [1024, 1024])
    w_gate = din("w_gate", [1024, D_FF])
    w_up = din("w_up", [1024, D_FF])
    w_down = din("w_down", [D_FF, 1024])
    gvec = din("gvec", [3, 1024])
    mu_d = din("mu", [128, 14])
    vec4_d = din("vec4", [128, 7 * 4])
    w2a2_d = din("w2a2", [128, 512])
    g2_d = din("g2", [128, 512])
    sinks_d = din("sinks", [1, 8])
    masks_d = din("masks", [128, 7 * 128])
    ident_d = din("ident", [128, 128])
    bones_d = din("bones", [128, 128])
    rmask_d = din("rmask", [128, 256])
    e16_d = din("e16", [128, 16])

    y_p = dout("y_p", [NOWN * 128, 1024])
    y_s = dout("y_s", [128, 1024])
    kwin_p = dout("kwin_p", [128, 128])
    vwin_p = dout("vwin_p", [128, 128])
    wkv_p = dout("wkv_p", [8, 64, 64])
    shift_p = dout("shift_p", [1, 1024])
    kwin_s = dout("kwin_s", [16, 128, 128])
    vwin_s = dout("vwin_s", [16, 128, 128])
    wkv_s = dout("wkv_s", [16, 8, 64, 64])
    shift_s = dout("shift_s", [16, 1024])
    x1s = nc.dram_tensor("x1s", [17 * 128, 1024], F32, kind="Internal").ap()
    build.stats = {}

    W0, A0, KK, KA, RK, GNG, GNB = range(7)

    def phase(mode, per):
        pA, pSa, pSb = mode == "A", mode == "Sa", mode == "Sb"
        with ExitStack() as st:
            def sb(name, shape, dt=F32):
                return st.enter_context(nc.sbuf_tensor(name + "_" + mode, shape, dt))

            def psb(name, shape, dt=F32):
                return st.enter_context(nc.psum_tensor(name + "_" + mode, shape, dt))

            def rot(name, n, shape, dt=F32):
                return [sb(f"{name}{q}", shape, dt) for q in range(n)]
            P = Prog(nc, '_' + mode)
            outkeys = []
            if pA or pSa:
                Win = sb("Win", [128, 8, 2560], BF16)
            if pA or pSb:
                Wout = sb("Wout", [128, 8, 1024], BF16)
            gmix = sb("gmix", [128, 1024])
            mu = sb("mu", [128, 14])
            vec4 = sb("vec4", [128, 7, 4])
            w2a2 = sb("w2a2", [128, 512], BF16)
            g2b = sb("g2b", [128, 512], BF16)
            esink = sb("esink", [128, 8])
            MK = sb("MK", [128, 7, 128], BF16)
            MKL = sb("MKL", [128, 2, 4, 128], BF16)
            ident = sb("ident", [128, 128], BF16)
            ident32 = sb("ident32", [128, 128])
            bones = sb("bones", [128, 128], BF16)
            rmask = sb("rmask", [128, 2, 128])
            e16 = sb("e16", [128, 16], BF16)

            def cload(e):
                r = []
                r.append(e.dma_start(out=ident[:], in_=ident_d[:, :]))
                r.append(e.dma_start(out=w2a2[:], in_=w2a2_d[:, :]))
                r.append(e.dma_start(out=g2b[:], in_=g2_d[:, :]))
                r.append(e.dma_start(out=MK[:], in_=masks_d.rearrange("p (m t) -> p m t", m=7)))
                r.append(e.dma_start(out=bones[:], in_=bones_d[:, :]))
                r.append(e.dma_start(out=e16[:], in_=e16_d[:, :]))
                return r
            P.op("pool", cload, writes=["const"], dma="constb", n=6)
            if pA or pSa:
                P.op("pool", lambda e: [e.dma_start(out=Win[:, kc, :], in_=w_in[kc * 128:(kc + 1) * 128, :]) for kc in range(8)],
                     writes=["Win"], dma="Win", n=8)
            if pA or pSb:
                P.op("pool", lambda e: [e.dma_start(out=Wout[:, kc, :], in_=w_out[kc * 128:(kc + 1) * 128, :]) for kc in range(8)],
                     writes=["Wout"], dma="Wout", n=8)

            def cload2(e):
                r = []
                r.append(e.dma_start(out=gmix[:], in_=gvec[0:1, :].broadcast_to([128, 1024])))
                r.append(e.dma_start(out=mu[:], in_=mu_d[:, :]))
                r.append(e.dma_start(out=vec4[:], in_=vec4_d.rearrange("p (a b) -> p a b", a=7)))
                r.append(e.dma_start(out=esink[:], in_=sinks_d[0:1, :].broadcast_to([128, 8])))
                r.append(e.dma_start(out=ident32[:], in_=ident_d[:, :]))
                r.append(e.dma_start(out=rmask[:], in_=rmask_d.rearrange("p (a t) -> p a t", a=2)))
                return r
            P.op("sp", cload2, writes=["const2"], dma="constf", n=6)
            P.op("act", lambda e: e.activation(out=esink[:], in_=esink[:], func=AF.Exp),
                 reads=["const2"], writes=["esink"])

            def mkl(e):
                r = None
                for kd in range(2):
                    for q in range(4):
                        r = e.tensor_copy(out=MKL[:, kd, q, :], in_=MK[:, 3 * kd + (q % 2), :])
                return r
            P.op("pool", mkl, reads=["const"], writes=["MKL"])

            def v4bc(ix, n=128):
                return vec4[:, ix, :].unsqueeze(2).broadcast_to([128, 4, n])

            R2 = 2 if pA else 1
            if pA or pSa:
                xt = rot("xt", 1 if pA else 1, [128, 1024])
                ss = rot("ss", 2, [128, 1])
                rs = rot("rs", 2, [128, 1])
                hb = rot("hb", R2, [128, 1024], BF16)
                hT = rot("hT", R2, [128, 8, 128], BF16)
            if pA:
                fT = rot("fT", 2, [128, 14, 128])
                qT = rot("qT", 4, [128, 4, 128], BF16)
                kvT = rot("kvT", 5, [128, 2, 128], BF16)
            else:
                fT, qT, kvT, fprev = per["fT"], per["qT"], per["kvT"], per["fprev"]
            NKV = len(kvT)
            if pSa:
                h32s = sb("h32s", [128, 1024])
                kvx = sb("kvx", [128, 4, 128])
                hp32 = sb("hp32", [16, 1024])
                hpb = sb("hpb", [16, 1024], BF16)
                hpT = sb("hpT", [128, 8, 16], BF16)
            if pA or pSb:
                Vaug = rot("Vaug", NKV, [128, 2, 65], BF16)
                xsT = sb("xsT", [128, 14, 128])
                fcar = sb("fcar", [128, 14])
                twal = sb("twal", [128, 128], BF16)
                sg = sb("sg", [128, 128], BF16)
                eT = sb("eT", [128, 4, 128])
                asT = sb("asT", [128, 4, 128])
                kkT = sb("kkT", [128, 4, 128])
                sqb = sb("sqb", [128, 4, 128], BF16)
                rn = sb("rn", [128, 4, 128])
                tmpA = sb("tmpA", [128, 4, 128])
                kmT = sb("kmT", [128, 4, 128])
                cumE = sb("cumE", [128, 4, 128])
                Eg = sb("Eg", [128, 4, 128])
                vb = sb("vb", [128, 4, 128], BF16)
                AR = rot("AR", R2, [128, 4, 2, 128], BF16)
                BTF = rot("BTF", R2, [128, 4, 128], BF16)
                KTF = rot("KTF", R2, [128, 4, 128], BF16)
                VTM = rot("VTM", R2, [128, 512], BF16)
                BTM = rot("BTM", R2, [128, 512], BF16)
                KTM = rot("KTM", R2, [128, 512], BF16)
                gC = rot("gC", R2, [128, 4, 16])
                gT = rot("gT", 3 if pA else 1, [128, 4, 128], BF16)
                bonT = rot("bonT", 3 if pA else 1, [128, 4, 128], BF16)
                if pSb:
                    SQ0 = sb("SQ0", [128, 8, 256], BF16)
                    NM = sb("NM", [128, 8, 384], BF16)
                    NJ = sb("NJ", [128, 2, 8, 128], BF16)
                    AJ = sb("AJ", [128, 2, 8, 128], BF16)
                else:
                    L0S = sb("L0S", [128, 8, 512], BF16)
                    A0S = sb("A0S", [128, 8, 128], BF16)
                    NA = sb("NA", [128, 2, 8, 256], BF16)
                Zb = rot("Zb", 2, [128, 512], BF16)
                Ub = sb("Ub", [128, 512], BF16)
                Hst = sb("Hst", [128, 4, 64])
                Hb = sb("Hb", [128, 4, 64], BF16)
                tS = sb("tS", [128, 4, 64])
                OTM = rot("OTM", R2, [128, 8, 64])
                osq = sb("osq", [128, 8, 64])
                st1 = sb("st1", [128, 8])
                st2 = sb("st2", [128, 8])
                st3 = sb("st3", [128, 8])
                onb = sb("onb", [128, 8, 64], BF16)
                PT = rot("PT", 4, [128, 4, 128], BF16)
                den = sb("den", [128, 8])
                attb = sb("attb", [128, 8, 64], BF16)
                mixT = rot("mixT", R2, [128, 8, 128], BF16)
                tmx = sb("tmx", [128, 4, 128])
                x1t = rot("x1t", 1, [128, 1024])
                if pA:
                    ATM = rot("ATM", R2, [128, 512], BF16)
                    BHF = sb("BHF", [128, 4, 128], BF16)
                    KHF = sb("KHF", [128, 4, 128], BF16)
                    Xb = rot("Xb", 2, [128, 2, 512], BF16)
                    WY = sb("WY", [128, 8, 128], BF16)
                    MTb = sb("MTb", [128, 4, 64], BF16)
                    WTF = sb("WTF", [128, 4, 128], BF16)
            if pA:
                kvx = tmx
                h32s = x1t[0]
            if pSb:
                S0q = sb("S0q", [64, 32, 64])
                H0s = sb("H0s", [128, 16, 4, 64])
                H0b = sb("H0b", [128, 16, 4, 64], BF16)
                Xex = sb("Xex", [128, 2, 16, 64], BF16)
                zh = sb("zh", [128, 4, 2, 128], BF16)
                KcT = sb("KcT", [128, 16, 128], BF16)
                Vca = sb("Vca", [128, 16, 2, 65], BF16)
                cstb = sb("cstb", [128, 16, 128], BF16)
                PTc = sb("PTc", [128, 2, 16, 32], BF16)
                PTx = rot("PTx", 2, [128, 16, 128], BF16)
                wso = sb("wso", [64, 4, 8, 64])
            h32k = "x1t0" if pA else "h32s"
            kvxk = "tmx" if pA else "kvx"

            pj = [psb(f"pj{q}", [128, 512]) for q in range(2)]
            tpb = psb("tpb", [128, 1024], BF16)
            l0 = [psb(f"l0{q}", [128, 512]) for q in range(2)]
            sqp = [psb(f"sqp{q}", [128, 512]) for q in range(2)]
            Zp = psb("Zp", [128, 512])
            cnts = {"pj": 0, "sqp": 0, "l0": 0}
            banks = {"pj": pj, "sqp": sqp, "l0": l0}

            def nextb(nm):
                cnts[nm] += 1
                q = cnts[nm] % 2
                return banks[nm][q], f"{nm}{q}"

            def nextpj():
                return nextb("pj")

            def nextsq():
                return nextb("sqp")

            def nextl0():
                return nextb("l0")

            if pA or pSb:
                P.op("pool", lambda e: e.memset(fcar[:], 0.0), writes=["fcar"])
                P.op("pool", lambda e: e.memset(Hst[:], 0.0), writes=["Hst"])
                P.op("pool", lambda e: e.memset(Hb[:], 0.0), writes=["Hb"])
                P.op("pool", lambda e: e.memset(tS[:], 0.0), writes=["tS"])
                for q in range(NKV):
                    P.op("pool", lambda e, q=q: e.memset(Vaug[q][:], 1.0), writes=[f"Vaug{q}"])
            if pSb:
                P.op("pool", lambda e: e.memset(Vca[:], 1.0), writes=["Vca"])
                for q in range(2):
                    P.op("pool", lambda e, q=q: e.memset(PTx[q][:], 0.0), writes=[f"PTx{q}"])

            def is_own(i):
                return i >= NPRE

            def is_samp(i):
                return i == NPT

            def S1(i):
                b = i % len(xt)
                kx = f"xt{b}"
                src = xs[:, :] if is_samp(i) else xw[i * 128:(i + 1) * 128, :]
                P.op("sp", lambda e: e.dma_start(out=xt[b][:], in_=src), writes=[kx], dma=kx)
                s2 = i % 2
                hq = i % len(hb)
                P.op("act", lambda e: e.activation(out=hb[hq][:], in_=xt[b][:], func=AF.Square, accum_out=ss[s2][:]),
                     reads=[kx], writes=[f"hb{hq}", f"ss{s2}"])
                P.op("act", lambda e: e.activation(out=rs[s2][:], in_=ss[s2][:], func=AF.Sqrt, scale=1.0 / 1024, bias=1e-6),
                     reads=[f"ss{s2}"], writes=[f"rs{s2}"])
                P.op("dve", lambda e: e.reciprocal(out=rs[s2][:], in_=rs[s2][:]), reads=[f"rs{s2}"], writes=[f"rs{s2}"])
                P.op("dve", lambda e: e.scalar_tensor_tensor(out=hb[hq][:], in0=xt[b][:], scalar=rs[s2][:, 0:1], in1=gmix[:],
                                                             op0=ALU.mult, op1=ALU.mult),
                     reads=[kx, f"rs{s2}", "const2"], writes=[f"hb{hq}"])
                if i == NPT - 1 or is_samp(i):
                    P.op("dve", lambda e: e.scalar_tensor_tensor(out=h32s[:], in0=xt[b][:], scalar=rs[s2][:, 0:1], in1=gmix[:],
                                                                 op0=ALU.mult, op1=ALU.mult),
                         reads=[kx, f"rs{s2}", "const2"], writes=[h32k])
                    if is_samp(i):
                        P.op("sp", lambda e: [e.dma_start(out=shift_s[q:q + 1, :], in_=h32s[8 * q + 7:8 * q + 8, :]) for q in range(16)],
                             reads=[h32k], writes=["o_shift_s"], dma="o_shift_s", n=16)
                        outkeys.append("o_shift_s")
                    else:
                        P.op("sp", lambda e: e.dma_start(out=shift_p[:, :], in_=h32s[127:128, :]), reads=[h32k],
                             writes=["o_shift_p"], dma="o_shift_p")
                        outkeys.append("o_shift_p")

                def tr(e):
                    r = None
                    for kc in range(8):
                        r = e.transpose(out=tpb[:, kc * 128:(kc + 1) * 128], in_=hb[hq][:, kc * 128:(kc + 1) * 128], identity=ident[:])
                    return r
                P.op("pe", tr, reads=[f"hb{hq}", "const"], writes=["tpb"])
                P.op("act", lambda e: e.copy(out=hT[hq][:].rearrange("p a b -> p (a b)"), in_=tpb[:]), reads=["tpb"], writes=[f"hT{hq}"])

            def proj_group(i, chunks, evac):
                hq = i % len(hT)
                bank, bk = nextpj()

                def mm(e):
                    r = None
                    for gi, c in enumerate(chunks):
                        for kc in range(8):
                            r = e.matmul(bank[:, gi * 128:(gi + 1) * 128], lhsT=Win[:, kc, c * 128:(c + 1) * 128],
                                         rhs=hT[hq][:, kc, :], start=(kc == 0), stop=(kc == 7))
                    return r
                P.op("pe", mm, reads=["Win", f"hT{hq}"], writes=[bk])
                evac(bank, bk)

            def S2(i):
                own = is_own(i)
                qq = i % len(qT)
                fq = i % len(fT)
                if own:
                    def ev_q(bank, bk):
                        P.op("act", lambda e: e.activation(out=qT[qq][:].rearrange("p a b -> p (a b)"), in_=bank[:], func=AF.Copy, scale=0.125),
                             reads=[bk], writes=[f"qT{qq}"])
                    proj_group(i, [0, 1, 2, 3], ev_q)
                if own or i == NPRE - 1:
                    k3 = i % NKV

                    def ev_kv(bank, bk):
                        P.op("act", lambda e: e.copy(out=kvT[k3][:].rearrange("p a b -> p (a b)"), in_=bank[:, 0:256]),
                             reads=[bk], writes=[f"kvT{k3}"])
                        if i == NPT - 1 or is_samp(i):
                            P.op("dve", lambda e: e.tensor_copy(out=kvx[:, 0:2, :].rearrange("p a b -> p (a b)"), in_=bank[:, 0:256]),
                                 reads=[bk, f"kvT{k3}"], writes=[kvxk])
                    proj_group(i, [4, 5], ev_kv)
                    if i == NPT - 1 or is_samp(i):
                        bank, bk = nextpj()

                        def trkv(e):
                            e.transpose(out=bank[:, 0:128], in_=kvx[:, 0, :], identity=ident32[:])
                            return e.transpose(out=bank[:, 128:256], in_=kvx[:, 1, :], identity=ident32[:])
                        P.op("pe", trkv, reads=[kvxk, "const2"], writes=[bk])
                        P.op("dve", lambda e: e.tensor_copy(out=kvx[:, 2:4, :].rearrange("p a b -> p (a b)"), in_=bank[:, 0:256]),
                             reads=[bk, kvxk], writes=[kvxk])
                        if is_samp(i):
                            def okv(e):
                                r = []
                                for q in range(16):
                                    r.append(e.dma_start(out=kwin_s[q, 120:128, :], in_=kvx[8 * q:8 * q + 8, 2, :]))
                                    r.append(e.dma_start(out=vwin_s[q, 120:128, :], in_=kvx[8 * q:8 * q + 8, 3, :]))
                                if not NO_D2D:
                                    r.append(e.dma_start(out=kwin_s[:, 0:120, :], in_=ck[:, 8:128, :]))
                                    r.append(e.dma_start(out=vwin_s[:, 0:120, :], in_=cv[:, 8:128, :]))
                                return r
                            P.op("sp", okv, reads=[kvxk], writes=["o_kv_s"], dma="o_kv_s", n=(32 if NO_D2D else 34))
                            outkeys.append("o_kv_s")
                        else:
                            def okv(e):
                                r = []
                                r.append(e.dma_start(out=kwin_p[:, :], in_=kvx[:, 2, :]))
                                r.append(e.dma_start(out=vwin_p[:, :], in_=kvx[:, 3, :]))
                                return r
                            P.op("sp", okv, reads=[kvxk], writes=["o_kv_p"], dma="o_kv_p", n=2)
                            outkeys.append("o_kv_p")
                for gi, chunks in enumerate([[6, 7, 8, 9], [10, 11, 12, 13], [14, 15, 16, 17], [18, 19]]):
                    def ev_f(bank, bk, gi=gi, chunks=chunks):
                        n = len(chunks) * 128
                        dst = fT[fq][:, gi * 4:gi * 4 + len(chunks), :].rearrange("p a b -> p (a b)")
                        if gi % 2 == 0:
                            P.op("act", lambda e: e.copy(out=dst, in_=bank[:, 0:n]), reads=[bk], writes=[f"fT{fq}_{gi}"])
                        else:
                            P.op("dve", lambda e: e.tensor_copy(out=dst, in_=bank[:, 0:n]), reads=[bk], writes=[f"fT{fq}_{gi}"])
                    proj_group(i, chunks, ev_f)
                if is_samp(i):
                    P.op("sp", lambda e: e.dma_start(out=hp32[:], in_=hprev[:, :]), writes=["hp32"], dma="hp32")
                    P.op("dve", lambda e: e.tensor_copy(out=hpb[:], in_=hp32[:]), reads=["hp32"], writes=["hpb"])

                    def trh(e):
                        r = None
                        for kc in range(8):
                            r = e.transpose(out=tpb[:, kc * 16:(kc + 1) * 16], in_=hpb[:, kc * 128:(kc + 1) * 128], identity=ident[0:16, 0:16])
                        return r
                    P.op("pe", trh, reads=["hpb", "const"], writes=["tpb"])
                    P.op("act", lambda e: e.copy(out=hpT[:].rearrange("p a b -> p (a b)"), in_=tpb[:, 0:128]), reads=["tpb"], writes=["hpT"])
                    for half in range(2):
                        bank, bk = nextpj()

                        def mmp(e, half=half, bank=bank):
                            r = None
                            for ci in range(7):
                                c = 6 + half * 7 + ci
                                for kc in range(8):
                                    r = e.matmul(bank[:, ci * 16:(ci + 1) * 16], lhsT=Win[:, kc, c * 128:(c + 1) * 128],
                                                 rhs=hpT[:, kc, :], start=(kc == 0), stop=(kc == 7))
                            return r
                        P.op("pe", mmp, reads=["Win", "hpT"], writes=[bk])
                        P.op("act", lambda e, half=half, bank=bank: e.copy(
                            out=fprev[:, half * 7:(half + 1) * 7, :].rearrange("p a b -> p (a b)"), in_=bank[:, 0:112]),
                            reads=[bk], writes=[f"fprev{half}"])

            def S3(i):
                s3 = i % R2
                own = is_own(i)
                samp = is_samp(i)
                fq = i % len(fT)
                fk = [f"fT{fq}_{g}" for g in range(4)]
                f = fT[fq]
                if samp:
                    f4 = f[:, :, :].rearrange("p c (s t) -> p c s t", t=8)
                    x4 = xsT[:, :, :].rearrange("p c (s t) -> p c s t", t=8)
                    P.op("pool", lambda e: e.tensor_tensor(out=x4[:, :, :, 1:8], in0=f4[:, :, :, 0:7], in1=f4[:, :, :, 1:8], op=ALU.subtract),
                         reads=fk, writes=["xsT"])
                    P.op("pool", lambda e: e.tensor_tensor(out=x4[:, :, :, 0], in0=fprev[:, :, :], in1=f4[:, :, :, 0], op=ALU.subtract),
                         reads=fk, writes=["xsT0"])
                else:
                    P.op("pool", lambda e: e.tensor_tensor(out=xsT[:, :, 1:128], in0=f[:, :, 0:127], in1=f[:, :, 1:128], op=ALU.subtract),
                         reads=fk, writes=["xsT"])
                    P.op("pool", lambda e: e.tensor_tensor(out=xsT[:, :, 0], in0=fcar[:, :], in1=f[:, :, 0], op=ALU.subtract),
                         reads=fk + ["fcar"], writes=["xsT0"])
                    P.op("pool", lambda e: e.tensor_copy(out=fcar[:, :], in_=f[:, :, 127]), reads=fk, writes=["fcar"])
                mu_bc = mu[:, :].unsqueeze(2).broadcast_to([128, 14, 128])
                P.op("dve", lambda e: e.tensor_tensor(out=xsT[:], in0=xsT[:], in1=mu_bc, op=ALU.mult),
                     reads=["xsT", "xsT0", "const2"], writes=["xsT", "xsT0"])
                P.op("pool", lambda e: e.tensor_tensor(out=xsT[:], in0=xsT[:], in1=f[:], op=ALU.add),
                     reads=["xsT", "xsT0"] + fk, writes=["xsT", "xsT0"])
                XK = ["xsT", "xsT0"]
                P.op("act", lambda e: e.activation(out=twal[0:64, :], in_=xsT[0:64, 12, :], func=AF.Tanh), reads=XK, writes=["twal_a"])
                P.op("act", lambda e: e.copy(out=twal[64:128, :], in_=xsT[64:128, 12, :]), reads=XK, writes=["twal_b"])
                P.op("act", lambda e: e.activation(out=sg[:], in_=xsT[:, 13, :], func=AF.Sigmoid), reads=XK, writes=["sg"])
                bw, bwk = nextpj()

                def mmw(e):
                    r = None
                    for cc in range(4):
                        r = e.matmul(bw[:, cc * 128:(cc + 1) * 128], lhsT=w2a2[0:64, cc * 128:(cc + 1) * 128], rhs=twal[0:64, :],
                                     start=True, stop=True)
                    return r
                P.op("pe", mmw, reads=["const", "twal_a"], writes=[bwk])
                for cc in range(4):
                    P.op("act", lambda e, cc=cc: e.activation(out=eT[:, cc, :], in_=bw[:, cc * 128:(cc + 1) * 128], func=AF.Sigmoid,
                                                              bias=vec4[:, W0, cc:cc + 1]),
                         reads=[bwk, "const2"], writes=[f"eT{cc}"])
                ba, bak = nextpj()

                def mma(e):
                    r = None
                    for cc in range(4):
                        r = e.matmul(ba[:, cc * 128:(cc + 1) * 128], lhsT=w2a2[64:128, cc * 128:(cc + 1) * 128], rhs=twal[64:128, :],
                                     start=True, stop=True)
                    return r
                P.op("pe", mma, reads=["const", "twal_b"], writes=[bak])
                for cc in range(4):
                    P.op("act", lambda e, cc=cc: e.activation(out=asT[:, cc, :], in_=ba[:, cc * 128:(cc + 1) * 128], func=AF.Sigmoid,
                                                              bias=vec4[:, A0, cc:cc + 1]),
                         reads=[bak, "const2"], writes=[f"asT{cc}"])
                EK = [f"eT{c}" for c in range(4)]
                AK = [f"asT{c}" for c in range(4)]
                if own:
                    bg, bgk = nextpj()

                    def mmg(e):
                        r = None
                        for cc in range(4):
                            r = e.matmul(bg[:, cc * 128:(cc + 1) * 128], lhsT=g2b[:, cc * 128:(cc + 1) * 128], rhs=sg[:], start=True, stop=True)
                        return r
                    P.op("pe", mmg, reads=["const", "sg"], writes=[bgk])
                    gq = i % len(gT)
                    P.op("act", lambda e: e.copy(out=gT[gq][:].rearrange("p a b -> p (a b)"), in_=bg[:]), reads=[bgk], writes=[f"gT{gq}"])
                P.op("dve", lambda e: e.tensor_tensor(out=kkT[:], in0=xsT[:, 4:8, :], in1=v4bc(KK), op=ALU.mult), reads=XK + ["const2"], writes=["kkT"])
                P.op("dve", lambda e: e.tensor_tensor(out=sqb[:], in0=kkT[:], in1=kkT[:], op=ALU.mult), reads=["kkT"], writes=["sqb"])
                bs, bsk = nextpj()
                P.op("pe", lambda e: e.matmul(bs[:], lhsT=bones[:], rhs=sqb[:].rearrange("p a b -> p (a b)"), start=True, stop=True),
                     reads=["const", "sqb"], writes=[bsk])
                P.op("act", lambda e: e.activation(out=rn[:].rearrange("p a b -> p (a b)"), in_=bs[:], func=AF.Sqrt, bias=1e-12),
                     reads=[bsk], writes=["rn"])
                P.op("dve", lambda e: e.reciprocal(out=rn[:], in_=rn[:]), reads=["rn"], writes=["rn"])
                P.op("dve", lambda e: e.tensor_tensor(out=kkT[:], in0=kkT[:], in1=rn[:], op=ALU.mult), reads=["kkT", "rn"], writes=["kkT"])
                P.op("dve", lambda e: e.scalar_tensor_tensor(out=tmpA[:], in0=asT[:], scalar=-1.0, in1=v4bc(KA), op0=ALU.add, op1=ALU.mult),
                     reads=AK + ["const2"], writes=["tmpA"])
                P.op("dve", lambda e: e.scalar_tensor_tensor(out=kmT[:], in0=tmpA[:], scalar=1.0, in1=xsT[:, 4:8, :], op0=ALU.add, op1=ALU.mult),
                     reads=["tmpA"] + XK, writes=["kmT"])
                rm = rmask[:, 1 if samp else 0, :]
                for cc in range(4):
                    P.op("dve", lambda e, cc=cc: e.tensor_tensor_scan(out=cumE[:, cc, :], data0=rm, data1=eT[:, cc, :], initial=0.0,
                                                                      op0=ALU.mult, op1=ALU.add),
                         reads=[f"eT{cc}", "const2"], writes=[f"cumE{cc}"])
                CK = [f"cumE{c}" for c in range(4)]
                P.op("pool", lambda e: e.tensor_tensor(out=rn[:], in0=cumE[:], in1=eT[:], op=ALU.subtract), reads=CK + EK + ["rn"], writes=["rn"])
                P.op("act", lambda e: e.activation(out=rn[:], in_=rn[:], func=AF.Exp, scale=-C0), reads=["rn"], writes=["rn"])
                P.op("act", lambda e: e.activation(out=Eg[:], in_=cumE[:], func=AF.Exp, scale=-C0), reads=CK, writes=["Eg"])
                P.op("act", lambda e: e.activation(out=cumE[:], in_=cumE[:], func=AF.Exp, scale=C0), reads=CK, writes=CK)
                Egi, Egx = cumE, rn
                if samp:
                    P.op("pool", lambda e: e.tensor_copy(out=gC[s3][:, :, :], in_=Eg[:, :, :].rearrange("p c (s t) -> p c s t", t=8)[:, :, :, 7]),
                         reads=["Eg"], writes=[f"gC{s3}"])
                else:
                    P.op("pool", lambda e: e.tensor_copy(out=gC[s3][:, :, 0], in_=Eg[:, :, 127]), reads=["Eg"], writes=[f"gC{s3}"])
                ARk = f"AR{s3}"
                P.op("dve", lambda e: e.tensor_tensor(out=AR[s3][:, :, 1, :], in0=xsT[:, 0:4, :], in1=Eg[:], op=ALU.mult),
                     reads=XK + ["Eg"], writes=[ARk + "r"])
                P.op("dve", lambda e: e.tensor_tensor(out=KTF[s3][:], in0=kmT[:], in1=Egi[:], op=ALU.mult), reads=["kmT"] + CK, writes=[f"KTF{s3}"])
                P.op("pool", lambda e: e.tensor_tensor(out=tmpA[:], in0=kkT[:], in1=asT[:], op=ALU.mult), reads=["kkT"] + AK, writes=["tmpA"])
                P.op("dve", lambda e: e.tensor_tensor(out=BTF[s3][:], in0=tmpA[:], in1=Egi[:], op=ALU.mult), reads=["tmpA"] + CK, writes=[f"BTF{s3}"])
                P.op("dve", lambda e: e.scalar_tensor_tensor(out=AR[s3][:, :, 0, :], in0=kkT[:], scalar=-1.0, in1=Egx[:], op0=ALU.mult, op1=ALU.mult),
                     reads=["kkT", "rn"], writes=[ARk + "a"])
                P.op("act", lambda e: e.copy(out=vb[:], in_=xsT[:, 8:12, :]), reads=XK, writes=["vb"])
                if own:
                    P.op("pool", lambda e: e.tensor_tensor(out=rn[:], in0=xsT[:, 0:4, :], in1=kmT[:], op=ALU.mult), reads=XK + ["kmT", "rn"], writes=["rn"])
                    P.op("dve", lambda e: e.tensor_tensor(out=sqb[:], in0=rn[:], in1=v4bc(RK), op=ALU.mult), reads=["rn", "const2"], writes=["sqb"])
                    bb, bbk = nextpj()
                    P.op("pe", lambda e: e.matmul(bb[:], lhsT=bones[:], rhs=sqb[:].rearrange("p a b -> p (a b)"), start=True, stop=True),
                         reads=["const", "sqb"], writes=[bbk])
                    P.op("dve", lambda e: e.tensor_tensor(out=bonT[i % len(bonT)][:].rearrange("p a b -> p (a b)"), in0=bb[:],
                                                          in1=xsT[:, 8:12, :].rearrange("p a b -> p (a b)"), op=ALU.mult),
                         reads=[bbk] + XK, writes=[f"bonT{i % len(bonT)}"])

                if pA:
                    gbc = gC[s3][:, :, 0:1].broadcast_to([128, 4, 128])
                    P.op("dve", lambda e: e.tensor_tensor(out=BHF[:], in0=BTF[s3][:], in1=gbc, op=ALU.mult), reads=[f"BTF{s3}", f"gC{s3}"], writes=["BHF"])
                    P.op("dve", lambda e: e.tensor_tensor(out=KHF[:], in0=KTF[s3][:], in1=gbc, op=ALU.mult), reads=[f"KTF{s3}", f"gC{s3}"], writes=["KHF"])
                    bsrc, ksrc, bsk, ksk = BHF, KHF, "BHF", "KHF"
                else:
                    bsrc, ksrc, bsk, ksk = BTF[s3], KTF[s3], f"BTF{s3}", f"KTF{s3}"

                def tr1(e):
                    r = None
                    for cc in range(4):
                        r = e.transpose(out=tpb[:, cc * 128:(cc + 1) * 128], in_=vb[:, cc, :], identity=ident[:])
                    for cc in range(4):
                        r = e.transpose(out=tpb[:, 512 + cc * 128:512 + (cc + 1) * 128], in_=bsrc[:, cc, :], identity=ident[:])
                    return r
                P.op("pe", tr1, reads=["vb", bsk, "const"], writes=["tpb"])
                P.op("act", lambda e: e.copy(out=VTM[s3][:], in_=tpb[:, 0:512]), reads=["tpb"], writes=[f"VTM{s3}"])
                P.op("dve", lambda e: e.tensor_copy(out=BTM[s3][:], in_=tpb[:, 512:1024]), reads=["tpb"], writes=[f"BTM{s3}"])

                def tr2(e):
                    r = None
                    for cc in range(4):
                        r = e.transpose(out=tpb[:, cc * 128:(cc + 1) * 128], in_=ksrc[:, cc, :], identity=ident[:])
                    if pA:
                        for cc in range(4):
                            r = e.transpose(out=tpb[:, 512 + cc * 128:512 + (cc + 1) * 128], in_=AR[s3][:, cc, 0, :], identity=ident[:])
                    return r
                P.op("pe", tr2, reads=[ksk, f"AR{s3}a", "const"], writes=["tpb"])
                P.op("act", lambda e: e.copy(out=KTM[s3][:], in_=tpb[:, 0:512]), reads=["tpb"], writes=[f"KTM{s3}"])
                if pA:
                    P.op("dve", lambda e: e.tensor_copy(out=ATM[s3][:], in_=tpb[:, 512:1024]), reads=["tpb"], writes=[f"ATM{s3}"])

            def nlev(i):
                return 3 if is_samp(i) else 7

            def S4(i):
                s3 = i % R2
                kd = 1 if is_samp(i) else 0
                ARk = [f"AR{s3}a", f"AR{s3}r"]
                for h in range(8):
                    cc, pb = h // 2, (h % 2) * 64
                    bank, bk = nextl0()

                    def mm0(e, cc=cc, pb=pb, bank=bank):
                        e.matmul(bank[:, 0:256], lhsT=BTF[s3][pb:pb + 64, cc, :], rhs=AR[s3][pb:pb + 64, cc, :, :], start=True, stop=True)
                        return e.matmul(bank[:, 256:512], lhsT=KTF[s3][pb:pb + 64, cc, :], rhs=AR[s3][pb:pb + 64, cc, :, :], start=True, stop=True)
                    P.op("pe", mm0, reads=ARk + [f"BTF{s3}", f"KTF{s3}"], writes=[bk])
                    P.op("dve", lambda e, h=h, bank=bank: e.tensor_tensor(out=SQ0[:, h, 0:128], in0=bank[:, 0:128], in1=MKL[:, kd, 0, :], op=ALU.mult),
                         reads=[bk, "MKL"], writes=[f"SQ0n{h}"])
                    P.op("dve", lambda e, h=h, bank=bank: e.tensor_tensor(out=NM[:, h, :], in0=bank[:, 128:512],
                                                                          in1=MKL[:, kd, 1:4, :].rearrange("p a b -> p (a b)"), op=ALU.mult),
                         reads=[bk, "MKL"], writes=[f"NM{h}"])
                for g4 in range(2):
                    bank, bk = nextsq()

                    def mma0(e, g4=g4, bank=bank):
                        r = None
                        for hh in range(4):
                            h = g4 * 4 + hh
                            cc, pb = h // 2, (h % 2) * 64
                            r = e.matmul(bank[:, hh * 128:(hh + 1) * 128], lhsT=AR[s3][pb:pb + 64, cc, 0, :], rhs=BTF[s3][pb:pb + 64, cc, :],
                                         start=True, stop=True)
                        return r
                    P.op("pe", mma0, reads=ARk + [f"BTF{s3}"], writes=[bk])
                    slbc = MK[:, 2 + 3 * kd, :].unsqueeze(1).broadcast_to([128, 4, 128])
                    P.op("dve", lambda e, g4=g4, bank=bank, slbc=slbc: e.tensor_tensor(
                        out=SQ0[:, g4 * 4:(g4 + 1) * 4, 128:256], in0=bank[:].rearrange("p (a b) -> p a b", a=4), in1=slbc, op=ALU.mult),
                        reads=[bk, "const"], writes=[f"SQ0a{g4 * 4 + q}" for q in range(4)])
                nl = nlev(i)
                for j in range(nl - 1):
                    for pr in range(4):
                        bank, bk = nextsq()
                        hs = (2 * pr, 2 * pr + 1)

                        def Nsrc(h, j=j):
                            return SQ0[:, h, 0:128] if j == 0 else NJ[:, (j - 1) % 2, h, :]

                        def Asrc(h, j=j):
                            return SQ0[:, h, 128:256] if j == 0 else AJ[:, (j - 1) % 2, h, :]
                        rk = []
                        for h in hs:
                            rk += ([f"SQ0n{h}", f"SQ0a{h}"] if j == 0 else [f"NJ{(j - 1) % 2}_{h}", f"AJ{(j - 1) % 2}_{h}"])
                        last = (j == nl - 2)

                        def mmsq(e, hs=hs, bank=bank, Nsrc=Nsrc, Asrc=Asrc, last=last):
                            r = None
                            for q, h in enumerate(hs):
                                r = e.matmul(bank[:, q * 256:q * 256 + 128], lhsT=Asrc(h), rhs=Nsrc(h), start=True, stop=True)
                                if not last:
                                    r = e.matmul(bank[:, q * 256 + 128:q * 256 + 256], lhsT=Nsrc(h), rhs=Asrc(h), start=True, stop=True)
                            return r
                        P.op("pe", mmsq, reads=rk, writes=[bk])
                        b3 = bank[:].rearrange("p (a b) -> p a b", a=2)
                        P.op("act", lambda e, j=j, pr=pr, b3=b3: e.copy(out=NJ[:, j % 2, 2 * pr:2 * pr + 2, :], in_=b3[:, :, 0:128]),
                             reads=[bk], writes=[f"NJ{j % 2}_{h}" for h in hs])
                        if not last:
                            P.op("dve", lambda e, j=j, pr=pr, b3=b3: e.tensor_copy(out=AJ[:, j % 2, 2 * pr:2 * pr + 2, :], in_=b3[:, :, 128:256]),
                                 reads=[bk], writes=[f"AJ{j % 2}_{h}" for h in hs])

            def S5(i):
                s3 = i % R2
                own = is_own(i)
                samp = is_samp(i)
                nl = nlev(i)
                ARk = [f"AR{s3}a", f"AR{s3}r"]
                if samp:
                    sample_h0(s3)

                def z0(e):
                    r = None
                    for h in range(8):
                        cc, pb = h // 2, (h % 2) * 64
                        if samp:
                            r = e.matmul(Zp[:, h * 64:(h + 1) * 64], lhsT=zh[pb:pb + 64, cc, 0, :], rhs=ident[pb:pb + 64, pb:pb + 64],
                                         start=(h == 0), stop=False, skip_group_check=True)
                        else:
                            r = e.matmul(Zp[:, h * 64:(h + 1) * 64], lhsT=AR[s3][pb:pb + 64, cc, 0, :], rhs=Hb[pb:pb + 64, cc, :],
                                         start=(h == 0), stop=False, skip_group_check=True)
                        r = e.matmul(Zp[:, h * 64:(h + 1) * 64], lhsT=NM[:, h, 128:256], rhs=VTM[s3][:, h * 64:(h + 1) * 64],
                                     start=False, stop=False, skip_group_check=True)
                    return r
                P.op("pe", z0, reads=ARk + ["Hb", "zh", "const", f"VTM{s3}"] + [f"NM{h}" for h in range(8)], writes=["Zp"])
                for j in range(nl):
                    zb = Zb[j % 2]
                    zk = f"Zb{j % 2}"
                    if j % 2 == 0:
                        P.op("act", lambda e, zb=zb: e.copy(out=zb[:], in_=Zp[:]), reads=["Zp"], writes=[zk])
                    else:
                        P.op("dve", lambda e, zb=zb: e.tensor_copy(out=zb[:], in_=Zp[:]), reads=["Zp"], writes=[zk])
                    rk = [zk] + ([f"SQ0n{h}" for h in range(8)] if j == 0 else [f"NJ{(j - 1) % 2}_{h}" for h in range(8)])

                    def ap(e, j=j, zb=zb):
                        r = None
                        for h in range(8):
                            lt = SQ0[:, h, 0:128] if j == 0 else NJ[:, (j - 1) % 2, h, :]
                            r = e.matmul(Zp[:, h * 64:(h + 1) * 64], lhsT=lt, rhs=zb[:, h * 64:(h + 1) * 64], start=False, stop=(j == nl - 1),
                                         skip_group_check=True)
                        return r
                    P.op("pe", ap, reads=rk, writes=["Zp"])
                P.op("act", lambda e: e.copy(out=Ub[:], in_=Zp[:]), reads=["Zp"], writes=["Ub"])
                if own:
                    ob, obk = nextpj()
                    oq = i % R2

                    def mo(e):
                        r = None
                        for h in range(8):
                            cc, pb = h // 2, (h % 2) * 64
                            o = ob[:, h * 64:(h + 1) * 64]
                            if samp:
                                e.matmul(o, lhsT=zh[pb:pb + 64, cc, 1, :], rhs=ident[pb:pb + 64, pb:pb + 64], start=True, stop=False)
                            else:
                                e.matmul(o, lhsT=AR[s3][pb:pb + 64, cc, 1, :], rhs=Hb[pb:pb + 64, cc, :], start=True, stop=False)
                            e.matmul(o, lhsT=NM[:, h, 0:128], rhs=Ub[:, h * 64:(h + 1) * 64], start=False, stop=False)
                            r = e.matmul(o, lhsT=NM[:, h, 256:384], rhs=VTM[s3][:, h * 64:(h + 1) * 64], start=False, stop=True)
                        return r
                    P.op("pe", mo, reads=ARk + ["Hb", "zh", "const", "Ub", f"VTM{s3}"] + [f"NM{h}" for h in range(8)], writes=[obk])
                    P.op("act", lambda e: e.copy(out=OTM[oq][:].rearrange("p a b -> p (a b)"), in_=ob[:]), reads=[obk], writes=[f"OTM{oq}"])
                if samp:
                    sample_state(s3)
                    return

                def su(e):
                    r = None
                    for h in range(8):
                        cc, pb = h // 2, (h % 2) * 64
                        o = Zp[pb:pb + 64, cc * 64:(cc + 1) * 64]
                        e.matmul(o, lhsT=BTM[s3][:, h * 64:(h + 1) * 64], rhs=Ub[:, h * 64:(h + 1) * 64], start=True, stop=False)
                        r = e.matmul(o, lhsT=KTM[s3][:, h * 64:(h + 1) * 64], rhs=VTM[s3][:, h * 64:(h + 1) * 64], start=False, stop=True)
                    return r
                P.op("pe", su, reads=["Ub", f"BTM{s3}", f"KTM{s3}", f"VTM{s3}"], writes=["Zp"])
                P.op("dve", lambda e: e.tensor_tensor(out=tS[:].rearrange("p a b -> p (a b)"), in0=Zp[:, 0:256],
                                                      in1=Hst[:].rearrange("p a b -> p (a b)"), op=ALU.add),
                     reads=["Zp", "Hst"], writes=["tS"])
                P.op("dve", lambda e: e.tensor_tensor(out=Hst[:], in0=tS[:], in1=gC[s3][:, :, 0:1].broadcast_to([128, 4, 64]), op=ALU.mult),
                     reads=["tS", f"gC{s3}"], writes=["Hst"])
                P.op("act", lambda e: e.copy(out=Hb[:], in_=Hst[:]), reads=["Hst"], writes=["Hb"])
                if i == NPT - 1:
                    bank, bk = nextpj()

                    def trs(e):
                        r = None
                        for cc in range(4):
                            r = e.transpose(out=bank[0:64, cc * 128:(cc + 1) * 128], in_=Hst[:, cc, :], identity=ident32[:])
                        return r
                    P.op("pe", trs, reads=["Hst", "const2"], writes=[bk])
                    P.op("dve", lambda e: e.tensor_copy(out=osq[0:64, :, :].rearrange("p a b -> p (a b)"), in_=bank[0:64, :]), reads=[bk, "osq"], writes=["osq"])
                    P.op("sp", lambda e: e.dma_start(out=wkv_p.rearrange("h v k -> v h k"), in_=osq[0:64, :, :]), reads=["osq"], writes=["o_wkv_p"], dma="o_wkv_p")
                    outkeys.append("o_wkv_p")

            def S4h(i, half):
                s3 = i % R2
                own = is_own(i)
                ARk = [f"AR{s3}a", f"AR{s3}r"]
                hs4 = list(range(4 * half, 4 * half + 4))
                lb_, lbk = l0[half], f"l0{half}"
                sb_, sbk = sqp[half], f"sqp{half}"
                for h in hs4:
                    cc, pb = h // 2, (h % 2) * 64

                    def mm0(e, cc=cc, pb=pb):
                        e.matmul(lb_[:, 0:256], lhsT=BTF[s3][pb:pb + 64, cc, :], rhs=AR[s3][pb:pb + 64, cc, :, :], start=True, stop=True)
                        return e.matmul(lb_[:, 256:512], lhsT=KTF[s3][pb:pb + 64, cc, :], rhs=AR[s3][pb:pb + 64, cc, :, :], start=True, stop=True)
                    P.op("pe", mm0, reads=ARk + [f"BTF{s3}", f"KTF{s3}"], writes=[lbk])
                    P.op("dve", lambda e, h=h: e.tensor_tensor(out=L0S[:, h, :], in0=lb_[:],
                                                               in1=MKL[:, 0, :, :].rearrange("p a b -> p (a b)"), op=ALU.mult),
                         reads=[lbk, "MKL"], writes=[f"L0S{h}"])

                def mma0(e):
                    r = None
                    for hh, h in enumerate(hs4):
                        cc, pb = h // 2, (h % 2) * 64
                        r = e.matmul(sb_[:, hh * 128:(hh + 1) * 128], lhsT=AR[s3][pb:pb + 64, cc, 0, :], rhs=BTF[s3][pb:pb + 64, cc, :],
                                     start=True, stop=True)
                    return r
                P.op("pe", mma0, reads=ARk + [f"BTF{s3}"], writes=[sbk])
                slbc = MK[:, 2, :].unsqueeze(1).broadcast_to([128, 4, 128])
                P.op("dve", lambda e: e.tensor_tensor(out=A0S[:, 4 * half:4 * half + 4, :], in0=sb_[:].rearrange("p (a b) -> p a b", a=4), in1=slbc, op=ALU.mult),
                     reads=[sbk, "const"], writes=[f"A0S{h}" for h in hs4])

                def x0(e):
                    r = None
                    for hh, h in enumerate(hs4):
                        e.matmul(lb_[:, hh * 128:hh * 128 + 64], lhsT=ident[:], rhs=ATM[s3][:, h * 64:(h + 1) * 64],
                                 start=(hh == 0), stop=False, skip_group_check=True)
                        r = e.matmul(lb_[:, hh * 128 + 64:(hh + 1) * 128], lhsT=L0S[:, h, 256:384], rhs=VTM[s3][:, h * 64:(h + 1) * 64],
                                     start=False, stop=False, skip_group_check=True)
                    return r
                P.op("pe", x0, reads=["const", f"ATM{s3}", f"VTM{s3}"] + [f"L0S{h}" for h in hs4], writes=[lbk])
                for j in range(7):
                    xb = Xb[j % 2]
                    xk = f"Xb{j % 2}_{half}"
                    if half == 0:
                        P.op("act", lambda e, xb=xb: e.copy(out=xb[:, half, :], in_=lb_[:]), reads=[lbk], writes=[xk])
                    else:
                        P.op("dve", lambda e, xb=xb: e.tensor_copy(out=xb[:, half, :], in_=lb_[:]), reads=[lbk], writes=[xk])
                    if j < 6:
                        last = (j == 5)
                        for pr in (2 * half, 2 * half + 1):
                            hs = (2 * pr, 2 * pr + 1)

                            def Nsrc(h, j=j):
                                return L0S[:, h, 0:128] if j == 0 else NA[:, (j - 1) % 2, h, 0:128]

                            def Asrc(h, j=j):
                                return A0S[:, h, :] if j == 0 else NA[:, (j - 1) % 2, h, 128:256]
                            rk = []
                            for h in hs:
                                rk += ([f"L0S{h}", f"A0S{h}"] if j == 0 else [f"NA{(j - 1) % 2}_{h}"])

                            def mmsq(e, hs=hs, Nsrc=Nsrc, Asrc=Asrc, last=last):
                                r = None
                                for q, h in enumerate(hs):
                                    r = e.matmul(sb_[:, q * 256:q * 256 + 128], lhsT=Asrc(h), rhs=Nsrc(h), start=True, stop=True)
                                    if not last:
                                        r = e.matmul(sb_[:, q * 256 + 128:q * 256 + 256], lhsT=Nsrc(h), rhs=Asrc(h), start=True, stop=True)
                                return r
                            P.op("pe", mmsq, reads=rk, writes=[sbk])
                            dstna = NA[:, j % 2, 2 * pr:2 * pr + 2, :].rearrange("p a b -> p (a b)")
                            if pr % 2 == 0:
                                P.op("act", lambda e, dstna=dstna: e.copy(out=dstna, in_=sb_[:]), reads=[sbk], writes=[f"NA{j % 2}_{h}" for h in hs])
                            else:
                                P.op("dve", lambda e, dstna=dstna: e.tensor_copy(out=dstna, in_=sb_[:]), reads=[sbk], writes=[f"NA{j % 2}_{h}" for h in hs])
                    rk = [xk] + ([f"L0S{h}" for h in hs4] if j == 0 else [f"NA{(j - 1) % 2}_{h}" for h in hs4])

                    def ap(e, j=j, xb=xb):
                        r = None
                        for hh, h in enumerate(hs4):
                            lt = L0S[:, h, 0:128] if j == 0 else NA[:, (j - 1) % 2, h, 0:128]
                            r = e.matmul(lb_[:, hh * 128:(hh + 1) * 128], lhsT=lt, rhs=xb[:, half, hh * 128:(hh + 1) * 128],
                                         start=False, stop=(j == 6), skip_group_check=True)
                        return r
                    P.op("pe", ap, reads=rk + [lbk], writes=[lbk])
                wk = f"WY{half}"
                if half == 0:
                    P.op("act", lambda e: e.copy(out=WY[:, 0:4, :].rearrange("p a b -> p (a b)"), in_=lb_[:]), reads=[lbk], writes=[wk])
                else:
                    P.op("dve", lambda e: e.tensor_copy(out=WY[:, 4:8, :].rearrange("p a b -> p (a b)"), in_=lb_[:]), reads=[lbk], writes=[wk])

                def mmt(e):
                    r = None
                    for h in hs4:
                        cc, pb = h // 2, (h % 2) * 64
                        r = e.matmul(sb_[pb:pb + 64, cc * 64:(cc + 1) * 64], lhsT=WY[:, h, 0:64], rhs=BTM[s3][:, h * 64:(h + 1) * 64], start=True, stop=True)
                    return r
                P.op("pe", mmt, reads=[wk, f"BTM{s3}"], writes=[sbk])
                P.op("act", lambda e: e.copy(out=MTb[:, 2 * half:2 * half + 2, :].rearrange("p a b -> p (a b)"), in_=sb_[:, 128 * half:128 * half + 128]),
                     reads=[sbk], writes=[f"MTb{half}"])
                if own:
                    wb16 = sb_[:].bitcast(BF16)

                    def trw(e):
                        r = None
                        for h in hs4:
                            cc, pb = h // 2, (h % 2) * 64
                            r = e.transpose(out=wb16[pb:pb + 64, cc * 128:(cc + 1) * 128], in_=WY[:, h, 0:64], identity=ident[:])
                        return r
                    P.op("pe", trw, reads=[wk, "const"], writes=[sbk])
                    P.op("act", lambda e: e.copy(out=WTF[:, 2 * half:2 * half + 2, :].rearrange("p a b -> p (a b)"), in_=wb16[:, 256 * half:256 * half + 256]),
                         reads=[sbk], writes=[f"WTF{half}"])

            def S4tail(i):
                s3 = i % R2

                def gp(e):
                    r = None
                    for h in range(8):
                        cc, pb = h // 2, (h % 2) * 64
                        o = Zp[pb:pb + 64, cc * 64:(cc + 1) * 64]
                        e.matmul(o, lhsT=BTM[s3][:, h * 64:(h + 1) * 64], rhs=WY[:, h, 64:128], start=(h < 2), stop=False, skip_group_check=True)
                        r = e.matmul(o, lhsT=KTM[s3][:, h * 64:(h + 1) * 64], rhs=VTM[s3][:, h * 64:(h + 1) * 64], start=False, stop=False, skip_group_check=True)
                    return r
                P.op("pe", gp, reads=["WY0", "WY1", f"BTM{s3}", f"KTM{s3}", f"VTM{s3}"], writes=["Zp"])

            def S5n(i):
                s3 = i % R2
                own = is_own(i)
                ARk = [f"AR{s3}a", f"AR{s3}r"]
                WK = ["WY0", "WY1"]
                if own:
                    ub, ubk = nextpj()

                    def mu_(e):
                        r = None
                        for h in range(8):
                            cc, pb = h // 2, (h % 2) * 64
                            o = ub[:, h * 64:(h + 1) * 64]
                            e.matmul(o, lhsT=WTF[pb:pb + 64, cc, :], rhs=Hb[pb:pb + 64, cc, :], start=True, stop=False)
                            r = e.matmul(o, lhsT=ident[:], rhs=WY[:, h, 64:128], start=False, stop=True)
                        return r
                    P.op("pe", mu_, reads=WK + ["WTF0", "WTF1", "Hb", "const"], writes=[ubk])
                    P.op("act", lambda e: e.copy(out=Ub[:], in_=ub[:]), reads=[ubk], writes=["Ub"])
                    ob, obk = nextpj()
                    oq = i % R2

                    def mo(e):
                        r = None
                        for h in range(8):
                            cc, pb = h // 2, (h % 2) * 64
                            o = ob[:, h * 64:(h + 1) * 64]
                            e.matmul(o, lhsT=AR[s3][pb:pb + 64, cc, 1, :], rhs=Hb[pb:pb + 64, cc, :], start=True, stop=False)
                            e.matmul(o, lhsT=L0S[:, h, 128:256], rhs=Ub[:, h * 64:(h + 1) * 64], start=False, stop=False)
                            r = e.matmul(o, lhsT=L0S[:, h, 384:512], rhs=VTM[s3][:, h * 64:(h + 1) * 64], start=False, stop=True)
                        return r
                    P.op("pe", mo, reads=ARk + ["Hb", "Ub", f"VTM{s3}"] + [f"L0S{h}" for h in range(8)], writes=[obk])
                    P.op("act", lambda e: e.copy(out=OTM[oq][:].rearrange("p a b -> p (a b)"), in_=ob[:]), reads=[obk], writes=[f"OTM{oq}"])

                def ch(e):
                    r = None
                    for h in range(8):
                        cc, pb = h // 2, (h % 2) * 64
                        r = e.matmul(Zp[pb:pb + 64, cc * 64:(cc + 1) * 64], lhsT=MTb[pb:pb + 64, cc, :], rhs=Hb[pb:pb + 64, cc, :],
                                     start=False, stop=True, skip_group_check=True)
                    return r
                P.op("pe", ch, reads=["MTb0", "MTb1", "Hb", "Zp"], writes=["Zp"])
                P.op("dve", lambda e: e.tensor_tensor(out=Hst[:].rearrange("p a b -> p (a b)"), in0=Zp[:, 0:256],
                                                      in1=tS[:].rearrange("p a b -> p (a b)"), op=ALU.add),
                     reads=["Zp", "tS"], writes=["Hst"])
                P.op("act", lambda e: e.copy(out=Hb[:], in_=Hst[:]), reads=["Hst"], writes=["Hb"])
                if i + 1 < NPT:
                    n3 = (i + 1) % R2
                    P.op("pool", lambda e: e.tensor_tensor(out=tS[:], in0=Hst[:], in1=gC[n3][:, :, 0:1].broadcast_to([128, 4, 64]), op=ALU.mult),
                         reads=["Hst", f"gC{n3}"], writes=["tS"])
                if i == NPT - 1:
                    bank, bk = nextpj()

                    def trs(e):
                        r = None
                        for cc in range(4):
                            r = e.transpose(out=bank[0:64, cc * 128:(cc + 1) * 128], in_=Hst[:, cc, :], identity=ident32[:])
                        return r
                    P.op("pe", trs, reads=["Hst", "const2"], writes=[bk])
                    P.op("dve", lambda e: e.tensor_copy(out=osq[0:64, :, :].rearrange("p a b -> p (a b)"), in_=bank[0:64, :]), reads=[bk, "osq"], writes=["osq"])
                    P.op("sp", lambda e: e.dma_start(out=wkv_p.rearrange("h v k -> v h k"), in_=osq[0:64, :, :]), reads=["osq"], writes=["o_wkv_p"], dma="o_wkv_p")
                    outkeys.append("o_wkv_p")

            def sample_h0(s3):
                for q4 in range(4):
                    P.op("sp", lambda e, q4=q4: e.dma_start(out=S0q[:], in_=swkv[q4 * 4:(q4 + 1) * 4].rearrange("s h v k -> v (s h) k")),
                         writes=["S0q"], dma="S0q")
                    for sl_ in range(4):
                        s = q4 * 4 + sl_
                        bank, bk = nextpj()

                        def trs(e, sl_=sl_, bank=bank):
                            r = None
                            for cc in range(4):
                                r = e.transpose(out=bank[:, cc * 64:(cc + 1) * 64],
                                                in_=S0q[:, sl_ * 8 + 2 * cc:sl_ * 8 + 2 * cc + 2, :].rearrange("p a b -> p (a b)"),
                                                identity=ident32[0:64, 0:64])
                            return r
                        P.op("pe", trs, reads=["S0q", "const2"], writes=[bk])
                        P.op("dve", lambda e, s=s, bank=bank: e.tensor_copy(out=H0s[:, s, :, :].rearrange("p a b -> p (a b)"), in_=bank[:, 0:256]),
                             reads=[bk], writes=[f"H0s{s}"])
                        P.op("act", lambda e, s=s, bank=bank: e.copy(out=H0b[:, s, :, :].rearrange("p a b -> p (a b)"), in_=bank[:, 0:256]),
                             reads=[bk], writes=[f"H0b{s}"])
                for cc in range(4):
                    bank, bk = nextpj()

                    def mmz(e, cc=cc, bank=bank):
                        r = None
                        for hh in range(2):
                            pb = hh * 64
                            for s in range(16):
                                r = e.matmul(bank[pb:pb + 64, s * 16:s * 16 + 16],
                                             lhsT=H0b[pb:pb + 64, s, cc, :],
                                             rhs=AR[s3][pb:pb + 64, cc, :, s * 8:(s + 1) * 8], start=True, stop=True)
                        return r
                    P.op("pe", mmz, reads=[f"H0b{s}" for s in range(16)] + [f"AR{s3}a", f"AR{s3}r"], writes=[bk])
                    src = bank[:, 0:256].rearrange("p (s a t) -> p a s t", s=16, a=2)
                    for ar in range(2):
                        dst = zh[:, cc, ar, :].rearrange("p (s t) -> p s t", t=8)
                        P.op("dve", lambda e, src=src, dst=dst, ar=ar: e.tensor_copy(out=dst, in_=src[:, ar, :, :]), reads=[bk], writes=["zh"])

            def sample_state(s3):
                e16bc = e16[:, :].unsqueeze(2).broadcast_to([128, 16, 64])
                HK = [f"H0s{s}" for s in range(16)]
                for h in range(8):
                    cc, pb = h // 2, (h % 2) * 64
                    P.op("dve", lambda e, h=h: e.tensor_tensor(out=Xex[:, 0, :, :], in0=Ub[:, h * 64:(h + 1) * 64].unsqueeze(1).broadcast_to([128, 16, 64]),
                                                               in1=e16bc, op=ALU.mult),
                         reads=["Ub", "const"], writes=["Xex0"])
                    P.op("pool", lambda e, h=h: e.tensor_tensor(out=Xex[:, 1, :, :], in0=VTM[s3][:, h * 64:(h + 1) * 64].unsqueeze(1).broadcast_to([128, 16, 64]),
                                                                in1=e16bc, op=ALU.mult),
                         reads=[f"VTM{s3}", "const"], writes=["Xex1"])
                    for half in range(2):
                        bank, bk = nextl0()

                        def mms(e, h=h, half=half, bank=bank, pb=pb):
                            e.matmul(bank[pb:pb + 64, :], lhsT=BTM[s3][:, h * 64:(h + 1) * 64],
                                     rhs=Xex[:, 0, half * 8:(half + 1) * 8, :].rearrange("p a b -> p (a b)"), start=True, stop=False)
                            return e.matmul(bank[pb:pb + 64, :], lhsT=KTM[s3][:, h * 64:(h + 1) * 64],
                                            rhs=Xex[:, 1, half * 8:(half + 1) * 8, :].rearrange("p a b -> p (a b)"), start=False, stop=True)
                        P.op("pe", mms, reads=["Xex0", "Xex1", f"BTM{s3}", f"KTM{s3}"], writes=[bk])
                        P.op("dve", lambda e, half=half, cc=cc, bank=bank, pb=pb: e.tensor_tensor(
                            out=H0s[pb:pb + 64, half * 8:(half + 1) * 8, cc, :], in0=bank[pb:pb + 64, :].rearrange("p (s v) -> p s v", s=8),
                            in1=H0s[pb:pb + 64, half * 8:(half + 1) * 8, cc, :], op=ALU.add),
                            reads=[bk] + HK, writes=HK)
                for cc in range(4):
                    P.op("dve", lambda e, cc=cc: e.tensor_tensor(out=H0s[:, :, cc, :], in0=H0s[:, :, cc, :],
                                                                 in1=gC[s3][:, cc, :].unsqueeze(2).broadcast_to([128, 16, 64]), op=ALU.mult),
                         reads=HK + [f"gC{s3}"], writes=HK)
                for q4 in range(4):
                    for sl_ in range(4):
                        s = q4 * 4 + sl_
                        bank, bk = nextpj()

                        def trw(e, s=s, bank=bank):
                            r = None
                            for cc in range(4):
                                r = e.transpose(out=bank[0:64, cc * 128:(cc + 1) * 128], in_=H0s[:, s, cc, :], identity=ident32[:])
                            return r
                        P.op("pe", trw, reads=HK + ["const2"], writes=[bk])
                        if s % 2 == 0:
                            P.op("act", lambda e, sl_=sl_, bank=bank: e.copy(out=wso[:, sl_, :, :].rearrange("p a b -> p (a b)"), in_=bank[0:64, :]),
                                 reads=[bk], writes=[f"wso{sl_}"])
                        else:
                            P.op("dve", lambda e, sl_=sl_, bank=bank: e.tensor_copy(out=wso[:, sl_, :, :].rearrange("p a b -> p (a b)"), in_=bank[0:64, :]),
                                 reads=[bk], writes=[f"wso{sl_}"])
                    P.op("sp", lambda e, q4=q4: e.dma_start(out=wkv_s[q4 * 4:(q4 + 1) * 4].rearrange("s h v k -> v s h k"), in_=wso[:]),
                         reads=[f"wso{q}" for q in range(4)], writes=[f"o_wkv_s{q4}"] + [f"wso{q}" for q in range(4)], dma="o_wkv_s")
                    outkeys.append(f"o_wkv_s{q4}")

            def S6(i):
                qq = i % len(qT)
                s2 = i % R2
                samp = is_samp(i)
                kc3 = i % NKV
                kp3 = (i - 1) % NKV
                if i == NPRE:
                    P.op("pe", lambda e: e.transpose(out=tpb[:, 0:128], in_=kvT[kp3][:, 1, :], identity=ident[:]), reads=[f"kvT{kp3}", "const"], writes=["tpb"])
                    P.op("act", lambda e: e.copy(out=Vaug[kp3][:, :, 0:64], in_=tpb[:, 0:128].rearrange("p (g d) -> p g d", g=2)),
                         reads=["tpb"], writes=[f"Vaug{kp3}"])
                P.op("pe", lambda e: e.transpose(out=tpb[:, 0:128], in_=kvT[kc3][:, 1, :], identity=ident[:]), reads=[f"kvT{kc3}", "const"], writes=["tpb"])
                P.op("act", lambda e: e.copy(out=Vaug[kc3][:, :, 0:64], in_=tpb[:, 0:128].rearrange("p (g d) -> p g d", g=2)),
                     reads=["tpb"], writes=[f"Vaug{kc3}"])
                if samp:
                    sample_cache(qq)
                for g in range(2):
                    for kt in range(2):
                        if samp and kt == 0:
                            continue
                        bank, bk = nextl0()
                        kb = kvT[kp3] if kt == 0 else kvT[kc3]
                        kbk = f"kvT{kp3}" if kt == 0 else f"kvT{kc3}"
                        P.op("pe", lambda e, bank=bank, kb=kb, g=g: e.matmul(bank[:], lhsT=kb[g * 64:(g + 1) * 64, 0, :],
                                                                             rhs=qT[qq][g * 64:(g + 1) * 64, :, :], start=True, stop=True),
                             reads=[kbk, f"qT{qq}"], writes=[bk])
                        pt = PT[g * 2 + kt]
                        ptk = f"PT{g * 2 + kt}"
                        P.op("act", lambda e, bank=bank, pt=pt: e.activation(out=pt[:].rearrange("p a b -> p (a b)"), in_=bank[:], func=AF.Exp),
                             reads=[bk], writes=[ptk])
                        if kt == 0:
                            mi = 6 if i == NPRE else 2
                        else:
                            mi = 4 if samp else 1
                        mbc = MK[:, mi, :].unsqueeze(1).broadcast_to([128, 4, 128])
                        P.op("dve", lambda e, pt=pt, mbc=mbc: e.tensor_tensor(out=pt[:], in0=pt[:], in1=mbc, op=ALU.mult),
                             reads=[ptk, "const"], writes=[ptk])
                for g in range(2):
                    bank, bk = nextsq()
                    b3 = bank[:, 0:260].rearrange("p (a b) -> p a b", a=4)
                    for j in range(4):
                        h = g * 4 + j
                        if samp:
                            px = PTx[h % 2]
                            pxk = f"PTx{h % 2}"
                            P.op("dve", lambda e, g=g, j=j, px=px: [e.tensor_tensor(
                                out=px[:, s, s * 8:(s + 1) * 8], in0=PTc[:, g, s, j * 8:(j + 1) * 8], in1=MK[:, 2, 0:8], op=ALU.mult) for s in range(16)][-1],
                                reads=[f"PTc{g}", "const"], writes=[pxk])

                        def pv(e, g=g, j=j, b3=b3, h=h):
                            r = None
                            if samp:
                                for s in range(16):
                                    e.matmul(b3[:, j, :], lhsT=PTx[h % 2][:, s, :], rhs=Vca[:, s, g, :], start=(s == 0), stop=False)
                            else:
                                e.matmul(b3[:, j, :], lhsT=PT[g * 2][:, j, :], rhs=Vaug[kp3][:, g, :], start=True, stop=False)
                            r = e.matmul(b3[:, j, :], lhsT=PT[g * 2 + 1][:, j, :], rhs=Vaug[kc3][:, g, :], start=False, stop=True)
                            return r
                        rd = [f"PT{g * 2 + 1}", f"Vaug{kc3}"] + ([f"PTx{h % 2}", "Vca"] if samp else [f"PT{g * 2}", f"Vaug{kp3}"])
                        P.op("pe", pv, reads=rd, writes=[bk + f"_{j}"] + ([bk] if j == 0 else []))
                    bkj = [bk] + [bk + f"_{j}" for j in range(4)]
                    P.op("dve", lambda e, g=g, b3=b3: e.tensor_tensor(out=den[:, g * 4:(g + 1) * 4], in0=b3[:, :, 64], in1=esink[:, g * 4:(g + 1) * 4], op=ALU.add),
                         reads=bkj + ["esink"], writes=[f"den{g}"])
                    P.op("dve", lambda e, g=g: e.reciprocal(out=den[:, g * 4:(g + 1) * 4], in_=den[:, g * 4:(g + 1) * 4]), reads=[f"den{g}"], writes=[f"den{g}"])
                    P.op("dve", lambda e, g=g, b3=b3: e.tensor_tensor(out=attb[:, g * 4:(g + 1) * 4, :], in0=b3[:, :, 0:64],
                                                                      in1=den[:, g * 4:(g + 1) * 4].unsqueeze(2).broadcast_to([128, 4, 64]), op=ALU.mult),
                         reads=bkj + [f"den{g}"], writes=[f"attb{g}"])

                def tra(e):
                    r = None
                    for c in range(4):
                        r = e.transpose(out=tpb[:, c * 128:(c + 1) * 128], in_=attb[:, 2 * c:2 * c + 2, :].rearrange("p a b -> p (a b)"), identity=ident[:])
                    return r
                P.op("pe", tra, reads=["attb0", "attb1", "const"], writes=["tpb"])
                P.op("act", lambda e: e.copy(out=mixT[s2][:, 0:4, :].rearrange("p a b -> p (a b)"), in_=tpb[:, 0:512]), reads=["tpb"], writes=[f"mixT{s2}a"])

            def sample_cache_loads():
                P.op("pool", lambda e: e.dma_start(out=cstb[:], in_=ck.rearrange("s k d -> k s d")), writes=["cstb"], dma="cstb")
                P.op("pool", lambda e: [e.dma_start(out=Vca[:, :, g, 0:64], in_=cv[:, :, g * 64:(g + 1) * 64].rearrange("s k d -> k s d")) for g in range(2)],
                     reads=["Vca"], writes=["Vca"], dma="Vca", n=2)

            def sample_cache(qq):
                for q in range(2):
                    def trc(e, q=q):
                        r = None
                        for ss_ in range(8):
                            r = e.transpose(out=tpb[:, ss_ * 128:(ss_ + 1) * 128], in_=cstb[:, q * 8 + ss_, :], identity=ident[:])
                        return r
                    P.op("pe", trc, reads=["cstb", "const"], writes=["tpb"])
                    P.op("act", lambda e, q=q: e.copy(out=KcT[:, q * 8:(q + 1) * 8, :].rearrange("p a b -> p (a b)"), in_=tpb[:]), reads=["tpb"], writes=[f"KcT{q}"])
                for g in range(2):
                    bank, bk = nextl0()

                    def scc(e, g=g, bank=bank):
                        r = None
                        for s in range(16):
                            r = e.matmul(bank[:, s * 32:(s + 1) * 32], lhsT=KcT[g * 64:(g + 1) * 64, s, :],
                                         rhs=qT[qq][g * 64:(g + 1) * 64, :, s * 8:(s + 1) * 8], start=True, stop=True)
                        return r
                    P.op("pe", scc, reads=["KcT0", "KcT1", f"qT{qq}"], writes=[bk])
                    P.op("act", lambda e, g=g, bank=bank: e.activation(out=PTc[:, g, :, :].rearrange("p a b -> p (a b)"), in_=bank[:], func=AF.Exp),
                         reads=[bk], writes=[f"PTc{g}"])

            def S7(i):
                s2 = i % R2
                s3 = i % len(gT)
                j = i - NPRE
                o = OTM[i % R2]
                ok = f"OTM{i % R2}"
                P.op("dve", lambda e: e.tensor_reduce(out=st1[:], in_=o[:], axis=AX.X, op=ALU.add), reads=[ok], writes=["st1"])
                P.op("pool", lambda e: e.tensor_tensor(out=osq[:], in0=o[:], in1=o[:], op=ALU.mult), reads=[ok], writes=["osq"])
                P.op("dve", lambda e: e.tensor_reduce(out=st2[:], in_=osq[:], axis=AX.X, op=ALU.add), reads=["osq"], writes=["st2"])
                P.op("dve", lambda e: e.tensor_scalar(out=st1[:], in0=st1[:], scalar1=1.0 / 64, scalar2=None, op0=ALU.mult), reads=["st1"], writes=["st1"])
                P.op("dve", lambda e: e.tensor_tensor(out=st3[:], in0=st1[:], in1=st1[:], op=ALU.mult), reads=["st1"], writes=["st3"])
                P.op("dve", lambda e: e.scalar_tensor_tensor(out=st2[:], in0=st2[:], scalar=1.0 / 64, in1=st3[:], op0=ALU.mult, op1=ALU.subtract),
                     reads=["st2", "st3"], writes=["st2"])
                P.op("act", lambda e: e.activation(out=st2[:], in_=st2[:], func=AF.Sqrt, bias=64e-5), reads=["st2"], writes=["st2"])
                P.op("dve", lambda e: e.reciprocal(out=st2[:], in_=st2[:]), reads=["st2"], writes=["st2"])
                P.op("dve", lambda e: e.tensor_tensor(out=osq[:], in0=o[:], in1=st1[:, :].unsqueeze(2).broadcast_to([128, 8, 64]), op=ALU.subtract),
                     reads=[ok, "st1", "osq"], writes=["osq"])
                P.op("dve", lambda e: e.tensor_tensor(out=onb[:], in0=osq[:], in1=st2[:, :].unsqueeze(2).broadcast_to([128, 8, 64]), op=ALU.mult),
                     reads=["osq", "st2"], writes=["onb"])

                def tro(e):
                    r = None
                    for c in range(4):
                        r = e.transpose(out=tpb[:, c * 128:(c + 1) * 128], in_=onb[:, 2 * c:2 * c + 2, :].rearrange("p a b -> p (a b)"), identity=ident[:])
                    return r
                P.op("pe", tro, reads=["onb", "const"], writes=["tpb"])
                P.op("dve", lambda e: e.tensor_tensor(out=tmx[:], in0=tpb[:, 0:512].rearrange("p (a b) -> p a b", a=4), in1=v4bc(GNG), op=ALU.mult),
                     reads=["tpb", "const2", "tmx"], writes=["tmx"])
                P.op("pool", lambda e: e.tensor_tensor(out=tmx[:], in0=tmx[:], in1=v4bc(GNB), op=ALU.add), reads=["tmx", "const2"], writes=["tmx"])
                P.op("pool", lambda e: e.tensor_tensor(out=tmx[:], in0=tmx[:], in1=bonT[s3][:], op=ALU.add), reads=["tmx", f"bonT{s3}"], writes=["tmx"])
                P.op("dve", lambda e: e.tensor_tensor(out=mixT[s2][:, 4:8, :], in0=tmx[:], in1=gT[s3][:], op=ALU.mult),
                     reads=["tmx", f"gT{s3}"], writes=[f"mixT{s2}b"])
                xb = x1t[0]
                xk = "x1t0"
                src = xs[:, :] if is_samp(i) else xw[i * 128:(i + 1) * 128, :]
                P.op("sp", lambda e: e.dma_start(out=xb[:], in_=src), writes=[xk], dma=xk)
                for half in range(2):
                    bank, bk = nextpj()

                    def mo(e, half=half, bank=bank):
                        r = None
                        for kc in range(8):
                            r = e.matmul(bank[:], lhsT=mixT[s2][:, kc, :], rhs=Wout[:, kc, half * 512:(half + 1) * 512], start=(kc == 0), stop=(kc == 7))
                        return r
                    P.op("pe", mo, reads=[f"mixT{s2}a", f"mixT{s2}b", "Wout"], writes=[bk])
                    P.op("dve", lambda e, half=half, bank=bank: e.tensor_tensor(out=xb[:, half * 512:(half + 1) * 512], in0=bank[:],
                                                                                in1=xb[:, half * 512:(half + 1) * 512], op=ALU.add),
                         reads=[bk, xk], writes=[xk])
                P.op("sp", lambda e: e.dma_start(out=x1s[j * 128:(j + 1) * 128, :], in_=xb[:]), reads=[xk], writes=[f"x1s{j}", xk], dma=xk)
                outkeys.append(f"x1s{j}")

            if pA:
                def cap(fns):
                    P.cap = []
                    for fn, i in fns:
                        if 0 <= i < NPT:
                            fn(i)
                    out = P.cap
                    P.cap = None
                    return out

                for step in range(NPT + 6):
                    for fn, i in ((S7, step - 5), (S6, step - 4)):
                        if 0 <= i < NPT and is_own(i):
                            fn(i)
                    if 0 <= step - 4 < NPT:
                        S5n(step - 4)
                    lists = [cap([(lambda i: S4h(i, 0), step - 3)]), cap([(lambda i: S4h(i, 1), step - 3)]),
                             cap([(S3, step - 2), (S2, step - 1), (S1, step)])]
                    chs = [ILV_CH, ILV_CH, 1]
                    pos = [0, 0, 0]
                    while any(pos[q] < len(lists[q]) for q in range(3)):
                        cand = [q for q in range(3) if pos[q] < len(lists[q])]
                        q = min(cand, key=lambda q: pos[q] / len(lists[q]))
                        for _ in range(chs[q]):
                            if pos[q] < len(lists[q]):
                                o = lists[q][pos[q]]
                                P.op(o[0], o[1], reads=o[2], writes=o[3], dma=o[4], n=o[5])
                                pos[q] += 1
                    if 0 <= step - 3 < NPT:
                        S4tail(step - 3)
            elif pSa:
                S1(NPT)
                S2(NPT)
                outkeys.extend(["qT0", "kvT0", "fprev0", "fprev1"] + [f"fT0_{g}" for g in range(4)])
            else:
                sample_cache_loads()
                for fn in (S3, S4, S5, S6, S7):
                    fn(NPT)
            P.op("sp", None, reads=list(outkeys))
            P.emit()
            build.stats[mode] = P.stats

    if "A" in PHASES:
        phase("A", None)
    with ExitStack() as stp:
        per = dict(
            fT=[stp.enter_context(nc.sbuf_tensor("fTs", [128, 14, 128], F32))],
            qT=[stp.enter_context(nc.sbuf_tensor("qTs", [128, 4, 128], BF16))],
            kvT=[stp.enter_context(nc.sbuf_tensor("kvTs", [128, 2, 128], BF16))],
            fprev=stp.enter_context(nc.sbuf_tensor("fprevs", [128, 14, 16], F32)),
        )
        if "Sa" in PHASES:
            phase("Sa", per)
        if "Sb" in PHASES:
            phase("Sb", per)

    if "B" not in PHASES:
        return nc
    with ExitStack() as st:
        def sb(name, shape, dt=F32):
            return st.enter_context(nc.sbuf_tensor(name, shape, dt))

        def psb(name, shape, dt=F32):
            return st.enter_context(nc.psum_tensor(name, shape, dt))
        P = Prog(nc, '_B')
        outk = []
        Wg = sb("Wg", [128, 8, D_FF], BF16)
        Wu = sb("Wu", [128, 8, D_FF], BF16)
        Wd = sb("Wd", [128, NFC, 1024], BF16)
        gffn = sb("gffn", [128, 1024])
        gfin = sb("gfin", [128, 1024])
        identb = sb("identb", [128, 128], BF16)

        P.op("pool", lambda e: e.dma_start(out=identb[:], in_=ident_d[:, :]), writes=["W"], dma="Wi")
        P.op("pool", lambda e: [e.dma_start(out=Wg[:, kc, :], in_=w_gate[kc * 128:(kc + 1) * 128, :]) for kc in range(8)],
             writes=["Wg"], dma="Wg", n=8)
        P.op("pool", lambda e: [e.dma_start(out=Wu[:, kc, :], in_=w_up[kc * 128:(kc + 1) * 128, :]) for kc in range(8)],
             writes=["Wu"], dma="Wu", n=8)
        P.op("pool", lambda e: [e.dma_start(out=Wd[:, fc, :], in_=w_down[fc * 128:(fc + 1) * 128, :]) for fc in range(NFC)],
             writes=["Wd"], dma="Wd", n=NFC)

        def cl(e):
            return [e.dma_start(out=gffn[:], in_=gvec[1:2, :].broadcast_to([128, 1024])),
                    e.dma_start(out=gfin[:], in_=gvec[2:3, :].broadcast_to([128, 1024]))]
        P.op("sp", cl, writes=["G"], dma="G", n=2)
        xgs = [sb(f"xg{q}", [128, 4, 1024]) for q in range(2)]
        junk2 = sb("junk2", [128, 1024], BF16)
        ub = sb("ub", [128, 1024], BF16)
        uT = sb("uT", [128, 8, 512], BF16)
        actT = sb("actT", [128, 11, 512], BF16)
        sgt = sb("sgt", [128, 512])
        ssb = sb("ssb", [128, 1])
        rsb = sb("rsb", [128, 1])
        yb = [sb(f"yb{q}", [128, 1024]) for q in range(2)]
        pg = [psb(f"pg{q}", [128, 512]) for q in range(2)]
        pu = [psb(f"pu{q}", [128, 512]) for q in range(2)]
        pd = [psb(f"pd{q}", [128, 512]) for q in range(2)]
        tpb2 = psb("tpb2", [128, 1024], BF16)
        cnt = [0]
        groups = [(0, 4), (4, 4), (8, 4), (12, 4), (16, 1)]

        def pro_load(g):
            t0, nt = groups[g]
            xg = xgs[g % 2]
            P.op("sp", lambda e: e.dma_start(out=xg[:, 0:nt, :], in_=x1s[t0 * 128:(t0 + nt) * 128, :].rearrange("(a p) d -> p a d", p=128)),
                 writes=[f"xg{g % 2}"], dma=f"xg{g % 2}")

        def pro_tile(g, a):
            xg = xgs[g % 2]
            xk = f"xg{g % 2}"
            P.op("act", lambda e: e.activation(out=junk2[:], in_=xg[:, a, :], func=AF.Square, accum_out=ssb[:]), reads=[xk], writes=["junk2", "ssb"])
            P.op("act", lambda e: e.activation(out=rsb[:], in_=ssb[:], func=AF.Sqrt, scale=1.0 / 1024, bias=1e-6), reads=["ssb"], writes=["rsb"])
            P.op("dve", lambda e: e.reciprocal(out=rsb[:], in_=rsb[:]), reads=["rsb"], writes=["rsb"])
            P.op("dve", lambda e: e.scalar_tensor_tensor(out=ub[:], in0=xg[:, a, :], scalar=rsb[:, 0:1], in1=gffn[:], op0=ALU.mult, op1=ALU.mult),
                 reads=[xk, "rsb", "G"], writes=["ub"])

            def tr(e):
                r = None
                for kc in range(8):
                    r = e.transpose(out=tpb2[:, kc * 128:(kc + 1) * 128], in_=ub[:, kc * 128:(kc + 1) * 128], identity=identb[:])
                return r
            P.op("pe", tr, reads=["ub", "W"], writes=["tpb2"])
            P.op("act", lambda e: e.copy(out=uT[:, :, a * 128:(a + 1) * 128], in_=tpb2[:].rearrange("p (a b) -> p a b", a=8)),
                 reads=["tpb2"], writes=[f"uT{a}"])

        pro_load(0)
        for a in range(groups[0][1]):
            pro_tile(0, a)
        for g, (t0, nt) in enumerate(groups):
            N = nt * 128
            xg = xgs[g % 2]
            xk = f"xg{g % 2}"
            if g + 1 < len(groups):
                pro_load(g + 1)
            uk = [f"uT{a}" for a in range(nt)]
            for hf in range(2):
                for fi in range(11):
                    fc = hf * 11 + fi
                    cnt[0] += 1
                    b = cnt[0] % 2

                    def mg(e, fc=fc, b=b, N=N):
                        r = None
                        for kc in range(8):
                            r = e.matmul(pg[b][:, 0:N], lhsT=Wg[:, kc, fc * 128:(fc + 1) * 128], rhs=uT[:, kc, 0:N], start=(kc == 0), stop=(kc == 7))
                        return r
                    P.op("pe", mg, reads=["Wg"] + uk, writes=[f"pg{b}"])

                    def mu_(e, fc=fc, b=b, N=N):
                        r = None
                        for kc in range(8):
                            r = e.matmul(pu[b][:, 0:N], lhsT=Wu[:, kc, fc * 128:(fc + 1) * 128], rhs=uT[:, kc, 0:N], start=(kc == 0), stop=(kc == 7))
                        return r
                    P.op("pe", mu_, reads=["Wu"] + uk, writes=[f"pu{b}"])
                    P.op("act", lambda e, b=b, N=N: e.activation(out=sgt[:, 0:N], in_=pg[b][:, 0:N], func=AF.Silu), reads=[f"pg{b}"], writes=["sgt"])
                    P.op("dve", lambda e, b=b, N=N, fi=fi: e.tensor_tensor(out=actT[:, fi, 0:N], in0=pu[b][:, 0:N], in1=sgt[:, 0:N], op=ALU.mult),
                         reads=[f"pu{b}", "sgt"], writes=[f"actT{fi}"])
                ak = [f"actT{fi}" for fi in range(11)]
                for a in range(nt):
                    for half in range(2):
                        cnt[0] += 1
                        b = cnt[0] % 2

                        def md(e, a=a, half=half, b=b, hf=hf):
                            r = None
                            for fi in range(11):
                                r = e.matmul(pd[b][:], lhsT=actT[:, fi, a * 128:(a + 1) * 128], rhs=Wd[:, hf * 11 + fi, half * 512:(half + 1) * 512],
                                             start=(fi == 0), stop=(fi == 10))
                            return r
                        P.op("pe", md, reads=["Wd"] + ak, writes=[f"pd{b}"])
                        P.op("dve", lambda e, a=a, half=half, b=b, xg=xg: e.tensor_tensor(out=xg[:, a, half * 512:(half + 1) * 512], in0=pd[b][:],
                                                                                          in1=xg[:, a, half * 512:(half + 1) * 512], op=ALU.add),
                             reads=[f"pd{b}", xk], writes=[xk])
                    if hf == 1 and g + 1 < len(groups) and a < groups[g + 1][1]:
                        pro_tile(g + 1, a)
            for a in range(nt):
                t = t0 + a
                y = yb[t % 2]
                yk = f"yb{t % 2}"
                P.op("act", lambda e, a=a, xg=xg: e.activation(out=junk2[:], in_=xg[:, a, :], func=AF.Square, accum_out=ssb[:]), reads=[xk], writes=["junk2", "ssb"])
                P.op("act", lambda e: e.activation(out=rsb[:], in_=ssb[:], func=AF.Sqrt, scale=1.0 / 1024, bias=1e-6), reads=["ssb"], writes=["rsb"])
                P.op("dve", lambda e: e.reciprocal(out=rsb[:], in_=rsb[:]), reads=["rsb"], writes=["rsb"])
                P.op("dve", lambda e, a=a, y=y, xg=xg: e.scalar_tensor_tensor(out=y[:], in0=xg[:, a, :], scalar=rsb[:, 0:1], in1=gfin[:], op0=ALU.mult, op1=ALU.mult),
                     reads=[xk, "rsb", "G"], writes=[yk])
                dst = y_s[:, :] if t == 16 else y_p[t * 128:(t + 1) * 128, :]
                P.op("sp", lambda e, y=y, dst=dst: e.dma_start(out=dst, in_=y[:]), reads=[yk], writes=[f"oy{t}", yk], dma=yk)
                outk.append(f"oy{t}")
        P.op("sp", None, reads=outk)
        P.emit()
        build.stats["B"] = P.stats
    return nc


def _consts(p):
    s = np.arange(128)[:, None]
    t = np.arange(128)[None, :]
    su = (s < t).astype(np.float32)
    ui = (s <= t).astype(np.float32)
    sl = (s > t).astype(np.float32)
    same = ((s // 8) == (t // 8)).astype(np.float32)
    mfirst = sl if p > 0 else np.zeros_like(sl)
    masks = np.stack([su, ui, sl, su * same, ui * same, sl * same, mfirst], axis=1).reshape(128, 7 * 128)
    ident = np.eye(128, dtype=np.float32)
    bones = ((s // 64) == (t // 64)).astype(np.float32)
    rm = np.ones((128, 2, 128), np.float32)
    rm[:, 1, :] = (np.arange(128) % 8 != 0).astype(np.float32)[None, :]
    e16 = ((np.arange(128)[:, None] // 8) == np.arange(16)[None, :]).astype(np.float32)
    return dict(masks=np.ascontiguousarray(masks), ident=ident, bones=bones, rmask=rm.reshape(128, 256), e16=e16)


_NC = [None]


def kernel(x_prompt, x_sample, cache_k, cache_v, state_wkv, state_shift, g_mix, w_in, attn_sinks,
           rwkv_mu, w0, w2, a0, a2, g2, k_k, k_a, r_k, gn_g, gn_b, w_out, g_ffn, w_gate, w_up,
           w_down, g_final):
    f = lambda a: np.ascontiguousarray(np.asarray(a, dtype=np.float32))
    x_prompt, x_sample = f(x_prompt), f(x_sample)
    w_in0 = f(w_in)[0]
    qperm = np.concatenate([np.r_[j * 64:(j + 1) * 64, (4 + j) * 64:(5 + j) * 64] for j in range(4)])
    w_in_p = np.ascontiguousarray(np.concatenate([w_in0[:, qperm], w_in0[:, 512:]], axis=1))
    fm4 = lambda v: f(v).reshape(4, 128).T
    vec4 = np.ascontiguousarray(np.stack([fm4(w0[0]), fm4(a0[0]), fm4(k_k[0]), fm4(k_a[0]), fm4(f(r_k)[0].reshape(-1)),
                                          fm4(gn_g[0]), fm4(gn_b[0])], axis=1).reshape(128, 28))
    shared = dict(
        w_in=w_in_p, w_out=f(w_out)[0], w_gate=f(w_gate)[0], w_up=f(w_up)[0], w_down=f(w_down)[0],
        gvec=np.ascontiguousarray(np.stack([f(g_mix)[0], f(g_ffn)[0], f(g_final)], axis=0)),
        mu=np.ascontiguousarray(f(rwkv_mu)[0].reshape(14, 128).T),
        vec4=vec4,
        w2a2=np.ascontiguousarray(np.concatenate([f(w2)[0], f(a2)[0]], axis=0)),
        g2=f(g2)[0],
        sinks=f(attn_sinks)[0].reshape(1, 8),
    )
    in_maps = []
    for c in range(8):
        b, p = c // 4, c % 4
        xwin = np.zeros((NPT * 128, 1024), np.float32)
        nreal = (p + 1) * 2048
        xwin[NPT * 128 - nreal:] = x_prompt[b, 0:nreal]
        m = dict(shared)
        m.update(_consts(p))
        m.update(
            xw=xwin,
            xs=np.ascontiguousarray(x_sample[16 * c:16 * c + 16].reshape(128, 1024)),
            hprev=f(state_shift)[0, 16 * c:16 * c + 16],
            ck=np.ascontiguousarray(f(cache_k)[0, 16 * c:16 * c + 16].reshape(16, 128, 128)),
            cv=np.ascontiguousarray(f(cache_v)[0, 16 * c:16 * c + 16].reshape(16, 128, 128)),
            swkv=np.ascontiguousarray(f(state_wkv)[0, 16 * c:16 * c + 16]),
        )
        in_maps.append(m)
    if _NC[0] is None:
        _NC[0] = build()
    res = run_bass_kernel_spmd(_NC[0], in_maps, core_ids=list(range(8)))
    R = res.results
    y_prompt = np.stack([np.concatenate([R[b * 4 + p]["y_p"] for p in range(4)], axis=0) for b in range(2)], axis=0)
    y_sample = np.concatenate([R[c]["y_s"].reshape(16, 8, 1024) for c in range(8)], axis=0)
    kp = np.stack([R[b * 4 + 3]["kwin_p"].reshape(128, 2, 64) for b in range(2)], axis=0)[None]
    vp = np.stack([R[b * 4 + 3]["vwin_p"].reshape(128, 2, 64) for b in range(2)], axis=0)[None]
    sp = np.stack([R[b * 4 + 3]["wkv_p"] for b in range(2)], axis=0)[None]
    hp = np.stack([R[b * 4 + 3]["shift_p"].reshape(1024) for b in range(2)], axis=0)[None]
    ks = np.concatenate([R[c]["kwin_s"].reshape(16, 128, 2, 64) for c in range(8)], axis=0)[None]
    vs = np.concatenate([R[c]["vwin_s"].reshape(16, 128, 2, 64) for c in range(8)], axis=0)[None]
    ss_ = np.concatenate([R[c]["wkv_s"] for c in range(8)], axis=0)[None]
    hs = np.concatenate([R[c]["shift_s"] for c in range(8)], axis=0)[None]
    return tuple(np.ascontiguousarray(a.astype(np.float32)) for a in (y_prompt, y_sample, kp, vp, sp, hp, ks, vs, ss_, hs))
```

```python
import numpy as np
from contextlib import ExitStack
import concourse.bass as bass
import concourse.mybir as mybir
from concourse.bass_utils import run_bass_kernel_spmd
from concourse.alu_op_type import AluOpType as ALU

AF = mybir.ActivationFunctionType
AX = mybir.AxisListType
F32 = mybir.dt.float32
BF16 = mybir.dt.bfloat16

SAME_ENGINE_SYNC = True
SAME_ENGINE_RAW_ONLY = False
ENGS = ("pe", "act", "dve", "pool", "sp")
C0 = float(np.exp(-0.5))
NPRE = 48
NOWN = 16
NPT = NPRE + NOWN
NT = NPT + 1
D_FF = 2816
NFC = 22
ILV_CH = 2
ILV_MODE = 0
SKIPOPS = set()
NO_ILV = False
PSUM_PREFIXES = ("pj", "tpb", "l0", "sqp", "Zp", "pg", "pu", "pd")
OPLIMIT = 10 ** 9
NO_D2D = False
PHASES = {"A", "Sa", "Sb", "B"}


class Prog:
    def __init__(self, nc, tag=''):
        self.nc = nc
        self.tag = tag
        self.ins = []
        self.last_w = {}
        self.readers = {}

    def op(self, eng, fn, reads=(), writes=(), dma=None, n=1):
        if getattr(self, "cap", None) is not None:
            self.cap.append((eng, fn, list(reads), list(writes), dma, n))
            return -1
        idx = len(self.ins)
        if idx >= OPLIMIT:
            return idx
        writes = list(writes) + [k for k in reads if k.startswith(PSUM_PREFIXES) and k not in writes]
        deps = set()
        raw = set()
        for k in reads:
            if k in self.last_w:
                deps.add(self.last_w[k])
                raw.add(self.last_w[k])
        for k in writes:
            if k in self.last_w:
                deps.add(self.last_w[k])
            for r in self.readers.get(k, ()):
                deps.add(r)
        deps.discard(idx)
        if fn is None:
            writes = []
        self.ins.append(dict(eng=eng, fn=fn, deps=deps, raw=raw, dma=dma, used=False, n=n))
        for k in reads:
            self.readers.setdefault(k, []).append(idx)
        for k in writes:
            self.last_w[k] = idx
            self.readers[k] = []
        return idx

    def emit(self):
        nc = self.nc
        ins = self.ins
        for r in ins:
            if SAME_ENGINE_RAW_ONLY:
                r["deps"] = {d for d in r["deps"]
                             if not (ins[d]["eng"] == r["eng"] and ins[d]["dma"] is None and r["dma"] is None and d not in r["raw"])}
            for d in r["deps"]:
                ins[d]["used"] = True
        cnt = {e: 0 for e in ENGS}
        dmav = {}
        for r in ins:
            if r["dma"] is not None:
                r["sem"] = "dma_" + r["dma"]
                dmav[r["sem"]] = dmav.get(r["sem"], 0) + 16 * r["n"]
                r["val"] = dmav[r["sem"]]
            elif r["used"]:
                cnt[r["eng"]] += 1
                r["sem"] = "eng_" + r["eng"]
                r["val"] = cnt[r["eng"]]
            else:
                r["sem"] = None
                r["val"] = 0
        known = {e: {} for e in ENGS}
        for r in ins:
            e = r["eng"]
            kn = known[e]
            wd = {}
            for d in sorted(r["deps"]):
                rd = ins[d]
                s, v = rd["sem"], rd["val"]
                if rd["eng"] == e and rd["dma"] is None and not SAME_ENGINE_SYNC:
                    continue
                if kn.get(s, 0) >= v:
                    continue
                wd[s] = max(wd.get(s, 0), v)
                for s2, v2 in rd["clock"].items():
                    if kn.get(s2, 0) < v2:
                        kn[s2] = v2
            r["waits"] = sorted(wd.items())
            ck = dict(kn)
            if r["sem"] is not None:
                ck[r["sem"]] = r["val"]
            r["clock"] = ck
        semnames = sorted({r["sem"] for r in ins if r["sem"] is not None})
        self.stats = dict(n=len(ins), nsem=len(semnames),
                          nwaits=sum(len(r["waits"]) for r in ins),
                          per_eng={e: sum(1 for r in ins if r["eng"] == e) for e in ENGS})
        with ExitStack() as st:
            sems = {s: st.enter_context(nc.semaphore(s + self.tag)) for s in semnames}
            block = st.enter_context(nc.Block())
            reg = {"pe": block.tensor, "act": block.scalar, "dve": block.vector,
                   "pool": block.gpsimd, "sp": block.sync}

            def make(e):
                def body(eng):
                    for r in ins:
                        if r["eng"] != e:
                            continue
                        for s, v in r["waits"]:
                            eng.wait_ge(sems[s], v)
                        if r["fn"] is None:
                            continue
                        out = r["fn"](eng)
                        if r["dma"] is not None:
                            outs = out if isinstance(out, (list, tuple)) else [out]
                            assert len(outs) == r["n"], (len(outs), r["n"])
                            for o in outs:
                                o.then_inc(sems[r["sem"]], 16)
                        elif r["sem"] is not None:
                            o = out[-1] if isinstance(out, (list, tuple)) else out
                            o.then_inc(sems[r["sem"]], 1)
                return body

            for e in ENGS:
                reg[e](make(e))


def build():
    nc = bass.Bass("TRN2", target_bir_lowering=False)

    def din(name, shape):
        return nc.dram_tensor(name, shape, F32, kind="ExternalInput").ap()

    def dout(name, shape):
        return nc.dram_tensor(name, shape, F32, kind="ExternalOutput").ap()

    xw = din("xw", [NPT * 128, 1024])
    xs = din("xs", [128, 1024])
    hprev = din("hprev", [16, 1024])
    ck = din("ck", [16, 128, 128])
    cv = din("cv", [16, 128, 128])
    swkv = din("swkv", [16, 8, 64, 64])
    w_in = din("w_in", [1024, 2560])
    w_out = din("w_out", [1024, 1024])
    w_gate = din("w_gate", [1024, D_FF])
    w_up = din("w_up", [1024, D_FF])
    w_down = din("w_down", [D_FF, 1024])
    gvec = din("gvec", [3, 1024])
    mu_d = din("mu", [128, 14])
    vec4_d = din("vec4", [128, 7 * 4])
    w2a2_d = din("w2a2", [128, 512])
    g2_d = din("g2", [128, 512])
    sinks_d = din("sinks", [1, 8])
    masks_d = din("masks", [128, 7 * 128])
    ident_d = din("ident", [128, 128])
    bones_d = din("bones", [128, 128])
    rmask_d = din("rmask", [128, 256])
    e16_d = din("e16", [128, 16])

    y_p = dout("y_p", [NOWN * 128, 1024])
    y_s = dout("y_s", [128, 1024])
    kwin_p = dout("kwin_p", [128, 128])
    vwin_p = dout("vwin_p", [128, 128])
    wkv_p = dout("wkv_p", [8, 64, 64])
    shift_p = dout("shift_p", [1, 1024])
    kwin_s = dout("kwin_s", [16, 128, 128])
    vwin_s = dout("vwin_s", [16, 128, 128])
    wkv_s = dout("wkv_s", [16, 8, 64, 64])
    shift_s = dout("shift_s", [16, 1024])
    x1s = nc.dram_tensor("x1s", [17 * 128, 1024], F32, kind="Internal").ap()
    build.stats = {}

    W0, A0, KK, KA, RK, GNG, GNB = range(7)

    def phase(mode, per):
        pA, pSa, pSb = mode == "A", mode == "Sa", mode == "Sb"
        with ExitStack() as st:
            def sb(name, shape, dt=F32):
                return st.enter_context(nc.sbuf_tensor(name + "_" + mode, shape, dt))

            def psb(name, shape, dt=F32):
                return st.enter_context(nc.psum_tensor(name + "_" + mode, shape, dt))

            def rot(name, n, shape, dt=F32):
                return [sb(f"{name}{q}", shape, dt) for q in range(n)]
            P = Prog(nc, '_' + mode)
            outkeys = []
            if pA or pSa:
                Win = sb("Win", [128, 8, 2560], BF16)
            if pA or pSb:
                Wout = sb("Wout", [128, 8, 1024], BF16)
            gmix = sb("gmix", [128, 1024])
            mu = sb("mu", [128, 14])
            vec4 = sb("vec4", [128, 7, 4])
            w2a2 = sb("w2a2", [128, 512], BF16)
            g2b = sb("g2b", [128, 512], BF16)
            esink = sb("esink", [128, 8])
            MK = sb("MK", [128, 7, 128], BF16)
            MKL = sb("MKL", [128, 2, 4, 128], BF16)
            ident = sb("ident", [128, 128], BF16)
            ident32 = sb("ident32", [128, 128])
            bones = sb("bones", [128, 128], BF16)
            rmask = sb("rmask", [128, 2, 128])
            e16 = sb("e16", [128, 16], BF16)

            def cload(e):
                r = []
                r.append(e.dma_start(out=ident[:], in_=ident_d[:, :]))
                r.append(e.dma_start(out=w2a2[:], in_=w2a2_d[:, :]))
                r.append(e.dma_start(out=g2b[:], in_=g2_d[:, :]))
                r.append(e.dma_start(out=MK[:], in_=masks_d.rearrange("p (m t) -> p m t", m=7)))
                r.append(e.dma_start(out=bones[:], in_=bones_d[:, :]))
                r.append(e.dma_start(out=e16[:], in_=e16_d[:, :]))
                return r
            P.op("pool", cload, writes=["const"], dma="constb", n=6)
            if pA or pSa:
                P.op("pool", lambda e: [e.dma_start(out=Win[:, kc, :], in_=w_in[kc * 128:(kc + 1) * 128, :]) for kc in range(8)],
                     writes=["Win"], dma="Win", n=8)
            if pA or pSb:
                P.op("pool", lambda e: [e.dma_start(out=Wout[:, kc, :], in_=w_out[kc * 128:(kc + 1) * 128, :]) for kc in range(8)],
                     writes=["Wout"], dma="Wout", n=8)

            def cload2(e):
                r = []
                r.append(e.dma_start(out=gmix[:], in_=gvec[0:1, :].broadcast_to([128, 1024])))
                r.append(e.dma_start(out=mu[:], in_=mu_d[:, :]))
                r.append(e.dma_start(out=vec4[:], in_=vec4_d.rearrange("p (a b) -> p a b", a=7)))
                r.append(e.dma_start(out=esink[:], in_=sinks_d[0:1, :].broadcast_to([128, 8])))
                r.append(e.dma_start(out=ident32[:], in_=ident_d[:, :]))
                r.append(e.dma_start(out=rmask[:], in_=rmask_d.rearrange("p (a t) -> p a t", a=2)))
                return r
            P.op("sp", cload2, writes=["const2"], dma="constf", n=6)
            P.op("act", lambda e: e.activation(out=esink[:], in_=esink[:], func=AF.Exp),
                 reads=["const2"], writes=["esink"])

            def mkl(e):
                r = None
                for kd in range(2):
                    for q in range(4):
                        r = e.tensor_copy(out=MKL[:, kd, q, :], in_=MK[:, 3 * kd + (q % 2), :])
                return r
            P.op("pool", mkl, reads=["const"], writes=["MKL"])

            def v4bc(ix, n=128):
                return vec4[:, ix, :].unsqueeze(2).broadcast_to([128, 4, n])

            R2 = 2 if pA else 1
            if pA or pSa:
                xt = rot("xt", 1 if pA else 1, [128, 1024])
                ss = rot("ss", 2, [128, 1])
                rs = rot("rs", 2, [128, 1])
                hb = rot("hb", R2, [128, 1024], BF16)
                hT = rot("hT", R2, [128, 8, 128], BF16)
            if pA:
                fT = rot("fT", 2, [128, 14, 128])
                qT = rot("qT", 4, [128, 4, 128], BF16)
                kvT = rot("kvT", 5, [128, 2, 128], BF16)
            else:
                fT, qT, kvT, fprev = per["fT"], per["qT"], per["kvT"], per["fprev"]
            NKV = len(kvT)
            if pSa:
                h32s = sb("h32s", [128, 1024])
                kvx = sb("kvx", [128, 4, 128])
                hp32 = sb("hp32", [16, 1024])
                hpb = sb("hpb", [16, 1024], BF16)
                hpT = sb("hpT", [128, 8, 16], BF16)
            if pA or pSb:
                Vaug = rot("Vaug", NKV, [128, 2, 65], BF16)
                xsT = sb("xsT", [128, 14, 128])
                fcar = sb("fcar", [128, 14])
                twal = sb("twal", [128, 128], BF16)
                sg = sb("sg", [128, 128], BF16)
                eT = sb("eT", [128, 4, 128])
                asT = sb("asT", [128, 4, 128])
                kkT = sb("kkT", [128, 4, 128])
                sqb = sb("sqb", [128, 4, 128], BF16)
                rn = sb("rn", [128, 4, 128])
                tmpA = sb("tmpA", [128, 4, 128])
                kmT = sb("kmT", [128, 4, 128])
                cumE = sb("cumE", [128, 4, 128])
                Eg = sb("Eg", [128, 4, 128])
                vb = sb("vb", [128, 4, 128], BF16)
                AR = rot("AR", R2, [128, 4, 2, 128], BF16)
                BTF = rot("BTF", R2, [128, 4, 128], BF16)
                KTF = rot("KTF", R2, [128, 4, 128], BF16)
                VTM = rot("VTM", R2, [128, 512], BF16)
                BTM = rot("BTM", R2, [128, 512], BF16)
                KTM = rot("KTM", R2, [128, 512], BF16)
                gC = rot("gC", R2, [128, 4, 16])
                gT = rot("gT", 3 if pA else 1, [128, 4, 128], BF16)
                bonT = rot("bonT", 3 if pA else 1, [128, 4, 128], BF16)
                if pSb:
                    SQ0 = sb("SQ0", [128, 8, 256], BF16)
                    NM = sb("NM", [128, 8, 384], BF16)
                    NJ = sb("NJ", [128, 2, 8, 128], BF16)
                    AJ = sb("AJ", [128, 2, 8, 128], BF16)
                else:
                    L0S = sb("L0S", [128, 8, 512], BF16)
                    A0S = sb("A0S", [128, 8, 128], BF16)
                    NA = sb("NA", [128, 2, 8, 256], BF16)
                Zb = rot("Zb", 2, [128, 512], BF16)
                Ub = sb("Ub", [128, 512], BF16)
                Hst = sb("Hst", [128, 4, 64])
                Hb = sb("Hb", [128, 4, 64], BF16)
                tS = sb("tS", [128, 4, 64])
                OTM = rot("OTM", R2, [128, 8, 64])
                osq = sb("osq", [128, 8, 64])
                st1 = sb("st1", [128, 8])
                st2 = sb("st2", [128, 8])
                st3 = sb("st3", [128, 8])
                onb = sb("onb", [128, 8, 64], BF16)
                PT = rot("PT", 4, [128, 4, 128], BF16)
                den = sb("den", [128, 8])
                attb = sb("attb", [128, 8, 64], BF16)
                mixT = rot("mixT", R2, [128, 8, 128], BF16)
                tmx = sb("tmx", [128, 4, 128])
                x1t = rot("x1t", 1, [128, 1024])
                if pA:
                    ATM = rot("ATM", R2, [128, 512], BF16)
                    BHF = sb("BHF", [128, 4, 128], BF16)
                    KHF = sb("KHF", [128, 4, 128], BF16)
                    Xb = rot("Xb", 2, [128, 2, 512], BF16)
                    WY = sb("WY", [128, 8, 128], BF16)
                    MTb = sb("MTb", [128, 4, 64], BF16)
                    WTF = sb("WTF", [128, 4, 128], BF16)
            if pA:
                kvx = tmx
                h32s = x1t[0]
            if pSb:
                S0q = sb("S0q", [64, 32, 64])
                H0s = sb("H0s", [128, 16, 4, 64])
                H0b = sb("H0b", [128, 16, 4, 64], BF16)
                Xex = sb("Xex", [128, 2, 16, 64], BF16)
                zh = sb("zh", [128, 4, 2, 128], BF16)
                KcT = sb("KcT", [128, 16, 128], BF16)
                Vca = sb("Vca", [128, 16, 2, 65], BF16)
                cstb = sb("cstb", [128, 16, 128], BF16)
                PTc = sb("PTc", [128, 2, 16, 32], BF16)
                PTx = rot("PTx", 2, [128, 16, 128], BF16)
                wso = sb("wso", [64, 4, 8, 64])
            h32k = "x1t0" if pA else "h32s"
            kvxk = "tmx" if pA else "kvx"

            pj = [psb(f"pj{q}", [128, 512]) for q in range(2)]
            tpb = psb("tpb", [128, 1024], BF16)
            l0 = [psb(f"l0{q}", [128, 512]) for q in range(2)]
            sqp = [psb(f"sqp{q}", [128, 512]) for q in range(2)]
            Zp = psb("Zp", [128, 512])
            cnts = {"pj": 0, "sqp": 0, "l0": 0}
            banks = {"pj": pj, "sqp": sqp, "l0": l0}

            def nextb(nm):
                cnts[nm] += 1
                q = cnts[nm] % 2
                return banks[nm][q], f"{nm}{q}"

            def nextpj():
                return nextb("pj")

            def nextsq():
                return nextb("sqp")

            def nextl0():
                return nextb("l0")

            if pA or pSb:
                P.op("pool", lambda e: e.memset(fcar[:], 0.0), writes=["fcar"])
                P.op("pool", lambda e: e.memset(Hst[:], 0.0), writes=["Hst"])
                P.op("pool", lambda e: e.memset(Hb[:], 0.0), writes=["Hb"])
                P.op("pool", lambda e: e.memset(tS[:], 0.0), writes=["tS"])
                for q in range(NKV):
                    P.op("pool", lambda e, q=q: e.memset(Vaug[q][:], 1.0), writes=[f"Vaug{q}"])
            if pSb:
                P.op("pool", lambda e: e.memset(Vca[:], 1.0), writes=["Vca"])
                for q in range(2):
                    P.op("pool", lambda e, q=q: e.memset(PTx[q][:], 0.0), writes=[f"PTx{q}"])

            def is_own(i):
                return i >= NPRE

            def is_samp(i):
                return i == NPT

            def S1(i):
                b = i % len(xt)
                kx = f"xt{b}"
                src = xs[:, :] if is_samp(i) else xw[i * 128:(i + 1) * 128, :]
                P.op("sp", lambda e: e.dma_start(out=xt[b][:], in_=src), writes=[kx], dma=kx)
                s2 = i % 2
                hq = i % len(hb)
                P.op("act", lambda e: e.activation(out=hb[hq][:], in_=xt[b][:], func=AF.Square, accum_out=ss[s2][:]),
                     reads=[kx], writes=[f"hb{hq}", f"ss{s2}"])
                P.op("act", lambda e: e.activation(out=rs[s2][:], in_=ss[s2][:], func=AF.Sqrt, scale=1.0 / 1024, bias=1e-6),
                     reads=[f"ss{s2}"], writes=[f"rs{s2}"])
                P.op("dve", lambda e: e.reciprocal(out=rs[s2][:], in_=rs[s2][:]), reads=[f"rs{s2}"], writes=[f"rs{s2}"])
                P.op("dve", lambda e: e.scalar_tensor_tensor(out=hb[hq][:], in0=xt[b][:], scalar=rs[s2][:, 0:1], in1=gmix[:],
                                                             op0=ALU.mult, op1=ALU.mult),
                     reads=[kx, f"rs{s2}", "const2"], writes=[f"hb{hq}"])
                if i == NPT - 1 or is_samp(i):
                    P.op("dve", lambda e: e.scalar_tensor_tensor(out=h32s[:], in0=xt[b][:], scalar=rs[s2][:, 0:1], in1=gmix[:],
                                                                 op0=ALU.mult, op1=ALU.mult),
                         reads=[kx, f"rs{s2}", "const2"], writes=[h32k])
                    if is_samp(i):
                        P.op("sp", lambda e: [e.dma_start(out=shift_s[q:q + 1, :], in_=h32s[8 * q + 7:8 * q + 8, :]) for q in range(16)],
                             reads=[h32k], writes=["o_shift_s"], dma="o_shift_s", n=16)
                        outkeys.append("o_shift_s")
                    else:
                        P.op("sp", lambda e: e.dma_start(out=shift_p[:, :], in_=h32s[127:128, :]), reads=[h32k],
                             writes=["o_shift_p"], dma="o_shift_p")
                        outkeys.append("o_shift_p")

                def tr(e):
                    r = None
                    for kc in range(8):
                        r = e.transpose(out=tpb[:, kc * 128:(kc + 1) * 128], in_=hb[hq][:, kc * 128:(kc + 1) * 128], identity=ident[:])
                    return r
                P.op("pe", tr, reads=[f"hb{hq}", "const"], writes=["tpb"])
                P.op("act", lambda e: e.copy(out=hT[hq][:].rearrange("p a b -> p (a b)"), in_=tpb[:]), reads=["tpb"], writes=[f"hT{hq}"])

            def proj_group(i, chunks, evac):
                hq = i % len(hT)
                bank, bk = nextpj()

                def mm(e):
                    r = None
                    for gi, c in enumerate(chunks):
                        for kc in range(8):
                            r = e.matmul(bank[:, gi * 128:(gi + 1) * 128], lhsT=Win[:, kc, c * 128:(c + 1) * 128],
                                         rhs=hT[hq][:, kc, :], start=(kc == 0), stop=(kc == 7))
                    return r
                P.op("pe", mm, reads=["Win", f"hT{hq}"], writes=[bk])
                evac(bank, bk)

            def S2(i):
                own = is_own(i)
                qq = i % len(qT)
                fq = i % len(fT)
                if own:
                    def ev_q(bank, bk):
                        P.op("act", lambda e: e.activation(out=qT[qq][:].rearrange("p a b -> p (a b)"), in_=bank[:], func=AF.Copy, scale=0.125),
                             reads=[bk], writes=[f"qT{qq}"])
                    proj_group(i, [0, 1, 2, 3], ev_q)
                if own or i == NPRE - 1:
                    k3 = i % NKV

                    def ev_kv(bank, bk):
                        P.op("act", lambda e: e.copy(out=kvT[k3][:].rearrange("p a b -> p (a b)"), in_=bank[:, 0:256]),
                             reads=[bk], writes=[f"kvT{k3}"])
                        if i == NPT - 1 or is_samp(i):
                            P.op("dve", lambda e: e.tensor_copy(out=kvx[:, 0:2, :].rearrange("p a b -> p (a b)"), in_=bank[:, 0:256]),
                                 reads=[bk, f"kvT{k3}"], writes=[kvxk])
                    proj_group(i, [4, 5], ev_kv)
                    if i == NPT - 1 or is_samp(i):
                        bank, bk = nextpj()

                        def trkv(e):
                            e.transpose(out=bank[:, 0:128], in_=kvx[:, 0, :], identity=ident32[:])
                            return e.transpose(out=bank[:, 128:256], in_=kvx[:, 1, :], identity=ident32[:])
                        P.op("pe", trkv, reads=[kvxk, "const2"], writes=[bk])
                        P.op("dve", lambda e: e.tensor_copy(out=kvx[:, 2:4, :].rearrange("p a b -> p (a b)"), in_=bank[:, 0:256]),
                             reads=[bk, kvxk], writes=[kvxk])
                        if is_samp(i):
                            def okv(e):
                                r = []
                                for q in range(16):
                                    r.append(e.dma_start(out=kwin_s[q, 120:128, :], in_=kvx[8 * q:8 * q + 8, 2, :]))
                                    r.append(e.dma_start(out=vwin_s[q, 120:128, :], in_=kvx[8 * q:8 * q + 8, 3, :]))
                                if not NO_D2D:
                                    r.append(e.dma_start(out=kwin_s[:, 0:120, :], in_=ck[:, 8:128, :]))
                                    r.append(e.dma_start(out=vwin_s[:, 0:120, :], in_=cv[:, 8:128, :]))
                                return r
                            P.op("sp", okv, reads=[kvxk], writes=["o_kv_s"], dma="o_kv_s", n=(32 if NO_D2D else 34))
                            outkeys.append("o_kv_s")
                        else:
                            def okv(e):
                                r = []
                                r.append(e.dma_start(out=kwin_p[:, :], in_=kvx[:, 2, :]))
                                r.append(e.dma_start(out=vwin_p[:, :], in_=kvx[:, 3, :]))
                                return r
                            P.op("sp", okv, reads=[kvxk], writes=["o_kv_p"], dma="o_kv_p", n=2)
                            outkeys.append("o_kv_p")
                for gi, chunks in enumerate([[6, 7, 8, 9], [10, 11, 12, 13], [14, 15, 16, 17], [18, 19]]):
                    def ev_f(bank, bk, gi=gi, chunks=chunks):
                        n = len(chunks) * 128
                        dst = fT[fq][:, gi * 4:gi * 4 + len(chunks), :].rearrange("p a b -> p (a b)")
                        if gi % 2 == 0:
                            P.op("act", lambda e: e.copy(out=dst, in_=bank[:, 0:n]), reads=[bk], writes=[f"fT{fq}_{gi}"])
                        else:
                            P.op("dve", lambda e: e.tensor_copy(out=dst, in_=bank[:, 0:n]), reads=[bk], writes=[f"fT{fq}_{gi}"])
                    proj_group(i, chunks, ev_f)
                if is_samp(i):
                    P.op("sp", lambda e: e.dma_start(out=hp32[:], in_=hprev[:, :]), writes=["hp32"], dma="hp32")
                    P.op("dve", lambda e: e.tensor_copy(out=hpb[:], in_=hp32[:]), reads=["hp32"], writes=["hpb"])

                    def trh(e):
                        r = None
                        for kc in range(8):
                            r = e.transpose(out=tpb[:, kc * 16:(kc + 1) * 16], in_=hpb[:, kc * 128:(kc + 1) * 128], identity=ident[0:16, 0:16])
                        return r
                    P.op("pe", trh, reads=["hpb", "const"], writes=["tpb"])
                    P.op("act", lambda e: e.copy(out=hpT[:].rearrange("p a b -> p (a b)"), in_=tpb[:, 0:128]), reads=["tpb"], writes=["hpT"])
                    for half in range(2):
                        bank, bk = nextpj()

                        def mmp(e, half=half, bank=bank):
                            r = None
                            for ci in range(7):
                                c = 6 + half * 7 + ci
                                for kc in range(8):
                                    r = e.matmul(bank[:, ci * 16:(ci + 1) * 16], lhsT=Win[:, kc, c * 128:(c + 1) * 128],
                                                 rhs=hpT[:, kc, :], start=(kc == 0), stop=(kc == 7))
                            return r
                        P.op("pe", mmp, reads=["Win", "hpT"], writes=[bk])
                        P.op("act", lambda e, half=half, bank=bank: e.copy(
                            out=fprev[:, half * 7:(half + 1) * 7, :].rearrange("p a b -> p (a b)"), in_=bank[:, 0:112]),
                            reads=[bk], writes=[f"fprev{half}"])

            def S3(i):
                s3 = i % R2
                own = is_own(i)
                samp = is_samp(i)
                fq = i % len(fT)
                fk = [f"fT{fq}_{g}" for g in range(4)]
                f = fT[fq]
                if samp:
                    f4 = f[:, :, :].rearrange("p c (s t) -> p c s t", t=8)
                    x4 = xsT[:, :, :].rearrange("p c (s t) -> p c s t", t=8)
                    P.op("pool", lambda e: e.tensor_tensor(out=x4[:, :, :, 1:8], in0=f4[:, :, :, 0:7], in1=f4[:, :, :, 1:8], op=ALU.subtract),
                         reads=fk, writes=["xsT"])
                    P.op("pool", lambda e: e.tensor_tensor(out=x4[:, :, :, 0], in0=fprev[:, :, :], in1=f4[:, :, :, 0], op=ALU.subtract),
                         reads=fk, writes=["xsT0"])
                else:
                    P.op("pool", lambda e: e.tensor_tensor(out=xsT[:, :, 1:128], in0=f[:, :, 0:127], in1=f[:, :, 1:128], op=ALU.subtract),
                         reads=fk, writes=["xsT"])
                    P.op("pool", lambda e: e.tensor_tensor(out=xsT[:, :, 0], in0=fcar[:, :], in1=f[:, :, 0], op=ALU.subtract),
                         reads=fk + ["fcar"], writes=["xsT0"])
                    P.op("pool", lambda e: e.tensor_copy(out=fcar[:, :], in_=f[:, :, 127]), reads=fk, writes=["fcar"])
                mu_bc = mu[:, :].unsqueeze(2).broadcast_to([128, 14, 128])
                P.op("dve", lambda e: e.tensor_tensor(out=xsT[:], in0=xsT[:], in1=mu_bc, op=ALU.mult),
                     reads=["xsT", "xsT0", "const2"], writes=["xsT", "xsT0"])
                P.op("pool", lambda e: e.tensor_tensor(out=xsT[:], in0=xsT[:], in1=f[:], op=ALU.add),
                     reads=["xsT", "xsT0"] + fk, writes=["xsT", "xsT0"])
                XK = ["xsT", "xsT0"]
                P.op("act", lambda e: e.activation(out=twal[0:64, :], in_=xsT[0:64, 12, :], func=AF.Tanh), reads=XK, writes=["twal_a"])
                P.op("act", lambda e: e.copy(out=twal[64:128, :], in_=xsT[64:128, 12, :]), reads=XK, writes=["twal_b"])
                P.op("act", lambda e: e.activation(out=sg[:], in_=xsT[:, 13, :], func=AF.Sigmoid), reads=XK, writes=["sg"])
                bw, bwk = nextpj()

                def mmw(e):
                    r = None
                    for cc in range(4):
                        r = e.matmul(bw[:, cc * 128:(cc + 1) * 128], lhsT=w2a2[0:64, cc * 128:(cc + 1) * 128], rhs=twal[0:64, :],
                                     start=True, stop=True)
                    return r
                P.op("pe", mmw, reads=["const", "twal_a"], writes=[bwk])
                for cc in range(4):
                    P.op("act", lambda e, cc=cc: e.activation(out=eT[:, cc, :], in_=bw[:, cc * 128:(cc + 1) * 128], func=AF.Sigmoid,
                                                              bias=vec4[:, W0, cc:cc + 1]),
                         reads=[bwk, "const2"], writes=[f"eT{cc}"])
                ba, bak = nextpj()

                def mma(e):
                    r = None
                    for cc in range(4):
                        r = e.matmul(ba[:, cc * 128:(cc + 1) * 128], lhsT=w2a2[64:128, cc * 128:(cc + 1) * 128], rhs=twal[64:128, :],
                                     start=True, stop=True)
                    return r
                P.op("pe", mma, reads=["const", "twal_b"], writes=[bak])
                for cc in range(4):
                    P.op("act", lambda e, cc=cc: e.activation(out=asT[:, cc, :], in_=ba[:, cc * 128:(cc + 1) * 128], func=AF.Sigmoid,
                                                              bias=vec4[:, A0, cc:cc + 1]),
                         reads=[bak, "const2"], writes=[f"asT{cc}"])
                EK = [f"eT{c}" for c in range(4)]
                AK = [f"asT{c}" for c in range(4)]
                if own:
                    bg, bgk = nextpj()

                    def mmg(e):
                        r = None
                        for cc in range(4):
                            r = e.matmul(bg[:, cc * 128:(cc + 1) * 128], lhsT=g2b[:, cc * 128:(cc + 1) * 128], rhs=sg[:], start=True, stop=True)
                        return r
                    P.op("pe", mmg, reads=["const", "sg"], writes=[bgk])
                    gq = i % len(gT)
                    P.op("act", lambda e: e.copy(out=gT[gq][:].rearrange("p a b -> p (a b)"), in_=bg[:]), reads=[bgk], writes=[f"gT{gq}"])
                P.op("dve", lambda e: e.tensor_tensor(out=kkT[:], in0=xsT[:, 4:8, :], in1=v4bc(KK), op=ALU.mult), reads=XK + ["const2"], writes=["kkT"])
                P.op("dve", lambda e: e.tensor_tensor(out=sqb[:], in0=kkT[:], in1=kkT[:], op=ALU.mult), reads=["kkT"], writes=["sqb"])
                bs, bsk = nextpj()
                P.op("pe", lambda e: e.matmul(bs[:], lhsT=bones[:], rhs=sqb[:].rearrange("p a b -> p (a b)"), start=True, stop=True),
                     reads=["const", "sqb"], writes=[bsk])
                P.op("act", lambda e: e.activation(out=rn[:].rearrange("p a b -> p (a b)"), in_=bs[:], func=AF.Sqrt, bias=1e-12),
                     reads=[bsk], writes=["rn"])
                P.op("dve", lambda e: e.reciprocal(out=rn[:], in_=rn[:]), reads=["rn"], writes=["rn"])
                P.op("dve", lambda e: e.tensor_tensor(out=kkT[:], in0=kkT[:], in1=rn[:], op=ALU.mult), reads=["kkT", "rn"], writes=["kkT"])
                P.op("dve", lambda e: e.scalar_tensor_tensor(out=tmpA[:], in0=asT[:], scalar=-1.0, in1=v4bc(KA), op0=ALU.add, op1=ALU.mult),
                     reads=AK + ["const2"], writes=["tmpA"])
                P.op("dve", lambda e: e.scalar_tensor_tensor(out=kmT[:], in0=tmpA[:], scalar=1.0, in1=xsT[:, 4:8, :], op0=ALU.add, op1=ALU.mult),
                     reads=["tmpA"] + XK, writes=["kmT"])
                rm = rmask[:, 1 if samp else 0, :]
                for cc in range(4):
                    P.op("dve", lambda e, cc=cc: e.tensor_tensor_scan(out=cumE[:, cc, :], data0=rm, data1=eT[:, cc, :], initial=0.0,
                                                                      op0=ALU.mult, op1=ALU.add),
                         reads=[f"eT{cc}", "const2"], writes=[f"cumE{cc}"])
                CK = [f"cumE{c}" for c in range(4)]
                P.op("pool", lambda e: e.tensor_tensor(out=rn[:], in0=cumE[:], in1=eT[:], op=ALU.subtract), reads=CK + EK + ["rn"], writes=["rn"])
                P.op("act", lambda e: e.activation(out=rn[:], in_=rn[:], func=AF.Exp, scale=-C0), reads=["rn"], writes=["rn"])
                P.op("act", lambda e: e.activation(out=Eg[:], in_=cumE[:], func=AF.Exp, scale=-C0), reads=CK, writes=["Eg"])
                P.op("act", lambda e: e.activation(out=cumE[:], in_=cumE[:], func=AF.Exp, scale=C0), reads=CK, writes=CK)
                Egi, Egx = cumE, rn
                if samp:
                    P.op("pool", lambda e: e.tensor_copy(out=gC[s3][:, :, :], in_=Eg[:, :, :].rearrange("p c (s t) -> p c s t", t=8)[:, :, :, 7]),
                         reads=["Eg"], writes=[f"gC{s3}"])
                else:
                    P.op("pool", lambda e: e.tensor_copy(out=gC[s3][:, :, 0], in_=Eg[:, :, 127]), reads=["Eg"], writes=[f"gC{s3}"])
                ARk = f"AR{s3}"
                P.op("dve", lambda e: e.tensor_tensor(out=AR[s3][:, :, 1, :], in0=xsT[:, 0:4, :], in1=Eg[:], op=ALU.mult),
                     reads=XK + ["Eg"], writes=[ARk + "r"])
                P.op("dve", lambda e: e.tensor_tensor(out=KTF[s3][:], in0=kmT[:], in1=Egi[:], op=ALU.mult), reads=["kmT"] + CK, writes=[f"KTF{s3}"])
                P.op("pool", lambda e: e.tensor_tensor(out=tmpA[:], in0=kkT[:], in1=asT[:], op=ALU.mult), reads=["kkT"] + AK, writes=["tmpA"])
                P.op("dve", lambda e: e.tensor_tensor(out=BTF[s3][:], in0=tmpA[:], in1=Egi[:], op=ALU.mult), reads=["tmpA"] + CK, writes=[f"BTF{s3}"])
                P.op("dve", lambda e: e.scalar_tensor_tensor(out=AR[s3][:, :, 0, :], in0=kkT[:], scalar=-1.0, in1=Egx[:], op0=ALU.mult, op1=ALU.mult),
                     reads=["kkT", "rn"], writes=[ARk + "a"])
                P.op("act", lambda e: e.copy(out=vb[:], in_=xsT[:, 8:12, :]), reads=XK, writes=["vb"])
                if own:
                    P.op("pool", lambda e: e.tensor_tensor(out=rn[:], in0=xsT[:, 0:4, :], in1=kmT[:], op=ALU.mult), reads=XK + ["kmT", "rn"], writes=["rn"])
                    P.op("dve", lambda e: e.tensor_tensor(out=sqb[:], in0=rn[:], in1=v4bc(RK), op=ALU.mult), reads=["rn", "const2"], writes=["sqb"])
                    bb, bbk = nextpj()
                    P.op("pe", lambda e: e.matmul(bb[:], lhsT=bones[:], rhs=sqb[:].rearrange("p a b -> p (a b)"), start=True, stop=True),
                         reads=["const", "sqb"], writes=[bbk])
                    P.op("dve", lambda e: e.tensor_tensor(out=bonT[i % len(bonT)][:].rearrange("p a b -> p (a b)"), in0=bb[:],
                                                          in1=xsT[:, 8:12, :].rearrange("p a b -> p (a b)"), op=ALU.mult),
                         reads=[bbk] + XK, writes=[f"bonT{i % len(bonT)}"])

                if pA:
                    gbc = gC[s3][:, :, 0:1].broadcast_to([128, 4, 128])
                    P.op("dve", lambda e: e.tensor_tensor(out=BHF[:], in0=BTF[s3][:], in1=gbc, op=ALU.mult), reads=[f"BTF{s3}", f"gC{s3}"], writes=["BHF"])
                    P.op("dve", lambda e: e.tensor_tensor(out=KHF[:], in0=KTF[s3][:], in1=gbc, op=ALU.mult), reads=[f"KTF{s3}", f"gC{s3}"], writes=["KHF"])
                    bsrc, ksrc, bsk, ksk = BHF, KHF, "BHF", "KHF"
                else:
                    bsrc, ksrc, bsk, ksk = BTF[s3], KTF[s3], f"BTF{s3}", f"KTF{s3}"

                def tr1(e):
                    r = None
                    for cc in range(4):
                        r = e.transpose(out=tpb[:, cc * 128:(cc + 1) * 128], in_=vb[:, cc, :], identity=ident[:])
                    for cc in range(4):
                        r = e.transpose(out=tpb[:, 512 + cc * 128:512 + (cc + 1) * 128], in_=bsrc[:, cc, :], identity=ident[:])
                    return r
                P.op("pe", tr1, reads=["vb", bsk, "const"], writes=["tpb"])
                P.op("act", lambda e: e.copy(out=VTM[s3][:], in_=tpb[:, 0:512]), reads=["tpb"], writes=[f"VTM{s3}"])
                P.op("dve", lambda e: e.tensor_copy(out=BTM[s3][:], in_=tpb[:, 512:1024]), reads=["tpb"], writes=[f"BTM{s3}"])

                def tr2(e):
                    r = None
                    for cc in range(4):
                        r = e.transpose(out=tpb[:, cc * 128:(cc + 1) * 128], in_=ksrc[:, cc, :], identity=ident[:])
                    if pA:
                        for cc in range(4):
                            r = e.transpose(out=tpb[:, 512 + cc * 128:512 + (cc + 1) * 128], in_=AR[s3][:, cc, 0, :], identity=ident[:])
                    return r
                P.op("pe", tr2, reads=[ksk, f"AR{s3}a", "const"], writes=["tpb"])
                P.op("act", lambda e: e.copy(out=KTM[s3][:], in_=tpb[:, 0:512]), reads=["tpb"], writes=[f"KTM{s3}"])
                if pA:
                    P.op("dve", lambda e: e.tensor_copy(out=ATM[s3][:], in_=tpb[:, 512:1024]), reads=["tpb"], writes=[f"ATM{s3}"])

            def nlev(i):
                return 3 if is_samp(i) else 7

            def S4(i):
                s3 = i % R2
                kd = 1 if is_samp(i) else 0
                ARk = [f"AR{s3}a", f"AR{s3}r"]
                for h in range(8):
                    cc, pb = h // 2, (h % 2) * 64
                    bank, bk = nextl0()

                    def mm0(e, cc=cc, pb=pb, bank=bank):
                        e.matmul(bank[:, 0:256], lhsT=BTF[s3][pb:pb + 64, cc, :], rhs=AR[s3][pb:pb + 64, cc, :, :], start=True, stop=True)
                        return e.matmul(bank[:, 256:512], lhsT=KTF[s3][pb:pb + 64, cc, :], rhs=AR[s3][pb:pb + 64, cc, :, :], start=True, stop=True)
                    P.op("pe", mm0, reads=ARk + [f"BTF{s3}", f"KTF{s3}"], writes=[bk])
                    P.op("dve", lambda e, h=h, bank=bank: e.tensor_tensor(out=SQ0[:, h, 0:128], in0=bank[:, 0:128], in1=MKL[:, kd, 0, :], op=ALU.mult),
                         reads=[bk, "MKL"], writes=[f"SQ0n{h}"])
                    P.op("dve", lambda e, h=h, bank=bank: e.tensor_tensor(out=NM[:, h, :], in0=bank[:, 128:512],
                                                                          in1=MKL[:, kd, 1:4, :].rearrange("p a b -> p (a b)"), op=ALU.mult),
                         reads=[bk, "MKL"], writes=[f"NM{h}"])
                for g4 in range(2):
                    bank, bk = nextsq()

                    def mma0(e, g4=g4, bank=bank):
                        r = None
                        for hh in range(4):
                            h = g4 * 4 + hh
                            cc, pb = h // 2, (h % 2) * 64
                            r = e.matmul(bank[:, hh * 128:(hh + 1) * 128], lhsT=AR[s3][pb:pb + 64, cc, 0, :], rhs=BTF[s3][pb:pb + 64, cc, :],
                                         start=True, stop=True)
                        return r
                    P.op("pe", mma0, reads=ARk + [f"BTF{s3}"], writes=[bk])
                    slbc = MK[:, 2 + 3 * kd, :].unsqueeze(1).broadcast_to([128, 4, 128])
                    P.op("dve", lambda e, g4=g4, bank=bank, slbc=slbc: e.tensor_tensor(
                        out=SQ0[:, g4 * 4:(g4 + 1) * 4, 128:256], in0=bank[:].rearrange("p (a b) -> p a b", a=4), in1=slbc, op=ALU.mult),
                        reads=[bk, "const"], writes=[f"SQ0a{g4 * 4 + q}" for q in range(4)])
                nl = nlev(i)
                for j in range(nl - 1):
                    for pr in range(4):
                        bank, bk = nextsq()
                        hs = (2 * pr, 2 * pr + 1)

                        def Nsrc(h, j=j):
                            return SQ0[:, h, 0:128] if j == 0 else NJ[:, (j - 1) % 2, h, :]

                        def Asrc(h, j=j):
                            return SQ0[:, h, 128:256] if j == 0 else AJ[:, (j - 1) % 2, h, :]
                        rk = []
                        for h in hs:
                            rk += ([f"SQ0n{h}", f"SQ0a{h}"] if j == 0 else [f"NJ{(j - 1) % 2}_{h}", f"AJ{(j - 1) % 2}_{h}"])
                        last = (j == nl - 2)

                        def mmsq(e, hs=hs, bank=bank, Nsrc=Nsrc, Asrc=Asrc, last=last):
                            r = None
                            for q, h in enumerate(hs):
                                r = e.matmul(bank[:, q * 256:q * 256 + 128], lhsT=Asrc(h), rhs=Nsrc(h), start=True, stop=True)
                                if not last:
                                    r = e.matmul(bank[:, q * 256 + 128:q * 256 + 256], lhsT=Nsrc(h), rhs=Asrc(h), start=True, stop=True)
                            return r
                        P.op("pe", mmsq, reads=rk, writes=[bk])
                        b3 = bank[:].rearrange("p (a b) -> p a b", a=2)
                        P.op("act", lambda e, j=j, pr=pr, b3=b3: e.copy(out=NJ[:, j % 2, 2 * pr:2 * pr + 2, :], in_=b3[:, :, 0:128]),
                             reads=[bk], writes=[f"NJ{j % 2}_{h}" for h in hs])
                        if not last:
                            P.op("dve", lambda e, j=j, pr=pr, b3=b3: e.tensor_copy(out=AJ[:, j % 2, 2 * pr:2 * pr + 2, :], in_=b3[:, :, 128:256]),
                                 reads=[bk], writes=[f"AJ{j % 2}_{h}" for h in hs])

            def S5(i):
                s3 = i % R2
                own = is_own(i)
                samp = is_samp(i)
                nl = nlev(i)
                ARk = [f"AR{s3}a", f"AR{s3}r"]
                if samp:
                    sample_h0(s3)

                def z0(e):
                    r = None
                    for h in range(8):
                        cc, pb = h // 2, (h % 2) * 64
                        if samp:
                            r = e.matmul(Zp[:, h * 64:(h + 1) * 64], lhsT=zh[pb:pb + 64, cc, 0, :], rhs=ident[pb:pb + 64, pb:pb + 64],
                                         start=(h == 0), stop=False, skip_group_check=True)
                        else:
                            r = e.matmul(Zp[:, h * 64:(h + 1) * 64], lhsT=AR[s3][pb:pb + 64, cc, 0, :], rhs=Hb[pb:pb + 64, cc, :],
                                         start=(h == 0), stop=False, skip_group_check=True)
                        r = e.matmul(Zp[:, h * 64:(h + 1) * 64], lhsT=NM[:, h, 128:256], rhs=VTM[s3][:, h * 64:(h + 1) * 64],
                                     start=False, stop=False, skip_group_check=True)
                    return r
                P.op("pe", z0, reads=ARk + ["Hb", "zh", "const", f"VTM{s3}"] + [f"NM{h}" for h in range(8)], writes=["Zp"])
                for j in range(nl):
                    zb = Zb[j % 2]
                    zk = f"Zb{j % 2}"
                    if j % 2 == 0:
                        P.op("act", lambda e, zb=zb: e.copy(out=zb[:], in_=Zp[:]), reads=["Zp"], writes=[zk])
                    else:
                        P.op("dve", lambda e, zb=zb: e.tensor_copy(out=zb[:], in_=Zp[:]), reads=["Zp"], writes=[zk])
                    rk = [zk] + ([f"SQ0n{h}" for h in range(8)] if j == 0 else [f"NJ{(j - 1) % 2}_{h}" for h in range(8)])

                    def ap(e, j=j, zb=zb):
                        r = None
                        for h in range(8):
                            lt = SQ0[:, h, 0:128] if j == 0 else NJ[:, (j - 1) % 2, h, :]
                            r = e.matmul(Zp[:, h * 64:(h + 1) * 64], lhsT=lt, rhs=zb[:, h * 64:(h + 1) * 64], start=False, stop=(j == nl - 1),
                                         skip_group_check=True)
                        return r
                    P.op("pe", ap, reads=rk, writes=["Zp"])
                P.op("act", lambda e: e.copy(out=Ub[:], in_=Zp[:]), reads=["Zp"], writes=["Ub"])
                if own:
                    ob, obk = nextpj()
                    oq = i % R2

                    def mo(e):
                        r = None
                        for h in range(8):
                            cc, pb = h // 2, (h % 2) * 64
                            o = ob[:, h * 64:(h + 1) * 64]
                            if samp:
                                e.matmul(o, lhsT=zh[pb:pb + 64, cc, 1, :], rhs=ident[pb:pb + 64, pb:pb + 64], start=True, stop=False)
                            else:
                                e.matmul(o, lhsT=AR[s3][pb:pb + 64, cc, 1, :], rhs=Hb[pb:pb + 64, cc, :], start=True, stop=False)
                            e.matmul(o, lhsT=NM[:, h, 0:128], rhs=Ub[:, h * 64:(h + 1) * 64], start=False, stop=False)
                            r = e.matmul(o, lhsT=NM[:, h, 256:384], rhs=VTM[s3][:, h * 64:(h + 1) * 64], start=False, stop=True)
                        return r
                    P.op("pe", mo, reads=ARk + ["Hb", "zh", "const", "Ub", f"VTM{s3}"] + [f"NM{h}" for h in range(8)], writes=[obk])
                    P.op("act", lambda e: e.copy(out=OTM[oq][:].rearrange("p a b -> p (a b)"), in_=ob[:]), reads=[obk], writes=[f"OTM{oq}"])
                if samp:
                    sample_state(s3)
                    return

                def su(e):
                    r = None
                    for h in range(8):
                        cc, pb = h // 2, (h % 2) * 64
                        o = Zp[pb:pb + 64, cc * 64:(cc + 1) * 64]
                        e.matmul(o, lhsT=BTM[s3][:, h * 64:(h + 1) * 64], rhs=Ub[:, h * 64:(h + 1) * 64], start=True, stop=False)
                        r = e.matmul(o, lhsT=KTM[s3][:, h * 64:(h + 1) * 64], rhs=VTM[s3][:, h * 64:(h + 1) * 64], start=False, stop=True)
                    return r
                P.op("pe", su, reads=["Ub", f"BTM{s3}", f"KTM{s3}", f"VTM{s3}"], writes=["Zp"])
                P.op("dve", lambda e: e.tensor_tensor(out=tS[:].rearrange("p a b -> p (a b)"), in0=Zp[:, 0:256],
                                                      in1=Hst[:].rearrange("p a b -> p (a b)"), op=ALU.add),
                     reads=["Zp", "Hst"], writes=["tS"])
                P.op("dve", lambda e: e.tensor_tensor(out=Hst[:], in0=tS[:], in1=gC[s3][:, :, 0:1].broadcast_to([128, 4, 64]), op=ALU.mult),
                     reads=["tS", f"gC{s3}"], writes=["Hst"])
                P.op("act", lambda e: e.copy(out=Hb[:], in_=Hst[:]), reads=["Hst"], writes=["Hb"])
                if i == NPT - 1:
                    bank, bk = nextpj()

                    def trs(e):
                        r = None
                        for cc in range(4):
                            r = e.transpose(out=bank[0:64, cc * 128:(cc + 1) * 128], in_=Hst[:, cc, :], identity=ident32[:])
                        return r
                    P.op("pe", trs, reads=["Hst", "const2"], writes=[bk])
                    P.op("dve", lambda e: e.tensor_copy(out=osq[0:64, :, :].rearrange("p a b -> p (a b)"), in_=bank[0:64, :]), reads=[bk, "osq"], writes=["osq"])
                    P.op("sp", lambda e: e.dma_start(out=wkv_p.rearrange("h v k -> v h k"), in_=osq[0:64, :, :]), reads=["osq"], writes=["o_wkv_p"], dma="o_wkv_p")
                    outkeys.append("o_wkv_p")

            def S4h(i, half):
                s3 = i % R2
                own = is_own(i)
                ARk = [f"AR{s3}a", f"AR{s3}r"]
                hs4 = list(range(4 * half, 4 * half + 4))
                lb_, lbk = l0[half], f"l0{half}"
                sb_, sbk = sqp[half], f"sqp{half}"
                for h in hs4:
                    cc, pb = h // 2, (h % 2) * 64

                    def mm0(e, cc=cc, pb=pb):
                        e.matmul(lb_[:, 0:256], lhsT=BTF[s3][pb:pb + 64, cc, :], rhs=AR[s3][pb:pb + 64, cc, :, :], start=True, stop=True)
                        return e.matmul(lb_[:, 256:512], lhsT=KTF[s3][pb:pb + 64, cc, :], rhs=AR[s3][pb:pb + 64, cc, :, :], start=True, stop=True)
                    P.op("pe", mm0, reads=ARk + [f"BTF{s3}", f"KTF{s3}"], writes=[lbk])
                    P.op("dve", lambda e, h=h: e.tensor_tensor(out=L0S[:, h, :], in0=lb_[:],
                                                               in1=MKL[:, 0, :, :].rearrange("p a b -> p (a b)"), op=ALU.mult),
                         reads=[lbk, "MKL"], writes=[f"L0S{h}"])

                def mma0(e):
                    r = None
                    for hh, h in enumerate(hs4):
                        cc, pb = h // 2, (h % 2) * 64
                        r = e.matmul(sb_[:, hh * 128:(hh + 1) * 128], lhsT=AR[s3][pb:pb + 64, cc, 0, :], rhs=BTF[s3][pb:pb + 64, cc, :],
                                     start=True, stop=True)
                    return r
                P.op("pe", mma0, reads=ARk + [f"BTF{s3}"], writes=[sbk])
                slbc = MK[:, 2, :].unsqueeze(1).broadcast_to([128, 4, 128])
                P.op("dve", lambda e: e.tensor_tensor(out=A0S[:, 4 * half:4 * half + 4, :], in0=sb_[:].rearrange("p (a b) -> p a b", a=4), in1=slbc, op=ALU.mult),
                     reads=[sbk, "const"], writes=[f"A0S{h}" for h in hs4])

                def x0(e):
                    r = None
                    for hh, h in enumerate(hs4):
                        e.matmul(lb_[:, hh * 128:hh * 128 + 64], lhsT=ident[:], rhs=ATM[s3][:, h * 64:(h + 1) * 64],
                                 start=(hh == 0), stop=False, skip_group_check=True)
                        r = e.matmul(lb_[:, hh * 128 + 64:(hh + 1) * 128], lhsT=L0S[:, h, 256:384], rhs=VTM[s3][:, h * 64:(h + 1) * 64],
                                     start=False, stop=False, skip_group_check=True)
                    return r
                P.op("pe", x0, reads=["const", f"ATM{s3}", f"VTM{s3}"] + [f"L0S{h}" for h in hs4], writes=[lbk])
                for j in range(7):
                    xb = Xb[j % 2]
                    xk = f"Xb{j % 2}_{half}"
                    if half == 0 or j % 2 == 0:
                        P.op("act", lambda e, xb=xb: e.copy(out=xb[:, half, :], in_=lb_[:]), reads=[lbk], writes=[xk])
                    else:
                        P.op("dve", lambda e, xb=xb: e.tensor_copy(out=xb[:, half, :], in_=lb_[:]), reads=[lbk], writes=[xk])
                    if j < 6:
                        last = (j == 5)
                        for pr in (2 * half, 2 * half + 1):
                            hs = (2 * pr, 2 * pr + 1)

                            def Nsrc(h, j=j):
                                return L0S[:, h, 0:128] if j == 0 else NA[:, (j - 1) % 2, h, 0:128]

                            def Asrc(h, j=j):
                                return A0S[:, h, :] if j == 0 else NA[:, (j - 1) % 2, h, 128:256]
                            rk = []
                            for h in hs:
                                rk += ([f"L0S{h}", f"A0S{h}"] if j == 0 else [f"NA{(j - 1) % 2}_{h}"])

                            def mmsq(e, hs=hs, Nsrc=Nsrc, Asrc=Asrc, last=last):
                                r = None
                                for q, h in enumerate(hs):
                                    r = e.matmul(sb_[:, q * 256:q * 256 + 128], lhsT=Asrc(h), rhs=Nsrc(h), start=True, stop=True)
                                    if not last:
                                        r = e.matmul(sb_[:, q * 256 + 128:q * 256 + 256], lhsT=Nsrc(h), rhs=Asrc(h), start=True, stop=True)
                                return r
                            P.op("pe", mmsq, reads=rk, writes=[sbk])
                            dstna = NA[:, j % 2, 2 * pr:2 * pr + 2, :].rearrange("p a b -> p (a b)")
                            if pr % 2 == 0:
                                P.op("act", lambda e, dstna=dstna: e.copy(out=dstna, in_=sb_[:]), reads=[sbk], writes=[f"NA{j % 2}_{h}" for h in hs])
                            else:
                                P.op("dve", lambda e, dstna=dstna: e.tensor_copy(out=dstna, in_=sb_[:]), reads=[sbk], writes=[f"NA{j % 2}_{h}" for h in hs])
                    rk = [xk] + ([f"L0S{h}" for h in hs4] if j == 0 else [f"NA{(j - 1) % 2}_{h}" for h in hs4])

                    def ap(e, j=j, xb=xb):
                        r = None
                        for hh, h in enumerate(hs4):
                            lt = L0S[:, h, 0:128] if j == 0 else NA[:, (j - 1) % 2, h, 0:128]
                            r = e.matmul(lb_[:, hh * 128:(hh + 1) * 128], lhsT=lt, rhs=xb[:, half, hh * 128:(hh + 1) * 128],
                                         start=False, stop=(j == 6), skip_group_check=True)
                        return r
                    P.op("pe", ap, reads=rk + [lbk], writes=[lbk])
                wk = f"WY{half}"
                if half == 0:
                    P.op("act", lambda e: e.copy(out=WY[:, 0:4, :].rearrange("p a b -> p (a b)"), in_=lb_[:]), reads=[lbk], writes=[wk])
                else:
                    P.op("dve", lambda e: e.tensor_copy(out=WY[:, 4:8, :].rearrange("p a b -> p (a b)"), in_=lb_[:]), reads=[lbk], writes=[wk])

                def mmt(e):
                    r = None
                    for h in hs4:
                        cc, pb = h // 2, (h % 2) * 64
                        r = e.matmul(sb_[pb:pb + 64, cc * 64:(cc + 1) * 64], lhsT=WY[:, h, 0:64], rhs=BTM[s3][:, h * 64:(h + 1) * 64], start=True, stop=True)
                    return r
                P.op("pe", mmt, reads=[wk, f"BTM{s3}"], writes=[sbk])
                P.op("act", lambda e: e.copy(out=MTb[:, 2 * half:2 * half + 2, :].rearrange("p a b -> p (a b)"), in_=sb_[:, 128 * half:128 * half + 128]),
                     reads=[sbk], writes=[f"MTb{half}"])
                if own:
                    wb16 = sb_[:].bitcast(BF16)

                    def trw(e):
                        r = None
                        for h in hs4:
                            cc, pb = h // 2, (h % 2) * 64
                            r = e.transpose(out=wb16[pb:pb + 64, cc * 128:(cc + 1) * 128], in_=WY[:, h, 0:64], identity=ident[:])
                        return r
                    P.op("pe", trw, reads=[wk, "const"], writes=[sbk])
                    P.op("act", lambda e: e.copy(out=WTF[:, 2 * half:2 * half + 2, :].rearrange("p a b -> p (a b)"), in_=wb16[:, 256 * half:256 * half + 256]),
                         reads=[sbk], writes=[f"WTF{half}"])

            def S4tail(i):
                s3 = i % R2

                def gp(e):
                    r = None
                    for h in range(8):
                        cc, pb = h // 2, (h % 2) * 64
                        o = Zp[pb:pb + 64, cc * 64:(cc + 1) * 64]
                        e.matmul(o, lhsT=BTM[s3][:, h * 64:(h + 1) * 64], rhs=WY[:, h, 64:128], start=(h < 2), stop=False, skip_group_check=True)
                        r = e.matmul(o, lhsT=KTM[s3][:, h * 64:(h + 1) * 64], rhs=VTM[s3][:, h * 64:(h + 1) * 64], start=False, stop=False, skip_group_check=True)
                    return r
                P.op("pe", gp, reads=["WY0", "WY1", f"BTM{s3}", f"KTM{s3}", f"VTM{s3}"], writes=["Zp"])

            def S5n(i):
                s3 = i % R2
                own = is_own(i)
                ARk = [f"AR{s3}a", f"AR{s3}r"]
                WK = ["WY0", "WY1"]
                if own:
                    ub, ubk = nextpj()

                    def mu_(e):
                        r = None
                        for h in range(8):
                            cc, pb = h // 2, (h % 2) * 64
                            o = ub[:, h * 64:(h + 1) * 64]
                            e.matmul(o, lhsT=WTF[pb:pb + 64, cc, :], rhs=Hb[pb:pb + 64, cc, :], start=True, stop=False)
                            r = e.matmul(o, lhsT=ident[:], rhs=WY[:, h, 64:128], start=False, stop=True)
                        return r
                    P.op("pe", mu_, reads=WK + ["WTF0", "WTF1", "Hb", "const"], writes=[ubk])
                    P.op("act", lambda e: e.copy(out=Ub[:], in_=ub[:]), reads=[ubk], writes=["Ub"])
                    ob, obk = nextpj()
                    oq = i % R2

                    def mo(e):
                        r = None
                        for h in range(8):
                            cc, pb = h // 2, (h % 2) * 64
                            o = ob[:, h * 64:(h + 1) * 64]
                            e.matmul(o, lhsT=AR[s3][pb:pb + 64, cc, 1, :], rhs=Hb[pb:pb + 64, cc, :], start=True, stop=False)
                            e.matmul(o, lhsT=L0S[:, h, 128:256], rhs=Ub[:, h * 64:(h + 1) * 64], start=False, stop=False)
                            r = e.matmul(o, lhsT=L0S[:, h, 384:512], rhs=VTM[s3][:, h * 64:(h + 1) * 64], start=False, stop=True)
                        return r
                    P.op("pe", mo, reads=ARk + ["Hb", "Ub", f"VTM{s3}"] + [f"L0S{h}" for h in range(8)], writes=[obk])
                    P.op("act", lambda e: e.copy(out=OTM[oq][:].rearrange("p a b -> p (a b)"), in_=ob[:]), reads=[obk], writes=[f"OTM{oq}"])

                def ch(e):
                    r = None
                    for h in range(8):
                        cc, pb = h // 2, (h % 2) * 64
                        r = e.matmul(Zp[pb:pb + 64, cc * 64:(cc + 1) * 64], lhsT=MTb[pb:pb + 64, cc, :], rhs=Hb[pb:pb + 64, cc, :],
                                     start=False, stop=True, skip_group_check=True)
                    return r
                P.op("pe", ch, reads=["MTb0", "MTb1", "Hb", "Zp"], writes=["Zp"])
                P.op("dve", lambda e: e.tensor_tensor(out=Hst[:].rearrange("p a b -> p (a b)"), in0=Zp[:, 0:256],
                                                      in1=tS[:].rearrange("p a b -> p (a b)"), op=ALU.add),
                     reads=["Zp", "tS"], writes=["Hst"])
                P.op("act", lambda e: e.copy(out=Hb[:], in_=Hst[:]), reads=["Hst"], writes=["Hb"])
                if i + 1 < NPT:
                    n3 = (i + 1) % R2
                    P.op("pool", lambda e: e.tensor_tensor(out=tS[:], in0=Hst[:], in1=gC[n3][:, :, 0:1].broadcast_to([128, 4, 64]), op=ALU.mult),
                         reads=["Hst", f"gC{n3}"], writes=["tS"])
                if i == NPT - 1:
                    bank, bk = nextpj()

                    def trs(e):
                        r = None
                        for cc in range(4):
                            r = e.transpose(out=bank[0:64, cc * 128:(cc + 1) * 128], in_=Hst[:, cc, :], identity=ident32[:])
                        return r
                    P.op("pe", trs, reads=["Hst", "const2"], writes=[bk])
                    P.op("dve", lambda e: e.tensor_copy(out=osq[0:64, :, :].rearrange("p a b -> p (a b)"), in_=bank[0:64, :]), reads=[bk, "osq"], writes=["osq"])
                    P.op("sp", lambda e: e.dma_start(out=wkv_p.rearrange("h v k -> v h k"), in_=osq[0:64, :, :]), reads=["osq"], writes=["o_wkv_p"], dma="o_wkv_p")
                    outkeys.append("o_wkv_p")

            def sample_h0(s3):
                for q4 in range(4):
                    P.op("sp", lambda e, q4=q4: e.dma_start(out=S0q[:], in_=swkv[q4 * 4:(q4 + 1) * 4].rearrange("s h v k -> v (s h) k")),
                         writes=["S0q"], dma="S0q")
                    for sl_ in range(4):
                        s = q4 * 4 + sl_
                        bank, bk = nextpj()

                        def trs(e, sl_=sl_, bank=bank):
                            r = None
                            for cc in range(4):
                                r = e.transpose(out=bank[:, cc * 64:(cc + 1) * 64],
                                                in_=S0q[:, sl_ * 8 + 2 * cc:sl_ * 8 + 2 * cc + 2, :].rearrange("p a b -> p (a b)"),
                                                identity=ident32[0:64, 0:64])
                            return r
                        P.op("pe", trs, reads=["S0q", "const2"], writes=[bk])
                        P.op("dve", lambda e, s=s, bank=bank: e.tensor_copy(out=H0s[:, s, :, :].rearrange("p a b -> p (a b)"), in_=bank[:, 0:256]),
                             reads=[bk], writes=[f"H0s{s}"])
                        P.op("act", lambda e, s=s, bank=bank: e.copy(out=H0b[:, s, :, :].rearrange("p a b -> p (a b)"), in_=bank[:, 0:256]),
                             reads=[bk], writes=[f"H0b{s}"])
                for cc in range(4):
                    bank, bk = nextpj()

                    def mmz(e, cc=cc, bank=bank):
                        r = None
                        for hh in range(2):
                            pb = hh * 64
                            for s in range(16):
                                r = e.matmul(bank[pb:pb + 64, s * 16:s * 16 + 16],
                                             lhsT=H0b[pb:pb + 64, s, cc, :],
                                             rhs=AR[s3][pb:pb + 64, cc, :, s * 8:(s + 1) * 8], start=True, stop=True)
                        return r
                    P.op("pe", mmz, reads=[f"H0b{s}" for s in range(16)] + [f"AR{s3}a", f"AR{s3}r"], writes=[bk])
                    src = bank[:, 0:256].rearrange("p (s a t) -> p a s t", s=16, a=2)
                    for ar in range(2):
                        dst = zh[:, cc, ar, :].rearrange("p (s t) -> p s t", t=8)
                        P.op("dve", lambda e, src=src, dst=dst, ar=ar: e.tensor_copy(out=dst, in_=src[:, ar, :, :]), reads=[bk], writes=["zh"])

            def sample_state(s3):
                e16bc = e16[:, :].unsqueeze(2).broadcast_to([128, 16, 64])
                HK = [f"H0s{s}" for s in range(16)]
                for h in range(8):
                    cc, pb = h // 2, (h % 2) * 64
                    P.op("dve", lambda e, h=h: e.tensor_tensor(out=Xex[:, 0, :, :], in0=Ub[:, h * 64:(h + 1) * 64].unsqueeze(1).broadcast_to([128, 16, 64]),
                                                               in1=e16bc, op=ALU.mult),
                         reads=["Ub", "const"], writes=["Xex0"])
                    P.op("pool", lambda e, h=h: e.tensor_tensor(out=Xex[:, 1, :, :], in0=VTM[s3][:, h * 64:(h + 1) * 64].unsqueeze(1).broadcast_to([128, 16, 64]),
                                                                in1=e16bc, op=ALU.mult),
                         reads=[f"VTM{s3}", "const"], writes=["Xex1"])
                    for half in range(2):
                        bank, bk = nextl0()

                        def mms(e, h=h, half=half, bank=bank, pb=pb):
                            e.matmul(bank[pb:pb + 64, :], lhsT=BTM[s3][:, h * 64:(h + 1) * 64],
                                     rhs=Xex[:, 0, half * 8:(half + 1) * 8, :].rearrange("p a b -> p (a b)"), start=True, stop=False)
                            return e.matmul(bank[pb:pb + 64, :], lhsT=KTM[s3][:, h * 64:(h + 1) * 64],
                                            rhs=Xex[:, 1, half * 8:(half + 1) * 8, :].rearrange("p a b -> p (a b)"), start=False, stop=True)
                        P.op("pe", mms, reads=["Xex0", "Xex1", f"BTM{s3}", f"KTM{s3}"], writes=[bk])
                        P.op("dve", lambda e, half=half, cc=cc, bank=bank, pb=pb: e.tensor_tensor(
                            out=H0s[pb:pb + 64, half * 8:(half + 1) * 8, cc, :], in0=bank[pb:pb + 64, :].rearrange("p (s v) -> p s v", s=8),
                            in1=H0s[pb:pb + 64, half * 8:(half + 1) * 8, cc, :], op=ALU.add),
                            reads=[bk] + HK, writes=HK)
                for cc in range(4):
                    P.op("dve", lambda e, cc=cc: e.tensor_tensor(out=H0s[:, :, cc, :], in0=H0s[:, :, cc, :],
                                                                 in1=gC[s3][:, cc, :].unsqueeze(2).broadcast_to([128, 16, 64]), op=ALU.mult),
                         reads=HK + [f"gC{s3}"], writes=HK)
                for q4 in range(4):
                    for sl_ in range(4):
                        s = q4 * 4 + sl_
                        bank, bk = nextpj()

                        def trw(e, s=s, bank=bank):
                            r = None
                            for cc in range(4):
                                r = e.transpose(out=bank[0:64, cc * 128:(cc + 1) * 128], in_=H0s[:, s, cc, :], identity=ident32[:])
                            return r
                        P.op("pe", trw, reads=HK + ["const2"], writes=[bk])
                        if s % 2 == 0:
                            P.op("act", lambda e, sl_=sl_, bank=bank: e.copy(out=wso[:, sl_, :, :].rearrange("p a b -> p (a b)"), in_=bank[0:64, :]),
                                 reads=[bk], writes=[f"wso{sl_}"])
                        else:
                            P.op("dve", lambda e, sl_=sl_, bank=bank: e.tensor_copy(out=wso[:, sl_, :, :].rearrange("p a b -> p (a b)"), in_=bank[0:64, :]),
                                 reads=[bk], writes=[f"wso{sl_}"])
                    P.op("sp", lambda e, q4=q4: e.dma_start(out=wkv_s[q4 * 4:(q4 + 1) * 4].rearrange("s h v k -> v s h k"), in_=wso[:]),
                         reads=[f"wso{q}" for q in range(4)], writes=[f"o_wkv_s{q4}"] + [f"wso{q}" for q in range(4)], dma="o_wkv_s")
                    outkeys.append(f"o_wkv_s{q4}")

            def S6(i):
                qq = i % len(qT)
                s2 = i % R2
                samp = is_samp(i)
                kc3 = i % NKV
                kp3 = (i - 1) % NKV
                if i == NPRE:
                    P.op("pe", lambda e: e.transpose(out=tpb[:, 0:128], in_=kvT[kp3][:, 1, :], identity=ident[:]), reads=[f"kvT{kp3}", "const"], writes=["tpb"])
                    P.op("act", lambda e: e.copy(out=Vaug[kp3][:, :, 0:64], in_=tpb[:, 0:128].rearrange("p (g d) -> p g d", g=2)),
                         reads=["tpb"], writes=[f"Vaug{kp3}"])
                P.op("pe", lambda e: e.transpose(out=tpb[:, 0:128], in_=kvT[kc3][:, 1, :], identity=ident[:]), reads=[f"kvT{kc3}", "const"], writes=["tpb"])
                P.op("act", lambda e: e.copy(out=Vaug[kc3][:, :, 0:64], in_=tpb[:, 0:128].rearrange("p (g d) -> p g d", g=2)),
                     reads=["tpb"], writes=[f"Vaug{kc3}"])
                if samp:
                    sample_cache(qq)
                for g in range(2):
                    for kt in range(2):
                        if samp and kt == 0:
                            continue
                        bank, bk = nextl0()
                        kb = kvT[kp3] if kt == 0 else kvT[kc3]
                        kbk = f"kvT{kp3}" if kt == 0 else f"kvT{kc3}"
                        P.op("pe", lambda e, bank=bank, kb=kb, g=g: e.matmul(bank[:], lhsT=kb[g * 64:(g + 1) * 64, 0, :],
                                                                             rhs=qT[qq][g * 64:(g + 1) * 64, :, :], start=True, stop=True),
                             reads=[kbk, f"qT{qq}"], writes=[bk])
                        pt = PT[g * 2 + kt]
                        ptk = f"PT{g * 2 + kt}"
                        P.op("act", lambda e, bank=bank, pt=pt: e.activation(out=pt[:].rearrange("p a b -> p (a b)"), in_=bank[:], func=AF.Exp),
                             reads=[bk], writes=[ptk])
                        if kt == 0:
                            mi = 6 if i == NPRE else 2
                        else:
                            mi = 4 if samp else 1
                        mbc = MK[:, mi, :].unsqueeze(1).broadcast_to([128, 4, 128])
                        P.op("dve", lambda e, pt=pt, mbc=mbc: e.tensor_tensor(out=pt[:], in0=pt[:], in1=mbc, op=ALU.mult),
                             reads=[ptk, "const"], writes=[ptk])
                for g in range(2):
                    bank, bk = nextsq()
                    b3 = bank[:, 0:260].rearrange("p (a b) -> p a b", a=4)
                    for j in range(4):
                        h = g * 4 + j
                        if samp:
                            px = PTx[h % 2]
                            pxk = f"PTx{h % 2}"
                            P.op("dve", lambda e, g=g, j=j, px=px: [e.tensor_tensor(
                                out=px[:, s, s * 8:(s + 1) * 8], in0=PTc[:, g, s, j * 8:(j + 1) * 8], in1=MK[:, 2, 0:8], op=ALU.mult) for s in range(16)][-1],
                                reads=[f"PTc{g}", "const"], writes=[pxk])

                        def pv(e, g=g, j=j, b3=b3, h=h):
                            r = None
                            if samp:
                                for s in range(16):
                                    e.matmul(b3[:, j, :], lhsT=PTx[h % 2][:, s, :], rhs=Vca[:, s, g, :], start=(s == 0), stop=False)
                            else:
                                e.matmul(b3[:, j, :], lhsT=PT[g * 2][:, j, :], rhs=Vaug[kp3][:, g, :], start=True, stop=False)
                            r = e.matmul(b3[:, j, :], lhsT=PT[g * 2 + 1][:, j, :], rhs=Vaug[kc3][:, g, :], start=False, stop=True)
                            return r
                        rd = [f"PT{g * 2 + 1}", f"Vaug{kc3}"] + ([f"PTx{h % 2}", "Vca"] if samp else [f"PT{g * 2}", f"Vaug{kp3}"])
                        P.op("pe", pv, reads=rd, writes=[bk + f"_{j}"] + ([bk] if j == 0 else []))
                    bkj = [bk] + [bk + f"_{j}" for j in range(4)]
                    P.op("dve", lambda e, g=g, b3=b3: e.tensor_tensor(out=den[:, g * 4:(g + 1) * 4], in0=b3[:, :, 64], in1=esink[:, g * 4:(g + 1) * 4], op=ALU.add),
                         reads=bkj + ["esink"], writes=[f"den{g}"])
                    P.op("dve", lambda e, g=g: e.reciprocal(out=den[:, g * 4:(g + 1) * 4], in_=den[:, g * 4:(g + 1) * 4]), reads=[f"den{g}"], writes=[f"den{g}"])
                    P.op("dve", lambda e, g=g, b3=b3: e.tensor_tensor(out=attb[:, g * 4:(g + 1) * 4, :], in0=b3[:, :, 0:64],
                                                                      in1=den[:, g * 4:(g + 1) * 4].unsqueeze(2).broadcast_to([128, 4, 64]), op=ALU.mult),
                         reads=bkj + [f"den{g}"], writes=[f"attb{g}"])

                def tra(e):
                    r = None
                    for c in range(4):
                        r = e.transpose(out=tpb[:, c * 128:(c + 1) * 128], in_=attb[:, 2 * c:2 * c + 2, :].rearrange("p a b -> p (a b)"), identity=ident[:])
                    return r
                P.op("pe", tra, reads=["attb0", "attb1", "const"], writes=["tpb"])
                P.op("act", lambda e: e.copy(out=mixT[s2][:, 0:4, :].rearrange("p a b -> p (a b)"), in_=tpb[:, 0:512]), reads=["tpb"], writes=[f"mixT{s2}a"])

            def sample_cache_loads():
                P.op("pool", lambda e: e.dma_start(out=cstb[:], in_=ck.rearrange("s k d -> k s d")), writes=["cstb"], dma="cstb")
                P.op("pool", lambda e: [e.dma_start(out=Vca[:, :, g, 0:64], in_=cv[:, :, g * 64:(g + 1) * 64].rearrange("s k d -> k s d")) for g in range(2)],
                     reads=["Vca"], writes=["Vca"], dma="Vca", n=2)

            def sample_cache(qq):
                for q in range(2):
                    def trc(e, q=q):
                        r = None
                        for ss_ in range(8):
                            r = e.transpose(out=tpb[:, ss_ * 128:(ss_ + 1) * 128], in_=cstb[:, q * 8 + ss_, :], identity=ident[:])
                        return r
                    P.op("pe", trc, reads=["cstb", "const"], writes=["tpb"])
                    P.op("act", lambda e, q=q: e.copy(out=KcT[:, q * 8:(q + 1) * 8, :].rearrange("p a b -> p (a b)"), in_=tpb[:]), reads=["tpb"], writes=[f"KcT{q}"])
                for g in range(2):
                    bank, bk = nextl0()

                    def scc(e, g=g, bank=bank):
                        r = None
                        for s in range(16):
                            r = e.matmul(bank[:, s * 32:(s + 1) * 32], lhsT=KcT[g * 64:(g + 1) * 64, s, :],
                                         rhs=qT[qq][g * 64:(g + 1) * 64, :, s * 8:(s + 1) * 8], start=True, stop=True)
                        return r
                    P.op("pe", scc, reads=["KcT0", "KcT1", f"qT{qq}"], writes=[bk])
                    P.op("act", lambda e, g=g, bank=bank: e.activation(out=PTc[:, g, :, :].rearrange("p a b -> p (a b)"), in_=bank[:], func=AF.Exp),
                         reads=[bk], writes=[f"PTc{g}"])

            def S7(i):
                s2 = i % R2
                s3 = i % len(gT)
                j = i - NPRE
                o = OTM[i % R2]
                ok = f"OTM{i % R2}"
                P.op("dve", lambda e: e.tensor_reduce(out=st1[:], in_=o[:], axis=AX.X, op=ALU.add), reads=[ok], writes=["st1"])
                P.op("pool", lambda e: e.tensor_tensor(out=osq[:], in0=o[:], in1=o[:], op=ALU.mult), reads=[ok], writes=["osq"])
                P.op("dve", lambda e: e.tensor_reduce(out=st2[:], in_=osq[:], axis=AX.X, op=ALU.add), reads=["osq"], writes=["st2"])
                P.op("dve", lambda e: e.tensor_scalar(out=st1[:], in0=st1[:], scalar1=1.0 / 64, scalar2=None, op0=ALU.mult), reads=["st1"], writes=["st1"])
                P.op("dve", lambda e: e.tensor_tensor(out=st3[:], in0=st1[:], in1=st1[:], op=ALU.mult), reads=["st1"], writes=["st3"])
                P.op("dve", lambda e: e.scalar_tensor_tensor(out=st2[:], in0=st2[:], scalar=1.0 / 64, in1=st3[:], op0=ALU.mult, op1=ALU.subtract),
                     reads=["st2", "st3"], writes=["st2"])
                P.op("act", lambda e: e.activation(out=st2[:], in_=st2[:], func=AF.Sqrt, bias=64e-5), reads=["st2"], writes=["st2"])
                P.op("dve", lambda e: e.reciprocal(out=st2[:], in_=st2[:]), reads=["st2"], writes=["st2"])
                P.op("dve", lambda e: e.tensor_tensor(out=osq[:], in0=o[:], in1=st1[:, :].unsqueeze(2).broadcast_to([128, 8, 64]), op=ALU.subtract),
                     reads=[ok, "st1", "osq"], writes=["osq"])
                P.op("dve", lambda e: e.tensor_tensor(out=onb[:], in0=osq[:], in1=st2[:, :].unsqueeze(2).broadcast_to([128, 8, 64]), op=ALU.mult),
                     reads=["osq", "st2"], writes=["onb"])

                def tro(e):
                    r = None
                    for c in range(4):
                        r = e.transpose(out=tpb[:, c * 128:(c + 1) * 128], in_=onb[:, 2 * c:2 * c + 2, :].rearrange("p a b -> p (a b)"), identity=ident[:])
                    return r
                P.op("pe", tro, reads=["onb", "const"], writes=["tpb"])
                P.op("dve", lambda e: e.tensor_tensor(out=tmx[:], in0=tpb[:, 0:512].rearrange("p (a b) -> p a b", a=4), in1=v4bc(GNG), op=ALU.mult),
                     reads=["tpb", "const2", "tmx"], writes=["tmx"])
                P.op("pool", lambda e: e.tensor_tensor(out=tmx[:], in0=tmx[:], in1=v4bc(GNB), op=ALU.add), reads=["tmx", "const2"], writes=["tmx"])
                P.op("pool", lambda e: e.tensor_tensor(out=tmx[:], in0=tmx[:], in1=bonT[s3][:], op=ALU.add), reads=["tmx", f"bonT{s3}"], writes=["tmx"])
                P.op("dve", lambda e: e.tensor_tensor(out=mixT[s2][:, 4:8, :], in0=tmx[:], in1=gT[s3][:], op=ALU.mult),
                     reads=["tmx", f"gT{s3}"], writes=[f"mixT{s2}b"])
                xb = x1t[0]
                xk = "x1t0"
                src = xs[:, :] if is_samp(i) else xw[i * 128:(i + 1) * 128, :]
                P.op("sp", lambda e: e.dma_start(out=xb[:], in_=src), writes=[xk], dma=xk)
                for half in range(2):
                    bank, bk = nextpj()

                    def mo(e, half=half, bank=bank):
                        r = None
                        for kc in range(8):
                            r = e.matmul(bank[:], lhsT=mixT[s2][:, kc, :], rhs=Wout[:, kc, half * 512:(half + 1) * 512], start=(kc == 0), stop=(kc == 7))
                        return r
                    P.op("pe", mo, reads=[f"mixT{s2}a", f"mixT{s2}b", "Wout"], writes=[bk])
                    P.op("dve", lambda e, half=half, bank=bank: e.tensor_tensor(out=xb[:, half * 512:(half + 1) * 512], in0=bank[:],
                                                                                in1=xb[:, half * 512:(half + 1) * 512], op=ALU.add),
                         reads=[bk, xk], writes=[xk])
                P.op("sp", lambda e: e.dma_start(out=x1s[j * 128:(j + 1) * 128, :], in_=xb[:]), reads=[xk], writes=[f"x1s{j}", xk], dma=xk)
                outkeys.append(f"x1s{j}")

            if pA:
                def cap(fns):
                    P.cap = []
                    for fn, i in fns:
                        if 0 <= i < NPT:
                            fn(i)
                    out = P.cap
                    P.cap = None
                    return out

                for step in range(NPT + 6):
                    for fn, i in ((S7, step - 5), (S6, step - 4)):
                        if 0 <= i < NPT and is_own(i):
                            fn(i)
                    if 0 <= step - 4 < NPT:
                        S5n(step - 4)
                    lists = [cap([(lambda i: S4h(i, 0), step - 3)]), cap([(lambda i: S4h(i, 1), step - 3)]),
                             cap([(S3, step - 2), (S2, step - 1), (S1, step)])]
                    chs = [ILV_CH, ILV_CH, 1]
                    pos = [0, 0, 0]
                    while any(pos[q] < len(lists[q]) for q in range(3)):
                        cand = [q for q in range(3) if pos[q] < len(lists[q])]
                        q = min(cand, key=lambda q: pos[q] / len(lists[q]))
                        for _ in range(chs[q]):
                            if pos[q] < len(lists[q]):
                                o = lists[q][pos[q]]
                                P.op(o[0], o[1], reads=o[2], writes=o[3], dma=o[4], n=o[5])
                                pos[q] += 1
                    if 0 <= step - 3 < NPT:
                        S4tail(step - 3)
            elif pSa:
                S1(NPT)
                S2(NPT)
                outkeys.extend(["qT0", "kvT0", "fprev0", "fprev1"] + [f"fT0_{g}" for g in range(4)])
            else:
                sample_cache_loads()
                for fn in (S3, S4, S5, S6, S7):
                    fn(NPT)
            P.op("sp", None, reads=list(outkeys))
            P.emit()
            build.stats[mode] = P.stats

    if "A" in PHASES:
        phase("A", None)
    with ExitStack() as stp:
        per = dict(
            fT=[stp.enter_context(nc.sbuf_tensor("fTs", [128, 14, 128], F32))],
            qT=[stp.enter_context(nc.sbuf_tensor("qTs", [128, 4, 128], BF16))],
            kvT=[stp.enter_context(nc.sbuf_tensor("kvTs", [128, 2, 128], BF16))],
            fprev=stp.enter_context(nc.sbuf_tensor("fprevs", [128, 14, 16], F32)),
        )
        if "Sa" in PHASES:
            phase("Sa", per)
        if "Sb" in PHASES:
            phase("Sb", per)

    if "B" not in PHASES:
        return nc
    with ExitStack() as st:
        def sb(name, shape, dt=F32):
            return st.enter_context(nc.sbuf_tensor(name, shape, dt))

        def psb(name, shape, dt=F32):
            return st.enter_context(nc.psum_tensor(name, shape, dt))
        P = Prog(nc, '_B')
        outk = []
        Wg = sb("Wg", [128, 8, D_FF], BF16)
        Wu = sb("Wu", [128, 8, D_FF], BF16)
        Wd = sb("Wd", [128, NFC, 1024], BF16)
        gffn = sb("gffn", [128, 1024])
        gfin = sb("gfin", [128, 1024])
        identb = sb("identb", [128, 128], BF16)

        P.op("pool", lambda e: e.dma_start(out=identb[:], in_=ident_d[:, :]), writes=["W"], dma="Wi")
        P.op("pool", lambda e: [e.dma_start(out=Wg[:, kc, :], in_=w_gate[kc * 128:(kc + 1) * 128, :]) for kc in range(8)],
             writes=["Wg"], dma="Wg", n=8)
        P.op("pool", lambda e: [e.dma_start(out=Wu[:, kc, :], in_=w_up[kc * 128:(kc + 1) * 128, :]) for kc in range(8)],
             writes=["Wu"], dma="Wu", n=8)
        P.op("pool", lambda e: [e.dma_start(out=Wd[:, fc, :], in_=w_down[fc * 128:(fc + 1) * 128, :]) for fc in range(NFC)],
             writes=["Wd"], dma="Wd", n=NFC)

        def cl(e):
            return [e.dma_start(out=gffn[:], in_=gvec[1:2, :].broadcast_to([128, 1024])),
                    e.dma_start(out=gfin[:], in_=gvec[2:3, :].broadcast_to([128, 1024]))]
        P.op("sp", cl, writes=["G"], dma="G", n=2)
        xgs = [sb(f"xg{q}", [128, 4, 1024]) for q in range(2)]
        junk2 = sb("junk2", [128, 1024], BF16)
        ub = sb("ub", [128, 1024], BF16)
        uT = sb("uT", [128, 8, 512], BF16)
        actT = sb("actT", [128, 11, 512], BF16)
        sgt = sb("sgt", [128, 512])
        ssb = sb("ssb", [128, 1])
        rsb = sb("rsb", [128, 1])
        yb = [sb(f"yb{q}", [128, 1024]) for q in range(2)]
        pg = [psb(f"pg{q}", [128, 512]) for q in range(2)]
        pu = [psb(f"pu{q}", [128, 512]) for q in range(2)]
        pd = [psb(f"pd{q}", [128, 512]) for q in range(2)]
        tpb2 = psb("tpb2", [128, 1024], BF16)
        cnt = [0]
        groups = [(0, 4), (4, 4), (8, 4), (12, 4), (16, 1)]

        def pro_load(g):
            t0, nt = groups[g]
            xg = xgs[g % 2]
            P.op("sp", lambda e: e.dma_start(out=xg[:, 0:nt, :], in_=x1s[t0 * 128:(t0 + nt) * 128, :].rearrange("(a p) d -> p a d", p=128)),
                 writes=[f"xg{g % 2}"], dma=f"xg{g % 2}")

        def pro_tile(g, a):
            xg = xgs[g % 2]
            xk = f"xg{g % 2}"
            P.op("act", lambda e: e.activation(out=junk2[:], in_=xg[:, a, :], func=AF.Square, accum_out=ssb[:]), reads=[xk], writes=["junk2", "ssb"])
            P.op("act", lambda e: e.activation(out=rsb[:], in_=ssb[:], func=AF.Sqrt, scale=1.0 / 1024, bias=1e-6), reads=["ssb"], writes=["rsb"])
            P.op("dve", lambda e: e.reciprocal(out=rsb[:], in_=rsb[:]), reads=["rsb"], writes=["rsb"])
            P.op("dve", lambda e: e.scalar_tensor_tensor(out=ub[:], in0=xg[:, a, :], scalar=rsb[:, 0:1], in1=gffn[:], op0=ALU.mult, op1=ALU.mult),
                 reads=[xk, "rsb", "G"], writes=["ub"])

            def tr(e):
                r = None
                for kc in range(8):
                    r = e.transpose(out=tpb2[:, kc * 128:(kc + 1) * 128], in_=ub[:, kc * 128:(kc + 1) * 128], identity=identb[:])
                return r
            P.op("pe", tr, reads=["ub", "W"], writes=["tpb2"])
            P.op("act", lambda e: e.copy(out=uT[:, :, a * 128:(a + 1) * 128], in_=tpb2[:].rearrange("p (a b) -> p a b", a=8)),
                 reads=["tpb2"], writes=[f"uT{a}"])

        pro_load(0)
        for a in range(groups[0][1]):
            pro_tile(0, a)
        for g, (t0, nt) in enumerate(groups):
            N = nt * 128
            xg = xgs[g % 2]
            xk = f"xg{g % 2}"
            if g + 1 < len(groups):
                pro_load(g + 1)
            uk = [f"uT{a}" for a in range(nt)]
            for hf in range(2):
                for fi in range(11):
                    fc = hf * 11 + fi
                    cnt[0] += 1
                    b = cnt[0] % 2

                    def mg(e, fc=fc, b=b, N=N):
                        r = None
                        for kc in range(8):
                            r = e.matmul(pg[b][:, 0:N], lhsT=Wg[:, kc, fc * 128:(fc + 1) * 128], rhs=uT[:, kc, 0:N], start=(kc == 0), stop=(kc == 7))
                        return r
                    P.op("pe", mg, reads=["Wg"] + uk, writes=[f"pg{b}"])

                    def mu_(e, fc=fc, b=b, N=N):
                        r = None
                        for kc in range(8):
                            r = e.matmul(pu[b][:, 0:N], lhsT=Wu[:, kc, fc * 128:(fc + 1) * 128], rhs=uT[:, kc, 0:N], start=(kc == 0), stop=(kc == 7))
                        return r
                    P.op("pe", mu_, reads=["Wu"] + uk, writes=[f"pu{b}"])
                    P.op("act", lambda e, b=b, N=N: e.activation(out=sgt[:, 0:N], in_=pg[b][:, 0:N], func=AF.Silu), reads=[f"pg{b}"], writes=["sgt"])
                    P.op("dve", lambda e, b=b, N=N, fi=fi: e.tensor_tensor(out=actT[:, fi, 0:N], in0=pu[b][:, 0:N], in1=sgt[:, 0:N], op=ALU.mult),
                         reads=[f"pu{b}", "sgt"], writes=[f"actT{fi}"])
                ak = [f"actT{fi}" for fi in range(11)]
                for a in range(nt):
                    for half in range(2):
                        cnt[0] += 1
                        b = cnt[0] % 2

                        def md(e, a=a, half=half, b=b, hf=hf):
                            r = None
                            for fi in range(11):
                                r = e.matmul(pd[b][:], lhsT=actT[:, fi, a * 128:(a + 1) * 128], rhs=Wd[:, hf * 11 + fi, half * 512:(half + 1) * 512],
                                             start=(fi == 0), stop=(fi == 10))
                            return r
                        P.op("pe", md, reads=["Wd"] + ak, writes=[f"pd{b}"])
                        P.op("dve", lambda e, a=a, half=half, b=b, xg=xg: e.tensor_tensor(out=xg[:, a, half * 512:(half + 1) * 512], in0=pd[b][:],
                                                                                          in1=xg[:, a, half * 512:(half + 1) * 512], op=ALU.add),
                             reads=[f"pd{b}", xk], writes=[xk])
                    if hf == 1 and g + 1 < len(groups) and a < groups[g + 1][1]:
                        pro_tile(g + 1, a)
            for a in range(nt):
                t = t0 + a
                y = yb[t % 2]
                yk = f"yb{t % 2}"
                P.op("act", lambda e, a=a, xg=xg: e.activation(out=junk2[:], in_=xg[:, a, :], func=AF.Square, accum_out=ssb[:]), reads=[xk], writes=["junk2", "ssb"])
                P.op("act", lambda e: e.activation(out=rsb[:], in_=ssb[:], func=AF.Sqrt, scale=1.0 / 1024, bias=1e-6), reads=["ssb"], writes=["rsb"])
                P.op("dve", lambda e: e.reciprocal(out=rsb[:], in_=rsb[:]), reads=["rsb"], writes=["rsb"])
                P.op("dve", lambda e, a=a, y=y, xg=xg: e.scalar_tensor_tensor(out=y[:], in0=xg[:, a, :], scalar=rsb[:, 0:1], in1=gfin[:], op0=ALU.mult, op1=ALU.mult),
                     reads=[xk, "rsb", "G"], writes=[yk])
                dst = y_s[:, :] if t == 16 else y_p[t * 128:(t + 1) * 128, :]
                P.op("sp", lambda e, y=y, dst=dst: e.dma_start(out=dst, in_=y[:]), reads=[yk], writes=[f"oy{t}", yk], dma=yk)
                outk.append(f"oy{t}")
        P.op("sp", None, reads=outk)
        P.emit()
        build.stats["B"] = P.stats
    return nc


def _consts(p):
    s = np.arange(128)[:, None]
    t = np.arange(128)[None, :]
    su = (s < t).astype(np.float32)
    ui = (s <= t).astype(np.float32)
    sl = (s > t).astype(np.float32)
    same = ((s // 8) == (t // 8)).astype(np.float32)
    mfirst = sl if p > 0 else np.zeros_like(sl)
    masks = np.stack([su, ui, sl, su * same, ui * same, sl * same, mfirst], axis=1).reshape(128, 7 * 128)
    ident = np.eye(128, dtype=np.float32)
    bones = ((s // 64) == (t // 64)).astype(np.float32)
    rm = np.ones((128, 2, 128), np.float32)
    rm[:, 1, :] = (np.arange(128) % 8 != 0).astype(np.float32)[None, :]
    e16 = ((np.arange(128)[:, None] // 8) == np.arange(16)[None, :]).astype(np.float32)
    return dict(masks=np.ascontiguousarray(masks), ident=ident, bones=bones, rmask=rm.reshape(128, 256), e16=e16)


_NC = [None]


def kernel(x_prompt, x_sample, cache_k, cache_v, state_wkv, state_shift, g_mix, w_in, attn_sinks,
           rwkv_mu, w0, w2, a0, a2, g2, k_k, k_a, r_k, gn_g, gn_b, w_out, g_ffn, w_gate, w_up,
           w_down, g_final):
    f = lambda a: np.ascontiguousarray(np.asarray(a, dtype=np.float32))
    x_prompt, x_sample = f(x_prompt), f(x_sample)
    w_in0 = f(w_in)[0]
    qperm = np.concatenate([np.r_[j * 64:(j + 1) * 64, (4 + j) * 64:(5 + j) * 64] for j in range(4)])
    w_in_p = np.ascontiguousarray(np.concatenate([w_in0[:, qperm], w_in0[:, 512:]], axis=1))
    fm4 = lambda v: f(v).reshape(4, 128).T
    vec4 = np.ascontiguousarray(np.stack([fm4(w0[0]), fm4(a0[0]), fm4(k_k[0]), fm4(k_a[0]), fm4(f(r_k)[0].reshape(-1)),
                                          fm4(gn_g[0]), fm4(gn_b[0])], axis=1).reshape(128, 28))
    shared = dict(
        w_in=w_in_p, w_out=f(w_out)[0], w_gate=f(w_gate)[0], w_up=f(w_up)[0], w_down=f(w_down)[0],
        gvec=np.ascontiguousarray(np.stack([f(g_mix)[0], f(g_ffn)[0], f(g_final)], axis=0)),
        mu=np.ascontiguousarray(f(rwkv_mu)[0].reshape(14, 128).T),
        vec4=vec4,
        w2a2=np.ascontiguousarray(np.concatenate([f(w2)[0], f(a2)[0]], axis=0)),
        g2=f(g2)[0],
        sinks=f(attn_sinks)[0].reshape(1, 8),
    )
    in_maps = []
    for c in range(8):
        b, p = c // 4, c % 4
        xwin = np.zeros((NPT * 128, 1024), np.float32)
        nreal = (p + 1) * 2048
        xwin[NPT * 128 - nreal:] = x_prompt[b, 0:nreal]
        m = dict(shared)
        m.update(_consts(p))
        m.update(
            xw=xwin,
            xs=np.ascontiguousarray(x_sample[16 * c:16 * c + 16].reshape(128, 1024)),
            hprev=f(state_shift)[0, 16 * c:16 * c + 16],
            ck=np.ascontiguousarray(f(cache_k)[0, 16 * c:16 * c + 16].reshape(16, 128, 128)),
            cv=np.ascontiguousarray(f(cache_v)[0, 16 * c:16 * c + 16].reshape(16, 128, 128)),
            swkv=np.ascontiguousarray(f(state_wkv)[0, 16 * c:16 * c + 16]),
        )
        in_maps.append(m)
    if _NC[0] is None:
        _NC[0] = build()
    res = run_bass_kernel_spmd(_NC[0], in_maps, core_ids=list(range(8)))
    R = res.results
    y_prompt = np.stack([np.concatenate([R[b * 4 + p]["y_p"] for p in range(4)], axis=0) for b in range(2)], axis=0)
    y_sample = np.concatenate([R[c]["y_s"].reshape(16, 8, 1024) for c in range(8)], axis=0)
    kp = np.stack([R[b * 4 + 3]["kwin_p"].reshape(128, 2, 64) for b in range(2)], axis=0)[None]
    vp = np.stack([R[b * 4 + 3]["vwin_p"].reshape(128, 2, 64) for b in range(2)], axis=0)[None]
    sp = np.stack([R[b * 4 + 3]["wkv_p"] for b in range(2)], axis=0)[None]
    hp = np.stack([R[b * 4 + 3]["shift_p"].reshape(1024) for b in range(2)], axis=0)[None]
    ks = np.concatenate([R[c]["kwin_s"].reshape(16, 128, 2, 64) for c in range(8)], axis=0)[None]
    vs = np.concatenate([R[c]["vwin_s"].reshape(16, 128, 2, 64) for c in range(8)], axis=0)[None]
    ss_ = np.concatenate([R[c]["wkv_s"] for c in range(8)], axis=0)[None]
    hs = np.concatenate([R[c]["shift_s"] for c in range(8)], axis=0)[None]
    return tuple(np.ascontiguousarray(a.astype(np.float32)) for a in (y_prompt, y_sample, kp, vp, sp, hp, ks, vs, ss_, hs))
```
